# Optimizing a Trainium2 kernel written in Bass

```python
import jax, jax.numpy as jnp
from jax import lax
import numpy as np

D_MODEL = 1024
BATCH = 8
SEQ = 2048
DEPTH = 1
DEC_BATCH = 8
DEC_SEQ = 8192
PAST_LEN = 128

RET_HEADS = 8
RET_HEAD_DIM = 64
RET_WIDTH = RET_HEADS * RET_HEAD_DIM
RET_CHUNK = 128
MLA_HEADS = 8
MLA_NOPE = 64
MLA_ROPE = 32
MLA_V = 64
MLA_Q_RANK = 384
MLA_KV_RANK = 256
MLA_WIDTH = MLA_HEADS * MLA_V
Q_BLOCK = 128
IN_WIDTH = 4 * RET_WIDTH + MLA_Q_RANK + MLA_KV_RANK + MLA_ROPE
MIX_WIDTH = RET_WIDTH + MLA_WIDTH
MEM_TOKENS = 256
MEM_HEADS = 4
MEM_HEAD_DIM = D_MODEL // MEM_HEADS
PEER_HEADS = 8
PEER_KEYS = 128
PEER_EXPERTS = PEER_KEYS * PEER_KEYS
PEER_TOPK = 16
PEER_QUERY_DIM = 256
PEER_BLOCK = 128

ROPE_BASE = 10000.0
NORM_EPS = 1e-6
GN_EPS = 1e-5

kernel_name = "hybrid_retention_mla_peer_encoder"


def rms_norm(x, w):
    xf = x.astype(jnp.float32)
    y = xf * lax.rsqrt(jnp.mean(xf * xf, axis=-1, keepdims=True) + NORM_EPS)
    return (y * w.astype(jnp.float32)).astype(x.dtype)


def rope(x, pos):
    d = x.shape[-1]
    inv_freq = 1.0 / (ROPE_BASE ** (jnp.arange(0, d, 2, dtype=jnp.float32) / d))
    ang = pos[:, None] * inv_freq[None, :]
    cos = jnp.cos(ang)[None, :, None, :].astype(x.dtype)
    sin = jnp.sin(ang)[None, :, None, :].astype(x.dtype)
    x1, x2 = x[..., : d // 2], x[..., d // 2:]
    return jnp.concatenate([x1 * cos - x2 * sin, x1 * sin + x2 * cos], axis=-1)


def retention_scan(q, k, v, log_g, strict):
    B, H, S, d = q.shape
    C = RET_CHUNK
    n = S // C
    to_chunks = lambda t: t.reshape(B, H, n, C, t.shape[-1]).transpose(2, 0, 1, 3, 4)
    idx = jnp.arange(C, dtype=jnp.float32)
    diff = idx[:, None] - idx[None, :]
    mask = (diff > 0) if strict else (diff >= 0)
    D = jnp.where(mask[None], jnp.exp(jnp.maximum(diff, 0.0)[None] * log_g[:, None, None]), 0.0)
    decay_q = jnp.exp((idx[None, :] + 1.0) * log_g[:, None])
    decay_k = jnp.exp((C - 1.0 - idx[None, :]) * log_g[:, None])
    decay_chunk = jnp.exp(C * log_g)

    def step(R, xs):
        qc, kc, vc = xs
        s = jnp.einsum('bhid,bhjd->bhij', qc, kc) * D[None]
        inner = jnp.einsum('bhij,bhjv->bhiv', s, vc)
        cross = jnp.einsum('bhid,bhdv->bhiv', qc, R) * decay_q[None, :, :, None]
        R = R * decay_chunk[None, :, None, None] + jnp.einsum('bhjd,hj,bhjv->bhdv', kc, decay_k, vc)
        return R, inner + cross

    R0 = jnp.zeros((B, H, d, v.shape[-1]), jnp.float32)
    _, out = lax.scan(step, R0, (to_chunks(q), to_chunks(k), to_chunks(v)))
    return out.transpose(1, 2, 0, 3, 4).reshape(B, H, S, v.shape[-1])


def bi_retention(q, k, v, dec_f, dec_b, gn_w):
    B, S, H, d = q.shape
    qf, kf, vf = (t.astype(jnp.float32).transpose(0, 2, 1, 3) for t in (q, k, v))
    kf = kf * (RET_HEAD_DIM ** -0.5)
    lg_f = jax.nn.log_sigmoid(dec_f.astype(jnp.float32))
    lg_b = jax.nn.log_sigmoid(dec_b.astype(jnp.float32))
    fwd = retention_scan(qf, kf, vf, lg_f, strict=False)
    bwd = jnp.flip(retention_scan(jnp.flip(qf, 2), jnp.flip(kf, 2), jnp.flip(vf, 2), lg_b, strict=True), 2)
    y = fwd + bwd
    mu = jnp.mean(y, axis=-1, keepdims=True)
    var = jnp.mean(jnp.square(y - mu), axis=-1, keepdims=True)
    y = (y - mu) * lax.rsqrt(var + GN_EPS) * gn_w.astype(jnp.float32).reshape(1, H, 1, d)
    return y.transpose(0, 2, 1, 3).reshape(B, S, H * d)


def mla_attention(q, k_nope, k_rope, v):
    B, S, H, _ = q.shape
    nb = S // Q_BLOCK
    scale = (MLA_NOPE + MLA_ROPE) ** -0.5
    qb = q.reshape(B, nb, Q_BLOCK, H, -1).transpose(1, 0, 2, 3, 4)

    def blk(qc):
        s = (jnp.einsum('bqhd,bkhd->bhqk', qc[..., :MLA_NOPE], k_nope)
             + jnp.einsum('bqhr,bkr->bhqk', qc[..., MLA_NOPE:], k_rope))
        p = jax.nn.softmax(s.astype(jnp.float32) * scale, axis=-1).astype(v.dtype)
        return jnp.einsum('bhqk,bkhd->bqhd', p, v)

    o = lax.map(blk, qb)
    return o.transpose(1, 0, 2, 3, 4).reshape(B, S, H * MLA_V)


def parallel_mixer(xn, w_in, dec_f, dec_b, gn_w, qn_w, w_uq, kvn_w, w_ukv, w_out):
    B, S, _ = xn.shape
    proj = xn @ w_in
    cuts = [RET_WIDTH, 2 * RET_WIDTH, 3 * RET_WIDTH, 4 * RET_WIDTH,
            4 * RET_WIDTH + MLA_Q_RANK, 4 * RET_WIDTH + MLA_Q_RANK + MLA_KV_RANK]
    rq, rk, rv, rg, cq, ckv, kr = jnp.split(proj, cuts, axis=-1)
    pos = jnp.arange(S, dtype=jnp.float32)
    rq = rope(rq.reshape(B, S, RET_HEADS, RET_HEAD_DIM), pos)
    rk = rope(rk.reshape(B, S, RET_HEADS, RET_HEAD_DIM), pos)
    rv = rv.reshape(B, S, RET_HEADS, RET_HEAD_DIM)
    ret = bi_retention(rq, rk, rv, dec_f, dec_b, gn_w)
    ret_out = (ret * jax.nn.silu(rg.astype(jnp.float32))).astype(xn.dtype)
    q = (rms_norm(cq, qn_w) @ w_uq).reshape(B, S, MLA_HEADS, MLA_NOPE + MLA_ROPE)
    q = jnp.concatenate([q[..., :MLA_NOPE], rope(q[..., MLA_NOPE:], pos)], axis=-1)
    kv = (rms_norm(ckv, kvn_w) @ w_ukv).reshape(B, S, MLA_HEADS, MLA_NOPE + MLA_V)
    k_nope, v = kv[..., :MLA_NOPE], kv[..., MLA_NOPE:]
    k_rope = rope(kr[:, :, None, :], pos)[:, :, 0, :]
    mla_out = mla_attention(q, k_nope, k_rope, v)
    return jnp.concatenate([ret_out, mla_out], axis=-1) @ w_out


def memory_attention(hn, memn, wq, wkv, wo):
    B, S, _ = hn.shape
    M = memn.shape[1]
    q = (hn @ wq).reshape(B, S, MEM_HEADS, MEM_HEAD_DIM)
    kv = (memn @ wkv).reshape(B, M, 2, MEM_HEADS, MEM_HEAD_DIM)
    k, v = kv[:, :, 0], kv[:, :, 1]
    s = jnp.einsum('bqhd,bkhd->bhqk', q, k).astype(jnp.float32) * (MEM_HEAD_DIM ** -0.5)
    p = jax.nn.softmax(s, axis=-1).astype(v.dtype)
    o = jnp.einsum('bhqk,bkhd->bqhd', p, v).reshape(B, S, D_MODEL)
    return o @ wo


def peer_ffn(xn, wq, sub_keys, U, V):
    B, S, D = xn.shape
    nb = (B * S) // PEER_BLOCK
    half = PEER_QUERY_DIM // 2
    xb = xn.reshape(nb, PEER_BLOCK, D)

    def blk(xt):
        q = (xt @ wq).reshape(PEER_BLOCK, PEER_HEADS, PEER_QUERY_DIM)
        s1 = jnp.einsum('thd,hkd->thk', q[..., :half], sub_keys[0])
        s2 = jnp.einsum('thd,hkd->thk', q[..., half:], sub_keys[1])
        v1, i1 = lax.top_k(s1, PEER_TOPK)
        v2, i2 = lax.top_k(s2, PEER_TOPK)
        cand = (v1[..., :, None] + v2[..., None, :]).reshape(PEER_BLOCK, PEER_HEADS, PEER_TOPK * PEER_TOPK)
        cidx = (i1[..., :, None] * PEER_KEYS + i2[..., None, :]).reshape(PEER_BLOCK, PEER_HEADS, PEER_TOPK * PEER_TOPK)
        ts, sel = lax.top_k(cand, PEER_TOPK)
        ids = jnp.take_along_axis(cidx, sel, axis=-1).reshape(PEER_BLOCK, PEER_HEADS * PEER_TOPK)
        g = jax.nn.softmax(ts.astype(jnp.float32), axis=-1).reshape(PEER_BLOCK, PEER_HEADS * PEER_TOPK)
        a = jnp.einsum('td,ted->te', xt, U[ids]).astype(jnp.float32)
        w = (g * jax.nn.gelu(a, approximate=False)).astype(xt.dtype)
        return jnp.einsum('te,ted->td', w, V[ids])

    return lax.map(blk, xb).reshape(B, S, D)


def encoder(x, mem, norm_mix_w, w_in, ret_decay_fwd, ret_decay_bwd, ret_gn_w, mla_q_norm_w, mla_w_uq,
            mla_kv_norm_w, mla_w_ukv, w_out, norm_ca_w, norm_mem_w, ca_wq, ca_wkv, ca_wo,
            norm_ffn_w, peer_wq, peer_sub_keys, peer_u, peer_v, final_norm_w):
    for l in range(DEPTH):
        x = x + parallel_mixer(rms_norm(x, norm_mix_w[l]), w_in[l], ret_decay_fwd[l], ret_decay_bwd[l],
                               ret_gn_w[l], mla_q_norm_w[l], mla_w_uq[l], mla_kv_norm_w[l], mla_w_ukv[l], w_out[l])
        x = x + memory_attention(rms_norm(x, norm_ca_w[l]), rms_norm(mem, norm_mem_w[l]),
                                 ca_wq[l], ca_wkv[l], ca_wo[l])
        x = x + peer_ffn(rms_norm(x, norm_ffn_w[l]), peer_wq[l], peer_sub_keys[l], peer_u[l], peer_v[l])
    return rms_norm(x, final_norm_w)


def setup_inputs(seed: int = 0) -> dict:
    key = jax.random.key(seed)
    ks = jax.random.split(key, 26)
    f32 = jnp.float32
    nrm = lambda k, shape, scale: jax.random.normal(k, shape, f32) * scale
    gain = lambda k, shape: 1.0 + 0.02 * jax.random.normal(k, shape, f32)
    h = jnp.arange(RET_HEADS, dtype=f32)
    p = 2.0 ** (-5.0 - h)
    decay_logit = jnp.log1p(-p) - jnp.log(p)
    L = DEPTH
    return {
        "x_prompt": nrm(ks[0], (BATCH, SEQ, D_MODEL), 1.0),
        "x_sample": nrm(ks[1], (DEC_BATCH, DEC_SEQ, D_MODEL), 1.0),
        "mem_prompt": nrm(ks[2], (BATCH, MEM_TOKENS, D_MODEL), 1.0),
        "mem_sample": nrm(ks[3], (DEC_BATCH, MEM_TOKENS, D_MODEL), 1.0),
        "norm_mix_w": gain(ks[4], (L, D_MODEL)),
        "w_in": nrm(ks[5], (L, D_MODEL, IN_WIDTH), D_MODEL ** -0.5),
        "ret_decay_fwd": decay_logit[None] + 0.01 * jax.random.normal(ks[6], (L, RET_HEADS), f32),
        "ret_decay_bwd": decay_logit[None] + 0.01 * jax.random.normal(ks[7], (L, RET_HEADS), f32),
        "ret_gn_w": gain(ks[8], (L, RET_WIDTH)),
        "mla_q_norm_w": gain(ks[9], (L, MLA_Q_RANK)),
        "mla_w_uq": nrm(ks[10], (L, MLA_Q_RANK, MLA_HEADS * (MLA_NOPE + MLA_ROPE)), MLA_Q_RANK ** -0.5),
        "mla_kv_norm_w": gain(ks[11], (L, MLA_KV_RANK)),
        "mla_w_ukv": nrm(ks[12], (L, MLA_KV_RANK, MLA_HEADS * (MLA_NOPE + MLA_V)), MLA_KV_RANK ** -0.5),
        "w_out": nrm(ks[13], (L, MIX_WIDTH, D_MODEL), MIX_WIDTH ** -0.5),
        "norm_ca_w": gain(ks[14], (L, D_MODEL)),
        "norm_mem_w": gain(ks[15], (L, D_MODEL)),
        "ca_wq": nrm(ks[16], (L, D_MODEL, D_MODEL), D_MODEL ** -0.5),
        "ca_wkv": nrm(ks[17], (L, D_MODEL, 2 * D_MODEL), D_MODEL ** -0.5),
        "ca_wo": nrm(ks[18], (L, D_MODEL, D_MODEL), D_MODEL ** -0.5),
        "norm_ffn_w": gain(ks[19], (L, D_MODEL)),
        "peer_wq": nrm(ks[20], (L, D_MODEL, PEER_HEADS * PEER_QUERY_DIM), D_MODEL ** -0.5),
        "peer_sub_keys": nrm(ks[21], (L, 2, PEER_HEADS, PEER_KEYS, PEER_QUERY_DIM // 2), (PEER_QUERY_DIM // 2) ** -0.5),
        "peer_u": nrm(ks[22], (L, PEER_EXPERTS, D_MODEL), D_MODEL ** -0.5),
        "peer_v": nrm(ks[23], (L, PEER_EXPERTS, D_MODEL), (PEER_HEADS * PEER_TOPK) ** -0.5),
        "final_norm_w": gain(ks[24], (D_MODEL,)),
    }


def reference(x_prompt, x_sample, mem_prompt, mem_sample, norm_mix_w, w_in, ret_decay_fwd, ret_decay_bwd,
              ret_gn_w, mla_q_norm_w, mla_w_uq, mla_kv_norm_w, mla_w_ukv, w_out, norm_ca_w, norm_mem_w,
              ca_wq, ca_wkv, ca_wo, norm_ffn_w, peer_wq, peer_sub_keys, peer_u, peer_v, final_norm_w):
    y_prompt = encoder(x_prompt, mem_prompt, norm_mix_w, w_in, ret_decay_fwd, ret_decay_bwd, ret_gn_w,
                       mla_q_norm_w, mla_w_uq, mla_kv_norm_w, mla_w_ukv, w_out, norm_ca_w, norm_mem_w,
                       ca_wq, ca_wkv, ca_wo, norm_ffn_w, peer_wq, peer_sub_keys, peer_u, peer_v, final_norm_w)
    y_sample = encoder(x_sample, mem_sample, norm_mix_w, w_in, ret_decay_fwd, ret_decay_bwd, ret_gn_w,
                       mla_q_norm_w, mla_w_uq, mla_kv_norm_w, mla_w_ukv, w_out, norm_ca_w, norm_mem_w,
                       ca_wq, ca_wkv, ca_wo, norm_ffn_w, peer_wq, peer_sub_keys, peer_u, peer_v, final_norm_w)
    return (y_prompt, y_sample)
```

```python
from contextlib import ExitStack
import numpy as np
import ml_dtypes
import concourse.bass as bass
import concourse.mybir as mybir
from concourse.bass_utils import run_bass_kernel_spmd

F32 = mybir.dt.float32
BF16 = mybir.dt.bfloat16
U32 = mybir.dt.uint32
AF = mybir.ActivationFunctionType
ALU = mybir.AluOpType
AX = mybir.AxisListType

SEM_LIMIT = 30000
D = 1024
NEG = -1.0e30


class Buf:
    __slots__ = ("name", "w", "r", "multi", "dsem", "dcnt", "excl")

    def __init__(self, name, multi=False, excl=False):
        self.name = name
        self.excl = excl
        self.w = {}
        self.r = {}
        self.multi = multi
        self.dsem = None
        self.dcnt = 0


class Eng:
    def __init__(self, kb, name, eng, own_wait=True):
        self.kb = kb
        self.name = name
        self.eng = eng
        self.sem = None
        self.val = 0
        self.seen = {}
        self.own_wait = own_wait
        self.n = 0

    def next_event(self):
        if self.sem is None or self.val >= SEM_LIMIT:
            self.sem = self.kb.new_sem(self.name)
            self.val = 0
        self.val += 1
        return (self.sem, self.val)


class Slots:
    def __init__(self, items):
        self.items = items
        self.i = 0

    def next(self):
        it = self.items[self.i % len(self.items)]
        self.i += 1
        return it


class KB:
    def __init__(self, nc, stack):
        self.nc = nc
        self.stack = stack
        self.nsem = 0
        self.pe = Eng(self, "pe", nc.tensor, own_wait=False)
        self.dve = Eng(self, "dve", nc.vector)
        self.act = Eng(self, "act", nc.scalar)
        self.pool = Eng(self, "pool", nc.gpsimd)
        self.sp = Eng(self, "sp", nc.sync)
        self.engs = [self.pe, self.dve, self.act, self.pool, self.sp]
        self.anchors = []
        self.free_sems = []
        self.nops = 0
        import os
        self.limit = int(os.environ.get("OPLIMIT", "1000000000"))
        self.phase_mark = 0
        self.persist = []
        self.uid = 0
        self.PS = None
        self.bank_bufs = None
        self.bank_pool = list(range(8))
        self.bank_i = 0

    def new_sem(self, name):
        self.nsem += 1
        return self.stack.enter_context(self.nc.semaphore(f"s{self.nsem}_{name}"))

    def sb(self, shape, dtype, name="sb", stack=None):
        self.uid += 1
        st = stack or self.stack
        return st.enter_context(self.nc.sbuf_tensor(f"{name}_{self.uid}", list(shape), dtype))

    def sbb(self, shape, dtype, name="sb", stack=None):
        return self.sb(shape, dtype, name, stack), Buf(name)

    def slots(self, shape, dtype, n, name, stack=None):
        return Slots([self.sbb(shape, dtype, f"{name}{i}", stack) for i in range(n)])

    def init_psum(self):
        self.PS = self.stack.enter_context(self.nc.psum_tensor("PSALL", [128, 8 * 512], F32))
        self.bank_bufs = [Buf(f"bank{i}", excl=True) for i in range(8)]

    def set_banks(self, pool):
        self.bank_pool = list(pool)
        self.bank_i = 0

    def bank(self, n=1):
        L = len(self.bank_pool)
        while True:
            i = self.bank_i % L
            idx = self.bank_pool[i:i + n]
            if len(idx) == n and all(idx[k] == idx[0] + k for k in range(n)):
                break
            self.bank_i += 1
        self.bank_i += n
        b0 = idx[0]
        return self.PS[:, b0 * 512:(b0 + n) * 512], [self.bank_bufs[b] for b in idx]

    def fixed_bank(self, b0, n=1):
        return self.PS[:, b0 * 512:(b0 + n) * 512], [self.bank_bufs[b] for b in range(b0, b0 + n)]

    def _waits(self, E, reads, writes):
        need = {}

        def add(d):
            for k, ev in d.items():
                if k not in need or need[k][1] < ev[1]:
                    need[k] = ev
        own = id(E.sem) if E.sem is not None else None
        for b in reads:
            add(b.w)
            if b.excl:
                add({k: ev for k, ev in b.r.items() if k != own})
        for b in writes:
            if not b.multi:
                add(b.w)
            add(b.r)
        for k, (sem, val) in need.items():
            if (not E.own_wait) and E.sem is not None and k == id(E.sem):
                continue
            if E.seen.get(k, 0) >= val:
                continue
            E.eng.wait_ge(sem, val)
            E.seen[k] = val

    def _record(self, ev, reads, writes):
        k = id(ev[0])
        for b in reads:
            if k not in b.r or b.r[k][1] < ev[1]:
                b.r[k] = ev
        for b in writes:
            if b.multi:
                if k not in b.w or b.w[k][1] < ev[1]:
                    b.w[k] = ev
            else:
                b.w = {k: ev}
            b.r = {}

    def op(self, E, fn, reads=(), writes=()):
        self.nops += 1
        if self.nops > self.limit:
            return None
        self._waits(E, reads, writes)
        inst = fn()
        ev = E.next_event()
        inst.then_inc(ev[0], 1)
        self._record(ev, reads, writes)
        E.n += 1
        return inst

    def dma(self, Q, out, in_, anchor, reads=(), writes=(), **kw):
        self.nops += 1
        if self.nops > self.limit:
            return None
        self._waits(Q, reads, writes)
        if anchor.dsem is None:
            if self.free_sems:
                anchor.dsem, anchor.dcnt = self.free_sems.pop()
            else:
                anchor.dsem, anchor.dcnt = self.new_sem("d_" + anchor.name), 0
            self.anchors.append(anchor)
        inst = Q.eng.dma_start(out=out, in_=in_, **kw)
        anchor.dcnt += 16
        ev = (anchor.dsem, anchor.dcnt)
        inst.then_inc(anchor.dsem, 16)
        self._record(ev, reads, writes)
        Q.n += 1
        return inst

    def barrier(self):
        evs = {}
        for E in self.engs:
            if E.sem is not None:
                evs[id(E.sem)] = (E.sem, E.val)
        for a in self.anchors:
            evs[id(a.dsem)] = (a.dsem, a.dcnt)
        S = self.sp
        for k, (sem, val) in evs.items():
            if S.seen.get(k, 0) >= val:
                continue
            S.eng.wait_ge(sem, val)
            S.seen[k] = val
        inst = S.eng.nop()
        ev = S.next_event()
        inst.then_inc(ev[0], 1)
        for E in self.engs:
            if E is S:
                continue
            E.eng.wait_ge(ev[0], ev[1])
            E.seen[id(ev[0])] = ev[1]
            for k, (sem, val) in evs.items():
                if E.seen.get(k, 0) < val:
                    E.seen[k] = val
        S.seen[id(ev[0])] = ev[1]


    def begin_phase(self):
        self.phase_mark = len(self.anchors)

    def end_phase(self):
        self.barrier()
        dead = self.anchors[self.phase_mark:]
        self.anchors = self.anchors[:self.phase_mark]
        for a in dead:
            if a.dcnt <= 20000:
                self.free_sems.append((a.dsem, a.dcnt))
            a.dsem = None
        for b in self.persist:
            b.w = {}
            b.r = {}


W_NAMES = ["norm_mix_w", "w_in", "ret_decay_fwd", "ret_decay_bwd", "ret_gn_w", "mla_q_norm_w", "mla_w_uq",
           "mla_kv_norm_w", "mla_w_ukv", "w_out", "norm_ca_w", "norm_mem_w", "ca_wq", "ca_wkv", "ca_wo",
           "norm_ffn_w", "peer_wq", "peer_sub_keys", "peer_u", "peer_v", "final_norm_w"]
W_SHAPES = {
    "norm_mix_w": [1, 1024], "w_in": [1, 1024, 2720], "ret_decay_fwd": [1, 8], "ret_decay_bwd": [1, 8],
    "ret_gn_w": [1, 512], "mla_q_norm_w": [1, 384], "mla_w_uq": [1, 384, 768], "mla_kv_norm_w": [1, 256],
    "mla_w_ukv": [1, 256, 1024], "w_out": [1, 1024, 1024], "norm_ca_w": [1, 1024], "norm_mem_w": [1, 1024],
    "ca_wq": [1, 1024, 1024], "ca_wkv": [1, 1024, 2048], "ca_wo": [1, 1024, 1024], "norm_ffn_w": [1, 1024],
    "peer_wq": [1, 1024, 2048], "peer_sub_keys": [1, 2, 8, 128, 128], "peer_u": [1, 16384, 1024],
    "peer_v": [1, 16384, 1024], "final_norm_w": [1024],
}


def host_consts(smax):
    pos = np.arange(smax, dtype=np.float32)
    c = {}
    invf = (1.0 / (np.float32(10000.0) ** (np.arange(0, 64, 2, dtype=np.float32) / np.float32(64)))).astype(np.float32)
    ang = (pos[:, None] * invf[None, :]).astype(np.float32)
    c["c_cosr"] = np.cos(ang).astype(np.float32)
    c["c_sinr"] = np.sin(ang).astype(np.float32)
    invm = (1.0 / (np.float32(10000.0) ** (np.arange(0, 32, 2, dtype=np.float32) / np.float32(32)))).astype(np.float32)
    angm = (pos[:, None] * invm[None, :]).astype(np.float32)
    cm = np.cos(angm).astype(np.float32).T
    sm = np.sin(angm).astype(np.float32).T
    ck = np.concatenate([cm, cm], 0)
    sk = np.concatenate([sm, sm], 0)
    sc = np.float32(96.0 ** -0.5)
    c["c_cq"] = np.concatenate([np.full((64, smax), sc, np.float32), ck * sc], 0).astype(np.float32)
    c["c_sq"] = np.concatenate([np.zeros((64, smax), np.float32), sk * sc], 0).astype(np.float32)
    c["c_ck"] = np.ascontiguousarray(ck)
    c["c_sk"] = np.ascontiguousarray(sk)
    io = np.arange(128, dtype=np.float32)
    c["c_iota"] = np.tile(io[None, :], (128, 1)).astype(np.float32)
    c["c_diff"] = (io[None, :] - io[:, None]).astype(np.float32)
    c["c_pcol"] = np.stack([127.0 - io, io], 1).astype(np.float32)
    c["c_ident"] = np.eye(128, dtype=np.float32)
    return c


CONST_SHAPES = lambda smax: {"c_cosr": [smax, 32], "c_sinr": [smax, 32], "c_cq": [96, smax], "c_sq": [96, smax],
                             "c_ck": [32, smax], "c_sk": [32, smax], "c_iota": [128, 128], "c_diff": [128, 128],
                             "c_pcol": [128, 2], "c_ident": [128, 128]}


def build(S_list, n_exp_chunks=128, debug=False, upto=99, prepass=True):
    nc = bass.Bass("TRN2", target_bir_lowering=False)
    smax = max(S_list)
    NE = n_exp_chunks
    A = {}
    for i, S in enumerate(S_list):
        A[f"x{i}"] = nc.dram_tensor(f"x{i}", [S, D], F32, kind="ExternalInput").ap()
        A[f"mem{i}"] = nc.dram_tensor(f"mem{i}", [256, D], F32, kind="ExternalInput").ap()
        A[f"y{i}"] = nc.dram_tensor(f"y{i}", [S, D], F32, kind="ExternalOutput").ap()
    for n in W_NAMES:
        A[n] = nc.dram_tensor(n, W_SHAPES[n], F32, kind="ExternalInput").ap()
    for n, shp in CONST_SHAPES(smax).items():
        A[n] = nc.dram_tensor(n, shp, F32, kind="ExternalInput").ap()
    skind = "ExternalOutput" if debug else "Internal"
    nt_max = smax // 128

    def scr(name, shape, dt):
        return nc.dram_tensor("z_" + name, shape, dt, kind=skind).ap()
    Z = dict(
        SG=scr("SG", [smax, 512], F32), YP=scr("YP", [smax, 512], F32),
        QB=scr("QB", [nt_max, 128, 512], BF16), BB=scr("BB", [nt_max, 128, 256], F32),
        QT=scr("QT", [8, 97, smax], BF16), KT=scr("KT", [8, 97, smax], BF16),
        VS=scr("VS", [smax, 8 * 65], BF16), KN=scr("KN", [8, smax], F32),
        MO=scr("MO", [8, 64, smax], BF16), X2=scr("X2", [smax, D], F32),
        UT=scr("UT", [128, 128, 1024], BF16), VB=scr("VB", [128, 128, 1024], BF16),
    )
    ZB = {k: Buf("z" + k, multi=True) for k in Z}

    with ExitStack() as top:
        kb = KB(nc, top)
        kb.init_psum()
        pe, dve, act, pool, sp = kb.pe, kb.dve, kb.act, kb.pool, kb.sp
        V = nc.vector
        ACT = nc.scalar
        G = nc.gpsimd
        PE = nc.tensor

        ident, b_ident = kb.sbb([128, 128], F32, "ident")
        iota, b_iota = kb.sbb([128, 128], F32, "iota")
        onesf, b_onesf = kb.sbb([128, 128], F32, "onesf")
        epsn, b_epsn = kb.sbb([128, 1], F32, "epsn")
        epsg, b_epsg = kb.sbb([128, 1], F32, "epsg")
        kb.dma(sp, ident[:], A["c_ident"][:, :], b_ident, writes=[b_ident])
        kb.dma(sp, iota[:], A["c_iota"][:, :], b_iota, writes=[b_iota])
        kb.op(dve, lambda: V.memset(onesf[:], 1.0), writes=[b_onesf])
        kb.op(dve, lambda: V.memset(epsn[:], 1e-6), writes=[b_epsn])
        kb.op(dve, lambda: V.memset(epsg[:], 1e-5), writes=[b_epsg])
        kb.persist = list(ZB.values()) + [b_ident, b_iota, b_onesf, b_epsn, b_epsg]
        kb.barrier()

        def load_cast(dst, dst_buf, src, stack=None):
            kb.dma(pool, dst, src, dst_buf, writes=[dst_buf])

        def norm_transpose(xt, bx, wcol, b_wcol, outT, b_outT, col0, work):
            junk, b_junk = work["junk"].next()
            ss, b_ss = work["ss"].next()
            xs, b_xs = work["xs"].next()
            kb.op(act, lambda: ACT.activation(out=junk[:], in_=xt, func=AF.Square, accum_out=ss[:, 0:1]),
                  reads=[bx], writes=[b_junk, b_ss])
            kb.op(act, lambda: ACT.activation(out=ss[:, 1:2], in_=ss[:, 0:1], func=AF.Sqrt, scale=1.0 / D, bias=epsn[:, 0:1]),
                  reads=[b_ss, b_epsn], writes=[b_ss])
            kb.op(dve, lambda: V.reciprocal(out=ss[:, 2:3], in_=ss[:, 1:2]), reads=[b_ss], writes=[b_ss])
            kb.op(pool, lambda: G.tensor_scalar(out=xs[:], in0=xt, scalar1=ss[:, 2:3], scalar2=None, op0=ALU.mult),
                  reads=[bx, b_ss], writes=[b_xs])
            pT, bpT = kb.bank(2)
            for kc in range(8):
                kb.op(pe, lambda kc=kc: PE.transpose(pT[:, kc * 128:(kc + 1) * 128], xs[:, kc * 128:(kc + 1) * 128], ident[:]),
                      reads=[b_xs, b_ident], writes=[bpT[kc // 4]])
            pTv = pT.rearrange("p (k t) -> p k t", k=8)
            kb.op(dve, lambda: V.tensor_tensor(out=outT[:, :, col0:col0 + 128], in0=pTv,
                                               in1=wcol[:, :].unsqueeze(2).to_broadcast([128, 8, 128]), op=ALU.mult),
                  reads=bpT + [b_wcol], writes=[b_outT])
            return ss, b_ss

        def load_cols(name, ncols, stack):
            t, b = kb.sbb([128, ncols], F32, name, stack)
            src = A[name].rearrange("o (c p) -> p (o c)", p=128) if len(W_SHAPES[name]) == 2 else A[name].rearrange("(c p) -> p c", p=128)
            with nc.allow_non_contiguous_dma(reason="tiny per-feature vector"):
                kb.dma(sp, t[:], src, b, writes=[b])
            return t, b

        def prepass_peer():
            with ExitStack() as st:
                us = kb.slots([128, 1024], F32, 2, "pp_u", st)
                uts = kb.slots([128, 1024], BF16, 2, "pp_ut", st)
                vs = kb.slots([128, 1024], BF16, 2, "pp_v", st)
                for i in range(NE):
                    ut, bu = us.next()
                    kb.dma(sp, ut[:], A["peer_u"][0, i * 128:(i + 1) * 128, :], bu, writes=[bu])
                    pT, bpT = kb.bank(2)
                    for kc in range(8):
                        kb.op(pe, lambda kc=kc: PE.transpose(pT[:, kc * 128:(kc + 1) * 128], ut[:, kc * 128:(kc + 1) * 128], ident[:]),
                              reads=[bu, b_ident], writes=[bpT[kc // 4]])
                    utt, butt = uts.next()
                    kb.op(act, lambda: ACT.copy(out=utt[:, 0:512], in_=pT[:, 0:512]), reads=[bpT[0]], writes=[butt])
                    kb.op(dve, lambda: V.tensor_copy(out=utt[:, 512:1024], in_=pT[:, 512:1024]), reads=[bpT[1]], writes=[butt])
                    kb.dma(sp, Z["UT"][i], utt[:], butt, reads=[butt], writes=[ZB["UT"]])
                    vt, bv = vs.next()
                    load_cast(vt[:], bv, A["peer_v"][0, i * 128:(i + 1) * 128, :])
                    kb.dma(sp, Z["VB"][i], vt[:], bv, reads=[bv], writes=[ZB["VB"]])
                kb.barrier()

        def phase1(si, S):
            nt = S // 128
            X = A[f"x{si}"]
            with ExitStack() as st:
                Win, b_Win = kb.sbb([128, 8, 2720], BF16, "Win", st)
                for kc in range(8):
                    for (c0, c1) in ((0, 1360), (1360, 2720)):
                        load_cast(Win[:, kc, c0:c1], b_Win, A["w_in"][0, kc * 128:(kc + 1) * 128, c0:c1])
                Wuq, b_Wuq = kb.sbb([128, 3, 768], BF16, "Wuq", st)
                load_cast(Wuq[:], b_Wuq, A["mla_w_uq"][0].rearrange("(r p) n -> p r n", p=128))
                Wuqr, b_Wuqr = kb.sbb([128, 3, 768], BF16, "Wuqr", st)
                Wukv, b_Wukv = kb.sbb([128, 2, 2, 8, 64], BF16, "Wukv", st)
                for c in range(2):
                    for t in range(2):
                        load_cast(Wukv[:, c, t, :, :], b_Wukv,
                                  A["mla_w_ukv"][0, c * 128:(c + 1) * 128, :].rearrange("p (h t d) -> p t h d", h=8, t=2)[:, t, :, :])
                Wkrr, b_Wkrr = kb.sbb([128, 8, 32], BF16, "Wkrr", st)
                kb.op(dve, lambda: V.memset(Wuqr[:], 0.0), writes=[b_Wuqr])
                Wuq4 = Wuq[:].rearrange("p r (h f) -> p r h f", h=8)
                Wuqr4 = Wuqr[:].rearrange("p r (h f) -> p r h f", h=8)
                for r in range(3):
                    kb.op(dve, lambda r=r: V.tensor_scalar(out=Wuqr4[:, r, :, 64:80], in0=Wuq4[:, r, :, 80:96], scalar1=-1.0, scalar2=None, op0=ALU.mult),
                          reads=[b_Wuq], writes=[b_Wuqr])
                    kb.op(dve, lambda r=r: V.tensor_copy(out=Wuqr4[:, r, :, 80:96], in_=Wuq4[:, r, :, 64:80]),
                          reads=[b_Wuq], writes=[b_Wuqr])
                kb.op(dve, lambda: V.tensor_scalar(out=Wkrr[:, :, 0:16], in0=Win[:, :, 2704:2720], scalar1=-1.0, scalar2=None, op0=ALU.mult),
                      reads=[b_Win], writes=[b_Wkrr])
                kb.op(dve, lambda: V.tensor_copy(out=Wkrr[:, :, 16:32], in_=Win[:, :, 2688:2704]), reads=[b_Win], writes=[b_Wkrr])
                nmw, b_nmw = load_cols("norm_mix_w", 8, st)
                nw5, b_nw5 = kb.sbb([128, 5], F32, "nw5", st)
                with nc.allow_non_contiguous_dma(reason="tiny per-feature vector"):
                    kb.dma(sp, nw5[:, 0:3], A["mla_q_norm_w"].rearrange("o (c p) -> p (o c)", p=128), b_nw5, writes=[b_nw5])
                    kb.dma(sp, nw5[:, 3:5], A["mla_kv_norm_w"].rearrange("o (c p) -> p (o c)", p=128), b_nw5, writes=[b_nw5])
                cosrs = kb.slots([128, 32], F32, 2, "cosr", st)
                sinrs = kb.slots([128, 32], F32, 2, "sinr", st)
                dec, b_dec = kb.sbb([128, 16], F32, "dec", st)
                kb.dma(sp, dec[:, 0:8], A["ret_decay_fwd"][0, :].partition_broadcast(128), b_dec, writes=[b_dec])
                kb.dma(sp, dec[:, 8:16], A["ret_decay_bwd"][0, :].partition_broadcast(128), b_dec, writes=[b_dec])
                lg, b_lg = kb.sbb([128, 16], F32, "lg", st)
                kb.op(act, lambda: ACT.activation(out=lg[:], in_=dec[:], func=AF.Exp, scale=-1.0), reads=[b_dec], writes=[b_lg])
                kb.op(act, lambda: ACT.activation(out=lg[:], in_=lg[:], func=AF.Ln, bias=1.0), reads=[b_lg], writes=[b_lg])
                kb.op(dve, lambda: V.tensor_scalar(out=lg[:], in0=lg[:], scalar1=-1.0, scalar2=None, op0=ALU.mult), reads=[b_lg], writes=[b_lg])
                lgp, b_lgp = kb.sbb([128, 4, 2], F32, "lgp", st)
                lgv = lg[:].rearrange("p (d a two) -> p a d two", d=2, two=2)
                kb.op(dve, lambda: V.tensor_copy(out=lgp[0:64, :, :], in_=lgv[0:64, :, :, 0]), reads=[b_lg], writes=[b_lgp])
                kb.op(dve, lambda: V.tensor_copy(out=lgp[64:128, :, :], in_=lgv[64:128, :, :, 1]), reads=[b_lg], writes=[b_lgp])
                diff, b_diff = kb.sbb([128, 128], F32, "diff", st)
                kb.dma(sp, diff[:], A["c_diff"][:, :], b_diff, writes=[b_diff])
                pcol, b_pcol = kb.sbb([128, 2], F32, "pcol", st)
                kb.dma(sp, pcol[:], A["c_pcol"][:, :], b_pcol, writes=[b_pcol])
                dpos, b_dpos = kb.sbb([128, 128], F32, "dpos", st)
                dneg, b_dneg = kb.sbb([128, 128], F32, "dneg", st)
                kb.op(dve, lambda: V.tensor_scalar(out=dpos[:], in0=diff[:], scalar1=0.0, scalar2=None, op0=ALU.max), reads=[b_diff], writes=[b_dpos])
                kb.op(dve, lambda: V.tensor_scalar(out=dneg[:], in0=diff[:], scalar1=-1.0, scalar2=0.0, op0=ALU.mult, op1=ALU.max), reads=[b_diff], writes=[b_dneg])
                DT, b_DT = kb.sbb([128, 8, 128], F32, "DT", st)
                tmpE, b_tmpE = kb.sbb([128, 128], F32, "tmpE", st)
                for h in range(8):
                    kb.op(act, lambda h=h: ACT.activation(out=DT[:, h, :], in_=dpos[:], func=AF.Exp, scale=lg[:, h:h + 1]),
                          reads=[b_dpos, b_lg], writes=[b_DT])
                    kb.op(pool, lambda h=h: G.affine_select(out=DT[:, h, :], in_=DT[:, h, :], pattern=[[1, 128]], compare_op=ALU.is_ge,
                                                            fill=0.0, base=0, channel_multiplier=-1), reads=[b_DT], writes=[b_DT])
                    kb.op(act, lambda h=h: ACT.activation(out=tmpE[:], in_=dneg[:], func=AF.Exp, scale=lg[:, 8 + h:9 + h]),
                          reads=[b_dneg, b_lg], writes=[b_tmpE])
                    kb.op(pool, lambda: G.affine_select(out=tmpE[:], in_=tmpE[:], pattern=[[-1, 128]], compare_op=ALU.is_gt,
                                                        fill=0.0, base=0, channel_multiplier=1), reads=[b_tmpE], writes=[b_tmpE])
                    kb.op(dve, lambda h=h: V.tensor_tensor(out=DT[:, h, :], in0=DT[:, h, :], in1=tmpE[:], op=ALU.add),
                          reads=[b_DT, b_tmpE], writes=[b_DT])
                ip1, b_ip1 = kb.sbb([128, 128], F32, "ip1", st)
                cmi, b_cmi = kb.sbb([128, 128], F32, "cmi", st)
                kb.op(dve, lambda: V.tensor_scalar(out=ip1[:], in0=iota[:], scalar1=1.0, scalar2=None, op0=ALU.add), reads=[b_iota], writes=[b_ip1])
                kb.op(dve, lambda: V.tensor_scalar(out=cmi[:], in0=iota[:], scalar1=-1.0, scalar2=128.0, op0=ALU.mult, op1=ALU.add), reads=[b_iota], writes=[b_cmi])
                QF, b_QF = kb.sbb([128, 4, 128], F32, "QF", st)
                QBt, b_QBt = kb.sbb([128, 4, 128], F32, "QBt", st)
                for a in range(4):
                    kb.op(act, lambda a=a: ACT.activation(out=QF[:, a, :], in_=ip1[:], func=AF.Exp, scale=lgp[:, a, 0:1]), reads=[b_ip1, b_lgp], writes=[b_QF])
                    kb.op(act, lambda a=a: ACT.activation(out=QBt[:, a, :], in_=cmi[:], func=AF.Exp, scale=lgp[:, a, 1:2]), reads=[b_cmi, b_lgp], writes=[b_QBt])
                tk, b_tk = kb.sbb([128, 16], F32, "tk", st)
                kb.op(act, lambda: ACT.activation(out=tk[:, 0:8], in_=lg[:, 0:8], func=AF.Exp, scale=pcol[:, 0:1]), reads=[b_lg, b_pcol], writes=[b_tk])
                kb.op(act, lambda: ACT.activation(out=tk[:, 8:16], in_=lg[:, 8:16], func=AF.Exp, scale=pcol[:, 1:2]), reads=[b_lg, b_pcol], writes=[b_tk])
                GC, b_GC = kb.sbb([128, 4, 2], F32, "GC", st)
                kb.op(act, lambda: ACT.activation(out=GC[:], in_=lgp[:], func=AF.Exp, scale=128.0), reads=[b_lgp], writes=[b_GC])
                Rf, b_Rf = kb.sbb([128, 4, 64], F32, "Rf", st)
                Rfb, b_Rfb = kb.sbb([128, 4, 128], BF16, "Rfb", st)
                kb.op(dve, lambda: V.memset(Rf[:], 0.0), writes=[b_Rf])
                kb.op(dve, lambda: V.memset(Rfb[:], 0.0), writes=[b_Rfb])
                work = dict(junk=kb.slots([128, 1024], BF16, 1, "junk", st), ss=kb.slots([128, 4], F32, 2, "ss", st),
                            xs=kb.slots([128, 1024], F32, 1, "xs", st))
                xts = kb.slots([128, 1024], F32, 2, "xt", st)
                xnTs = kb.slots([128, 8, 128], BF16, 2, "xnT", st)
                qks = kb.slots([128, 16, 64], F32, 1, "qk", st)
                qkrs = kb.slots([128, 16, 64], F32, 1, "qkr", st)
                tmps = kb.slots([128, 16, 32], F32, 4, "ropetmp", st)
                vbs = kb.slots([128, 512], BF16, 2, "vb", st)
                sgs = kb.slots([128, 512], F32, 2, "sg", st)
                qTs = kb.slots([128, 8, 128], BF16, 2, "qTz", st)
                for (q_, bq_) in qTs.items:
                    kb.op(dve, lambda q_=q_: V.memset(q_[:], 0.0), writes=[bq_])
                kTs = kb.slots([128, 4, 128], BF16, 2, "kT", st)
                qfTs = kb.slots([128, 4, 128], BF16, 2, "qfT", st)
                qbTs = kb.slots([128, 4, 128], BF16, 2, "qbT", st)
                kdfs = kb.slots([128, 8, 64], BF16, 2, "kdf", st)
                kdbs = kb.slots([128, 8, 64], BF16, 2, "kdb", st)
                sDs = kb.slots([128, 8, 128], BF16, 2, "sD", st)
                yps = kb.slots([128, 512], F32, 2, "yp", st)
                bbs = kb.slots([128, 4, 64], F32, 2, "bb", st)
                sqs = kb.slots([128, 5, 128], F32, 1, "sq", st)
                rsbs = kb.slots([128, 2, 128], F32, 2, "rsb", st)
                cns = kb.slots([128, 5, 128], BF16, 2, "cn", st)
                cqts = kb.slots([96, 128], F32, 2, "cqt", st)
                sqts = kb.slots([96, 128], F32, 2, "sqt", st)
                ckts = kb.slots([32, 128], F32, 2, "ckt", st)
                skts = kb.slots([32, 128], F32, 2, "skt", st)
                t1s = kb.slots([96, 8, 128], F32, 1, "t1", st)
                t2s = kb.slots([96, 8, 128], F32, 1, "t2", st)
                QTts = kb.slots([96, 8, 128], BF16, 2, "QTt", st)
                KNts = kb.slots([64, 8, 128], BF16, 2, "KNt", st)
                KRts = kb.slots([32, 128], BF16, 2, "KRt", st)
                kr1s = kb.slots([32, 128], F32, 2, "kr1", st)
                kr2s = kb.slots([32, 128], F32, 2, "kr2", st)
                Vts = kb.slots([128, 8, 65], BF16, 2, "Vt", st)
                for (vt_, bv_) in Vts.items:
                    kb.op(dve, lambda vt_=vt_: V.memset(vt_[:, :, 64:65], 1.0), writes=[bv_])
                sq2s = kb.slots([96, 1024], F32, 1, "sq2", st)
                sq3s = kb.slots([32, 128], F32, 1, "sq3", st)
                nrows = kb.slots([1, 2, 1024], F32, 1, "nrow", st)
                nbrs = kb.slots([1, 1024], BF16, 2, "nbr", st)
                one8, b_one8 = kb.sbb([1, 1024], BF16, "one8", st)
                kb.op(dve, lambda: V.memset(one8[:], 1.0), writes=[b_one8])
                krn = kb.slots([1, 128], F32, 2, "krn", st)

                for t in range(nt):
                    r0 = t * 128
                    xt, bx = xts.next()
                    kb.dma(sp, xt[:], X[r0:r0 + 128, :], bx, writes=[bx])
                    xnT, bxnT = xnTs.next()
                    norm_transpose(xt[:], bx, nmw, b_nmw, xnT, bxnT, 0, work)
                    pq, bq = kb.bank(1)
                    pk, bk = kb.bank(1)
                    pv, bv = kb.bank(1)
                    pg, bg = kb.bank(1)
                    for n, (pp, bpp) in enumerate(((pq, bq), (pk, bk), (pv, bv), (pg, bg))):
                        for kc in range(8):
                            kb.op(pe, lambda pp=pp, n=n, kc=kc: PE.matmul(pp, lhsT=xnT[:, kc, :], rhs=Win[:, kc, n * 512:(n + 1) * 512],
                                                                        start=(kc == 0), stop=(kc == 7)),
                                  reads=[bxnT, b_Win], writes=bpp)
                    qk, bqk = qks.next()
                    qkf = qk[:].rearrange("p a d -> p (a d)")
                    kb.op(act, lambda: ACT.copy(out=qkf[:, 0:512], in_=pq), reads=bq, writes=[bqk])
                    kb.op(act, lambda: ACT.activation(out=qkf[:, 512:1024], in_=pk, func=AF.Copy, scale=0.125), reads=bk, writes=[bqk])
                    vb, bvb = vbs.next()
                    kb.op(dve, lambda: V.tensor_copy(out=vb[:], in_=pv), reads=bv, writes=[bvb])
                    sg, bsg = sgs.next()
                    kb.op(act, lambda: ACT.activation(out=sg[:], in_=pg, func=AF.Silu), reads=bg, writes=[bsg])
                    kb.dma(sp, Z["SG"][r0:r0 + 128, :], sg[:], bsg, reads=[bsg], writes=[ZB["SG"]])
                    qkr, bqkr = qkrs.next()
                    cosr, b_cosr = cosrs.next()
                    sinr, b_sinr = sinrs.next()
                    kb.dma(sp, cosr[:], A["c_cosr"][r0:r0 + 128, :], b_cosr, writes=[b_cosr])
                    kb.dma(sp, sinr[:], A["c_sinr"][r0:r0 + 128, :], b_sinr, writes=[b_sinr])
                    cb = cosr[:, :].unsqueeze(1).to_broadcast([128, 16, 32])
                    sbc = sinr[:, :].unsqueeze(1).to_broadcast([128, 16, 32])
                    ta, bta = tmps.next()
                    tb, btb = tmps.next()
                    tc_, btc = tmps.next()
                    td, btd = tmps.next()
                    kb.op(dve, lambda: V.tensor_tensor(out=ta[:], in0=qk[:, :, 0:32], in1=cb, op=ALU.mult), reads=[bqk, b_cosr], writes=[bta])
                    kb.op(dve, lambda: V.tensor_tensor(out=tb[:], in0=qk[:, :, 32:64], in1=sbc, op=ALU.mult), reads=[bqk, b_sinr], writes=[btb])
                    kb.op(dve, lambda: V.tensor_tensor(out=qkr[:, :, 0:32], in0=ta[:], in1=tb[:], op=ALU.subtract), reads=[bta, btb], writes=[bqkr])
                    kb.op(pool, lambda: G.tensor_tensor(out=tc_[:], in0=qk[:, :, 0:32], in1=sbc, op=ALU.mult), reads=[bqk, b_sinr], writes=[btc])
                    kb.op(pool, lambda: G.tensor_tensor(out=td[:], in0=qk[:, :, 32:64], in1=cb, op=ALU.mult), reads=[bqk, b_cosr], writes=[btd])
                    kb.op(pool, lambda: G.tensor_tensor(out=qkr[:, :, 32:64], in0=tc_[:], in1=td[:], op=ALU.add), reads=[btc, btd], writes=[bqkr])
                    qkrf = qkr[:].rearrange("p a d -> p (a d)")
                    pT, bpT = kb.bank(2)
                    for j in range(8):
                        kb.op(pe, lambda j=j: PE.transpose(pT[:, j * 128:(j + 1) * 128], qkrf[:, j * 128:(j + 1) * 128], ident[:]),
                              reads=[bqkr, b_ident], writes=[bpT[j // 4]])
                    pTv = pT.rearrange("p (j t) -> p j t", j=8)
                    qT, bqT = qTs.next()
                    kT, bkT = kTs.next()
                    qfT, bqfT = qfTs.next()
                    qbT, bqbT = qbTs.next()
                    kb.op(act, lambda: ACT.copy(out=qT[0:64, 0:8:2, :], in_=pTv[0:64, 0:4, :]), reads=[bpT[0]], writes=[bqT])
                    kb.op(act, lambda: ACT.copy(out=qT[64:128, 1:8:2, :], in_=pTv[64:128, 0:4, :]), reads=[bpT[0]], writes=[bqT])
                    kb.op(act, lambda: ACT.copy(out=kT[:], in_=pTv[:, 4:8, :]), reads=[bpT[1]], writes=[bkT])
                    kb.op(dve, lambda: V.tensor_tensor(out=qfT[:], in0=pTv[:, 0:4, :], in1=QF[:], op=ALU.mult), reads=[bpT[0], b_QF], writes=[bqfT])
                    kb.op(dve, lambda: V.tensor_tensor(out=qbT[:], in0=pTv[:, 0:4, :], in1=QBt[:], op=ALU.mult), reads=[bpT[0], b_QBt], writes=[bqbT])
                    kb.dma(sp, Z["QB"][t], qbT[:].rearrange("p a t -> p (a t)"), bqbT, reads=[bqbT], writes=[ZB["QB"]])
                    kdf, bkdf = kdfs.next()
                    kdb, bkdb = kdbs.next()
                    kb.op(dve, lambda: V.tensor_tensor(out=kdf[:], in0=qkr[:, 8:16, :], in1=tk[:, 0:8].unsqueeze(2).to_broadcast([128, 8, 64]), op=ALU.mult),
                          reads=[bqkr, b_tk], writes=[bkdf])
                    kb.op(dve, lambda: V.tensor_tensor(out=kdb[:], in0=qkr[:, 8:16, :], in1=tk[:, 8:16].unsqueeze(2).to_broadcast([128, 8, 64]), op=ALU.mult),
                          reads=[bqkr, b_tk], writes=[bkdb])
                    pS, bpS = kb.bank(2)
                    for h in range(8):
                        a, off = h // 2, (h % 2) * 64
                        kb.op(pe, lambda h=h, a=a, off=off: PE.matmul(pS[:, h * 128:(h + 1) * 128], lhsT=kT[:, a, :], rhs=qT[:, h, :],
                                                                      start=True, stop=True),
                              reads=[bkT, bqT], writes=[bpS[h // 4]])
                    sD, bsD = sDs.next()
                    pSv = pS.rearrange("p (h t) -> p h t", h=8)
                    kb.op(dve, lambda: V.tensor_tensor(out=sD[:, 0:4, :], in0=pSv[:, 0:4, :], in1=DT[:, 0:4, :], op=ALU.mult), reads=[bpS[0], b_DT], writes=[bsD])
                    kb.op(dve, lambda: V.tensor_tensor(out=sD[:, 4:8, :], in0=pSv[:, 4:8, :], in1=DT[:, 4:8, :], op=ALU.mult), reads=[bpS[1], b_DT], writes=[bsD])
                    pY, bpY = kb.bank(1)
                    for a in range(4):
                        kb.op(pe, lambda a=a: PE.matmul(pY[:, a * 128:(a + 1) * 128], lhsT=qfT[:, a, :], rhs=Rfb[:, a, :], start=True, stop=False),
                              reads=[bqfT, b_Rfb], writes=bpY)
                        for h in (2 * a, 2 * a + 1):
                            kb.op(pe, lambda h=h: PE.matmul(pY[:, h * 64:(h + 1) * 64], lhsT=sD[:, h, :], rhs=vb[:, h * 64:(h + 1) * 64], start=False, stop=(h % 2 == 1)),
                                  reads=[bsD, bvb], writes=bpY)
                    yp, byp = yps.next()
                    kb.op(act, lambda: ACT.copy(out=yp[:], in_=pY), reads=bpY, writes=[byp])
                    kb.dma(sp, Z["YP"][r0:r0 + 128, :], yp[:], byp, reads=[byp], writes=[ZB["YP"]])
                    pBf, bpBf = kb.bank(1)
                    pBb, bpBb = kb.bank(1)
                    kdf2 = kdf[:].rearrange("p h d -> p (h d)")
                    kdb2 = kdb[:].rearrange("p h d -> p (h d)")
                    for a in range(4):
                        kb.op(pe, lambda a=a: PE.matmul(pBf[:, a * 128:(a + 1) * 128], lhsT=kdf2[:, a * 128:(a + 1) * 128], rhs=vb[:, a * 128:(a + 1) * 128],
                                                        start=True, stop=True), reads=[bkdf, bvb], writes=bpBf)
                        kb.op(pe, lambda a=a: PE.matmul(pBb[:, a * 128:(a + 1) * 128], lhsT=kdb2[:, a * 128:(a + 1) * 128], rhs=vb[:, a * 128:(a + 1) * 128],
                                                        start=True, stop=True), reads=[bkdb, bvb], writes=bpBb)
                    pBfv = pBf.rearrange("p (a x) -> p a x", a=4)
                    pBbv = pBb.rearrange("p (a x) -> p a x", a=4)
                    kb.op(dve, lambda: V.tensor_tensor(out=Rf[:], in0=Rf[:], in1=GC[:, :, 0:1].to_broadcast([128, 4, 64]), op=ALU.mult),
                          reads=[b_Rf, b_GC], writes=[b_Rf])
                    kb.op(dve, lambda: V.tensor_tensor(out=Rf[0:64], in0=Rf[0:64], in1=pBfv[0:64, :, 0:64], op=ALU.add), reads=[b_Rf] + bpBf, writes=[b_Rf])
                    kb.op(dve, lambda: V.tensor_tensor(out=Rf[64:128], in0=Rf[64:128], in1=pBfv[64:128, :, 64:128], op=ALU.add), reads=[b_Rf] + bpBf, writes=[b_Rf])
                    kb.op(dve, lambda: V.tensor_copy(out=Rfb[0:64, :, 0:64], in_=Rf[0:64]), reads=[b_Rf], writes=[b_Rfb])
                    kb.op(dve, lambda: V.tensor_copy(out=Rfb[64:128, :, 64:128], in_=Rf[64:128]), reads=[b_Rf], writes=[b_Rfb])
                    bb, bbb = bbs.next()
                    kb.op(act, lambda: ACT.copy(out=bb[0:64], in_=pBbv[0:64, :, 0:64]), reads=bpBb, writes=[bbb])
                    kb.op(act, lambda: ACT.copy(out=bb[64:128], in_=pBbv[64:128, :, 64:128]), reads=bpBb, writes=[bbb])
                    kb.dma(sp, Z["BB"][t], bb[:].rearrange("p a v -> p (a v)"), bbb, reads=[bbb], writes=[ZB["BB"]])

                    pC, bpC = kb.bank(2)
                    for r in range(5):
                        for kc in range(8):
                            kb.op(pe, lambda r=r, kc=kc: PE.matmul(pC[:, r * 128:(r + 1) * 128], lhsT=Win[:, kc, 2048 + r * 128:2048 + (r + 1) * 128],
                                                                   rhs=xnT[:, kc, :], start=(kc == 0), stop=(kc == 7)),
                                  reads=[b_Win, bxnT], writes=[bpC[r // 4]])
                    sq, bsq = sqs.next()
                    kb.op(act, lambda: ACT.activation(out=sq[:, 0:4, :].rearrange("p r t -> p (r t)"), in_=pC[:, 0:512], func=AF.Square), reads=[bpC[0]], writes=[bsq])
                    kb.op(act, lambda: ACT.activation(out=sq[:, 4, :], in_=pC[:, 512:640], func=AF.Square), reads=[bpC[1]], writes=[bsq])
                    pN, bpN = kb.bank(1)
                    for r in range(3):
                        kb.op(pe, lambda r=r: PE.matmul(pN[:, 0:128], lhsT=onesf[:], rhs=sq[:, r, :], start=(r == 0), stop=(r == 2)), reads=[b_onesf, bsq], writes=bpN)
                    for r in range(2):
                        kb.op(pe, lambda r=r: PE.matmul(pN[:, 128:256], lhsT=onesf[:], rhs=sq[:, 3 + r, :], start=(r == 0), stop=(r == 1)), reads=[b_onesf, bsq], writes=bpN)
                    rsb, brsb = rsbs.next()
                    kb.op(act, lambda: ACT.activation(out=rsb[:, 0, :], in_=pN[:, 0:128], func=AF.Sqrt, scale=1.0 / 384, bias=epsn[:, 0:1]), reads=bpN + [b_epsn], writes=[brsb])
                    kb.op(act, lambda: ACT.activation(out=rsb[:, 1, :], in_=pN[:, 128:256], func=AF.Sqrt, scale=1.0 / 256, bias=epsn[:, 0:1]), reads=bpN + [b_epsn], writes=[brsb])
                    kb.op(dve, lambda: V.reciprocal(out=rsb[:], in_=rsb[:]), reads=[brsb], writes=[brsb])
                    cn, bcn = cns.next()
                    for r in range(5):
                        kb.op(dve, lambda r=r: V.scalar_tensor_tensor(out=cn[:, r, :], in0=pC[:, r * 128:(r + 1) * 128], scalar=nw5[:, r:r + 1],
                                                                      in1=rsb[:, 0 if r < 3 else 1, :], op0=ALU.mult, op1=ALU.mult),
                              reads=[bpC[r // 4], b_nw5, brsb], writes=[bcn])
                    pQ, bpQ = kb.bank(2)
                    pQr, bpQr = kb.bank(2)
                    for h in range(8):
                        for r in range(3):
                            kb.op(pe, lambda h=h, r=r: PE.matmul(pQ[0:96, h * 128:(h + 1) * 128], lhsT=Wuq[:, r, h * 96:(h + 1) * 96], rhs=cn[:, r, :],
                                                                 start=(r == 0), stop=(r == 2)), reads=[b_Wuq, bcn], writes=[bpQ[h // 4]])
                        for r in range(3):
                            kb.op(pe, lambda h=h, r=r: PE.matmul(pQr[0:96, h * 128:(h + 1) * 128], lhsT=Wuqr[:, r, h * 96:(h + 1) * 96], rhs=cn[:, r, :],
                                                                 start=(r == 0), stop=(r == 2)), reads=[b_Wuqr, bcn], writes=[bpQr[h // 4]])
                    cqt, bcqt = cqts.next()
                    sqt, bsqt = sqts.next()
                    kb.dma(sp, cqt[:], A["c_cq"][:, r0:r0 + 128], bcqt, writes=[bcqt])
                    kb.dma(sp, sqt[:], A["c_sq"][:, r0:r0 + 128], bsqt, writes=[bsqt])
                    t1, bt1 = t1s.next()
                    t2, bt2 = t2s.next()
                    kb.op(dve, lambda: V.tensor_tensor(out=t1[:], in0=pQ[0:96, :].rearrange("p (h t) -> p h t", h=8),
                                                       in1=cqt[:].unsqueeze(1).to_broadcast([96, 8, 128]), op=ALU.mult), reads=bpQ + [bcqt], writes=[bt1])
                    kb.op(dve, lambda: V.tensor_tensor(out=t2[:], in0=pQr[0:96, :].rearrange("p (h t) -> p h t", h=8),
                                                       in1=sqt[:].unsqueeze(1).to_broadcast([96, 8, 128]), op=ALU.mult), reads=bpQr + [bsqt], writes=[bt2])
                    QTt, bQTt = QTts.next()
                    kb.op(dve, lambda: V.tensor_tensor(out=QTt[:], in0=t1[:], in1=t2[:], op=ALU.add), reads=[bt1, bt2], writes=[bQTt])
                    kb.dma(sp, Z["QT"][:, 0:96, r0:r0 + 128].rearrange("h f t -> f h t"), QTt[:], bQTt, reads=[bQTt], writes=[ZB["QT"]])
                    pKN, bpKN = kb.bank(2)
                    for h in range(8):
                        for c in range(2):
                            kb.op(pe, lambda h=h, c=c: PE.matmul(pKN[0:64, h * 128:(h + 1) * 128], lhsT=Wukv[:, c, 0, h, :], rhs=cn[:, 3 + c, :],
                                                                 start=(c == 0), stop=(c == 1)), reads=[b_Wukv, bcn], writes=[bpKN[h // 4]])
                    KNt, bKNt = KNts.next()
                    kb.op(act, lambda: ACT.copy(out=KNt[:].rearrange("p h t -> p (h t)"), in_=pKN[0:64, :]), reads=bpKN, writes=[bKNt])
                    kb.dma(sp, Z["KT"][:, 0:64, r0:r0 + 128].rearrange("h f t -> f h t"), KNt[:], bKNt, reads=[bKNt], writes=[ZB["KT"]])
                    pKR, bpKR = kb.bank(1)
                    for kc in range(8):
                        kb.op(pe, lambda kc=kc: PE.matmul(pKR[0:32, 0:128], lhsT=Win[:, kc, 2688:2720], rhs=xnT[:, kc, :], start=(kc == 0), stop=(kc == 7)),
                              reads=[b_Win, bxnT], writes=bpKR)
                    for kc in range(8):
                        kb.op(pe, lambda kc=kc: PE.matmul(pKR[0:32, 128:256], lhsT=Wkrr[:, kc, :], rhs=xnT[:, kc, :], start=(kc == 0), stop=(kc == 7)),
                              reads=[b_Wkrr, bxnT], writes=bpKR)
                    ckt, bckt = ckts.next()
                    skt, bskt = skts.next()
                    kb.dma(sp, ckt[:], A["c_ck"][:, r0:r0 + 128], bckt, writes=[bckt])
                    kb.dma(sp, skt[:], A["c_sk"][:, r0:r0 + 128], bskt, writes=[bskt])
                    kr1, bkr1 = kr1s.next()
                    kr2, bkr2 = kr2s.next()
                    kb.op(dve, lambda: V.tensor_tensor(out=kr1[:], in0=pKR[0:32, 0:128], in1=ckt[:], op=ALU.mult), reads=bpKR + [bckt], writes=[bkr1])
                    kb.op(dve, lambda: V.tensor_tensor(out=kr2[:], in0=pKR[0:32, 128:256], in1=skt[:], op=ALU.mult), reads=bpKR + [bskt], writes=[bkr2])
                    KRt, bKRt = KRts.next()
                    kb.op(dve, lambda: V.tensor_tensor(out=KRt[:], in0=kr1[:], in1=kr2[:], op=ALU.add), reads=[bkr1, bkr2], writes=[bKRt])
                    for h in range(8):
                        kb.dma(sp, Z["KT"][h, 64:96, r0:r0 + 128], KRt[:], bKRt, reads=[bKRt], writes=[ZB["KT"]])
                    pV, bpV = kb.bank(1)
                    for c in range(2):
                        kb.op(pe, lambda c=c: PE.matmul(pV, lhsT=cn[:, 3 + c, :], rhs=Wukv[:, c, 1, :, :].rearrange("p h d -> p (h d)"),
                                                        start=(c == 0), stop=(c == 1)), reads=[bcn, b_Wukv], writes=bpV)
                    Vt, bVt = Vts.next()
                    kb.op(act, lambda: ACT.copy(out=Vt[:, :, 0:64], in_=pV.rearrange("p (h d) -> p h d", h=8)), reads=bpV, writes=[bVt])
                    kb.dma(sp, Z["VS"][r0:r0 + 128, :], Vt[:].rearrange("p h d -> p (h d)"), bVt, reads=[bVt], writes=[ZB["VS"]])
                    sq2, bsq2 = sq2s.next()
                    nrow, bnrow = nrows.next()
                    kb.op(act, lambda: ACT.activation(out=sq2[:], in_=QTt[:].rearrange("p h t -> p (h t)"), func=AF.Square), reads=[bQTt], writes=[bsq2])
                    pR, bpR = kb.bank(2)
                    for n in range(2):
                        kb.op(pe, lambda n=n: PE.matmul(pR[0:1, n * 512:(n + 1) * 512], lhsT=onesf[0:96, 0:1], rhs=sq2[:, n * 512:(n + 1) * 512], start=True, stop=True),
                              reads=[b_onesf, bsq2], writes=[bpR[n]])
                    kb.op(dve, lambda: V.tensor_copy(out=nrow[:, 0, :], in_=pR[0:1, :]), reads=bpR, writes=[bnrow])
                    nbr, bnbr = nbrs.next()
                    kb.op(act, lambda: ACT.activation(out=nrow[:, 0, :], in_=nrow[:, 0, :], func=AF.Sqrt), reads=[bnrow], writes=[bnrow])
                    kb.op(dve, lambda: V.tensor_scalar(out=nbr[:], in0=nrow[:, 0, :], scalar1=-1.0, scalar2=None, op0=ALU.mult), reads=[bnrow], writes=[bnbr])
                    kb.dma(sp, Z["QT"][:, 96:97, r0:r0 + 128].rearrange("h o t -> o h t"), nbr[:].rearrange("o (h t) -> o h t", h=8), bnbr, reads=[bnbr], writes=[ZB["QT"]])
                    kb.dma(sp, Z["KT"][:, 96:97, r0:r0 + 128].rearrange("h o t -> o h t"), one8[:].rearrange("o (h t) -> o h t", h=8), b_one8, reads=[b_one8], writes=[ZB["KT"]])
                    kb.op(act, lambda: ACT.activation(out=sq2[0:64, :], in_=KNt[:].rearrange("p h t -> p (h t)"), func=AF.Square), reads=[bKNt, bsq2], writes=[bsq2])
                    sq3, bsq3 = sq3s.next()
                    kb.op(act, lambda: ACT.activation(out=sq3[:], in_=KRt[:], func=AF.Square), reads=[bKRt], writes=[bsq3])
                    pR2, bpR2 = kb.bank(2)
                    for n in range(2):
                        kb.op(pe, lambda n=n: PE.matmul(pR2[0:1, n * 512:(n + 1) * 512], lhsT=onesf[0:64, 0:1], rhs=sq2[0:64, n * 512:(n + 1) * 512], start=True, stop=True),
                              reads=[b_onesf, bsq2], writes=[bpR2[n]])
                    pR3, bpR3 = kb.bank(1)
                    kb.op(pe, lambda: PE.matmul(pR3[0:1, 0:128], lhsT=onesf[0:32, 0:1], rhs=sq3[:], start=True, stop=True), reads=[b_onesf, bsq3], writes=bpR3)
                    kr_, bkr_ = krn.next()
                    kb.op(dve, lambda: V.tensor_copy(out=kr_[:], in_=pR3[0:1, 0:128]), reads=bpR3, writes=[bkr_])
                    kb.op(dve, lambda: V.tensor_tensor(out=nrow[:, 1, :].rearrange("o (h t) -> o h t", h=8), in0=pR2[0:1, :].rearrange("o (h t) -> o h t", h=8),
                                                       in1=kr_[:].unsqueeze(1).to_broadcast([1, 8, 128]), op=ALU.add), reads=bpR2 + [bkr_], writes=[bnrow])
                    kb.dma(sp, Z["KN"][:, r0:r0 + 128].rearrange("(o h) t -> o h t", o=1), nrow[:, 1, :].rearrange("o (h t) -> o h t", h=8), bnrow, reads=[bnrow], writes=[ZB["KN"]])
                kb.barrier()

        def phase2(si, S):
            nkt = S // 128
            QBLK = 512 if S >= 512 else S
            nqb = S // QBLK
            with ExitStack() as st:
                KThs = kb.slots([97, S], BF16, 2, "KTh", st)
                QThs = kb.slots([97, S], BF16, 2, "QTh", st)
                Vhs = kb.slots([128, nkt, 65], BF16, 2, "Vh", st)
                kn2, b_kn2 = kb.sbb([128, S // 128], F32, "kn2", st)
                kst, b_kst = kb.sbb([128, 4], F32, "kst", st)
                krow, b_krow = kb.sbb([1, 132], F32, "krow", st)
                kmx, b_kmx = kb.sbb([128, 1], F32, "kmx", st)
                PTs = kb.slots([128, QBLK], BF16, 3, "PT", st)
                osbs = kb.slots([65, QBLK], F32, 2, "osb", st)
                rds = kb.slots([64, QBLK], F32, 2, "rd", st)
                mos = kb.slots([64, QBLK], BF16, 2, "mo", st)
                Esel, b_Esel = kb.sbb([65, 64], F32, "Esel", st)
                kb.op(dve, lambda: V.memset(Esel[:], 0.0), writes=[b_Esel])
                kb.op(dve, lambda: V.memset(Esel[64:65, :], 1.0), writes=[b_Esel])
                kb.set_banks(range(2, 8))
                for h in range(8):
                    KTh, bKTh = KThs.next()
                    QTh, bQTh = QThs.next()
                    Vh, bVh = Vhs.next()
                    kb.dma(sp, KTh[:, :], Z["KT"][h, :, 0:S], bKTh, reads=[ZB["KT"]], writes=[bKTh])
                    kb.dma(sp, QTh[:, :], Z["QT"][h, :, 0:S], bQTh, reads=[ZB["QT"]], writes=[bQTh])
                    with nc.allow_non_contiguous_dma(reason="per-head V rows (130B)"):
                        kb.dma(sp, Vh[:], Z["VS"][0:S, h * 65:(h + 1) * 65].rearrange("(kt p) c -> p kt c", p=128), bVh, reads=[ZB["VS"]], writes=[bVh])
                    kb.dma(sp, kn2[:], Z["KN"][h, 0:S].rearrange("(p f) -> p f", p=128), b_kn2, reads=[ZB["KN"]], writes=[b_kn2])
                    kb.op(dve, lambda: V.tensor_reduce(out=kst[:, 0:1], in_=kn2[:], axis=AX.X, op=ALU.max), reads=[b_kn2], writes=[b_kst])
                    pK1, bpK1 = kb.bank(1)
                    kb.op(pe, lambda pK1=pK1: PE.transpose(pK1[0:1, 0:128], kst[:, 0:1], ident[:]), reads=[b_kst, b_ident], writes=bpK1)
                    kb.op(dve, lambda pK1=pK1: V.tensor_copy(out=krow[:, 0:128], in_=pK1[0:1, 0:128]), reads=bpK1, writes=[b_krow])
                    kb.op(dve, lambda: V.tensor_reduce(out=krow[:, 128:129], in_=krow[:, 0:128], axis=AX.X, op=ALU.max), reads=[b_krow], writes=[b_krow])
                    pK2, bpK2 = kb.bank(1)
                    kb.op(pe, lambda pK2=pK2: PE.matmul(pK2[:, 0:1], lhsT=onesf[0:1, :], rhs=krow[:, 128:129], start=True, stop=True), reads=[b_onesf, b_krow], writes=bpK2)
                    kb.op(act, lambda pK2=pK2: ACT.activation(out=kmx[:], in_=pK2[:, 0:1], func=AF.Sqrt), reads=bpK2, writes=[b_kmx])
                    kb.op(dve, lambda QTh=QTh: V.tensor_scalar(out=QTh[96:97, :], in0=QTh[96:97, :], scalar1=kmx[96:97, 0:1], scalar2=None, op0=ALU.mult),
                          reads=[bQTh, b_kmx], writes=[bQTh])
                    for qb in range(nqb):
                        q0 = qb * QBLK
                        pO, bpO = kb.fixed_bank(qb % 2, 1)
                        prev = None
                        for kt in range(nkt):
                            pS, bpS = kb.bank(1)
                            kb.op(pe, lambda kt=kt, pS=pS: PE.matmul(pS[:, 0:QBLK], lhsT=KTh[:, kt * 128:(kt + 1) * 128], rhs=QTh[:, q0:q0 + QBLK], start=True, stop=True),
                                  reads=[bKTh, bQTh], writes=bpS)
                            PT, bPT = PTs.next()
                            kb.op(act, lambda pS=pS, PT=PT: ACT.activation(out=PT[:], in_=pS[:, 0:QBLK], func=AF.Exp), reads=bpS, writes=[bPT])
                            if prev is not None:
                                pkt, pPT, pbPT = prev
                                kb.op(pe, lambda pkt=pkt, pPT=pPT: PE.matmul(pO[0:65, 0:QBLK], lhsT=Vh[:, pkt, :], rhs=pPT[:], start=(pkt == 0), stop=False),
                                      reads=[bVh, pbPT], writes=bpO)
                            prev = (kt, PT, bPT)
                        pkt, pPT, pbPT = prev
                        kb.op(pe, lambda pkt=pkt, pPT=pPT: PE.matmul(pO[0:65, 0:QBLK], lhsT=Vh[:, pkt, :], rhs=pPT[:], start=(pkt == 0), stop=True),
                              reads=[bVh, pbPT], writes=bpO)
                        osb, bosb = osbs.next()
                        kb.op(dve, lambda: V.tensor_copy(out=osb[:], in_=pO[0:65, 0:QBLK]), reads=bpO, writes=[bosb])
                        pD, bpD = kb.bank(1)
                        kb.op(pe, lambda: PE.matmul(pD[0:64, 0:QBLK], lhsT=Esel[:], rhs=osb[:], start=True, stop=True), reads=[b_Esel, bosb], writes=bpD)
                        rd, brd = rds.next()
                        kb.op(dve, lambda: V.reciprocal(out=rd[:], in_=pD[0:64, 0:QBLK]), reads=bpD, writes=[brd])
                        mo, bmo = mos.next()
                        kb.op(dve, lambda: V.tensor_tensor(out=mo[:], in0=osb[0:64, :], in1=rd[:], op=ALU.mult), reads=[bosb, brd], writes=[bmo])
                        kb.dma(sp, Z["MO"][h, :, q0:q0 + QBLK], mo[:], bmo, reads=[bmo], writes=[ZB["MO"]])
                kb.barrier()

        def phase3a(si, S):
            nt = S // 128
            X = A[f"x{si}"]
            MEM = A[f"mem{si}"]
            kb.set_banks(range(8))
            with ExitStack() as st:
                work = dict(junk=kb.slots([128, 1024], BF16, 1, "junk", st), ss=kb.slots([128, 4], F32, 2, "ss", st),
                            xs=kb.slots([128, 1024], F32, 1, "xs", st))
                KmT, b_KmT = kb.sbb([128, 4, 2, 256], BF16, "KmT", st)
                Vm, b_Vm = kb.sbb([128, 2, 1024], BF16, "Vm", st)
                ncw, b_ncw = load_cols("norm_ca_w", 8, st)
                with ExitStack() as st2:
                    Wkv, b_Wkv = kb.sbb([128, 8, 2048], BF16, "Wkv", st2)
                    for kc in range(8):
                        load_cast(Wkv[:, kc, :], b_Wkv, A["ca_wkv"][0, kc * 128:(kc + 1) * 128, :])
                    nmm, b_nmm = load_cols("norm_mem_w", 8, st2)
                    memT, b_memT = kb.sbb([128, 8, 256], BF16, "memT", st2)
                    mts = kb.slots([128, 1024], F32, 2, "memt", st2)
                    for m in range(2):
                        mt, bmt = mts.next()
                        kb.dma(sp, mt[:], MEM[m * 128:(m + 1) * 128, :], bmt, writes=[bmt])
                        norm_transpose(mt[:], bmt, nmm, b_nmm, memT, b_memT, m * 128, work)
                    for j in range(8):
                        pK, bpK = kb.bank(1)
                        for kc in range(8):
                            kb.op(pe, lambda j=j, kc=kc, pK=pK: PE.matmul(pK[:, 0:256], lhsT=Wkv[:, kc, j * 128:(j + 1) * 128], rhs=memT[:, kc, :], start=(kc == 0), stop=(kc == 7)),
                                  reads=[b_Wkv, b_memT], writes=bpK)
                        kb.op(act, lambda j=j, pK=pK: ACT.copy(out=KmT[:, j // 2, j % 2, :], in_=pK[:, 0:256]), reads=bpK, writes=[b_KmT])
                    for mc in range(2):
                        for n in range(2):
                            pVm, bpVm = kb.bank(1)
                            for kc in range(8):
                                kb.op(pe, lambda mc=mc, n=n, kc=kc, pVm=pVm: PE.matmul(pVm, lhsT=memT[:, kc, mc * 128:(mc + 1) * 128], rhs=Wkv[:, kc, 1024 + n * 512:1024 + (n + 1) * 512],
                                                                                  start=(kc == 0), stop=(kc == 7)), reads=[b_memT, b_Wkv], writes=bpVm)
                            kb.op(act, lambda mc=mc, n=n, pVm=pVm: ACT.copy(out=Vm[:, mc, n * 512:(n + 1) * 512], in_=pVm), reads=bpVm, writes=[b_Vm])
                    kb.barrier()
                Wor, b_Wor = kb.sbb([128, 4, 1024], BF16, "Wor", st)
                for r in range(4):
                    load_cast(Wor[:, r, :], b_Wor, A["w_out"][0, r * 128:(r + 1) * 128, :])
                Wom, b_Wom = kb.sbb([64, 8, 1024], BF16, "Wom", st)
                for h in range(8):
                    load_cast(Wom[:, h, :], b_Wom, A["w_out"][0, 512 + h * 64:512 + (h + 1) * 64, :])
                Wq, b_Wq = kb.sbb([128, 8, 1024], BF16, "Wq", st)
                Wo, b_Wo = kb.sbb([128, 8, 1024], BF16, "Wo", st)
                for kc in range(8):
                    load_cast(Wq[:, kc, :], b_Wq, A["ca_wq"][0, kc * 128:(kc + 1) * 128, :])
                    load_cast(Wo[:, kc, :], b_Wo, A["ca_wo"][0, kc * 128:(kc + 1) * 128, :])
                gnw, b_gnw = kb.sbb([128, 512], F32, "gnw", st)
                kb.dma(sp, gnw[:], A["ret_gn_w"][0, :].partition_broadcast(128), b_gnw, writes=[b_gnw])
                dec, b_dec = kb.sbb([128, 8], F32, "dec3", st)
                kb.dma(sp, dec[:], A["ret_decay_bwd"][0, :].partition_broadcast(128), b_dec, writes=[b_dec])
                kb.op(act, lambda: ACT.activation(out=dec[:], in_=dec[:], func=AF.Exp, scale=-1.0), reads=[b_dec], writes=[b_dec])
                kb.op(act, lambda: ACT.activation(out=dec[:], in_=dec[:], func=AF.Ln, bias=1.0), reads=[b_dec], writes=[b_dec])
                kb.op(act, lambda: ACT.activation(out=dec[:], in_=dec[:], func=AF.Exp, scale=-128.0), reads=[b_dec], writes=[b_dec])
                GCb, b_GCb = kb.sbb([128, 4], F32, "GCb", st)
                decv = dec[:].rearrange("p (a two) -> p a two", two=2)
                kb.op(dve, lambda: V.tensor_copy(out=GCb[0:64, :], in_=decv[0:64, :, 0]), reads=[b_dec], writes=[b_GCb])
                kb.op(dve, lambda: V.tensor_copy(out=GCb[64:128, :], in_=decv[64:128, :, 1]), reads=[b_dec], writes=[b_GCb])
                Rb, b_Rb = kb.sbb([128, 4, 64], F32, "Rb", st)
                Rbb, b_Rbb = kb.sbb([128, 4, 128], BF16, "Rbb", st)
                kb.op(dve, lambda: V.memset(Rb[:], 0.0), writes=[b_Rb])
                kb.op(dve, lambda: V.memset(Rbb[:], 0.0), writes=[b_Rbb])
                yps = kb.slots([128, 512], F32, 2, "yp3", st)
                sgs = kb.slots([128, 512], F32, 2, "sg3", st)
                qbTs = kb.slots([128, 4, 128], BF16, 2, "qbT3", st)
                bbs = kb.slots([128, 4, 64], F32, 2, "bb3", st)
                xts = kb.slots([128, 1024], F32, 2, "xt3", st)
                mots = kb.slots([64, 8, 128], BF16, 2, "mot", st)
                ys = kb.slots([128, 8, 64], F32, 2, "y3", st)
                ycs = kb.slots([128, 8, 64], F32, 2, "yc3", st)
                y2s = kb.slots([128, 8, 64], F32, 2, "ysq3", st)
                sts = kb.slots([128, 8, 4], F32, 2, "st3", st)
                ros = kb.slots([128, 512], F32, 2, "ro3", st)
                roTs = kb.slots([128, 4, 128], BF16, 2, "roT", st)
                x1s = kb.slots([128, 1024], F32, 2, "x1", st)
                hnTs = kb.slots([128, 8, 128], BF16, 2, "hnT", st)
                qcTs = kb.slots([128, 8, 128], BF16, 2, "qcT", st)
                mxs = kb.slots([128, 4, 4], F32, 2, "mx3", st)
                Ps = kb.slots([128, 4, 256], F32, 2, "P3", st)
                PnTs = kb.slots([128, 8, 128], BF16, 2, "PnT", st)
                oTs = kb.slots([128, 8, 128], BF16, 2, "oT", st)
                x2s = kb.slots([128, 1024], F32, 2, "x2", st)
                for t in reversed(range(nt)):
                    r0 = t * 128
                    yp, byp = yps.next()
                    sg, bsg = sgs.next()
                    qbT, bqbT = qbTs.next()
                    bb, bbb = bbs.next()
                    xt, bx = xts.next()
                    mot, bmot = mots.next()
                    kb.dma(sp, yp[:], Z["YP"][r0:r0 + 128, :], byp, reads=[ZB["YP"]], writes=[byp])
                    kb.dma(sp, sg[:], Z["SG"][r0:r0 + 128, :], bsg, reads=[ZB["SG"]], writes=[bsg])
                    kb.dma(sp, qbT[:].rearrange("p a t -> p (a t)"), Z["QB"][t], bqbT, reads=[ZB["QB"]], writes=[bqbT])
                    kb.dma(sp, bb[:].rearrange("p a v -> p (a v)"), Z["BB"][t], bbb, reads=[ZB["BB"]], writes=[bbb])
                    kb.dma(sp, xt[:], X[r0:r0 + 128, :], bx, writes=[bx])
                    kb.dma(sp, mot[:], Z["MO"][:, :, r0:r0 + 128].rearrange("h v t -> v h t"), bmot, reads=[ZB["MO"]], writes=[bmot])
                    pY, bpY = kb.bank(1)
                    for a in range(4):
                        kb.op(pe, lambda a=a: PE.matmul(pY[:, a * 128:(a + 1) * 128], lhsT=qbT[:, a, :], rhs=Rbb[:, a, :], start=True, stop=True),
                              reads=[bqbT, b_Rbb], writes=bpY)
                    y, by = ys.next()
                    kb.op(dve, lambda: V.tensor_tensor(out=y[:].rearrange("p h d -> p (h d)"), in0=pY, in1=yp[:], op=ALU.add), reads=bpY + [byp], writes=[by])
                    kb.op(dve, lambda: V.tensor_tensor(out=Rb[:], in0=Rb[:], in1=GCb[:, :].unsqueeze(2).to_broadcast([128, 4, 64]), op=ALU.mult), reads=[b_Rb, b_GCb], writes=[b_Rb])
                    kb.op(dve, lambda: V.tensor_tensor(out=Rb[:], in0=Rb[:], in1=bb[:], op=ALU.add), reads=[b_Rb, bbb], writes=[b_Rb])
                    kb.op(dve, lambda: V.tensor_copy(out=Rbb[0:64, :, 0:64], in_=Rb[0:64]), reads=[b_Rb], writes=[b_Rbb])
                    kb.op(dve, lambda: V.tensor_copy(out=Rbb[64:128, :, 64:128], in_=Rb[64:128]), reads=[b_Rb], writes=[b_Rbb])
                    stt, bst = sts.next()
                    yc, byc = ycs.next()
                    ysq, bysq = y2s.next()
                    kb.op(dve, lambda: V.tensor_reduce(out=stt[:, :, 0], in_=y[:], axis=AX.X, op=ALU.add), reads=[by], writes=[bst])
                    kb.op(dve, lambda: V.tensor_scalar(out=stt[:, :, 1], in0=stt[:, :, 0], scalar1=1.0 / 64, scalar2=None, op0=ALU.mult), reads=[bst], writes=[bst])
                    kb.op(dve, lambda: V.tensor_tensor(out=yc[:], in0=y[:], in1=stt[:, :, 1:2].to_broadcast([128, 8, 64]), op=ALU.subtract), reads=[by, bst], writes=[byc])
                    kb.op(pool, lambda: G.tensor_tensor(out=ysq[:], in0=yc[:], in1=yc[:], op=ALU.mult), reads=[byc], writes=[bysq])
                    kb.op(dve, lambda: V.tensor_reduce(out=stt[:, :, 2], in_=ysq[:], axis=AX.X, op=ALU.add), reads=[bysq], writes=[bst])
                    kb.op(act, lambda: ACT.activation(out=stt[:, :, 3], in_=stt[:, :, 2], func=AF.Sqrt, scale=1.0 / 64, bias=epsg[:, 0:1]), reads=[bst, b_epsg], writes=[bst])
                    kb.op(dve, lambda: V.reciprocal(out=stt[:, :, 3], in_=stt[:, :, 3]), reads=[bst], writes=[bst])
                    kb.op(dve, lambda: V.tensor_tensor(out=yc[:], in0=yc[:], in1=stt[:, :, 3:4].to_broadcast([128, 8, 64]), op=ALU.mult), reads=[byc, bst], writes=[byc])
                    ro, bro = ros.next()
                    kb.op(pool, lambda: G.tensor_tensor(out=ro[:], in0=yc[:].rearrange("p h d -> p (h d)"), in1=gnw[:], op=ALU.mult), reads=[byc, b_gnw], writes=[bro])
                    kb.op(pool, lambda: G.tensor_tensor(out=ro[:], in0=ro[:], in1=sg[:], op=ALU.mult), reads=[bro, bsg], writes=[bro])
                    pT, bpT = kb.bank(1)
                    for r in range(4):
                        kb.op(pe, lambda r=r: PE.transpose(pT[:, r * 128:(r + 1) * 128], ro[:, r * 128:(r + 1) * 128], ident[:]), reads=[bro, b_ident], writes=bpT)
                    roT, broT = roTs.next()
                    kb.op(act, lambda: ACT.copy(out=roT[:].rearrange("p r t -> p (r t)"), in_=pT), reads=bpT, writes=[broT])
                    pM, bpM = kb.bank(2)
                    for n in range(2):
                        for r in range(4):
                            kb.op(pe, lambda n=n, r=r: PE.matmul(pM[:, n * 512:(n + 1) * 512], lhsT=roT[:, r, :], rhs=Wor[:, r, n * 512:(n + 1) * 512], start=(r == 0), stop=False),
                                  reads=[broT, b_Wor], writes=[bpM[n]])
                        for h in range(8):
                            kb.op(pe, lambda n=n, h=h: PE.matmul(pM[:, n * 512:(n + 1) * 512], lhsT=mot[:, h, :], rhs=Wom[:, h, n * 512:(n + 1) * 512], start=False, stop=(h == 7)),
                                  reads=[bmot, b_Wom], writes=[bpM[n]])
                    x1, bx1 = x1s.next()
                    kb.op(dve, lambda: V.tensor_tensor(out=x1[:], in0=pM, in1=xt[:], op=ALU.add), reads=bpM + [bx], writes=[bx1])
                    hnT, bhnT = hnTs.next()
                    norm_transpose(x1[:], bx1, ncw, b_ncw, hnT, bhnT, 0, work)
                    pQ, bpQ = kb.bank(2)
                    for j in range(8):
                        for kc in range(8):
                            kb.op(pe, lambda j=j, kc=kc: PE.matmul(pQ[:, j * 128:(j + 1) * 128], lhsT=Wq[:, kc, j * 128:(j + 1) * 128], rhs=hnT[:, kc, :], start=(kc == 0), stop=(kc == 7)),
                                  reads=[b_Wq, bhnT], writes=[bpQ[j // 4]])
                    qcT, bqcT = qcTs.next()
                    kb.op(act, lambda: ACT.activation(out=qcT[:].rearrange("p j t -> p (j t)"), in_=pQ, func=AF.Copy, scale=1.0 / 16), reads=bpQ, writes=[bqcT])
                    pS, bpS = kb.bank(2)
                    for hd in range(4):
                        for dc in range(2):
                            kb.op(pe, lambda hd=hd, dc=dc: PE.matmul(pS[:, hd * 256:(hd + 1) * 256], lhsT=qcT[:, hd * 2 + dc, :], rhs=KmT[:, hd, dc, :], start=(dc == 0), stop=(dc == 1)),
                                  reads=[bqcT, b_KmT], writes=[bpS[hd // 2]])
                    mx, bmx = mxs.next()
                    kb.op(dve, lambda: V.tensor_reduce(out=mx[:, :, 0], in_=pS.rearrange("p (h m) -> p h m", h=4), axis=AX.X, op=ALU.max), reads=bpS, writes=[bmx])
                    kb.op(dve, lambda: V.tensor_scalar(out=mx[:, :, 1], in0=mx[:, :, 0], scalar1=-1.0, scalar2=None, op0=ALU.mult), reads=[bmx], writes=[bmx])
                    P_, bP = Ps.next()
                    for hd in range(4):
                        kb.op(act, lambda hd=hd: ACT.activation(out=P_[:, hd, :], in_=pS[:, hd * 256:(hd + 1) * 256], func=AF.Exp, bias=mx[:, hd, 1:2], accum_out=mx[:, hd, 2:3]),
                              reads=[bpS[hd // 2], bmx], writes=[bP, bmx])
                    kb.op(dve, lambda: V.reciprocal(out=mx[:, :, 3], in_=mx[:, :, 2]), reads=[bmx], writes=[bmx])
                    kb.op(dve, lambda: V.tensor_tensor(out=P_[:], in0=P_[:], in1=mx[:, :, 3:4].to_broadcast([128, 4, 256]), op=ALU.mult), reads=[bP, bmx], writes=[bP])
                    pPT, bpPT = kb.bank(2)
                    Pf = P_[:].rearrange("p h m -> p (h m)")
                    for j in range(8):
                        kb.op(pe, lambda j=j: PE.transpose(pPT[:, j * 128:(j + 1) * 128], Pf[:, j * 128:(j + 1) * 128], ident[:]), reads=[bP, b_ident], writes=[bpPT[j // 4]])
                    PnT, bPnT = PnTs.next()
                    kb.op(act, lambda: ACT.copy(out=PnT[:].rearrange("p j t -> p (j t)"), in_=pPT), reads=bpPT, writes=[bPnT])
                    pO, bpO = kb.bank(2)
                    for hd in range(4):
                        for dd in range(2):
                            for mc in range(2):
                                kb.op(pe, lambda hd=hd, dd=dd, mc=mc: PE.matmul(pO[:, (hd * 2 + dd) * 128:(hd * 2 + dd + 1) * 128],
                                                                                lhsT=Vm[:, mc, hd * 256 + dd * 128:hd * 256 + (dd + 1) * 128], rhs=PnT[:, hd * 2 + mc, :],
                                                                                start=(mc == 0), stop=(mc == 1)), reads=[b_Vm, bPnT], writes=[bpO[(hd * 2 + dd) // 4]])
                    oT, boT = oTs.next()
                    kb.op(act, lambda: ACT.copy(out=oT[:].rearrange("p j t -> p (j t)"), in_=pO), reads=bpO, writes=[boT])
                    pC, bpC = kb.bank(2)
                    for n in range(2):
                        for j in range(8):
                            kb.op(pe, lambda n=n, j=j: PE.matmul(pC[:, n * 512:(n + 1) * 512], lhsT=oT[:, j, :], rhs=Wo[:, j, n * 512:(n + 1) * 512], start=(j == 0), stop=(j == 7)),
                                  reads=[boT, b_Wo], writes=[bpC[n]])
                    x2, bx2 = x2s.next()
                    kb.op(dve, lambda: V.tensor_tensor(out=x2[:], in0=pC, in1=x1[:], op=ALU.add), reads=bpC + [bx1], writes=[bx2])
                    kb.dma(sp, Z["X2"][r0:r0 + 128, :], x2[:], bx2, reads=[bx2], writes=[ZB["X2"]])
                kb.barrier()

        def phase3b(si, S):
            TB = 256
            nb = S // TB
            Y = A[f"y{si}"]
            with ExitStack() as st:
                work = dict(junk=kb.slots([128, 1024], BF16, 1, "junk", st), ss=kb.slots([128, 4], F32, 2, "ss", st),
                            xs=kb.slots([128, 1024], F32, 1, "xs", st))
                kb.set_banks(range(8))
                Wpq, b_Wpq = kb.sbb([128, 8, 2048], BF16, "Wpq", st)
                for kc in range(8):
                    load_cast(Wpq[:, kc, :], b_Wpq, A["peer_wq"][0, kc * 128:(kc + 1) * 128, :])
                nfw, b_nfw = load_cols("norm_ffn_w", 8, st)
                fnw, b_fnw = kb.sbb([128, 1024], F32, "fnw", st)
                kb.dma(sp, fnw[:], A["final_norm_w"].partition_broadcast(128), b_fnw, writes=[b_fnw])
                SKT, b_SKT = kb.sbb([128, 16, 128], BF16, "SKT", st)
                sks = kb.slots([128, 128], F32, 2, "skld", st)
                for g in range(16):
                    h, half = g // 2, g % 2
                    skt, bskt = sks.next()
                    kb.dma(sp, skt[:], A["peer_sub_keys"][0, half, h, :, :], bskt, writes=[bskt])
                    pT, bpT = kb.bank(1)
                    kb.op(pe, lambda skt=skt, pT=pT: PE.transpose(pT[:, 0:128], skt[:], ident[:]), reads=[bskt, b_ident], writes=bpT)
                    kb.op(act, lambda g=g, pT=pT: ACT.copy(out=SKT[:, g, :], in_=pT[:, 0:128]), reads=bpT, writes=[b_SKT])
                iota16, b_iota16 = kb.sbb([128, 16], F32, "iota16", st)
                kb.op(dve, lambda: V.tensor_copy(out=iota16[:], in_=iota[:, 0:16]), reads=[b_iota], writes=[b_iota16])
                x2ts = kb.slots([128, 2, 1024], F32, 1, "x2b", st)
                xn3Ts = kb.slots([128, 8, TB], BF16, 1, "xn3T", st)
                qpTs = kb.slots([128, 16, TB], BF16, 1, "qpT", st)
                reps = kb.slots([128, 256], F32, 2, "rep", st)
                v16s = kb.slots([128, 16, 16], F32, 1, "v16", st)
                ix16s = kb.slots([128, 16, 16], U32, 1, "ix16", st)
                ixfs = kb.slots([128, 16, 16], F32, 1, "ixf", st)
                cands = kb.slots([128, 8, 256], F32, 1, "cand", st)
                m2s = kb.slots([128, 8, 16], F32, 1, "m2", st)
                p2s = kb.slots([128, 8, 16], U32, 1, "p2", st)
                abs_ = kb.slots([128, 2, 128], U32, 1, "abu", st)
                abfs = kb.slots([128, 2, 128], F32, 1, "abf", st)
                ohs = kb.slots([128, 8, 16, 16], BF16, 1, "oh", st)
                sel3s = kb.slots([128, 3, 128], F32, 1, "sel3", st)
                zs = kb.slots([128, 8, 2], F32, 1, "z", st)
                T3s = kb.slots([128, 3, 128], F32, 1, "T3", st)
                Rs = kb.slots([128, 32, 128], BF16, 1, "Roh", st)
                L0s = kb.slots([128, 32, 128], BF16, 1, "L0oh", st)
                Gm, b_Gm = kb.sbb([128, TB, 128], BF16, "Gm", st)
                UTs = kb.slots([128, 8, 128], BF16, 2, "UTi", st)
                Vis = kb.slots([128, 1024], BF16, 2, "Vi", st)
                gas = kb.slots([128, TB], F32, 2, "ga", st)
                Wds = kb.slots([128, TB], BF16, 2, "Wd", st)
                x3s = kb.slots([128, 1024], F32, 1, "x3", st)
                outs = kb.slots([128, 1024], F32, 1, "outt", st)
                for b in range(nb):
                    r0 = b * TB
                    kb.set_banks(range(8))
                    x2b, bx2b = x2ts.next()
                    kb.dma(sp, x2b[:], Z["X2"][r0:r0 + TB, :].rearrange("(t p) d -> p t d", p=128), bx2b, reads=[ZB["X2"]], writes=[bx2b])
                    xn3T, bxn3T = xn3Ts.next()
                    for tt in range(2):
                        norm_transpose(x2b[:, tt, :], bx2b, nfw, b_nfw, xn3T, bxn3T, tt * 128, work)
                    qpT, bqpT = qpTs.next()
                    for g in range(16):
                        pQ, bpQ = kb.bank(1)
                        for kc in range(8):
                            kb.op(pe, lambda g=g, kc=kc, pQ=pQ: PE.matmul(pQ[:, 0:TB], lhsT=Wpq[:, kc, g * 128:(g + 1) * 128], rhs=xn3T[:, kc, :], start=(kc == 0), stop=(kc == 7)),
                                  reads=[b_Wpq, bxn3T], writes=bpQ)
                        if g % 2 == 0:
                            kb.op(act, lambda g=g, pQ=pQ: ACT.copy(out=qpT[:, g, :], in_=pQ[:, 0:TB]), reads=bpQ, writes=[bqpT])
                        else:
                            kb.op(dve, lambda g=g, pQ=pQ: V.tensor_copy(out=qpT[:, g, :], in_=pQ[:, 0:TB]), reads=bpQ, writes=[bqpT])
                    for tt in range(2):
                        pSs = []
                        for q4 in range(4):
                            pS, bpS = kb.bank(1)
                            pSs.append((pS, bpS))
                            for gg in range(4):
                                g = q4 * 4 + gg
                                kb.op(pe, lambda g=g, gg=gg, pS=pS: PE.matmul(pS[:, gg * 128:(gg + 1) * 128], lhsT=qpT[:, g, tt * 128:(tt + 1) * 128], rhs=SKT[:, g, :], start=True, stop=True),
                                      reads=[bqpT, b_SKT], writes=bpS)
                        v16, bv16 = v16s.next()
                        ix16, bix16 = ix16s.next()
                        for g in range(16):
                            pS, bpS = pSs[g // 4]
                            src = pS[:, (g % 4) * 128:(g % 4 + 1) * 128]
                            rep, brep = reps.next()
                            kb.op(dve, lambda g=g, src=src: V.max(out=v16[:, g, 0:8], in_=src), reads=bpS, writes=[bv16])
                            kb.op(dve, lambda g=g, rep=rep, src=src: V.match_replace(out=rep[:, 0:128], in_to_replace=v16[:, g, 0:8], in_values=src, imm_value=NEG), reads=bpS + [bv16], writes=[brep])
                            kb.op(dve, lambda g=g, rep=rep: V.max(out=v16[:, g, 8:16], in_=rep[:, 0:128]), reads=[brep], writes=[bv16])
                            kb.op(dve, lambda g=g, src=src: V.max_index(out=ix16[:, g, 0:8], in_max=v16[:, g, 0:8], in_values=src), reads=bpS + [bv16], writes=[bix16])
                            kb.op(dve, lambda g=g, rep=rep: V.max_index(out=ix16[:, g, 8:16], in_max=v16[:, g, 8:16], in_values=rep[:, 0:128]), reads=[brep, bv16], writes=[bix16])
                        ixf, bixf = ixfs.next()
                        kb.op(dve, lambda: V.tensor_copy(out=ixf[:], in_=ix16[:]), reads=[bix16], writes=[bixf])
                        cand, bcand = cands.next()
                        v4 = v16[:].rearrange("p (h two) k -> p h two k", two=2)
                        kb.op(dve, lambda: V.tensor_tensor(out=cand[:].rearrange("p h (a b) -> p h a b", a=16),
                                                            in0=v4[:, :, 0, :].unsqueeze(3).to_broadcast([128, 8, 16, 16]),
                                                            in1=v4[:, :, 1, :].unsqueeze(2).to_broadcast([128, 8, 16, 16]), op=ALU.add), reads=[bv16], writes=[bcand])
                        m2, bm2 = m2s.next()
                        p2, bp2 = p2s.next()
                        for h in range(8):
                            rep, brep = reps.next()
                            kb.op(dve, lambda h=h: V.max(out=m2[:, h, 0:8], in_=cand[:, h, :]), reads=[bcand], writes=[bm2])
                            kb.op(dve, lambda h=h, rep=rep: V.match_replace(out=rep[:], in_to_replace=m2[:, h, 0:8], in_values=cand[:, h, :], imm_value=NEG), reads=[bcand, bm2], writes=[brep])
                            kb.op(dve, lambda h=h, rep=rep: V.max(out=m2[:, h, 8:16], in_=rep[:]), reads=[brep], writes=[bm2])
                            kb.op(dve, lambda h=h: V.max_index(out=p2[:, h, 0:8], in_max=m2[:, h, 0:8], in_values=cand[:, h, :]), reads=[bcand, bm2], writes=[bp2])
                            kb.op(dve, lambda h=h, rep=rep: V.max_index(out=p2[:, h, 8:16], in_max=m2[:, h, 8:16], in_values=rep[:]), reads=[brep, bm2], writes=[bp2])
                        abu, babu = abs_.next()
                        abf, babf = abfs.next()
                        p2f = p2[:].rearrange("p h k -> p (h k)")
                        kb.op(dve, lambda: V.tensor_single_scalar(out=abu[:, 0, :], in_=p2f, scalar=4, op=ALU.logical_shift_right), reads=[bp2], writes=[babu])
                        kb.op(dve, lambda: V.tensor_single_scalar(out=abu[:, 1, :], in_=p2f, scalar=15, op=ALU.bitwise_and), reads=[bp2], writes=[babu])
                        kb.op(dve, lambda: V.tensor_copy(out=abf[:], in_=abu[:]), reads=[babu], writes=[babf])
                        sel3, bsel3 = sel3s.next()
                        ix4 = ixf[:].rearrange("p (h two) k -> p h two k", two=2)
                        for w in range(2):
                            oh, boh = ohs.next()
                            kb.op(dve, lambda w=w, oh=oh: V.tensor_tensor(out=oh[:], in0=abf[:, w, :].rearrange("p (h k) -> p h k", h=8).unsqueeze(3).to_broadcast([128, 8, 16, 16]),
                                                                          in1=iota16[:].unsqueeze(1).unsqueeze(1).to_broadcast([128, 8, 16, 16]), op=ALU.is_equal),
                                  reads=[babf, b_iota16], writes=[boh])
                            kb.op(dve, lambda w=w, oh=oh: V.tensor_tensor(out=oh[:], in0=oh[:], in1=ix4[:, :, w, :].unsqueeze(2).to_broadcast([128, 8, 16, 16]), op=ALU.mult),
                                  reads=[boh, bixf], writes=[boh])
                            kb.op(dve, lambda w=w, oh=oh: V.tensor_reduce(out=sel3[:, w, :].rearrange("p (h k) -> p h k", h=8), in_=oh[:], axis=AX.X, op=ALU.add),
                                  reads=[boh], writes=[bsel3])
                        z, bz = zs.next()
                        g3 = sel3[:, 2, :].rearrange("p (h k) -> p h k", h=8)
                        kb.op(dve, lambda: V.tensor_tensor(out=g3, in0=m2[:], in1=m2[:, :, 0:1].to_broadcast([128, 8, 16]), op=ALU.subtract), reads=[bm2], writes=[bsel3])
                        kb.op(act, lambda: ACT.activation(out=sel3[:, 2, :], in_=sel3[:, 2, :], func=AF.Exp), reads=[bsel3], writes=[bsel3])
                        kb.op(dve, lambda: V.tensor_reduce(out=z[:, :, 0], in_=g3, axis=AX.X, op=ALU.add), reads=[bsel3], writes=[bz])
                        kb.op(dve, lambda: V.reciprocal(out=z[:, :, 1], in_=z[:, :, 0]), reads=[bz], writes=[bz])
                        kb.op(dve, lambda: V.tensor_tensor(out=g3, in0=g3, in1=z[:, :, 1:2].to_broadcast([128, 8, 16]), op=ALU.mult), reads=[bsel3, bz], writes=[bsel3])
                        pT, bpT = kb.bank(1)
                        for w in range(3):
                            kb.op(pe, lambda w=w: PE.transpose(pT[:, w * 128:(w + 1) * 128], sel3[:, w, :], ident[:]), reads=[bsel3, b_ident], writes=bpT)
                        T3, bT3 = T3s.next()
                        kb.op(act, lambda: ACT.copy(out=T3[:].rearrange("p w t -> p (w t)"), in_=pT[:, 0:384]), reads=bpT, writes=[bT3])
                        for hf in range(4):
                            t0 = hf * 32
                            R_, bR = Rs.next()
                            L_, bL = L0s.next()
                            io_b = iota[:].unsqueeze(1).to_broadcast([128, 32, 128])
                            kb.op(dve, lambda R_=R_, t0=t0: V.tensor_tensor(out=R_[:], in0=io_b, in1=T3[:, 1, t0:t0 + 32].unsqueeze(2).to_broadcast([128, 32, 128]), op=ALU.is_equal),
                                  reads=[b_iota, bT3], writes=[bR])
                            kb.op(dve, lambda L_=L_, t0=t0: V.tensor_tensor(out=L_[:], in0=io_b, in1=T3[:, 0, t0:t0 + 32].unsqueeze(2).to_broadcast([128, 32, 128]), op=ALU.is_equal),
                                  reads=[b_iota, bT3], writes=[bL])
                            kb.op(dve, lambda L_=L_, t0=t0: V.tensor_tensor(out=L_[:], in0=L_[:], in1=T3[:, 2, t0:t0 + 32].unsqueeze(2).to_broadcast([128, 32, 128]), op=ALU.mult),
                                  reads=[bL, bT3], writes=[bL])
                            for q in range(8):
                                pG, bpG = kb.bank(1)
                                for u in range(4):
                                    tl = q * 4 + u
                                    kb.op(pe, lambda tl=tl, u=u, pG=pG, R_=R_, L_=L_: PE.matmul(pG[:, u * 128:(u + 1) * 128], lhsT=R_[:, tl, :], rhs=L_[:, tl, :], start=True, stop=True),
                                          reads=[bR, bL], writes=bpG)
                                tg = tt * 128 + t0 + q * 4
                                if q % 2 == 0:
                                    kb.op(act, lambda tg=tg, pG=pG: ACT.copy(out=Gm[:, tg:tg + 4, :].rearrange("p t i -> p (t i)"), in_=pG), reads=bpG, writes=[b_Gm])
                                else:
                                    kb.op(dve, lambda tg=tg, pG=pG: V.tensor_copy(out=Gm[:, tg:tg + 4, :].rearrange("p t i -> p (t i)"), in_=pG), reads=bpG, writes=[b_Gm])
                    pO0, bpO0 = kb.fixed_bank(0, 2)
                    pO1, bpO1 = kb.fixed_bank(2, 2)
                    pOs = ((pO0, bpO0), (pO1, bpO1))
                    kb.set_banks(range(4, 8))
                    for i in range(NE):
                        UTi, bUTi = UTs.next()
                        Vi, bVi = Vis.next()
                        kb.dma(sp, UTi[:].rearrange("p k j -> p (k j)"), Z["UT"][i], bUTi, reads=[ZB["UT"]], writes=[bUTi])
                        kb.dma(sp, Vi[:], Z["VB"][i], bVi, reads=[ZB["VB"]], writes=[bVi])
                        pA, bpA = kb.bank(1)
                        for kc in range(8):
                            kb.op(pe, lambda kc=kc, pA=pA, UTi=UTi: PE.matmul(pA[:, 0:TB], lhsT=UTi[:, kc, :], rhs=xn3T[:, kc, :], start=(kc == 0), stop=(kc == 7)),
                                  reads=[bUTi, bxn3T], writes=bpA)
                        ga, bga = gas.next()
                        kb.op(act, lambda pA=pA, ga=ga: ACT.activation(out=ga[:], in_=pA[:, 0:TB], func=AF.Gelu), reads=bpA, writes=[bga])
                        Wd, bWd = Wds.next()
                        kb.op(dve, lambda i=i, ga=ga, Wd=Wd: V.tensor_tensor(out=Wd[:], in0=ga[:], in1=Gm[:, :, i], op=ALU.mult), reads=[bga, b_Gm], writes=[bWd])
                        for tt in range(2):
                            pO, bpO = pOs[tt]
                            for n in range(2):
                                kb.op(pe, lambda tt=tt, n=n, pO=pO, Wd=Wd, Vi=Vi, i=i: PE.matmul(pO[:, n * 512:(n + 1) * 512], lhsT=Wd[:, tt * 128:(tt + 1) * 128], rhs=Vi[:, n * 512:(n + 1) * 512],
                                                                                         start=(i == 0), stop=(i == NE - 1)), reads=[bWd, bVi], writes=[bpO[n]])
                    for tt in range(2):
                        pO, bpO = pOs[tt]
                        x3, bx3 = x3s.next()
                        kb.op(dve, lambda pO=pO, x3=x3, tt=tt: V.tensor_tensor(out=x3[:], in0=pO, in1=x2b[:, tt, :], op=ALU.add), reads=bpO + [bx2b], writes=[bx3])
                        junk, b_junk = work["junk"].next()
                        ss, b_ss = work["ss"].next()
                        kb.op(act, lambda x3=x3, junk=junk, ss=ss: ACT.activation(out=junk[:], in_=x3[:], func=AF.Square, accum_out=ss[:, 0:1]), reads=[bx3], writes=[b_junk, b_ss])
                        kb.op(act, lambda ss=ss: ACT.activation(out=ss[:, 1:2], in_=ss[:, 0:1], func=AF.Sqrt, scale=1.0 / D, bias=epsn[:, 0:1]), reads=[b_ss, b_epsn], writes=[b_ss])
                        kb.op(dve, lambda ss=ss: V.reciprocal(out=ss[:, 2:3], in_=ss[:, 1:2]), reads=[b_ss], writes=[b_ss])
                        ot, bot = outs.next()
                        kb.op(dve, lambda x3=x3, ot=ot, ss=ss: V.scalar_tensor_tensor(out=ot[:], in0=x3[:], scalar=ss[:, 2:3], in1=fnw[:], op0=ALU.mult, op1=ALU.mult),
                              reads=[bx3, b_ss, b_fnw], writes=[bot])
                        kb.dma(sp, Y[r0 + tt * 128:r0 + (tt + 1) * 128, :], ot[:], bot, reads=[bot])
                kb.barrier()

        if prepass:
            kb.begin_phase()
            prepass_peer()
            kb.end_phase()
        nph = 0
        for si, S in enumerate(S_list):
            for ph in (phase1, phase2, phase3a, phase3b):
                if nph >= upto:
                    break
                nph += 1
                kb.set_banks(range(8))
                kb.begin_phase()
                ph(si, S)
                kb.end_phase()
        kb.barrier()
        kb.stats = {E.name: E.n for E in kb.engs}
        kb.stats["nsem"] = kb.nsem
        kb.stats["nops"] = kb.nops
        build.last_stats = kb.stats
    return nc


_CACHE = {}


def kernel(**inputs):
    ncores = 8
    S0 = inputs["x_prompt"].shape[1]
    S1 = inputs["x_sample"].shape[1]
    key = (S0, S1)
    if key not in _CACHE:
        _CACHE[key] = build([S0, S1])
    nc = _CACHE[key]
    consts = host_consts(max(S0, S1))
    in_maps = []
    for c in range(ncores):
        m = {"x0": np.ascontiguousarray(inputs["x_prompt"][c], dtype=np.float32),
             "x1": np.ascontiguousarray(inputs["x_sample"][c], dtype=np.float32),
             "mem0": np.ascontiguousarray(inputs["mem_prompt"][c], dtype=np.float32),
             "mem1": np.ascontiguousarray(inputs["mem_sample"][c], dtype=np.float32)}
        for n in W_NAMES:
            m[n] = np.ascontiguousarray(inputs[n], dtype=np.float32)
        m.update(consts)
        in_maps.append(m)
    res = run_bass_kernel_spmd(nc, in_maps, core_ids=list(range(ncores)))
    y0 = np.stack([np.asarray(res.results[c]["y0"], dtype=np.float32) for c in range(ncores)], 0)
    y1 = np.stack([np.asarray(res.results[c]["y1"], dtype=np.float32) for c in range(ncores)], 0)
    return (y0, y1)
```

```python
from contextlib import ExitStack
import numpy as np
import ml_dtypes
import concourse.bass as bass
import concourse.mybir as mybir
from concourse.bass_utils import run_bass_kernel_spmd

F32 = mybir.dt.float32
BF16 = mybir.dt.bfloat16
U32 = mybir.dt.uint32
AF = mybir.ActivationFunctionType
ALU = mybir.AluOpType
AX = mybir.AxisListType

SEM_LIMIT = 30000
D = 1024
NEG = -1.0e30


class Buf:
    __slots__ = ("name", "w", "r", "multi", "dsem", "dcnt", "excl")

    def __init__(self, name, multi=False, excl=False):
        self.name = name
        self.excl = excl
        self.w = {}
        self.r = {}
        self.multi = multi
        self.dsem = None
        self.dcnt = 0


class Eng:
    def __init__(self, kb, name, eng, own_wait=True):
        self.kb = kb
        self.name = name
        self.eng = eng
        self.sem = None
        self.val = 0
        self.seen = {}
        self.own_wait = own_wait
        self.n = 0

    def next_event(self):
        if self.sem is None or self.val >= SEM_LIMIT:
            self.sem = self.kb.new_sem(self.name)
            self.val = 0
        self.val += 1
        return (self.sem, self.val)


class Slots:
    def __init__(self, items):
        self.items = items
        self.i = 0

    def next(self):
        it = self.items[self.i % len(self.items)]
        self.i += 1
        return it


class KB:
    def __init__(self, nc, stack):
        self.nc = nc
        self.stack = stack
        self.nsem = 0
        self.pe = Eng(self, "pe", nc.tensor, own_wait=False)
        self.dve = Eng(self, "dve", nc.vector)
        self.act = Eng(self, "act", nc.scalar)
        self.pool = Eng(self, "pool", nc.gpsimd)
        self.sp = Eng(self, "sp", nc.sync)
        self.engs = [self.pe, self.dve, self.act, self.pool, self.sp]
        self.anchors = []
        self.free_sems = []
        self.nops = 0
        import os
        self.limit = int(os.environ.get("OPLIMIT", "1000000000"))
        self.phase_mark = 0
        self.persist = []
        self.uid = 0
        self.PS = None
        self.bank_bufs = None
        self.bank_pool = list(range(8))
        self.bank_i = 0

    def new_sem(self, name):
        self.nsem += 1
        return self.stack.enter_context(self.nc.semaphore(f"s{self.nsem}_{name}"))

    def sb(self, shape, dtype, name="sb", stack=None):
        self.uid += 1
        st = stack or self.stack
        return st.enter_context(self.nc.sbuf_tensor(f"{name}_{self.uid}", list(shape), dtype))

    def sbb(self, shape, dtype, name="sb", stack=None):
        return self.sb(shape, dtype, name, stack), Buf(name)

    def slots(self, shape, dtype, n, name, stack=None):
        return Slots([self.sbb(shape, dtype, f"{name}{i}", stack) for i in range(n)])

    def init_psum(self):
        self.PS = self.stack.enter_context(self.nc.psum_tensor("PSALL", [128, 8 * 512], F32))
        self.bank_bufs = [Buf(f"bank{i}", excl=True) for i in range(8)]

    def set_banks(self, pool):
        self.bank_pool = list(pool)
        self.bank_i = 0

    def bank(self, n=1):
        L = len(self.bank_pool)
        while True:
            i = self.bank_i % L
            idx = self.bank_pool[i:i + n]
            if len(idx) == n and all(idx[k] == idx[0] + k for k in range(n)):
                break
            self.bank_i += 1
        self.bank_i += n
        b0 = idx[0]
        return self.PS[:, b0 * 512:(b0 + n) * 512], [self.bank_bufs[b] for b in idx]

    def fixed_bank(self, b0, n=1):
        return self.PS[:, b0 * 512:(b0 + n) * 512], [self.bank_bufs[b] for b in range(b0, b0 + n)]

    def _waits(self, E, reads, writes):
        need = {}

        def add(d):
            for k, ev in d.items():
                if k not in need or need[k][1] < ev[1]:
                    need[k] = ev
        own = id(E.sem) if E.sem is not None else None
        for b in reads:
            add(b.w)
            if b.excl:
                add({k: ev for k, ev in b.r.items() if k != own})
        for b in writes:
            if not b.multi:
                add(b.w)
            add(b.r)
        for k, (sem, val) in need.items():
            if (not E.own_wait) and E.sem is not None and k == id(E.sem):
                continue
            if E.seen.get(k, 0) >= val:
                continue
            E.eng.wait_ge(sem, val)
            E.seen[k] = val

    def _record(self, ev, reads, writes):
        k = id(ev[0])
        for b in reads:
            if k not in b.r or b.r[k][1] < ev[1]:
                b.r[k] = ev
        for b in writes:
            if b.multi:
                if k not in b.w or b.w[k][1] < ev[1]:
                    b.w[k] = ev
            else:
                b.w = {k: ev}
            b.r = {}

    def op(self, E, fn, reads=(), writes=()):
        self.nops += 1
        if self.nops > self.limit:
            return None
        self._waits(E, reads, writes)
        inst = fn()
        ev = E.next_event()
        inst.then_inc(ev[0], 1)
        self._record(ev, reads, writes)
        E.n += 1
        return inst

    def dma(self, Q, out, in_, anchor, reads=(), writes=(), **kw):
        self.nops += 1
        if self.nops > self.limit:
            return None
        self._waits(Q, reads, writes)
        if anchor.dsem is None:
            if self.free_sems:
                anchor.dsem, anchor.dcnt = self.free_sems.pop()
            else:
                anchor.dsem, anchor.dcnt = self.new_sem("d_" + anchor.name), 0
            self.anchors.append(anchor)
        inst = Q.eng.dma_start(out=out, in_=in_, **kw)
        anchor.dcnt += 16
        ev = (anchor.dsem, anchor.dcnt)
        inst.then_inc(anchor.dsem, 16)
        self._record(ev, reads, writes)
        Q.n += 1
        return inst

    def barrier(self):
        evs = {}
        for E in self.engs:
            if E.sem is not None:
                evs[id(E.sem)] = (E.sem, E.val)
        for a in self.anchors:
            evs[id(a.dsem)] = (a.dsem, a.dcnt)
        S = self.sp
        for k, (sem, val) in evs.items():
            if S.seen.get(k, 0) >= val:
                continue
            S.eng.wait_ge(sem, val)
            S.seen[k] = val
        inst = S.eng.nop()
        ev = S.next_event()
        inst.then_inc(ev[0], 1)
        for E in self.engs:
            if E is S:
                continue
            E.eng.wait_ge(ev[0], ev[1])
            E.seen[id(ev[0])] = ev[1]
            for k, (sem, val) in evs.items():
                if E.seen.get(k, 0) < val:
                    E.seen[k] = val
        S.seen[id(ev[0])] = ev[1]


    def begin_phase(self):
        self.phase_mark = len(self.anchors)

    def end_phase(self):
        self.barrier()
        dead = self.anchors[self.phase_mark:]
        self.anchors = self.anchors[:self.phase_mark]
        for a in dead:
            if a.dcnt <= 20000:
                self.free_sems.append((a.dsem, a.dcnt))
            a.dsem = None
        for b in self.persist:
            b.w = {}
            b.r = {}


W_NAMES = ["norm_mix_w", "w_in", "ret_decay_fwd", "ret_decay_bwd", "ret_gn_w", "mla_q_norm_w", "mla_w_uq",
           "mla_kv_norm_w", "mla_w_ukv", "w_out", "norm_ca_w", "norm_mem_w", "ca_wq", "ca_wkv", "ca_wo",
           "norm_ffn_w", "peer_wq", "peer_sub_keys", "peer_u", "peer_v", "final_norm_w"]
W_SHAPES = {
    "norm_mix_w": [1, 1024], "w_in": [1, 1024, 2720], "ret_decay_fwd": [1, 8], "ret_decay_bwd": [1, 8],
    "ret_gn_w": [1, 512], "mla_q_norm_w": [1, 384], "mla_w_uq": [1, 384, 768], "mla_kv_norm_w": [1, 256],
    "mla_w_ukv": [1, 256, 1024], "w_out": [1, 1024, 1024], "norm_ca_w": [1, 1024], "norm_mem_w": [1, 1024],
    "ca_wq": [1, 1024, 1024], "ca_wkv": [1, 1024, 2048], "ca_wo": [1, 1024, 1024], "norm_ffn_w": [1, 1024],
    "peer_wq": [1, 1024, 2048], "peer_sub_keys": [1, 2, 8, 128, 128], "peer_u": [1, 16384, 1024],
    "peer_v": [1, 16384, 1024], "final_norm_w": [1024],
}


def host_consts(smax):
    pos = np.arange(smax, dtype=np.float32)
    c = {}
    invf = (1.0 / (np.float32(10000.0) ** (np.arange(0, 64, 2, dtype=np.float32) / np.float32(64)))).astype(np.float32)
    ang = (pos[:, None] * invf[None, :]).astype(np.float32)
    c["c_cosr"] = np.cos(ang).astype(np.float32)
    c["c_sinr"] = np.sin(ang).astype(np.float32)
    invm = (1.0 / (np.float32(10000.0) ** (np.arange(0, 32, 2, dtype=np.float32) / np.float32(32)))).astype(np.float32)
    angm = (pos[:, None] * invm[None, :]).astype(np.float32)
    cm = np.cos(angm).astype(np.float32).T
    sm = np.sin(angm).astype(np.float32).T
    ck = np.concatenate([cm, cm], 0)
    sk = np.concatenate([sm, sm], 0)
    sc = np.float32(96.0 ** -0.5)
    c["c_cq"] = np.concatenate([np.full((64, smax), sc, np.float32), ck * sc], 0).astype(np.float32)
    c["c_sq"] = np.concatenate([np.zeros((64, smax), np.float32), sk * sc], 0).astype(np.float32)
    c["c_ck"] = np.ascontiguousarray(ck)
    c["c_sk"] = np.ascontiguousarray(sk)
    io = np.arange(128, dtype=np.float32)
    c["c_iota"] = np.tile(io[None, :], (128, 1)).astype(np.float32)
    c["c_diff"] = (io[None, :] - io[:, None]).astype(np.float32)
    c["c_pcol"] = np.stack([127.0 - io, io], 1).astype(np.float32)
    c["c_ident"] = np.eye(128, dtype=np.float32)
    return c


CONST_SHAPES = lambda smax: {"c_cosr": [smax, 32], "c_sinr": [smax, 32], "c_cq": [96, smax], "c_sq": [96, smax],
                             "c_ck": [32, smax], "c_sk": [32, smax], "c_iota": [128, 128], "c_diff": [128, 128],
                             "c_pcol": [128, 2], "c_ident": [128, 128]}


def build(S_list, n_exp_chunks=128, debug=False, upto=99, prepass=True):
    nc = bass.Bass("TRN2", target_bir_lowering=False)
    smax = max(S_list)
    NE = n_exp_chunks
    A = {}
    for i, S in enumerate(S_list):
        A[f"x{i}"] = nc.dram_tensor(f"x{i}", [S, D], F32, kind="ExternalInput").ap()
        A[f"mem{i}"] = nc.dram_tensor(f"mem{i}", [256, D], F32, kind="ExternalInput").ap()
        A[f"y{i}"] = nc.dram_tensor(f"y{i}", [S, D], F32, kind="ExternalOutput").ap()
    for n in W_NAMES:
        A[n] = nc.dram_tensor(n, W_SHAPES[n], F32, kind="ExternalInput").ap()
    for n, shp in CONST_SHAPES(smax).items():
        A[n] = nc.dram_tensor(n, shp, F32, kind="ExternalInput").ap()
    skind = "ExternalOutput" if debug else "Internal"
    nt_max = smax // 128

    def scr(name, shape, dt):
        return nc.dram_tensor("z_" + name, shape, dt, kind=skind).ap()
    Z = dict(
        SG=scr("SG", [smax, 512], F32), YP=scr("YP", [smax, 512], F32),
        QB=scr("QB", [nt_max, 128, 512], BF16), BB=scr("BB", [nt_max, 128, 256], F32),
        QT=scr("QT", [8, 97, smax], BF16), KT=scr("KT", [8, 97, smax], BF16),
        VS=scr("VS", [smax, 8 * 65], BF16), KN=scr("KN", [8, smax], F32),
        MO=scr("MO", [8, 64, smax], BF16), X2=scr("X2", [smax, D], F32),
        UT=scr("UT", [128, 128, 1024], BF16), VB=scr("VB", [128, 128, 1024], BF16),
    )
    ZB = {k: Buf("z" + k, multi=True) for k in Z}

    with ExitStack() as top:
        kb = KB(nc, top)
        kb.init_psum()
        pe, dve, act, pool, sp = kb.pe, kb.dve, kb.act, kb.pool, kb.sp
        V = nc.vector
        ACT = nc.scalar
        G = nc.gpsimd
        PE = nc.tensor

        ident, b_ident = kb.sbb([128, 128], F32, "ident")
        iota, b_iota = kb.sbb([128, 128], F32, "iota")
        onesf, b_onesf = kb.sbb([128, 128], F32, "onesf")
        epsn, b_epsn = kb.sbb([128, 1], F32, "epsn")
        epsg, b_epsg = kb.sbb([128, 1], F32, "epsg")
        kb.dma(sp, ident[:], A["c_ident"][:, :], b_ident, writes=[b_ident])
        kb.dma(sp, iota[:], A["c_iota"][:, :], b_iota, writes=[b_iota])
        kb.op(dve, lambda: V.memset(onesf[:], 1.0), writes=[b_onesf])
        kb.op(dve, lambda: V.memset(epsn[:], 1e-6), writes=[b_epsn])
        kb.op(dve, lambda: V.memset(epsg[:], 1e-5), writes=[b_epsg])
        kb.persist = list(ZB.values()) + [b_ident, b_iota, b_onesf, b_epsn, b_epsg]
        kb.barrier()

        def load_cast(dst, dst_buf, src, stack=None):
            kb.dma(pool, dst, src, dst_buf, writes=[dst_buf])

        def norm_transpose(xt, bx, wcol, b_wcol, outT, b_outT, col0, work):
            junk, b_junk = work["junk"].next()
            ss, b_ss = work["ss"].next()
            xs, b_xs = work["xs"].next()
            kb.op(act, lambda: ACT.activation(out=junk[:], in_=xt, func=AF.Square, accum_out=ss[:, 0:1]),
                  reads=[bx], writes=[b_junk, b_ss])
            kb.op(act, lambda: ACT.activation(out=ss[:, 1:2], in_=ss[:, 0:1], func=AF.Sqrt, scale=1.0 / D, bias=epsn[:, 0:1]),
                  reads=[b_ss, b_epsn], writes=[b_ss])
            kb.op(dve, lambda: V.reciprocal(out=ss[:, 2:3], in_=ss[:, 1:2]), reads=[b_ss], writes=[b_ss])
            kb.op(pool, lambda: G.tensor_scalar(out=xs[:], in0=xt, scalar1=ss[:, 2:3], scalar2=None, op0=ALU.mult),
                  reads=[bx, b_ss], writes=[b_xs])
            pT, bpT = kb.bank(2)
            for kc in range(8):
                kb.op(pe, lambda kc=kc: PE.transpose(pT[:, kc * 128:(kc + 1) * 128], xs[:, kc * 128:(kc + 1) * 128], ident[:]),
                      reads=[b_xs, b_ident], writes=[bpT[kc // 4]])
            pTv = pT.rearrange("p (k t) -> p k t", k=8)
            kb.op(dve, lambda: V.tensor_tensor(out=outT[:, :, col0:col0 + 128], in0=pTv,
                                               in1=wcol[:, :].unsqueeze(2).to_broadcast([128, 8, 128]), op=ALU.mult),
                  reads=bpT + [b_wcol], writes=[b_outT])
            return ss, b_ss

        def load_cols(name, ncols, stack):
            t, b = kb.sbb([128, ncols], F32, name, stack)
            src = A[name].rearrange("o (c p) -> p (o c)", p=128) if len(W_SHAPES[name]) == 2 else A[name].rearrange("(c p) -> p c", p=128)
            with nc.allow_non_contiguous_dma(reason="tiny per-feature vector"):
                kb.dma(sp, t[:], src, b, writes=[b])
            return t, b

        def prepass_peer():
            with ExitStack() as st:
                us = kb.slots([128, 1024], F32, 2, "pp_u", st)
                uts = kb.slots([128, 1024], BF16, 2, "pp_ut", st)
                vs = kb.slots([128, 1024], BF16, 2, "pp_v", st)
                for i in range(NE):
                    ut, bu = us.next()
                    kb.dma(sp, ut[:], A["peer_u"][0, i * 128:(i + 1) * 128, :], bu, writes=[bu])
                    pT, bpT = kb.bank(2)
                    for kc in range(8):
                        kb.op(pe, lambda kc=kc: PE.transpose(pT[:, kc * 128:(kc + 1) * 128], ut[:, kc * 128:(kc + 1) * 128], ident[:]),
                              reads=[bu, b_ident], writes=[bpT[kc // 4]])
                    utt, butt = uts.next()
                    kb.op(act, lambda: ACT.copy(out=utt[:, 0:512], in_=pT[:, 0:512]), reads=[bpT[0]], writes=[butt])
                    kb.op(dve, lambda: V.tensor_copy(out=utt[:, 512:1024], in_=pT[:, 512:1024]), reads=[bpT[1]], writes=[butt])
                    kb.dma(sp, Z["UT"][i], utt[:], butt, reads=[butt], writes=[ZB["UT"]])
                    vt, bv = vs.next()
                    load_cast(vt[:], bv, A["peer_v"][0, i * 128:(i + 1) * 128, :])
                    kb.dma(sp, Z["VB"][i], vt[:], bv, reads=[bv], writes=[ZB["VB"]])
                kb.barrier()

        def phase1(si, S):
            nt = S // 128
            X = A[f"x{si}"]
            with ExitStack() as st:
                Win, b_Win = kb.sbb([128, 8, 2720], BF16, "Win", st)
                for kc in range(8):
                    for (c0, c1) in ((0, 1360), (1360, 2720)):
                        load_cast(Win[:, kc, c0:c1], b_Win, A["w_in"][0, kc * 128:(kc + 1) * 128, c0:c1])
                Wuq, b_Wuq = kb.sbb([128, 3, 768], BF16, "Wuq", st)
                load_cast(Wuq[:], b_Wuq, A["mla_w_uq"][0].rearrange("(r p) n -> p r n", p=128))
                Wuqr, b_Wuqr = kb.sbb([128, 3, 768], BF16, "Wuqr", st)
                Wukv, b_Wukv = kb.sbb([128, 2, 2, 8, 64], BF16, "Wukv", st)
                for c in range(2):
                    for t in range(2):
                        load_cast(Wukv[:, c, t, :, :], b_Wukv,
                                  A["mla_w_ukv"][0, c * 128:(c + 1) * 128, :].rearrange("p (h t d) -> p t h d", h=8, t=2)[:, t, :, :])
                Wkrr, b_Wkrr = kb.sbb([128, 8, 32], BF16, "Wkrr", st)
                kb.op(dve, lambda: V.memset(Wuqr[:], 0.0), writes=[b_Wuqr])
                Wuq4 = Wuq[:].rearrange("p r (h f) -> p r h f", h=8)
                Wuqr4 = Wuqr[:].rearrange("p r (h f) -> p r h f", h=8)
                for r in range(3):
                    kb.op(dve, lambda r=r: V.tensor_scalar(out=Wuqr4[:, r, :, 64:80], in0=Wuq4[:, r, :, 80:96], scalar1=-1.0, scalar2=None, op0=ALU.mult),
                          reads=[b_Wuq], writes=[b_Wuqr])
                    kb.op(dve, lambda r=r: V.tensor_copy(out=Wuqr4[:, r, :, 80:96], in_=Wuq4[:, r, :, 64:80]),
                          reads=[b_Wuq], writes=[b_Wuqr])
                kb.op(dve, lambda: V.tensor_scalar(out=Wkrr[:, :, 0:16], in0=Win[:, :, 2704:2720], scalar1=-1.0, scalar2=None, op0=ALU.mult),
                      reads=[b_Win], writes=[b_Wkrr])
                kb.op(dve, lambda: V.tensor_copy(out=Wkrr[:, :, 16:32], in_=Win[:, :, 2688:2704]), reads=[b_Win], writes=[b_Wkrr])
                nmw, b_nmw = load_cols("norm_mix_w", 8, st)
                nw5, b_nw5 = kb.sbb([128, 5], F32, "nw5", st)
                with nc.allow_non_contiguous_dma(reason="tiny per-feature vector"):
                    kb.dma(sp, nw5[:, 0:3], A["mla_q_norm_w"].rearrange("o (c p) -> p (o c)", p=128), b_nw5, writes=[b_nw5])
                    kb.dma(sp, nw5[:, 3:5], A["mla_kv_norm_w"].rearrange("o (c p) -> p (o c)", p=128), b_nw5, writes=[b_nw5])
                cosrs = kb.slots([128, 32], F32, 2, "cosr", st)
                sinrs = kb.slots([128, 32], F32, 2, "sinr", st)
                dec, b_dec = kb.sbb([128, 16], F32, "dec", st)
                kb.dma(sp, dec[:, 0:8], A["ret_decay_fwd"][0, :].partition_broadcast(128), b_dec, writes=[b_dec])
                kb.dma(sp, dec[:, 8:16], A["ret_decay_bwd"][0, :].partition_broadcast(128), b_dec, writes=[b_dec])
                lg, b_lg = kb.sbb([128, 16], F32, "lg", st)
                kb.op(act, lambda: ACT.activation(out=lg[:], in_=dec[:], func=AF.Exp, scale=-1.0), reads=[b_dec], writes=[b_lg])
                kb.op(act, lambda: ACT.activation(out=lg[:], in_=lg[:], func=AF.Ln, bias=1.0), reads=[b_lg], writes=[b_lg])
                kb.op(dve, lambda: V.tensor_scalar(out=lg[:], in0=lg[:], scalar1=-1.0, scalar2=None, op0=ALU.mult), reads=[b_lg], writes=[b_lg])
                lgp, b_lgp = kb.sbb([128, 4, 2], F32, "lgp", st)
                lgv = lg[:].rearrange("p (d a two) -> p a d two", d=2, two=2)
                kb.op(dve, lambda: V.tensor_copy(out=lgp[0:64, :, :], in_=lgv[0:64, :, :, 0]), reads=[b_lg], writes=[b_lgp])
                kb.op(dve, lambda: V.tensor_copy(out=lgp[64:128, :, :], in_=lgv[64:128, :, :, 1]), reads=[b_lg], writes=[b_lgp])
                diff, b_diff = kb.sbb([128, 128], F32, "diff", st)
                kb.dma(sp, diff[:], A["c_diff"][:, :], b_diff, writes=[b_diff])
                pcol, b_pcol = kb.sbb([128, 2], F32, "pcol", st)
                kb.dma(sp, pcol[:], A["c_pcol"][:, :], b_pcol, writes=[b_pcol])
                dpos, b_dpos = kb.sbb([128, 128], F32, "dpos", st)
                dneg, b_dneg = kb.sbb([128, 128], F32, "dneg", st)
                kb.op(dve, lambda: V.tensor_scalar(out=dpos[:], in0=diff[:], scalar1=0.0, scalar2=None, op0=ALU.max), reads=[b_diff], writes=[b_dpos])
                kb.op(dve, lambda: V.tensor_scalar(out=dneg[:], in0=diff[:], scalar1=-1.0, scalar2=0.0, op0=ALU.mult, op1=ALU.max), reads=[b_diff], writes=[b_dneg])
                DT, b_DT = kb.sbb([128, 8, 128], F32, "DT", st)
                tmpE, b_tmpE = kb.sbb([128, 128], F32, "tmpE", st)
                for h in range(8):
                    kb.op(act, lambda h=h: ACT.activation(out=DT[:, h, :], in_=dpos[:], func=AF.Exp, scale=lg[:, h:h + 1]),
                          reads=[b_dpos, b_lg], writes=[b_DT])
                    kb.op(pool, lambda h=h: G.affine_select(out=DT[:, h, :], in_=DT[:, h, :], pattern=[[1, 128]], compare_op=ALU.is_ge,
                                                            fill=0.0, base=0, channel_multiplier=-1), reads=[b_DT], writes=[b_DT])
                    kb.op(act, lambda h=h: ACT.activation(out=tmpE[:], in_=dneg[:], func=AF.Exp, scale=lg[:, 8 + h:9 + h]),
                          reads=[b_dneg, b_lg], writes=[b_tmpE])
                    kb.op(pool, lambda: G.affine_select(out=tmpE[:], in_=tmpE[:], pattern=[[-1, 128]], compare_op=ALU.is_gt,
                                                        fill=0.0, base=0, channel_multiplier=1), reads=[b_tmpE], writes=[b_tmpE])
                    kb.op(dve, lambda h=h: V.tensor_tensor(out=DT[:, h, :], in0=DT[:, h, :], in1=tmpE[:], op=ALU.add),
                          reads=[b_DT, b_tmpE], writes=[b_DT])
                ip1, b_ip1 = kb.sbb([128, 128], F32, "ip1", st)
                cmi, b_cmi = kb.sbb([128, 128], F32, "cmi", st)
                kb.op(dve, lambda: V.tensor_scalar(out=ip1[:], in0=iota[:], scalar1=1.0, scalar2=None, op0=ALU.add), reads=[b_iota], writes=[b_ip1])
                kb.op(dve, lambda: V.tensor_scalar(out=cmi[:], in0=iota[:], scalar1=-1.0, scalar2=128.0, op0=ALU.mult, op1=ALU.add), reads=[b_iota], writes=[b_cmi])
                QF, b_QF = kb.sbb([128, 4, 128], F32, "QF", st)
                QBt, b_QBt = kb.sbb([128, 4, 128], F32, "QBt", st)
                for a in range(4):
                    kb.op(act, lambda a=a: ACT.activation(out=QF[:, a, :], in_=ip1[:], func=AF.Exp, scale=lgp[:, a, 0:1]), reads=[b_ip1, b_lgp], writes=[b_QF])
                    kb.op(act, lambda a=a: ACT.activation(out=QBt[:, a, :], in_=cmi[:], func=AF.Exp, scale=lgp[:, a, 1:2]), reads=[b_cmi, b_lgp], writes=[b_QBt])
                tk, b_tk = kb.sbb([128, 16], F32, "tk", st)
                kb.op(act, lambda: ACT.activation(out=tk[:, 0:8], in_=lg[:, 0:8], func=AF.Exp, scale=pcol[:, 0:1]), reads=[b_lg, b_pcol], writes=[b_tk])
                kb.op(act, lambda: ACT.activation(out=tk[:, 8:16], in_=lg[:, 8:16], func=AF.Exp, scale=pcol[:, 1:2]), reads=[b_lg, b_pcol], writes=[b_tk])
                GC, b_GC = kb.sbb([128, 4, 2], F32, "GC", st)
                kb.op(act, lambda: ACT.activation(out=GC[:], in_=lgp[:], func=AF.Exp, scale=128.0), reads=[b_lgp], writes=[b_GC])
                Rf, b_Rf = kb.sbb([128, 4, 64], F32, "Rf", st)
                Rfb, b_Rfb = kb.sbb([128, 4, 128], BF16, "Rfb", st)
                kb.op(dve, lambda: V.memset(Rf[:], 0.0), writes=[b_Rf])
                kb.op(dve, lambda: V.memset(Rfb[:], 0.0), writes=[b_Rfb])
                work = dict(junk=kb.slots([128, 1024], BF16, 1, "junk", st), ss=kb.slots([128, 4], F32, 2, "ss", st),
                            xs=kb.slots([128, 1024], F32, 1, "xs", st))
                xts = kb.slots([128, 1024], F32, 2, "xt", st)
                xnTs = kb.slots([128, 8, 128], BF16, 2, "xnT", st)
                qks = kb.slots([128, 16, 64], F32, 2, "qk", st)
                qkrs = kb.slots([128, 16, 64], F32, 2, "qkr", st)
                tmps = kb.slots([128, 16, 32], F32, 4, "ropetmp", st)
                vbs = kb.slots([128, 512], BF16, 2, "vb", st)
                sgs = kb.slots([128, 512], F32, 2, "sg", st)
                qTs = kb.slots([128, 8, 128], BF16, 2, "qTz", st)
                for (q_, bq_) in qTs.items:
                    kb.op(dve, lambda q_=q_: V.memset(q_[:], 0.0), writes=[bq_])
                kTs = kb.slots([128, 4, 128], BF16, 2, "kT", st)
                qfTs = kb.slots([128, 4, 128], BF16, 2, "qfT", st)
                qbTs = kb.slots([128, 4, 128], BF16, 2, "qbT", st)
                kdfs = kb.slots([128, 8, 64], BF16, 2, "kdf", st)
                kdbs = kb.slots([128, 8, 64], BF16, 2, "kdb", st)
                sDs = kb.slots([128, 8, 128], BF16, 2, "sD", st)
                yps = kb.slots([128, 512], F32, 2, "yp", st)
                bbs = kb.slots([128, 4, 64], F32, 2, "bb", st)
                sqs = kb.slots([128, 5, 128], F32, 2, "sq", st)
                rsbs = kb.slots([128, 2, 128], F32, 2, "rsb", st)
                cns = kb.slots([128, 5, 128], BF16, 2, "cn", st)
                cqts = kb.slots([96, 128], F32, 2, "cqt", st)
                sqts = kb.slots([96, 128], F32, 2, "sqt", st)
                ckts = kb.slots([32, 128], F32, 2, "ckt", st)
                skts = kb.slots([32, 128], F32, 2, "skt", st)
                t1s = kb.slots([96, 8, 128], F32, 1, "t1", st)
                t2s = kb.slots([96, 8, 128], F32, 1, "t2", st)
                QTts = kb.slots([96, 8, 128], BF16, 2, "QTt", st)
                KNts = kb.slots([64, 8, 128], BF16, 2, "KNt", st)
                KRts = kb.slots([32, 128], BF16, 2, "KRt", st)
                kr1s = kb.slots([32, 128], F32, 2, "kr1", st)
                kr2s = kb.slots([32, 128], F32, 2, "kr2", st)
                Vts = kb.slots([128, 8, 65], BF16, 2, "Vt", st)
                for (vt_, bv_) in Vts.items:
                    kb.op(dve, lambda vt_=vt_: V.memset(vt_[:, :, 64:65], 1.0), writes=[bv_])
                sq2s = kb.slots([96, 1024], F32, 1, "sq2", st)
                sq3s = kb.slots([32, 128], F32, 1, "sq3", st)
                nrows = kb.slots([1, 2, 1024], F32, 1, "nrow", st)
                nbrs = kb.slots([1, 1024], BF16, 2, "nbr", st)
                one8, b_one8 = kb.sbb([1, 1024], BF16, "one8", st)
                kb.op(dve, lambda: V.memset(one8[:], 1.0), writes=[b_one8])
                krn = kb.slots([1, 128], F32, 2, "krn", st)

                for t in range(nt):
                    r0 = t * 128
                    xt, bx = xts.next()
                    kb.dma(sp, xt[:], X[r0:r0 + 128, :], bx, writes=[bx])
                    xnT, bxnT = xnTs.next()
                    norm_transpose(xt[:], bx, nmw, b_nmw, xnT, bxnT, 0, work)
                    pq, bq = kb.bank(1)
                    pk, bk = kb.bank(1)
                    pv, bv = kb.bank(1)
                    pg, bg = kb.bank(1)
                    for n, (pp, bpp) in enumerate(((pq, bq), (pk, bk), (pv, bv), (pg, bg))):
                        for kc in range(8):
                            kb.op(pe, lambda pp=pp, n=n, kc=kc: PE.matmul(pp, lhsT=xnT[:, kc, :], rhs=Win[:, kc, n * 512:(n + 1) * 512],
                                                                        start=(kc == 0), stop=(kc == 7)),
                                  reads=[bxnT, b_Win], writes=bpp)
                    qk, bqk = qks.next()
                    qkf = qk[:].rearrange("p a d -> p (a d)")
                    kb.op(act, lambda: ACT.copy(out=qkf[:, 0:512], in_=pq), reads=bq, writes=[bqk])
                    kb.op(act, lambda: ACT.activation(out=qkf[:, 512:1024], in_=pk, func=AF.Copy, scale=0.125), reads=bk, writes=[bqk])
                    vb, bvb = vbs.next()
                    kb.op(dve, lambda: V.tensor_copy(out=vb[:], in_=pv), reads=bv, writes=[bvb])
                    sg, bsg = sgs.next()
                    kb.op(act, lambda: ACT.activation(out=sg[:], in_=pg, func=AF.Silu), reads=bg, writes=[bsg])
                    kb.dma(sp, Z["SG"][r0:r0 + 128, :], sg[:], bsg, reads=[bsg], writes=[ZB["SG"]])
                    qkr, bqkr = qkrs.next()
                    cosr, b_cosr = cosrs.next()
                    sinr, b_sinr = sinrs.next()
                    kb.dma(sp, cosr[:], A["c_cosr"][r0:r0 + 128, :], b_cosr, writes=[b_cosr])
                    kb.dma(sp, sinr[:], A["c_sinr"][r0:r0 + 128, :], b_sinr, writes=[b_sinr])
                    cb = cosr[:, :].unsqueeze(1).to_broadcast([128, 16, 32])
                    sbc = sinr[:, :].unsqueeze(1).to_broadcast([128, 16, 32])
                    ta, bta = tmps.next()
                    tb, btb = tmps.next()
                    tc_, btc = tmps.next()
                    td, btd = tmps.next()
                    kb.op(dve, lambda: V.tensor_tensor(out=ta[:], in0=qk[:, :, 0:32], in1=cb, op=ALU.mult), reads=[bqk, b_cosr], writes=[bta])
                    kb.op(dve, lambda: V.tensor_tensor(out=tb[:], in0=qk[:, :, 32:64], in1=sbc, op=ALU.mult), reads=[bqk, b_sinr], writes=[btb])
                    kb.op(dve, lambda: V.tensor_tensor(out=qkr[:, :, 0:32], in0=ta[:], in1=tb[:], op=ALU.subtract), reads=[bta, btb], writes=[bqkr])
                    kb.op(pool, lambda: G.tensor_tensor(out=tc_[:], in0=qk[:, :, 0:32], in1=sbc, op=ALU.mult), reads=[bqk, b_sinr], writes=[btc])
                    kb.op(pool, lambda: G.tensor_tensor(out=td[:], in0=qk[:, :, 32:64], in1=cb, op=ALU.mult), reads=[bqk, b_cosr], writes=[btd])
                    kb.op(pool, lambda: G.tensor_tensor(out=qkr[:, :, 32:64], in0=tc_[:], in1=td[:], op=ALU.add), reads=[btc, btd], writes=[bqkr])
                    qkrf = qkr[:].rearrange("p a d -> p (a d)")
                    pT, bpT = kb.bank(2)
                    for j in range(8):
                        kb.op(pe, lambda j=j: PE.transpose(pT[:, j * 128:(j + 1) * 128], qkrf[:, j * 128:(j + 1) * 128], ident[:]),
                              reads=[bqkr, b_ident], writes=[bpT[j // 4]])
                    pTv = pT.rearrange("p (j t) -> p j t", j=8)
                    qT, bqT = qTs.next()
                    kT, bkT = kTs.next()
                    qfT, bqfT = qfTs.next()
                    qbT, bqbT = qbTs.next()
                    kb.op(act, lambda: ACT.copy(out=qT[0:64, 0:8:2, :], in_=pTv[0:64, 0:4, :]), reads=[bpT[0]], writes=[bqT])
                    kb.op(act, lambda: ACT.copy(out=qT[64:128, 1:8:2, :], in_=pTv[64:128, 0:4, :]), reads=[bpT[0]], writes=[bqT])
                    kb.op(act, lambda: ACT.copy(out=kT[:], in_=pTv[:, 4:8, :]), reads=[bpT[1]], writes=[bkT])
                    kb.op(dve, lambda: V.tensor_tensor(out=qfT[:], in0=pTv[:, 0:4, :], in1=QF[:], op=ALU.mult), reads=[bpT[0], b_QF], writes=[bqfT])
                    kb.op(dve, lambda: V.tensor_tensor(out=qbT[:], in0=pTv[:, 0:4, :], in1=QBt[:], op=ALU.mult), reads=[bpT[0], b_QBt], writes=[bqbT])
                    kb.dma(sp, Z["QB"][t], qbT[:].rearrange("p a t -> p (a t)"), bqbT, reads=[bqbT], writes=[ZB["QB"]])
                    kdf, bkdf = kdfs.next()
                    kdb, bkdb = kdbs.next()
                    kb.op(dve, lambda: V.tensor_tensor(out=kdf[:], in0=qkr[:, 8:16, :], in1=tk[:, 0:8].unsqueeze(2).to_broadcast([128, 8, 64]), op=ALU.mult),
                          reads=[bqkr, b_tk], writes=[bkdf])
                    kb.op(dve, lambda: V.tensor_tensor(out=kdb[:], in0=qkr[:, 8:16, :], in1=tk[:, 8:16].unsqueeze(2).to_broadcast([128, 8, 64]), op=ALU.mult),
                          reads=[bqkr, b_tk], writes=[bkdb])
                    pS, bpS = kb.bank(2)
                    for h in range(8):
                        a, off = h // 2, (h % 2) * 64
                        kb.op(pe, lambda h=h, a=a, off=off: PE.matmul(pS[:, h * 128:(h + 1) * 128], lhsT=kT[:, a, :], rhs=qT[:, h, :],
                                                                      start=True, stop=True),
                              reads=[bkT, bqT], writes=[bpS[h // 4]])
                    sD, bsD = sDs.next()
                    pSv = pS.rearrange("p (h t) -> p h t", h=8)
                    kb.op(dve, lambda: V.tensor_tensor(out=sD[:, 0:4, :], in0=pSv[:, 0:4, :], in1=DT[:, 0:4, :], op=ALU.mult), reads=[bpS[0], b_DT], writes=[bsD])
                    kb.op(dve, lambda: V.tensor_tensor(out=sD[:, 4:8, :], in0=pSv[:, 4:8, :], in1=DT[:, 4:8, :], op=ALU.mult), reads=[bpS[1], b_DT], writes=[bsD])
                    pY, bpY = kb.bank(1)
                    for a in range(4):
                        kb.op(pe, lambda a=a: PE.matmul(pY[:, a * 128:(a + 1) * 128], lhsT=qfT[:, a, :], rhs=Rfb[:, a, :], start=True, stop=False),
                              reads=[bqfT, b_Rfb], writes=bpY)
                        for h in (2 * a, 2 * a + 1):
                            kb.op(pe, lambda h=h: PE.matmul(pY[:, h * 64:(h + 1) * 64], lhsT=sD[:, h, :], rhs=vb[:, h * 64:(h + 1) * 64], start=False, stop=(h % 2 == 1)),
                                  reads=[bsD, bvb], writes=bpY)
                    yp, byp = yps.next()
                    kb.op(act, lambda: ACT.copy(out=yp[:], in_=pY), reads=bpY, writes=[byp])
                    kb.dma(sp, Z["YP"][r0:r0 + 128, :], yp[:], byp, reads=[byp], writes=[ZB["YP"]])
                    pBf, bpBf = kb.bank(1)
                    pBb, bpBb = kb.bank(1)
                    kdf2 = kdf[:].rearrange("p h d -> p (h d)")
                    kdb2 = kdb[:].rearrange("p h d -> p (h d)")
                    for a in range(4):
                        kb.op(pe, lambda a=a: PE.matmul(pBf[:, a * 128:(a + 1) * 128], lhsT=kdf2[:, a * 128:(a + 1) * 128], rhs=vb[:, a * 128:(a + 1) * 128],
                                                        start=True, stop=True), reads=[bkdf, bvb], writes=bpBf)
                        kb.op(pe, lambda a=a: PE.matmul(pBb[:, a * 128:(a + 1) * 128], lhsT=kdb2[:, a * 128:(a + 1) * 128], rhs=vb[:, a * 128:(a + 1) * 128],
                                                        start=True, stop=True), reads=[bkdb, bvb], writes=bpBb)
                    pBfv = pBf.rearrange("p (a x) -> p a x", a=4)
                    pBbv = pBb.rearrange("p (a x) -> p a x", a=4)
                    kb.op(dve, lambda: V.tensor_tensor(out=Rf[:], in0=Rf[:], in1=GC[:, :, 0:1].to_broadcast([128, 4, 64]), op=ALU.mult),
                          reads=[b_Rf, b_GC], writes=[b_Rf])
                    kb.op(dve, lambda: V.tensor_tensor(out=Rf[0:64], in0=Rf[0:64], in1=pBfv[0:64, :, 0:64], op=ALU.add), reads=[b_Rf] + bpBf, writes=[b_Rf])
                    kb.op(dve, lambda: V.tensor_tensor(out=Rf[64:128], in0=Rf[64:128], in1=pBfv[64:128, :, 64:128], op=ALU.add), reads=[b_Rf] + bpBf, writes=[b_Rf])
                    kb.op(dve, lambda: V.tensor_copy(out=Rfb[0:64, :, 0:64], in_=Rf[0:64]), reads=[b_Rf], writes=[b_Rfb])
                    kb.op(dve, lambda: V.tensor_copy(out=Rfb[64:128, :, 64:128], in_=Rf[64:128]), reads=[b_Rf], writes=[b_Rfb])
                    bb, bbb = bbs.next()
                    kb.op(act, lambda: ACT.copy(out=bb[0:64], in_=pBbv[0:64, :, 0:64]), reads=bpBb, writes=[bbb])
                    kb.op(act, lambda: ACT.copy(out=bb[64:128], in_=pBbv[64:128, :, 64:128]), reads=bpBb, writes=[bbb])
                    kb.dma(sp, Z["BB"][t], bb[:].rearrange("p a v -> p (a v)"), bbb, reads=[bbb], writes=[ZB["BB"]])

                    pC, bpC = kb.bank(2)
                    for r in range(5):
                        for kc in range(8):
                            kb.op(pe, lambda r=r, kc=kc: PE.matmul(pC[:, r * 128:(r + 1) * 128], lhsT=Win[:, kc, 2048 + r * 128:2048 + (r + 1) * 128],
                                                                   rhs=xnT[:, kc, :], start=(kc == 0), stop=(kc == 7)),
                                  reads=[b_Win, bxnT], writes=[bpC[r // 4]])
                    sq, bsq = sqs.next()
                    kb.op(act, lambda: ACT.activation(out=sq[:, 0:4, :].rearrange("p r t -> p (r t)"), in_=pC[:, 0:512], func=AF.Square), reads=[bpC[0]], writes=[bsq])
                    kb.op(act, lambda: ACT.activation(out=sq[:, 4, :], in_=pC[:, 512:640], func=AF.Square), reads=[bpC[1]], writes=[bsq])
                    pN, bpN = kb.bank(1)
                    for r in range(3):
                        kb.op(pe, lambda r=r: PE.matmul(pN[:, 0:128], lhsT=onesf[:], rhs=sq[:, r, :], start=(r == 0), stop=(r == 2)), reads=[b_onesf, bsq], writes=bpN)
                    for r in range(2):
                        kb.op(pe, lambda r=r: PE.matmul(pN[:, 128:256], lhsT=onesf[:], rhs=sq[:, 3 + r, :], start=(r == 0), stop=(r == 1)), reads=[b_onesf, bsq], writes=bpN)
                    rsb, brsb = rsbs.next()
                    kb.op(act, lambda: ACT.activation(out=rsb[:, 0, :], in_=pN[:, 0:128], func=AF.Sqrt, scale=1.0 / 384, bias=epsn[:, 0:1]), reads=bpN + [b_epsn], writes=[brsb])
                    kb.op(act, lambda: ACT.activation(out=rsb[:, 1, :], in_=pN[:, 128:256], func=AF.Sqrt, scale=1.0 / 256, bias=epsn[:, 0:1]), reads=bpN + [b_epsn], writes=[brsb])
                    kb.op(dve, lambda: V.reciprocal(out=rsb[:], in_=rsb[:]), reads=[brsb], writes=[brsb])
                    cn, bcn = cns.next()
                    for r in range(5):
                        kb.op(dve, lambda r=r: V.scalar_tensor_tensor(out=cn[:, r, :], in0=pC[:, r * 128:(r + 1) * 128], scalar=nw5[:, r:r + 1],
                                                                      in1=rsb[:, 0 if r < 3 else 1, :], op0=ALU.mult, op1=ALU.mult),
                              reads=[bpC[r // 4], b_nw5, brsb], writes=[bcn])
                    pQ, bpQ = kb.bank(2)
                    pQr, bpQr = kb.bank(2)
                    for h in range(8):
                        for r in range(3):
                            kb.op(pe, lambda h=h, r=r: PE.matmul(pQ[0:96, h * 128:(h + 1) * 128], lhsT=Wuq[:, r, h * 96:(h + 1) * 96], rhs=cn[:, r, :],
                                                                 start=(r == 0), stop=(r == 2)), reads=[b_Wuq, bcn], writes=[bpQ[h // 4]])
                        for r in range(3):
                            kb.op(pe, lambda h=h, r=r: PE.matmul(pQr[0:96, h * 128:(h + 1) * 128], lhsT=Wuqr[:, r, h * 96:(h + 1) * 96], rhs=cn[:, r, :],
                                                                 start=(r == 0), stop=(r == 2)), reads=[b_Wuqr, bcn], writes=[bpQr[h // 4]])
                    cqt, bcqt = cqts.next()
                    sqt, bsqt = sqts.next()
                    kb.dma(sp, cqt[:], A["c_cq"][:, r0:r0 + 128], bcqt, writes=[bcqt])
                    kb.dma(sp, sqt[:], A["c_sq"][:, r0:r0 + 128], bsqt, writes=[bsqt])
                    t1, bt1 = t1s.next()
                    t2, bt2 = t2s.next()
                    kb.op(dve, lambda: V.tensor_tensor(out=t1[:], in0=pQ[0:96, :].rearrange("p (h t) -> p h t", h=8),
                                                       in1=cqt[:].unsqueeze(1).to_broadcast([96, 8, 128]), op=ALU.mult), reads=bpQ + [bcqt], writes=[bt1])
                    kb.op(dve, lambda: V.tensor_tensor(out=t2[:], in0=pQr[0:96, :].rearrange("p (h t) -> p h t", h=8),
                                                       in1=sqt[:].unsqueeze(1).to_broadcast([96, 8, 128]), op=ALU.mult), reads=bpQr + [bsqt], writes=[bt2])
                    QTt, bQTt = QTts.next()
                    kb.op(dve, lambda: V.tensor_tensor(out=QTt[:], in0=t1[:], in1=t2[:], op=ALU.add), reads=[bt1, bt2], writes=[bQTt])
                    kb.dma(sp, Z["QT"][:, 0:96, r0:r0 + 128].rearrange("h f t -> f h t"), QTt[:], bQTt, reads=[bQTt], writes=[ZB["QT"]])
                    pKN, bpKN = kb.bank(2)
                    for h in range(8):
                        for c in range(2):
                            kb.op(pe, lambda h=h, c=c: PE.matmul(pKN[0:64, h * 128:(h + 1) * 128], lhsT=Wukv[:, c, 0, h, :], rhs=cn[:, 3 + c, :],
                                                                 start=(c == 0), stop=(c == 1)), reads=[b_Wukv, bcn], writes=[bpKN[h // 4]])
                    KNt, bKNt = KNts.next()
                    kb.op(act, lambda: ACT.copy(out=KNt[:].rearrange("p h t -> p (h t)"), in_=pKN[0:64, :]), reads=bpKN, writes=[bKNt])
                    kb.dma(sp, Z["KT"][:, 0:64, r0:r0 + 128].rearrange("h f t -> f h t"), KNt[:], bKNt, reads=[bKNt], writes=[ZB["KT"]])
                    pKR, bpKR = kb.bank(1)
                    for kc in range(8):
                        kb.op(pe, lambda kc=kc: PE.matmul(pKR[0:32, 0:128], lhsT=Win[:, kc, 2688:2720], rhs=xnT[:, kc, :], start=(kc == 0), stop=(kc == 7)),
                              reads=[b_Win, bxnT], writes=bpKR)
                    for kc in range(8):
                        kb.op(pe, lambda kc=kc: PE.matmul(pKR[0:32, 128:256], lhsT=Wkrr[:, kc, :], rhs=xnT[:, kc, :], start=(kc == 0), stop=(kc == 7)),
                              reads=[b_Wkrr, bxnT], writes=bpKR)
                    ckt, bckt = ckts.next()
                    skt, bskt = skts.next()
                    kb.dma(sp, ckt[:], A["c_ck"][:, r0:r0 + 128], bckt, writes=[bckt])
                    kb.dma(sp, skt[:], A["c_sk"][:, r0:r0 + 128], bskt, writes=[bskt])
                    kr1, bkr1 = kr1s.next()
                    kr2, bkr2 = kr2s.next()
                    kb.op(dve, lambda: V.tensor_tensor(out=kr1[:], in0=pKR[0:32, 0:128], in1=ckt[:], op=ALU.mult), reads=bpKR + [bckt], writes=[bkr1])
                    kb.op(dve, lambda: V.tensor_tensor(out=kr2[:], in0=pKR[0:32, 128:256], in1=skt[:], op=ALU.mult), reads=bpKR + [bskt], writes=[bkr2])
                    KRt, bKRt = KRts.next()
                    kb.op(dve, lambda: V.tensor_tensor(out=KRt[:], in0=kr1[:], in1=kr2[:], op=ALU.add), reads=[bkr1, bkr2], writes=[bKRt])
                    for h in range(8):
                        kb.dma(sp, Z["KT"][h, 64:96, r0:r0 + 128], KRt[:], bKRt, reads=[bKRt], writes=[ZB["KT"]])
                    pV, bpV = kb.bank(1)
                    for c in range(2):
                        kb.op(pe, lambda c=c: PE.matmul(pV, lhsT=cn[:, 3 + c, :], rhs=Wukv[:, c, 1, :, :].rearrange("p h d -> p (h d)"),
                                                        start=(c == 0), stop=(c == 1)), reads=[bcn, b_Wukv], writes=bpV)
                    Vt, bVt = Vts.next()
                    kb.op(act, lambda: ACT.copy(out=Vt[:, :, 0:64], in_=pV.rearrange("p (h d) -> p h d", h=8)), reads=bpV, writes=[bVt])
                    kb.dma(sp, Z["VS"][r0:r0 + 128, :], Vt[:].rearrange("p h d -> p (h d)"), bVt, reads=[bVt], writes=[ZB["VS"]])
                    sq2, bsq2 = sq2s.next()
                    nrow, bnrow = nrows.next()
                    kb.op(act, lambda: ACT.activation(out=sq2[:], in_=QTt[:].rearrange("p h t -> p (h t)"), func=AF.Square), reads=[bQTt], writes=[bsq2])
                    pR, bpR = kb.bank(2)
                    for n in range(2):
                        kb.op(pe, lambda n=n: PE.matmul(pR[0:1, n * 512:(n + 1) * 512], lhsT=onesf[0:96, 0:1], rhs=sq2[:, n * 512:(n + 1) * 512], start=True, stop=True),
                              reads=[b_onesf, bsq2], writes=[bpR[n]])
                    kb.op(dve, lambda: V.tensor_copy(out=nrow[:, 0, :], in_=pR[0:1, :]), reads=bpR, writes=[bnrow])
                    nbr, bnbr = nbrs.next()
                    kb.op(act, lambda: ACT.activation(out=nrow[:, 0, :], in_=nrow[:, 0, :], func=AF.Sqrt), reads=[bnrow], writes=[bnrow])
                    kb.op(dve, lambda: V.tensor_scalar(out=nbr[:], in0=nrow[:, 0, :], scalar1=-1.0, scalar2=None, op0=ALU.mult), reads=[bnrow], writes=[bnbr])
                    kb.dma(sp, Z["QT"][:, 96:97, r0:r0 + 128].rearrange("h o t -> o h t"), nbr[:].rearrange("o (h t) -> o h t", h=8), bnbr, reads=[bnbr], writes=[ZB["QT"]])
                    kb.dma(sp, Z["KT"][:, 96:97, r0:r0 + 128].rearrange("h o t -> o h t"), one8[:].rearrange("o (h t) -> o h t", h=8), b_one8, reads=[b_one8], writes=[ZB["KT"]])
                    kb.op(act, lambda: ACT.activation(out=sq2[0:64, :], in_=KNt[:].rearrange("p h t -> p (h t)"), func=AF.Square), reads=[bKNt, bsq2], writes=[bsq2])
                    sq3, bsq3 = sq3s.next()
                    kb.op(act, lambda: ACT.activation(out=sq3[:], in_=KRt[:], func=AF.Square), reads=[bKRt], writes=[bsq3])
                    pR2, bpR2 = kb.bank(2)
                    for n in range(2):
                        kb.op(pe, lambda n=n: PE.matmul(pR2[0:1, n * 512:(n + 1) * 512], lhsT=onesf[0:64, 0:1], rhs=sq2[0:64, n * 512:(n + 1) * 512], start=True, stop=True),
                              reads=[b_onesf, bsq2], writes=[bpR2[n]])
                    pR3, bpR3 = kb.bank(1)
                    kb.op(pe, lambda: PE.matmul(pR3[0:1, 0:128], lhsT=onesf[0:32, 0:1], rhs=sq3[:], start=True, stop=True), reads=[b_onesf, bsq3], writes=bpR3)
                    kr_, bkr_ = krn.next()
                    kb.op(dve, lambda: V.tensor_copy(out=kr_[:], in_=pR3[0:1, 0:128]), reads=bpR3, writes=[bkr_])
                    kb.op(dve, lambda: V.tensor_tensor(out=nrow[:, 1, :].rearrange("o (h t) -> o h t", h=8), in0=pR2[0:1, :].rearrange("o (h t) -> o h t", h=8),
                                                       in1=kr_[:].unsqueeze(1).to_broadcast([1, 8, 128]), op=ALU.add), reads=bpR2 + [bkr_], writes=[bnrow])
                    kb.dma(sp, Z["KN"][:, r0:r0 + 128].rearrange("(o h) t -> o h t", o=1), nrow[:, 1, :].rearrange("o (h t) -> o h t", h=8), bnrow, reads=[bnrow], writes=[ZB["KN"]])
                kb.barrier()

        def phase2(si, S):
            nkt = S // 128
            QBLK = 512 if S >= 512 else S
            nqb = S // QBLK
            with ExitStack() as st:
                KThs = kb.slots([97, S], BF16, 2, "KTh", st)
                QThs = kb.slots([97, S], BF16, 2, "QTh", st)
                Vhs = kb.slots([128, nkt, 65], BF16, 2, "Vh", st)
                kn2, b_kn2 = kb.sbb([128, S // 128], F32, "kn2", st)
                kst, b_kst = kb.sbb([128, 4], F32, "kst", st)
                krow, b_krow = kb.sbb([1, 132], F32, "krow", st)
                kmx, b_kmx = kb.sbb([128, 1], F32, "kmx", st)
                PTs = kb.slots([128, QBLK], BF16, 5, "PT", st)
                osbs = kb.slots([65, QBLK], F32, 2, "osb", st)
                rds = kb.slots([64, QBLK], F32, 2, "rd", st)
                mos = kb.slots([64, QBLK], BF16, 2, "mo", st)
                Esel, b_Esel = kb.sbb([65, 64], F32, "Esel", st)
                kb.op(dve, lambda: V.memset(Esel[:], 0.0), writes=[b_Esel])
                kb.op(dve, lambda: V.memset(Esel[64:65, :], 1.0), writes=[b_Esel])
                kb.set_banks(range(2, 8))
                for h in range(8):
                    KTh, bKTh = KThs.next()
                    QTh, bQTh = QThs.next()
                    Vh, bVh = Vhs.next()
                    kb.dma(sp, KTh[:, :], Z["KT"][h, :, 0:S], bKTh, reads=[ZB["KT"]], writes=[bKTh])
                    kb.dma(sp, QTh[:, :], Z["QT"][h, :, 0:S], bQTh, reads=[ZB["QT"]], writes=[bQTh])
                    with nc.allow_non_contiguous_dma(reason="per-head V rows (130B)"):
                        kb.dma(sp, Vh[:], Z["VS"][0:S, h * 65:(h + 1) * 65].rearrange("(kt p) c -> p kt c", p=128), bVh, reads=[ZB["VS"]], writes=[bVh])
                    kb.dma(sp, kn2[:], Z["KN"][h, 0:S].rearrange("(p f) -> p f", p=128), b_kn2, reads=[ZB["KN"]], writes=[b_kn2])
                    kb.op(dve, lambda: V.tensor_reduce(out=kst[:, 0:1], in_=kn2[:], axis=AX.X, op=ALU.max), reads=[b_kn2], writes=[b_kst])
                    pK1, bpK1 = kb.bank(1)
                    kb.op(pe, lambda pK1=pK1: PE.transpose(pK1[0:1, 0:128], kst[:, 0:1], ident[:]), reads=[b_kst, b_ident], writes=bpK1)
                    kb.op(dve, lambda pK1=pK1: V.tensor_copy(out=krow[:, 0:128], in_=pK1[0:1, 0:128]), reads=bpK1, writes=[b_krow])
                    kb.op(dve, lambda: V.tensor_reduce(out=krow[:, 128:129], in_=krow[:, 0:128], axis=AX.X, op=ALU.max), reads=[b_krow], writes=[b_krow])
                    pK2, bpK2 = kb.bank(1)
                    kb.op(pe, lambda pK2=pK2: PE.matmul(pK2[:, 0:1], lhsT=onesf[0:1, :], rhs=krow[:, 128:129], start=True, stop=True), reads=[b_onesf, b_krow], writes=bpK2)
                    kb.op(act, lambda pK2=pK2: ACT.activation(out=kmx[:], in_=pK2[:, 0:1], func=AF.Sqrt), reads=bpK2, writes=[b_kmx])
                    kb.op(dve, lambda QTh=QTh: V.tensor_scalar(out=QTh[96:97, :], in0=QTh[96:97, :], scalar1=kmx[96:97, 0:1], scalar2=None, op0=ALU.mult),
                          reads=[bQTh, b_kmx], writes=[bQTh])
                    for qb in range(nqb):
                        q0 = qb * QBLK
                        pO, bpO = kb.fixed_bank(qb % 2, 1)
                        LA = 2
                        pend = []

                        def emit_s(kt):
                            pS, bpS = kb.bank(1)
                            kb.op(pe, lambda: PE.matmul(pS[:, 0:QBLK], lhsT=KTh[:, kt * 128:(kt + 1) * 128], rhs=QTh[:, q0:q0 + QBLK], start=True, stop=True),
                                  reads=[bKTh, bQTh], writes=bpS)
                            PT, bPT = PTs.next()
                            kb.op(act, lambda: ACT.activation(out=PT[:], in_=pS[:, 0:QBLK], func=AF.Exp), reads=bpS, writes=[bPT])
                            pend.append((kt, PT, bPT))

                        def emit_pv():
                            pkt, pPT, pbPT = pend.pop(0)
                            kb.op(pe, lambda: PE.matmul(pO[0:65, 0:QBLK], lhsT=Vh[:, pkt, :], rhs=pPT[:], start=(pkt == 0), stop=(pkt == nkt - 1)),
                                  reads=[bVh, pbPT], writes=bpO)
                        for kt in range(nkt):
                            emit_s(kt)
                            if len(pend) > LA:
                                emit_pv()
                        while pend:
                            emit_pv()
                        osb, bosb = osbs.next()
                        kb.op(dve, lambda: V.tensor_copy(out=osb[:], in_=pO[0:65, 0:QBLK]), reads=bpO, writes=[bosb])
                        pD, bpD = kb.bank(1)
                        kb.op(pe, lambda: PE.matmul(pD[0:64, 0:QBLK], lhsT=Esel[:], rhs=osb[:], start=True, stop=True), reads=[b_Esel, bosb], writes=bpD)
                        rd, brd = rds.next()
                        kb.op(dve, lambda: V.reciprocal(out=rd[:], in_=pD[0:64, 0:QBLK]), reads=bpD, writes=[brd])
                        mo, bmo = mos.next()
                        kb.op(dve, lambda: V.tensor_tensor(out=mo[:], in0=osb[0:64, :], in1=rd[:], op=ALU.mult), reads=[bosb, brd], writes=[bmo])
                        kb.dma(sp, Z["MO"][h, :, q0:q0 + QBLK], mo[:], bmo, reads=[bmo], writes=[ZB["MO"]])
                kb.barrier()

        def phase3a(si, S):
            nt = S // 128
            X = A[f"x{si}"]
            MEM = A[f"mem{si}"]
            kb.set_banks(range(8))
            with ExitStack() as st:
                work = dict(junk=kb.slots([128, 1024], BF16, 1, "junk", st), ss=kb.slots([128, 4], F32, 2, "ss", st),
                            xs=kb.slots([128, 1024], F32, 2, "xs", st))
                KmT, b_KmT = kb.sbb([128, 4, 2, 256], BF16, "KmT", st)
                Vm, b_Vm = kb.sbb([128, 2, 1024], BF16, "Vm", st)
                ncw, b_ncw = load_cols("norm_ca_w", 8, st)
                with ExitStack() as st2:
                    Wkv, b_Wkv = kb.sbb([128, 8, 2048], BF16, "Wkv", st2)
                    for kc in range(8):
                        load_cast(Wkv[:, kc, :], b_Wkv, A["ca_wkv"][0, kc * 128:(kc + 1) * 128, :])
                    nmm, b_nmm = load_cols("norm_mem_w", 8, st2)
                    memT, b_memT = kb.sbb([128, 8, 256], BF16, "memT", st2)
                    mts = kb.slots([128, 1024], F32, 2, "memt", st2)
                    for m in range(2):
                        mt, bmt = mts.next()
                        kb.dma(sp, mt[:], MEM[m * 128:(m + 1) * 128, :], bmt, writes=[bmt])
                        norm_transpose(mt[:], bmt, nmm, b_nmm, memT, b_memT, m * 128, work)
                    for j in range(8):
                        pK, bpK = kb.bank(1)
                        for kc in range(8):
                            kb.op(pe, lambda j=j, kc=kc, pK=pK: PE.matmul(pK[:, 0:256], lhsT=Wkv[:, kc, j * 128:(j + 1) * 128], rhs=memT[:, kc, :], start=(kc == 0), stop=(kc == 7)),
                                  reads=[b_Wkv, b_memT], writes=bpK)
                        kb.op(act, lambda j=j, pK=pK: ACT.copy(out=KmT[:, j // 2, j % 2, :], in_=pK[:, 0:256]), reads=bpK, writes=[b_KmT])
                    for mc in range(2):
                        for n in range(2):
                            pVm, bpVm = kb.bank(1)
                            for kc in range(8):
                                kb.op(pe, lambda mc=mc, n=n, kc=kc, pVm=pVm: PE.matmul(pVm, lhsT=memT[:, kc, mc * 128:(mc + 1) * 128], rhs=Wkv[:, kc, 1024 + n * 512:1024 + (n + 1) * 512],
                                                                                  start=(kc == 0), stop=(kc == 7)), reads=[b_memT, b_Wkv], writes=bpVm)
                            kb.op(act, lambda mc=mc, n=n, pVm=pVm: ACT.copy(out=Vm[:, mc, n * 512:(n + 1) * 512], in_=pVm), reads=bpVm, writes=[b_Vm])
                    kb.barrier()
                Wor, b_Wor = kb.sbb([128, 4, 1024], BF16, "Wor", st)
                for r in range(4):
                    load_cast(Wor[:, r, :], b_Wor, A["w_out"][0, r * 128:(r + 1) * 128, :])
                Wom, b_Wom = kb.sbb([64, 8, 1024], BF16, "Wom", st)
                for h in range(8):
                    load_cast(Wom[:, h, :], b_Wom, A["w_out"][0, 512 + h * 64:512 + (h + 1) * 64, :])
                Wq, b_Wq = kb.sbb([128, 8, 1024], BF16, "Wq", st)
                Wo, b_Wo = kb.sbb([128, 8, 1024], BF16, "Wo", st)
                for kc in range(8):
                    load_cast(Wq[:, kc, :], b_Wq, A["ca_wq"][0, kc * 128:(kc + 1) * 128, :])
                    load_cast(Wo[:, kc, :], b_Wo, A["ca_wo"][0, kc * 128:(kc + 1) * 128, :])
                gnw, b_gnw = kb.sbb([128, 512], F32, "gnw", st)
                kb.dma(sp, gnw[:], A["ret_gn_w"][0, :].partition_broadcast(128), b_gnw, writes=[b_gnw])
                dec, b_dec = kb.sbb([128, 8], F32, "dec3", st)
                kb.dma(sp, dec[:], A["ret_decay_bwd"][0, :].partition_broadcast(128), b_dec, writes=[b_dec])
                kb.op(act, lambda: ACT.activation(out=dec[:], in_=dec[:], func=AF.Exp, scale=-1.0), reads=[b_dec], writes=[b_dec])
                kb.op(act, lambda: ACT.activation(out=dec[:], in_=dec[:], func=AF.Ln, bias=1.0), reads=[b_dec], writes=[b_dec])
                kb.op(act, lambda: ACT.activation(out=dec[:], in_=dec[:], func=AF.Exp, scale=-128.0), reads=[b_dec], writes=[b_dec])
                GCb, b_GCb = kb.sbb([128, 4], F32, "GCb", st)
                decv = dec[:].rearrange("p (a two) -> p a two", two=2)
                kb.op(dve, lambda: V.tensor_copy(out=GCb[0:64, :], in_=decv[0:64, :, 0]), reads=[b_dec], writes=[b_GCb])
                kb.op(dve, lambda: V.tensor_copy(out=GCb[64:128, :], in_=decv[64:128, :, 1]), reads=[b_dec], writes=[b_GCb])
                Rb, b_Rb = kb.sbb([128, 4, 64], F32, "Rb", st)
                Rbb, b_Rbb = kb.sbb([128, 4, 128], BF16, "Rbb", st)
                kb.op(dve, lambda: V.memset(Rb[:], 0.0), writes=[b_Rb])
                kb.op(dve, lambda: V.memset(Rbb[:], 0.0), writes=[b_Rbb])
                yps = kb.slots([128, 512], F32, 2, "yp3", st)
                sgs = kb.slots([128, 512], F32, 2, "sg3", st)
                qbTs = kb.slots([128, 4, 128], BF16, 2, "qbT3", st)
                bbs = kb.slots([128, 4, 64], F32, 2, "bb3", st)
                xts = kb.slots([128, 1024], F32, 2, "xt3", st)
                mots = kb.slots([64, 8, 128], BF16, 2, "mot", st)
                ys = kb.slots([128, 8, 64], F32, 2, "y3", st)
                ycs = kb.slots([128, 8, 64], F32, 2, "yc3", st)
                y2s = kb.slots([128, 8, 64], F32, 2, "ysq3", st)
                sts = kb.slots([128, 8, 4], F32, 2, "st3", st)
                ros = kb.slots([128, 512], F32, 2, "ro3", st)
                roTs = kb.slots([128, 4, 128], BF16, 2, "roT", st)
                x1s = kb.slots([128, 1024], F32, 2, "x1", st)
                hnTs = kb.slots([128, 8, 128], BF16, 2, "hnT", st)
                qcTs = kb.slots([128, 8, 128], BF16, 2, "qcT", st)
                mxs = kb.slots([128, 4, 4], F32, 2, "mx3", st)
                Ps = kb.slots([128, 4, 256], F32, 2, "P3", st)
                PnTs = kb.slots([128, 8, 128], BF16, 2, "PnT", st)
                oTs = kb.slots([128, 8, 128], BF16, 2, "oT", st)
                x2s = kb.slots([128, 1024], F32, 2, "x2", st)
                for t in reversed(range(nt)):
                    r0 = t * 128
                    yp, byp = yps.next()
                    sg, bsg = sgs.next()
                    qbT, bqbT = qbTs.next()
                    bb, bbb = bbs.next()
                    xt, bx = xts.next()
                    mot, bmot = mots.next()
                    kb.dma(sp, yp[:], Z["YP"][r0:r0 + 128, :], byp, reads=[ZB["YP"]], writes=[byp])
                    kb.dma(sp, sg[:], Z["SG"][r0:r0 + 128, :], bsg, reads=[ZB["SG"]], writes=[bsg])
                    kb.dma(sp, qbT[:].rearrange("p a t -> p (a t)"), Z["QB"][t], bqbT, reads=[ZB["QB"]], writes=[bqbT])
                    kb.dma(sp, bb[:].rearrange("p a v -> p (a v)"), Z["BB"][t], bbb, reads=[ZB["BB"]], writes=[bbb])
                    kb.dma(sp, xt[:], X[r0:r0 + 128, :], bx, writes=[bx])
                    kb.dma(sp, mot[:], Z["MO"][:, :, r0:r0 + 128].rearrange("h v t -> v h t"), bmot, reads=[ZB["MO"]], writes=[bmot])
                    pY, bpY = kb.bank(1)
                    for a in range(4):
                        kb.op(pe, lambda a=a: PE.matmul(pY[:, a * 128:(a + 1) * 128], lhsT=qbT[:, a, :], rhs=Rbb[:, a, :], start=True, stop=True),
                              reads=[bqbT, b_Rbb], writes=bpY)
                    y, by = ys.next()
                    kb.op(dve, lambda: V.tensor_tensor(out=y[:].rearrange("p h d -> p (h d)"), in0=pY, in1=yp[:], op=ALU.add), reads=bpY + [byp], writes=[by])
                    kb.op(dve, lambda: V.tensor_tensor(out=Rb[:], in0=Rb[:], in1=GCb[:, :].unsqueeze(2).to_broadcast([128, 4, 64]), op=ALU.mult), reads=[b_Rb, b_GCb], writes=[b_Rb])
                    kb.op(dve, lambda: V.tensor_tensor(out=Rb[:], in0=Rb[:], in1=bb[:], op=ALU.add), reads=[b_Rb, bbb], writes=[b_Rb])
                    kb.op(dve, lambda: V.tensor_copy(out=Rbb[0:64, :, 0:64], in_=Rb[0:64]), reads=[b_Rb], writes=[b_Rbb])
                    kb.op(dve, lambda: V.tensor_copy(out=Rbb[64:128, :, 64:128], in_=Rb[64:128]), reads=[b_Rb], writes=[b_Rbb])
                    stt, bst = sts.next()
                    yc, byc = ycs.next()
                    ysq, bysq = y2s.next()
                    kb.op(dve, lambda: V.tensor_reduce(out=stt[:, :, 0], in_=y[:], axis=AX.X, op=ALU.add), reads=[by], writes=[bst])
                    kb.op(dve, lambda: V.tensor_scalar(out=stt[:, :, 1], in0=stt[:, :, 0], scalar1=1.0 / 64, scalar2=None, op0=ALU.mult), reads=[bst], writes=[bst])
                    kb.op(dve, lambda: V.tensor_tensor(out=yc[:], in0=y[:], in1=stt[:, :, 1:2].to_broadcast([128, 8, 64]), op=ALU.subtract), reads=[by, bst], writes=[byc])
                    kb.op(pool, lambda: G.tensor_tensor(out=ysq[:], in0=yc[:], in1=yc[:], op=ALU.mult), reads=[byc], writes=[bysq])
                    kb.op(dve, lambda: V.tensor_reduce(out=stt[:, :, 2], in_=ysq[:], axis=AX.X, op=ALU.add), reads=[bysq], writes=[bst])
                    kb.op(act, lambda: ACT.activation(out=stt[:, :, 3], in_=stt[:, :, 2], func=AF.Sqrt, scale=1.0 / 64, bias=epsg[:, 0:1]), reads=[bst, b_epsg], writes=[bst])
                    kb.op(dve, lambda: V.reciprocal(out=stt[:, :, 3], in_=stt[:, :, 3]), reads=[bst], writes=[bst])
                    kb.op(dve, lambda: V.tensor_tensor(out=yc[:], in0=yc[:], in1=stt[:, :, 3:4].to_broadcast([128, 8, 64]), op=ALU.mult), reads=[byc, bst], writes=[byc])
                    ro, bro = ros.next()
                    kb.op(pool, lambda: G.tensor_tensor(out=ro[:], in0=yc[:].rearrange("p h d -> p (h d)"), in1=gnw[:], op=ALU.mult), reads=[byc, b_gnw], writes=[bro])
                    kb.op(pool, lambda: G.tensor_tensor(out=ro[:], in0=ro[:], in1=sg[:], op=ALU.mult), reads=[bro, bsg], writes=[bro])
                    pT, bpT = kb.bank(1)
                    for r in range(4):
                        kb.op(pe, lambda r=r: PE.transpose(pT[:, r * 128:(r + 1) * 128], ro[:, r * 128:(r + 1) * 128], ident[:]), reads=[bro, b_ident], writes=bpT)
                    roT, broT = roTs.next()
                    kb.op(act, lambda: ACT.copy(out=roT[:].rearrange("p r t -> p (r t)"), in_=pT), reads=bpT, writes=[broT])
                    pM, bpM = kb.bank(2)
                    for n in range(2):
                        for r in range(4):
                            kb.op(pe, lambda n=n, r=r: PE.matmul(pM[:, n * 512:(n + 1) * 512], lhsT=roT[:, r, :], rhs=Wor[:, r, n * 512:(n + 1) * 512], start=(r == 0), stop=False),
                                  reads=[broT, b_Wor], writes=[bpM[n]])
                        for h in range(8):
                            kb.op(pe, lambda n=n, h=h: PE.matmul(pM[:, n * 512:(n + 1) * 512], lhsT=mot[:, h, :], rhs=Wom[:, h, n * 512:(n + 1) * 512], start=False, stop=(h == 7)),
                                  reads=[bmot, b_Wom], writes=[bpM[n]])
                    x1, bx1 = x1s.next()
                    kb.op(dve, lambda: V.tensor_tensor(out=x1[:], in0=pM, in1=xt[:], op=ALU.add), reads=bpM + [bx], writes=[bx1])
                    hnT, bhnT = hnTs.next()
                    norm_transpose(x1[:], bx1, ncw, b_ncw, hnT, bhnT, 0, work)
                    pQ, bpQ = kb.bank(2)
                    for j in range(8):
                        for kc in range(8):
                            kb.op(pe, lambda j=j, kc=kc: PE.matmul(pQ[:, j * 128:(j + 1) * 128], lhsT=Wq[:, kc, j * 128:(j + 1) * 128], rhs=hnT[:, kc, :], start=(kc == 0), stop=(kc == 7)),
                                  reads=[b_Wq, bhnT], writes=[bpQ[j // 4]])
                    qcT, bqcT = qcTs.next()
                    kb.op(act, lambda: ACT.activation(out=qcT[:].rearrange("p j t -> p (j t)"), in_=pQ, func=AF.Copy, scale=1.0 / 16), reads=bpQ, writes=[bqcT])
                    pS, bpS = kb.bank(2)
                    for hd in range(4):
                        for dc in range(2):
                            kb.op(pe, lambda hd=hd, dc=dc: PE.matmul(pS[:, hd * 256:(hd + 1) * 256], lhsT=qcT[:, hd * 2 + dc, :], rhs=KmT[:, hd, dc, :], start=(dc == 0), stop=(dc == 1)),
                                  reads=[bqcT, b_KmT], writes=[bpS[hd // 2]])
                    mx, bmx = mxs.next()
                    kb.op(dve, lambda: V.tensor_reduce(out=mx[:, :, 0], in_=pS.rearrange("p (h m) -> p h m", h=4), axis=AX.X, op=ALU.max), reads=bpS, writes=[bmx])
                    kb.op(dve, lambda: V.tensor_scalar(out=mx[:, :, 1], in0=mx[:, :, 0], scalar1=-1.0, scalar2=None, op0=ALU.mult), reads=[bmx], writes=[bmx])
                    P_, bP = Ps.next()
                    for hd in range(4):
                        kb.op(act, lambda hd=hd: ACT.activation(out=P_[:, hd, :], in_=pS[:, hd * 256:(hd + 1) * 256], func=AF.Exp, bias=mx[:, hd, 1:2], accum_out=mx[:, hd, 2:3]),
                              reads=[bpS[hd // 2], bmx], writes=[bP, bmx])
                    kb.op(dve, lambda: V.reciprocal(out=mx[:, :, 3], in_=mx[:, :, 2]), reads=[bmx], writes=[bmx])
                    kb.op(dve, lambda: V.tensor_tensor(out=P_[:], in0=P_[:], in1=mx[:, :, 3:4].to_broadcast([128, 4, 256]), op=ALU.mult), reads=[bP, bmx], writes=[bP])
                    pPT, bpPT = kb.bank(2)
                    Pf = P_[:].rearrange("p h m -> p (h m)")
                    for j in range(8):
                        kb.op(pe, lambda j=j: PE.transpose(pPT[:, j * 128:(j + 1) * 128], Pf[:, j * 128:(j + 1) * 128], ident[:]), reads=[bP, b_ident], writes=[bpPT[j // 4]])
                    PnT, bPnT = PnTs.next()
                    kb.op(act, lambda: ACT.copy(out=PnT[:].rearrange("p j t -> p (j t)"), in_=pPT), reads=bpPT, writes=[bPnT])
                    pO, bpO = kb.bank(2)
                    for hd in range(4):
                        for dd in range(2):
                            for mc in range(2):
                                kb.op(pe, lambda hd=hd, dd=dd, mc=mc: PE.matmul(pO[:, (hd * 2 + dd) * 128:(hd * 2 + dd + 1) * 128],
                                                                                lhsT=Vm[:, mc, hd * 256 + dd * 128:hd * 256 + (dd + 1) * 128], rhs=PnT[:, hd * 2 + mc, :],
                                                                                start=(mc == 0), stop=(mc == 1)), reads=[b_Vm, bPnT], writes=[bpO[(hd * 2 + dd) // 4]])
                    oT, boT = oTs.next()
                    kb.op(act, lambda: ACT.copy(out=oT[:].rearrange("p j t -> p (j t)"), in_=pO), reads=bpO, writes=[boT])
                    pC, bpC = kb.bank(2)
                    for n in range(2):
                        for j in range(8):
                            kb.op(pe, lambda n=n, j=j: PE.matmul(pC[:, n * 512:(n + 1) * 512], lhsT=oT[:, j, :], rhs=Wo[:, j, n * 512:(n + 1) * 512], start=(j == 0), stop=(j == 7)),
                                  reads=[boT, b_Wo], writes=[bpC[n]])
                    x2, bx2 = x2s.next()
                    kb.op(dve, lambda: V.tensor_tensor(out=x2[:], in0=pC, in1=x1[:], op=ALU.add), reads=bpC + [bx1], writes=[bx2])
                    kb.dma(sp, Z["X2"][r0:r0 + 128, :], x2[:], bx2, reads=[bx2], writes=[ZB["X2"]])
                kb.barrier()

        def phase3b(si, S):
            TB = 256
            nb = S // TB
            Y = A[f"y{si}"]
            with ExitStack() as st:
                work = dict(junk=kb.slots([128, 1024], BF16, 1, "junk", st), ss=kb.slots([128, 4], F32, 2, "ss", st),
                            xs=kb.slots([128, 1024], F32, 1, "xs", st))
                kb.set_banks(range(8))
                Wpq, b_Wpq = kb.sbb([128, 8, 2048], BF16, "Wpq", st)
                for kc in range(8):
                    load_cast(Wpq[:, kc, :], b_Wpq, A["peer_wq"][0, kc * 128:(kc + 1) * 128, :])
                nfw, b_nfw = load_cols("norm_ffn_w", 8, st)
                fnw, b_fnw = kb.sbb([128, 1024], F32, "fnw", st)
                kb.dma(sp, fnw[:], A["final_norm_w"].partition_broadcast(128), b_fnw, writes=[b_fnw])
                SKT, b_SKT = kb.sbb([128, 16, 128], BF16, "SKT", st)
                sks = kb.slots([128, 128], F32, 2, "skld", st)
                for g in range(16):
                    h, half = g // 2, g % 2
                    skt, bskt = sks.next()
                    kb.dma(sp, skt[:], A["peer_sub_keys"][0, half, h, :, :], bskt, writes=[bskt])
                    pT, bpT = kb.bank(1)
                    kb.op(pe, lambda skt=skt, pT=pT: PE.transpose(pT[:, 0:128], skt[:], ident[:]), reads=[bskt, b_ident], writes=bpT)
                    kb.op(act, lambda g=g, pT=pT: ACT.copy(out=SKT[:, g, :], in_=pT[:, 0:128]), reads=bpT, writes=[b_SKT])
                iota16, b_iota16 = kb.sbb([128, 16], F32, "iota16", st)
                kb.op(dve, lambda: V.tensor_copy(out=iota16[:], in_=iota[:, 0:16]), reads=[b_iota], writes=[b_iota16])
                x2ts = kb.slots([128, 2, 1024], F32, 1, "x2b", st)
                xn3Ts = kb.slots([128, 8, TB], BF16, 1, "xn3T", st)
                qpTs = kb.slots([128, 16, TB], BF16, 1, "qpT", st)
                reps = kb.slots([128, 256], F32, 2, "rep", st)
                v16s = kb.slots([128, 16, 16], F32, 1, "v16", st)
                ix16s = kb.slots([128, 16, 16], U32, 1, "ix16", st)
                ixfs = kb.slots([128, 16, 16], F32, 1, "ixf", st)
                cands = kb.slots([128, 8, 256], F32, 1, "cand", st)
                m2s = kb.slots([128, 8, 16], F32, 1, "m2", st)
                p2s = kb.slots([128, 8, 16], U32, 1, "p2", st)
                abs_ = kb.slots([128, 2, 128], U32, 1, "abu", st)
                abfs = kb.slots([128, 2, 128], F32, 1, "abf", st)
                ohs = kb.slots([128, 8, 16, 16], BF16, 1, "oh", st)
                sel3s = kb.slots([128, 3, 128], F32, 1, "sel3", st)
                zs = kb.slots([128, 8, 2], F32, 1, "z", st)
                T3s = kb.slots([128, 3, 128], F32, 1, "T3", st)
                Rs = kb.slots([128, 32, 128], BF16, 1, "Roh", st)
                L0s = kb.slots([128, 32, 128], BF16, 1, "L0oh", st)
                Gm, b_Gm = kb.sbb([128, TB, 128], BF16, "Gm", st)
                UTs = kb.slots([128, 8, 128], BF16, 3, "UTi", st)
                Vis = kb.slots([128, 1024], BF16, 4, "Vi", st)
                gas = kb.slots([128, TB], F32, 3, "ga", st)
                Wds = kb.slots([128, TB], BF16, 4, "Wd", st)
                x3s = kb.slots([128, 1024], F32, 1, "x3", st)
                outs = kb.slots([128, 1024], F32, 1, "outt", st)
                for b in range(nb):
                    r0 = b * TB
                    kb.set_banks(range(8))
                    x2b, bx2b = x2ts.next()
                    kb.dma(sp, x2b[:], Z["X2"][r0:r0 + TB, :].rearrange("(t p) d -> p t d", p=128), bx2b, reads=[ZB["X2"]], writes=[bx2b])
                    xn3T, bxn3T = xn3Ts.next()
                    for tt in range(2):
                        norm_transpose(x2b[:, tt, :], bx2b, nfw, b_nfw, xn3T, bxn3T, tt * 128, work)
                    qpT, bqpT = qpTs.next()
                    for g in range(16):
                        pQ, bpQ = kb.bank(1)
                        for kc in range(8):
                            kb.op(pe, lambda g=g, kc=kc, pQ=pQ: PE.matmul(pQ[:, 0:TB], lhsT=Wpq[:, kc, g * 128:(g + 1) * 128], rhs=xn3T[:, kc, :], start=(kc == 0), stop=(kc == 7)),
                                  reads=[b_Wpq, bxn3T], writes=bpQ)
                        if g % 2 == 0:
                            kb.op(act, lambda g=g, pQ=pQ: ACT.copy(out=qpT[:, g, :], in_=pQ[:, 0:TB]), reads=bpQ, writes=[bqpT])
                        else:
                            kb.op(dve, lambda g=g, pQ=pQ: V.tensor_copy(out=qpT[:, g, :], in_=pQ[:, 0:TB]), reads=bpQ, writes=[bqpT])
                    for tt in range(2):
                        pSs = []
                        for q4 in range(4):
                            pS, bpS = kb.bank(1)
                            pSs.append((pS, bpS))
                            for gg in range(4):
                                g = q4 * 4 + gg
                                kb.op(pe, lambda g=g, gg=gg, pS=pS: PE.matmul(pS[:, gg * 128:(gg + 1) * 128], lhsT=qpT[:, g, tt * 128:(tt + 1) * 128], rhs=SKT[:, g, :], start=True, stop=True),
                                      reads=[bqpT, b_SKT], writes=bpS)
                        v16, bv16 = v16s.next()
                        ix16, bix16 = ix16s.next()
                        for g in range(16):
                            pS, bpS = pSs[g // 4]
                            src = pS[:, (g % 4) * 128:(g % 4 + 1) * 128]
                            rep, brep = reps.next()
                            kb.op(dve, lambda g=g, src=src: V.max(out=v16[:, g, 0:8], in_=src), reads=bpS, writes=[bv16])
                            kb.op(dve, lambda g=g, rep=rep, src=src: V.match_replace(out=rep[:, 0:128], in_to_replace=v16[:, g, 0:8], in_values=src, imm_value=NEG), reads=bpS + [bv16], writes=[brep])
                            kb.op(dve, lambda g=g, rep=rep: V.max(out=v16[:, g, 8:16], in_=rep[:, 0:128]), reads=[brep], writes=[bv16])
                            kb.op(dve, lambda g=g, src=src: V.max_index(out=ix16[:, g, 0:8], in_max=v16[:, g, 0:8], in_values=src), reads=bpS + [bv16], writes=[bix16])
                            kb.op(dve, lambda g=g, rep=rep: V.max_index(out=ix16[:, g, 8:16], in_max=v16[:, g, 8:16], in_values=rep[:, 0:128]), reads=[brep, bv16], writes=[bix16])
                        ixf, bixf = ixfs.next()
                        kb.op(dve, lambda: V.tensor_copy(out=ixf[:], in_=ix16[:]), reads=[bix16], writes=[bixf])
                        cand, bcand = cands.next()
                        v4 = v16[:].rearrange("p (h two) k -> p h two k", two=2)
                        kb.op(dve, lambda: V.tensor_tensor(out=cand[:].rearrange("p h (a b) -> p h a b", a=16),
                                                            in0=v4[:, :, 0, :].unsqueeze(3).to_broadcast([128, 8, 16, 16]),
                                                            in1=v4[:, :, 1, :].unsqueeze(2).to_broadcast([128, 8, 16, 16]), op=ALU.add), reads=[bv16], writes=[bcand])
                        m2, bm2 = m2s.next()
                        p2, bp2 = p2s.next()
                        for h in range(8):
                            rep, brep = reps.next()
                            kb.op(dve, lambda h=h: V.max(out=m2[:, h, 0:8], in_=cand[:, h, :]), reads=[bcand], writes=[bm2])
                            kb.op(dve, lambda h=h, rep=rep: V.match_replace(out=rep[:], in_to_replace=m2[:, h, 0:8], in_values=cand[:, h, :], imm_value=NEG), reads=[bcand, bm2], writes=[brep])
                            kb.op(dve, lambda h=h, rep=rep: V.max(out=m2[:, h, 8:16], in_=rep[:]), reads=[brep], writes=[bm2])
                            kb.op(dve, lambda h=h: V.max_index(out=p2[:, h, 0:8], in_max=m2[:, h, 0:8], in_values=cand[:, h, :]), reads=[bcand, bm2], writes=[bp2])
                            kb.op(dve, lambda h=h, rep=rep: V.max_index(out=p2[:, h, 8:16], in_max=m2[:, h, 8:16], in_values=rep[:]), reads=[brep, bm2], writes=[bp2])
                        abu, babu = abs_.next()
                        abf, babf = abfs.next()
                        p2f = p2[:].rearrange("p h k -> p (h k)")
                        kb.op(dve, lambda: V.tensor_single_scalar(out=abu[:, 0, :], in_=p2f, scalar=4, op=ALU.logical_shift_right), reads=[bp2], writes=[babu])
                        kb.op(dve, lambda: V.tensor_single_scalar(out=abu[:, 1, :], in_=p2f, scalar=15, op=ALU.bitwise_and), reads=[bp2], writes=[babu])
                        kb.op(dve, lambda: V.tensor_copy(out=abf[:], in_=abu[:]), reads=[babu], writes=[babf])
                        sel3, bsel3 = sel3s.next()
                        ix4 = ixf[:].rearrange("p (h two) k -> p h two k", two=2)
                        for w in range(2):
                            oh, boh = ohs.next()
                            kb.op(dve, lambda w=w, oh=oh: V.tensor_tensor(out=oh[:], in0=abf[:, w, :].rearrange("p (h k) -> p h k", h=8).unsqueeze(3).to_broadcast([128, 8, 16, 16]),
                                                                          in1=iota16[:].unsqueeze(1).unsqueeze(1).to_broadcast([128, 8, 16, 16]), op=ALU.is_equal),
                                  reads=[babf, b_iota16], writes=[boh])
                            kb.op(dve, lambda w=w, oh=oh: V.tensor_tensor(out=oh[:], in0=oh[:], in1=ix4[:, :, w, :].unsqueeze(2).to_broadcast([128, 8, 16, 16]), op=ALU.mult),
                                  reads=[boh, bixf], writes=[boh])
                            kb.op(dve, lambda w=w, oh=oh: V.tensor_reduce(out=sel3[:, w, :].rearrange("p (h k) -> p h k", h=8), in_=oh[:], axis=AX.X, op=ALU.add),
                                  reads=[boh], writes=[bsel3])
                        z, bz = zs.next()
                        g3 = sel3[:, 2, :].rearrange("p (h k) -> p h k", h=8)
                        kb.op(dve, lambda: V.tensor_tensor(out=g3, in0=m2[:], in1=m2[:, :, 0:1].to_broadcast([128, 8, 16]), op=ALU.subtract), reads=[bm2], writes=[bsel3])
                        kb.op(act, lambda: ACT.activation(out=sel3[:, 2, :], in_=sel3[:, 2, :], func=AF.Exp), reads=[bsel3], writes=[bsel3])
                        kb.op(dve, lambda: V.tensor_reduce(out=z[:, :, 0], in_=g3, axis=AX.X, op=ALU.add), reads=[bsel3], writes=[bz])
                        kb.op(dve, lambda: V.reciprocal(out=z[:, :, 1], in_=z[:, :, 0]), reads=[bz], writes=[bz])
                        kb.op(dve, lambda: V.tensor_tensor(out=g3, in0=g3, in1=z[:, :, 1:2].to_broadcast([128, 8, 16]), op=ALU.mult), reads=[bsel3, bz], writes=[bsel3])
                        pT, bpT = kb.bank(1)
                        for w in range(3):
                            kb.op(pe, lambda w=w: PE.transpose(pT[:, w * 128:(w + 1) * 128], sel3[:, w, :], ident[:]), reads=[bsel3, b_ident], writes=bpT)
                        T3, bT3 = T3s.next()
                        kb.op(act, lambda: ACT.copy(out=T3[:].rearrange("p w t -> p (w t)"), in_=pT[:, 0:384]), reads=bpT, writes=[bT3])
                        for hf in range(4):
                            t0 = hf * 32
                            R_, bR = Rs.next()
                            L_, bL = L0s.next()
                            io_b = iota[:].unsqueeze(1).to_broadcast([128, 32, 128])
                            kb.op(dve, lambda R_=R_, t0=t0: V.tensor_tensor(out=R_[:], in0=io_b, in1=T3[:, 1, t0:t0 + 32].unsqueeze(2).to_broadcast([128, 32, 128]), op=ALU.is_equal),
                                  reads=[b_iota, bT3], writes=[bR])
                            kb.op(dve, lambda L_=L_, t0=t0: V.tensor_tensor(out=L_[:], in0=io_b, in1=T3[:, 0, t0:t0 + 32].unsqueeze(2).to_broadcast([128, 32, 128]), op=ALU.is_equal),
                                  reads=[b_iota, bT3], writes=[bL])
                            kb.op(dve, lambda L_=L_, t0=t0: V.tensor_tensor(out=L_[:], in0=L_[:], in1=T3[:, 2, t0:t0 + 32].unsqueeze(2).to_broadcast([128, 32, 128]), op=ALU.mult),
                                  reads=[bL, bT3], writes=[bL])
                            for q in range(8):
                                pG, bpG = kb.bank(1)
                                for u in range(4):
                                    tl = q * 4 + u
                                    kb.op(pe, lambda tl=tl, u=u, pG=pG, R_=R_, L_=L_: PE.matmul(pG[:, u * 128:(u + 1) * 128], lhsT=R_[:, tl, :], rhs=L_[:, tl, :], start=True, stop=True),
                                          reads=[bR, bL], writes=bpG)
                                tg = tt * 128 + t0 + q * 4
                                if q % 2 == 0:
                                    kb.op(act, lambda tg=tg, pG=pG: ACT.copy(out=Gm[:, tg:tg + 4, :].rearrange("p t i -> p (t i)"), in_=pG), reads=bpG, writes=[b_Gm])
                                else:
                                    kb.op(dve, lambda tg=tg, pG=pG: V.tensor_copy(out=Gm[:, tg:tg + 4, :].rearrange("p t i -> p (t i)"), in_=pG), reads=bpG, writes=[b_Gm])
                    pO0, bpO0 = kb.fixed_bank(0, 2)
                    pO1, bpO1 = kb.fixed_bank(2, 2)
                    pOs = ((pO0, bpO0), (pO1, bpO1))
                    kb.set_banks(range(4, 8))
                    def emit_A(i):
                        UTi, bUTi = UTs.next()
                        Vi, bVi = Vis.next()
                        kb.dma(sp, UTi[:].rearrange("p k j -> p (k j)"), Z["UT"][i], bUTi, reads=[ZB["UT"]], writes=[bUTi])
                        kb.dma(sp, Vi[:], Z["VB"][i], bVi, reads=[ZB["VB"]], writes=[bVi])
                        pA, bpA = kb.bank(1)
                        for kc in range(8):
                            kb.op(pe, lambda kc=kc: PE.matmul(pA[:, 0:TB], lhsT=UTi[:, kc, :], rhs=xn3T[:, kc, :], start=(kc == 0), stop=(kc == 7)),
                                  reads=[bUTi, bxn3T], writes=bpA)
                        ga, bga = gas.next()
                        kb.op(act, lambda: ACT.activation(out=ga[:], in_=pA[:, 0:TB], func=AF.Gelu), reads=bpA, writes=[bga])
                        Wd, bWd = Wds.next()
                        kb.op(dve, lambda: V.tensor_tensor(out=Wd[:], in0=ga[:], in1=Gm[:, :, i], op=ALU.mult), reads=[bga, b_Gm], writes=[bWd])
                        return (Wd, bWd, Vi, bVi)

                    def emit_O(i, cur):
                        Wd, bWd, Vi, bVi = cur
                        for tt in range(2):
                            pO, bpO = pOs[tt]
                            for n in range(2):
                                kb.op(pe, lambda tt=tt, n=n, pO=pO: PE.matmul(pO[:, n * 512:(n + 1) * 512], lhsT=Wd[:, tt * 128:(tt + 1) * 128], rhs=Vi[:, n * 512:(n + 1) * 512],
                                                                          start=(i == 0), stop=(i == NE - 1)), reads=[bWd, bVi], writes=[bpO[n]])
                    LAP = 2
                    pendA = []
                    for i in range(NE):
                        pendA.append((i, emit_A(i)))
                        if len(pendA) > LAP:
                            j, cur = pendA.pop(0)
                            emit_O(j, cur)
                    while pendA:
                        j, cur = pendA.pop(0)
                        emit_O(j, cur)
                    for tt in range(2):
                        pO, bpO = pOs[tt]
                        x3, bx3 = x3s.next()
                        kb.op(dve, lambda pO=pO, x3=x3, tt=tt: V.tensor_tensor(out=x3[:], in0=pO, in1=x2b[:, tt, :], op=ALU.add), reads=bpO + [bx2b], writes=[bx3])
                        junk, b_junk = work["junk"].next()
                        ss, b_ss = work["ss"].next()
                        kb.op(act, lambda x3=x3, junk=junk, ss=ss: ACT.activation(out=junk[:], in_=x3[:], func=AF.Square, accum_out=ss[:, 0:1]), reads=[bx3], writes=[b_junk, b_ss])
                        kb.op(act, lambda ss=ss: ACT.activation(out=ss[:, 1:2], in_=ss[:, 0:1], func=AF.Sqrt, scale=1.0 / D, bias=epsn[:, 0:1]), reads=[b_ss, b_epsn], writes=[b_ss])
                        kb.op(dve, lambda ss=ss: V.reciprocal(out=ss[:, 2:3], in_=ss[:, 1:2]), reads=[b_ss], writes=[b_ss])
                        ot, bot = outs.next()
                        kb.op(dve, lambda x3=x3, ot=ot, ss=ss: V.scalar_tensor_tensor(out=ot[:], in0=x3[:], scalar=ss[:, 2:3], in1=fnw[:], op0=ALU.mult, op1=ALU.mult),
                              reads=[bx3, b_ss, b_fnw], writes=[bot])
                        kb.dma(sp, Y[r0 + tt * 128:r0 + (tt + 1) * 128, :], ot[:], bot, reads=[bot])
                kb.barrier()

        if prepass:
            kb.begin_phase()
            prepass_peer()
            kb.end_phase()
        nph = 0
        for si, S in enumerate(S_list):
            for ph in (phase1, phase2, phase3a, phase3b):
                if nph >= upto:
                    break
                nph += 1
                kb.set_banks(range(8))
                kb.begin_phase()
                ph(si, S)
                kb.end_phase()
        kb.barrier()
        kb.stats = {E.name: E.n for E in kb.engs}
        kb.stats["nsem"] = kb.nsem
        kb.stats["nops"] = kb.nops
        build.last_stats = kb.stats
    return nc


_CACHE = {}


def kernel(**inputs):
    ncores = 8
    S0 = inputs["x_prompt"].shape[1]
    S1 = inputs["x_sample"].shape[1]
    key = (S0, S1)
    if key not in _CACHE:
        _CACHE[key] = build([S0, S1])
    nc = _CACHE[key]
    consts = host_consts(max(S0, S1))
    in_maps = []
    for c in range(ncores):
        m = {"x0": np.ascontiguousarray(inputs["x_prompt"][c], dtype=np.float32),
             "x1": np.ascontiguousarray(inputs["x_sample"][c], dtype=np.float32),
             "mem0": np.ascontiguousarray(inputs["mem_prompt"][c], dtype=np.float32),
             "mem1": np.ascontiguousarray(inputs["mem_sample"][c], dtype=np.float32)}
        for n in W_NAMES:
            m[n] = np.ascontiguousarray(inputs[n], dtype=np.float32)
        m.update(consts)
        in_maps.append(m)
    res = run_bass_kernel_spmd(nc, in_maps, core_ids=list(range(ncores)))
    y0 = np.stack([np.asarray(res.results[c]["y0"], dtype=np.float32) for c in range(ncores)], 0)
    y1 = np.stack([np.asarray(res.results[c]["y1"], dtype=np.float32) for c in range(ncores)], 0)
    return (y0, y1)
```

```python
from contextlib import ExitStack
import numpy as np
import ml_dtypes
import concourse.bass as bass
import concourse.mybir as mybir
from concourse.bass_utils import run_bass_kernel_spmd

F32 = mybir.dt.float32
BF16 = mybir.dt.bfloat16
U32 = mybir.dt.uint32
AF = mybir.ActivationFunctionType
ALU = mybir.AluOpType
AX = mybir.AxisListType

SEM_LIMIT = 30000
D = 1024
NEG = -1.0e30


class Buf:
    __slots__ = ("name", "w", "r", "multi", "dsem", "dcnt", "excl")

    def __init__(self, name, multi=False, excl=False):
        self.name = name
        self.excl = excl
        self.w = {}
        self.r = {}
        self.multi = multi
        self.dsem = None
        self.dcnt = 0


class Eng:
    def __init__(self, kb, name, eng, own_wait=True):
        self.kb = kb
        self.name = name
        self.eng = eng
        self.sem = None
        self.val = 0
        self.seen = {}
        self.own_wait = own_wait
        self.n = 0

    def next_event(self):
        if self.sem is None or self.val >= SEM_LIMIT:
            self.sem = self.kb.new_sem(self.name)
            self.val = 0
        self.val += 1
        return (self.sem, self.val)


class Slots:
    def __init__(self, items):
        self.items = items
        self.i = 0

    def next(self):
        it = self.items[self.i % len(self.items)]
        self.i += 1
        return it


class KB:
    def __init__(self, nc, stack):
        self.nc = nc
        self.stack = stack
        self.nsem = 0
        self.pe = Eng(self, "pe", nc.tensor, own_wait=False)
        self.dve = Eng(self, "dve", nc.vector)
        self.act = Eng(self, "act", nc.scalar)
        self.pool = Eng(self, "pool", nc.gpsimd)
        self.sp = Eng(self, "sp", nc.sync)
        self.engs = [self.pe, self.dve, self.act, self.pool, self.sp]
        self.anchors = []
        self.free_sems = []
        self.nops = 0
        import os
        self.limit = int(os.environ.get("OPLIMIT", "1000000000"))
        self.phase_mark = 0
        self.persist = []
        self.uid = 0
        self.PS = None
        self.bank_bufs = None
        self.bank_pool = list(range(8))
        self.bank_i = 0

    def new_sem(self, name):
        self.nsem += 1
        return self.stack.enter_context(self.nc.semaphore(f"s{self.nsem}_{name}"))

    def sb(self, shape, dtype, name="sb", stack=None):
        self.uid += 1
        st = stack or self.stack
        return st.enter_context(self.nc.sbuf_tensor(f"{name}_{self.uid}", list(shape), dtype))

    def sbb(self, shape, dtype, name="sb", stack=None):
        return self.sb(shape, dtype, name, stack), Buf(name)

    def slots(self, shape, dtype, n, name, stack=None):
        return Slots([self.sbb(shape, dtype, f"{name}{i}", stack) for i in range(n)])

    def init_psum(self):
        self.PS = self.stack.enter_context(self.nc.psum_tensor("PSALL", [128, 8 * 512], F32))
        self.bank_bufs = [Buf(f"bank{i}", excl=True) for i in range(8)]

    def set_banks(self, pool):
        self.bank_pool = list(pool)
        self.bank_i = 0

    def bank(self, n=1):
        L = len(self.bank_pool)
        while True:
            i = self.bank_i % L
            idx = self.bank_pool[i:i + n]
            if len(idx) == n and all(idx[k] == idx[0] + k for k in range(n)):
                break
            self.bank_i += 1
        self.bank_i += n
        b0 = idx[0]
        return self.PS[:, b0 * 512:(b0 + n) * 512], [self.bank_bufs[b] for b in idx]

    def fixed_bank(self, b0, n=1):
        return self.PS[:, b0 * 512:(b0 + n) * 512], [self.bank_bufs[b] for b in range(b0, b0 + n)]

    def _waits(self, E, reads, writes):
        need = {}

        def add(d):
            for k, ev in d.items():
                if k not in need or need[k][1] < ev[1]:
                    need[k] = ev
        own = id(E.sem) if E.sem is not None else None
        for b in reads:
            add(b.w)
            if b.excl:
                add({k: ev for k, ev in b.r.items() if k != own})
        for b in writes:
            if not b.multi:
                add(b.w)
            add(b.r)
        for k, (sem, val) in need.items():
            if (not E.own_wait) and E.sem is not None and k == id(E.sem):
                continue
            if E.seen.get(k, 0) >= val:
                continue
            E.eng.wait_ge(sem, val)
            E.seen[k] = val

    def _record(self, ev, reads, writes):
        k = id(ev[0])
        for b in reads:
            if k not in b.r or b.r[k][1] < ev[1]:
                b.r[k] = ev
        for b in writes:
            if b.multi:
                if k not in b.w or b.w[k][1] < ev[1]:
                    b.w[k] = ev
            else:
                b.w = {k: ev}
            b.r = {}

    def op(self, E, fn, reads=(), writes=()):
        self.nops += 1
        if self.nops > self.limit:
            return None
        self._waits(E, reads, writes)
        inst = fn()
        ev = E.next_event()
        inst.then_inc(ev[0], 1)
        self._record(ev, reads, writes)
        E.n += 1
        return inst

    def dma(self, Q, out, in_, anchor, reads=(), writes=(), **kw):
        self.nops += 1
        if self.nops > self.limit:
            return None
        self._waits(Q, reads, writes)
        if anchor.dsem is None:
            if self.free_sems:
                anchor.dsem, anchor.dcnt = self.free_sems.pop()
            else:
                anchor.dsem, anchor.dcnt = self.new_sem("d_" + anchor.name), 0
            self.anchors.append(anchor)
        inst = Q.eng.dma_start(out=out, in_=in_, **kw)
        anchor.dcnt += 16
        ev = (anchor.dsem, anchor.dcnt)
        inst.then_inc(anchor.dsem, 16)
        self._record(ev, reads, writes)
        Q.n += 1
        return inst

    def barrier(self):
        evs = {}
        for E in self.engs:
            if E.sem is not None:
                evs[id(E.sem)] = (E.sem, E.val)
        for a in self.anchors:
            evs[id(a.dsem)] = (a.dsem, a.dcnt)
        S = self.sp
        for k, (sem, val) in evs.items():
            if S.seen.get(k, 0) >= val:
                continue
            S.eng.wait_ge(sem, val)
            S.seen[k] = val
        inst = S.eng.nop()
        ev = S.next_event()
        inst.then_inc(ev[0], 1)
        for E in self.engs:
            if E is S:
                continue
            E.eng.wait_ge(ev[0], ev[1])
            E.seen[id(ev[0])] = ev[1]
            for k, (sem, val) in evs.items():
                if E.seen.get(k, 0) < val:
                    E.seen[k] = val
        S.seen[id(ev[0])] = ev[1]


    def begin_phase(self):
        self.phase_mark = len(self.anchors)

    def end_phase(self):
        self.barrier()
        dead = self.anchors[self.phase_mark:]
        self.anchors = self.anchors[:self.phase_mark]
        for a in dead:
            if a.dcnt <= 20000:
                self.free_sems.append((a.dsem, a.dcnt))
            a.dsem = None
        for b in self.persist:
            b.w = {}
            b.r = {}


W_NAMES = ["norm_mix_w", "w_in", "ret_decay_fwd", "ret_decay_bwd", "ret_gn_w", "mla_q_norm_w", "mla_w_uq",
           "mla_kv_norm_w", "mla_w_ukv", "w_out", "norm_ca_w", "norm_mem_w", "ca_wq", "ca_wkv", "ca_wo",
           "norm_ffn_w", "peer_wq", "peer_sub_keys", "peer_u", "peer_v", "final_norm_w"]
W_SHAPES = {
    "norm_mix_w": [1, 1024], "w_in": [1, 1024, 2720], "ret_decay_fwd": [1, 8], "ret_decay_bwd": [1, 8],
    "ret_gn_w": [1, 512], "mla_q_norm_w": [1, 384], "mla_w_uq": [1, 384, 768], "mla_kv_norm_w": [1, 256],
    "mla_w_ukv": [1, 256, 1024], "w_out": [1, 1024, 1024], "norm_ca_w": [1, 1024], "norm_mem_w": [1, 1024],
    "ca_wq": [1, 1024, 1024], "ca_wkv": [1, 1024, 2048], "ca_wo": [1, 1024, 1024], "norm_ffn_w": [1, 1024],
    "peer_wq": [1, 1024, 2048], "peer_sub_keys": [1, 2, 8, 128, 128], "peer_u": [1, 16384, 1024],
    "peer_v": [1, 16384, 1024], "final_norm_w": [1024],
}


def host_consts(smax):
    pos = np.arange(smax, dtype=np.float32)
    c = {}
    invf = (1.0 / (np.float32(10000.0) ** (np.arange(0, 64, 2, dtype=np.float32) / np.float32(64)))).astype(np.float32)
    ang = (pos[:, None] * invf[None, :]).astype(np.float32)
    c["c_cosr"] = np.cos(ang).astype(np.float32)
    c["c_sinr"] = np.sin(ang).astype(np.float32)
    invm = (1.0 / (np.float32(10000.0) ** (np.arange(0, 32, 2, dtype=np.float32) / np.float32(32)))).astype(np.float32)
    angm = (pos[:, None] * invm[None, :]).astype(np.float32)
    cm = np.cos(angm).astype(np.float32).T
    sm = np.sin(angm).astype(np.float32).T
    ck = np.concatenate([cm, cm], 0)
    sk = np.concatenate([sm, sm], 0)
    sc = np.float32(96.0 ** -0.5)
    c["c_cq"] = np.concatenate([np.full((64, smax), sc, np.float32), ck * sc], 0).astype(np.float32)
    c["c_sq"] = np.concatenate([np.zeros((64, smax), np.float32), sk * sc], 0).astype(np.float32)
    c["c_ck"] = np.ascontiguousarray(ck)
    c["c_sk"] = np.ascontiguousarray(sk)
    io = np.arange(128, dtype=np.float32)
    c["c_iota"] = np.tile(io[None, :], (128, 1)).astype(np.float32)
    c["c_diff"] = (io[None, :] - io[:, None]).astype(np.float32)
    c["c_pcol"] = np.stack([127.0 - io, io], 1).astype(np.float32)
    c["c_ident"] = np.eye(128, dtype=np.float32)
    return c


CONST_SHAPES = lambda smax: {"c_cosr": [smax, 32], "c_sinr": [smax, 32], "c_cq": [96, smax], "c_sq": [96, smax],
                             "c_ck": [32, smax], "c_sk": [32, smax], "c_iota": [128, 128], "c_diff": [128, 128],
                             "c_pcol": [128, 2], "c_ident": [128, 128]}


def build(S_list, n_exp_chunks=128, debug=False, upto=99, prepass=True):
    nc = bass.Bass("TRN2", target_bir_lowering=False)
    smax = max(S_list)
    NE = n_exp_chunks
    A = {}
    for i, S in enumerate(S_list):
        A[f"x{i}"] = nc.dram_tensor(f"x{i}", [S, D], F32, kind="ExternalInput").ap()
        A[f"mem{i}"] = nc.dram_tensor(f"mem{i}", [256, D], F32, kind="ExternalInput").ap()
        A[f"y{i}"] = nc.dram_tensor(f"y{i}", [S, D], F32, kind="ExternalOutput").ap()
    for n in W_NAMES:
        A[n] = nc.dram_tensor(n, W_SHAPES[n], F32, kind="ExternalInput").ap()
    for n, shp in CONST_SHAPES(smax).items():
        A[n] = nc.dram_tensor(n, shp, F32, kind="ExternalInput").ap()
    skind = "ExternalOutput" if debug else "Internal"
    nt_max = smax // 128

    def scr(name, shape, dt):
        return nc.dram_tensor("z_" + name, shape, dt, kind=skind).ap()
    Z = dict(
        SG=scr("SG", [smax, 512], F32), YP=scr("YP", [smax, 512], F32),
        QB=scr("QB", [nt_max, 128, 512], BF16), BB=scr("BB", [nt_max, 128, 256], F32),
        QT=scr("QT", [8, 97, smax], BF16), KT=scr("KT", [8, 97, smax], BF16),
        VS=scr("VS", [smax, 8 * 65], BF16), KN=scr("KN", [8, smax], F32),
        MO=scr("MO", [8, 64, smax], BF16), X2=scr("X2", [smax, D], F32),
        UT=scr("UT", [128, 128, 1024], BF16), VB=scr("VB", [128, 128, 1024], BF16),
    )
    ZB = {k: Buf("z" + k, multi=True) for k in Z}

    with ExitStack() as top:
        kb = KB(nc, top)
        kb.init_psum()
        pe, dve, act, pool, sp = kb.pe, kb.dve, kb.act, kb.pool, kb.sp
        V = nc.vector
        ACT = nc.scalar
        G = nc.gpsimd
        PE = nc.tensor

        ident, b_ident = kb.sbb([128, 128], F32, "ident")
        iota, b_iota = kb.sbb([128, 128], F32, "iota")
        onesf, b_onesf = kb.sbb([128, 128], F32, "onesf")
        epsn, b_epsn = kb.sbb([128, 1], F32, "epsn")
        epsg, b_epsg = kb.sbb([128, 1], F32, "epsg")
        kb.dma(sp, ident[:], A["c_ident"][:, :], b_ident, writes=[b_ident])
        kb.dma(sp, iota[:], A["c_iota"][:, :], b_iota, writes=[b_iota])
        kb.op(dve, lambda: V.memset(onesf[:], 1.0), writes=[b_onesf])
        kb.op(dve, lambda: V.memset(epsn[:], 1e-6), writes=[b_epsn])
        kb.op(dve, lambda: V.memset(epsg[:], 1e-5), writes=[b_epsg])
        kb.persist = list(ZB.values()) + [b_ident, b_iota, b_onesf, b_epsn, b_epsg]
        kb.barrier()

        def load_cast(dst, dst_buf, src, stack=None):
            kb.dma(pool, dst, src, dst_buf, writes=[dst_buf])

        def norm_transpose(xt, bx, wcol, b_wcol, outT, b_outT, col0, work, bankfn=None):
            junk, b_junk = work["junk"].next()
            ss, b_ss = work["ss"].next()
            xs, b_xs = work["xs"].next()
            kb.op(act, lambda: ACT.activation(out=junk[:], in_=xt, func=AF.Square, accum_out=ss[:, 0:1]),
                  reads=[bx], writes=[b_junk, b_ss])
            kb.op(act, lambda: ACT.activation(out=ss[:, 1:2], in_=ss[:, 0:1], func=AF.Sqrt, scale=1.0 / D, bias=epsn[:, 0:1]),
                  reads=[b_ss, b_epsn], writes=[b_ss])
            kb.op(dve, lambda: V.reciprocal(out=ss[:, 2:3], in_=ss[:, 1:2]), reads=[b_ss], writes=[b_ss])
            kb.op(pool, lambda: G.tensor_scalar(out=xs[:], in0=xt, scalar1=ss[:, 2:3], scalar2=None, op0=ALU.mult),
                  reads=[bx, b_ss], writes=[b_xs])
            pT, bpT = (bankfn or kb.bank)(2)
            for kc in range(8):
                kb.op(pe, lambda kc=kc: PE.transpose(pT[:, kc * 128:(kc + 1) * 128], xs[:, kc * 128:(kc + 1) * 128], ident[:]),
                      reads=[b_xs, b_ident], writes=[bpT[kc // 4]])
            pTv = pT.rearrange("p (k t) -> p k t", k=8)
            kb.op(dve, lambda: V.tensor_tensor(out=outT[:, :, col0:col0 + 128], in0=pTv,
                                               in1=wcol[:, :].unsqueeze(2).to_broadcast([128, 8, 128]), op=ALU.mult),
                  reads=bpT + [b_wcol], writes=[b_outT])
            return ss, b_ss

        def load_cols(name, ncols, stack):
            t, b = kb.sbb([128, ncols], F32, name, stack)
            src = A[name].rearrange("o (c p) -> p (o c)", p=128) if len(W_SHAPES[name]) == 2 else A[name].rearrange("(c p) -> p c", p=128)
            with nc.allow_non_contiguous_dma(reason="tiny per-feature vector"):
                kb.dma(sp, t[:], src, b, writes=[b])
            return t, b

        def prepass_peer():
            with ExitStack() as st:
                us = kb.slots([128, 1024], F32, 2, "pp_u", st)
                uts = kb.slots([128, 1024], BF16, 2, "pp_ut", st)
                vs = kb.slots([128, 1024], BF16, 2, "pp_v", st)
                for i in range(NE):
                    ut, bu = us.next()
                    kb.dma(sp, ut[:], A["peer_u"][0, i * 128:(i + 1) * 128, :], bu, writes=[bu])
                    pT, bpT = kb.bank(2)
                    for kc in range(8):
                        kb.op(pe, lambda kc=kc: PE.transpose(pT[:, kc * 128:(kc + 1) * 128], ut[:, kc * 128:(kc + 1) * 128], ident[:]),
                              reads=[bu, b_ident], writes=[bpT[kc // 4]])
                    utt, butt = uts.next()
                    kb.op(act, lambda: ACT.copy(out=utt[:, 0:512], in_=pT[:, 0:512]), reads=[bpT[0]], writes=[butt])
                    kb.op(dve, lambda: V.tensor_copy(out=utt[:, 512:1024], in_=pT[:, 512:1024]), reads=[bpT[1]], writes=[butt])
                    kb.dma(sp, Z["UT"][i], utt[:], butt, reads=[butt], writes=[ZB["UT"]])
                    vt, bv = vs.next()
                    load_cast(vt[:], bv, A["peer_v"][0, i * 128:(i + 1) * 128, :])
                    kb.dma(sp, Z["VB"][i], vt[:], bv, reads=[bv], writes=[ZB["VB"]])
                kb.barrier()

        def phase1(si, S):
            nt = S // 128
            X = A[f"x{si}"]
            with ExitStack() as st:
                Win, b_Win = kb.sbb([128, 8, 2720], BF16, "Win", st)
                for kc in range(8):
                    for (c0, c1) in ((0, 1360), (1360, 2720)):
                        load_cast(Win[:, kc, c0:c1], b_Win, A["w_in"][0, kc * 128:(kc + 1) * 128, c0:c1])
                Wuq, b_Wuq = kb.sbb([128, 3, 768], BF16, "Wuq", st)
                load_cast(Wuq[:], b_Wuq, A["mla_w_uq"][0].rearrange("(r p) n -> p r n", p=128))
                Wuqr, b_Wuqr = kb.sbb([128, 3, 768], BF16, "Wuqr", st)
                Wukv, b_Wukv = kb.sbb([128, 2, 2, 8, 64], BF16, "Wukv", st)
                for c in range(2):
                    for t in range(2):
                        load_cast(Wukv[:, c, t, :, :], b_Wukv,
                                  A["mla_w_ukv"][0, c * 128:(c + 1) * 128, :].rearrange("p (h t d) -> p t h d", h=8, t=2)[:, t, :, :])
                Wkrr, b_Wkrr = kb.sbb([128, 8, 32], BF16, "Wkrr", st)
                kb.op(dve, lambda: V.memset(Wuqr[:], 0.0), writes=[b_Wuqr])
                Wuq4 = Wuq[:].rearrange("p r (h f) -> p r h f", h=8)
                Wuqr4 = Wuqr[:].rearrange("p r (h f) -> p r h f", h=8)
                for r in range(3):
                    kb.op(dve, lambda r=r: V.tensor_scalar(out=Wuqr4[:, r, :, 64:80], in0=Wuq4[:, r, :, 80:96], scalar1=-1.0, scalar2=None, op0=ALU.mult),
                          reads=[b_Wuq], writes=[b_Wuqr])
                    kb.op(dve, lambda r=r: V.tensor_copy(out=Wuqr4[:, r, :, 80:96], in_=Wuq4[:, r, :, 64:80]),
                          reads=[b_Wuq], writes=[b_Wuqr])
                kb.op(dve, lambda: V.tensor_scalar(out=Wkrr[:, :, 0:16], in0=Win[:, :, 2704:2720], scalar1=-1.0, scalar2=None, op0=ALU.mult),
                      reads=[b_Win], writes=[b_Wkrr])
                kb.op(dve, lambda: V.tensor_copy(out=Wkrr[:, :, 16:32], in_=Win[:, :, 2688:2704]), reads=[b_Win], writes=[b_Wkrr])
                nmw, b_nmw = load_cols("norm_mix_w", 8, st)
                nw5, b_nw5 = kb.sbb([128, 5], F32, "nw5", st)
                with nc.allow_non_contiguous_dma(reason="tiny per-feature vector"):
                    kb.dma(sp, nw5[:, 0:3], A["mla_q_norm_w"].rearrange("o (c p) -> p (o c)", p=128), b_nw5, writes=[b_nw5])
                    kb.dma(sp, nw5[:, 3:5], A["mla_kv_norm_w"].rearrange("o (c p) -> p (o c)", p=128), b_nw5, writes=[b_nw5])
                cosrs = kb.slots([128, 32], F32, 2, "cosr", st)
                sinrs = kb.slots([128, 32], F32, 2, "sinr", st)
                dec, b_dec = kb.sbb([128, 16], F32, "dec", st)
                kb.dma(sp, dec[:, 0:8], A["ret_decay_fwd"][0, :].partition_broadcast(128), b_dec, writes=[b_dec])
                kb.dma(sp, dec[:, 8:16], A["ret_decay_bwd"][0, :].partition_broadcast(128), b_dec, writes=[b_dec])
                lg, b_lg = kb.sbb([128, 16], F32, "lg", st)
                kb.op(act, lambda: ACT.activation(out=lg[:], in_=dec[:], func=AF.Exp, scale=-1.0), reads=[b_dec], writes=[b_lg])
                kb.op(act, lambda: ACT.activation(out=lg[:], in_=lg[:], func=AF.Ln, bias=1.0), reads=[b_lg], writes=[b_lg])
                kb.op(dve, lambda: V.tensor_scalar(out=lg[:], in0=lg[:], scalar1=-1.0, scalar2=None, op0=ALU.mult), reads=[b_lg], writes=[b_lg])
                lgp, b_lgp = kb.sbb([128, 4, 2], F32, "lgp", st)
                lgv = lg[:].rearrange("p (d a two) -> p a d two", d=2, two=2)
                kb.op(dve, lambda: V.tensor_copy(out=lgp[0:64, :, :], in_=lgv[0:64, :, :, 0]), reads=[b_lg], writes=[b_lgp])
                kb.op(dve, lambda: V.tensor_copy(out=lgp[64:128, :, :], in_=lgv[64:128, :, :, 1]), reads=[b_lg], writes=[b_lgp])
                diff, b_diff = kb.sbb([128, 128], F32, "diff", st)
                kb.dma(sp, diff[:], A["c_diff"][:, :], b_diff, writes=[b_diff])
                pcol, b_pcol = kb.sbb([128, 2], F32, "pcol", st)
                kb.dma(sp, pcol[:], A["c_pcol"][:, :], b_pcol, writes=[b_pcol])
                dpos, b_dpos = kb.sbb([128, 128], F32, "dpos", st)
                dneg, b_dneg = kb.sbb([128, 128], F32, "dneg", st)
                kb.op(dve, lambda: V.tensor_scalar(out=dpos[:], in0=diff[:], scalar1=0.0, scalar2=None, op0=ALU.max), reads=[b_diff], writes=[b_dpos])
                kb.op(dve, lambda: V.tensor_scalar(out=dneg[:], in0=diff[:], scalar1=-1.0, scalar2=0.0, op0=ALU.mult, op1=ALU.max), reads=[b_diff], writes=[b_dneg])
                DT, b_DT = kb.sbb([128, 8, 128], F32, "DT", st)
                tmpE, b_tmpE = kb.sbb([128, 128], F32, "tmpE", st)
                for h in range(8):
                    kb.op(act, lambda h=h: ACT.activation(out=DT[:, h, :], in_=dpos[:], func=AF.Exp, scale=lg[:, h:h + 1]),
                          reads=[b_dpos, b_lg], writes=[b_DT])
                    kb.op(pool, lambda h=h: G.affine_select(out=DT[:, h, :], in_=DT[:, h, :], pattern=[[1, 128]], compare_op=ALU.is_ge,
                                                            fill=0.0, base=0, channel_multiplier=-1), reads=[b_DT], writes=[b_DT])
                    kb.op(act, lambda h=h: ACT.activation(out=tmpE[:], in_=dneg[:], func=AF.Exp, scale=lg[:, 8 + h:9 + h]),
                          reads=[b_dneg, b_lg], writes=[b_tmpE])
                    kb.op(pool, lambda: G.affine_select(out=tmpE[:], in_=tmpE[:], pattern=[[-1, 128]], compare_op=ALU.is_gt,
                                                        fill=0.0, base=0, channel_multiplier=1), reads=[b_tmpE], writes=[b_tmpE])
                    kb.op(dve, lambda h=h: V.tensor_tensor(out=DT[:, h, :], in0=DT[:, h, :], in1=tmpE[:], op=ALU.add),
                          reads=[b_DT, b_tmpE], writes=[b_DT])
                ip1, b_ip1 = kb.sbb([128, 128], F32, "ip1", st)
                cmi, b_cmi = kb.sbb([128, 128], F32, "cmi", st)
                kb.op(dve, lambda: V.tensor_scalar(out=ip1[:], in0=iota[:], scalar1=1.0, scalar2=None, op0=ALU.add), reads=[b_iota], writes=[b_ip1])
                kb.op(dve, lambda: V.tensor_scalar(out=cmi[:], in0=iota[:], scalar1=-1.0, scalar2=128.0, op0=ALU.mult, op1=ALU.add), reads=[b_iota], writes=[b_cmi])
                QF, b_QF = kb.sbb([128, 4, 128], F32, "QF", st)
                QBt, b_QBt = kb.sbb([128, 4, 128], F32, "QBt", st)
                for a in range(4):
                    kb.op(act, lambda a=a: ACT.activation(out=QF[:, a, :], in_=ip1[:], func=AF.Exp, scale=lgp[:, a, 0:1]), reads=[b_ip1, b_lgp], writes=[b_QF])
                    kb.op(act, lambda a=a: ACT.activation(out=QBt[:, a, :], in_=cmi[:], func=AF.Exp, scale=lgp[:, a, 1:2]), reads=[b_cmi, b_lgp], writes=[b_QBt])
                tk, b_tk = kb.sbb([128, 16], F32, "tk", st)
                kb.op(act, lambda: ACT.activation(out=tk[:, 0:8], in_=lg[:, 0:8], func=AF.Exp, scale=pcol[:, 0:1]), reads=[b_lg, b_pcol], writes=[b_tk])
                kb.op(act, lambda: ACT.activation(out=tk[:, 8:16], in_=lg[:, 8:16], func=AF.Exp, scale=pcol[:, 1:2]), reads=[b_lg, b_pcol], writes=[b_tk])
                GC, b_GC = kb.sbb([128, 4, 2], F32, "GC", st)
                kb.op(act, lambda: ACT.activation(out=GC[:], in_=lgp[:], func=AF.Exp, scale=128.0), reads=[b_lgp], writes=[b_GC])
                Rf, b_Rf = kb.sbb([128, 4, 64], F32, "Rf", st)
                Rfb, b_Rfb = kb.sbb([128, 4, 128], BF16, "Rfb", st)
                kb.op(dve, lambda: V.memset(Rf[:], 0.0), writes=[b_Rf])
                kb.op(dve, lambda: V.memset(Rfb[:], 0.0), writes=[b_Rfb])
                work = dict(junk=kb.slots([128, 1024], BF16, 1, "junk", st), ss=kb.slots([128, 4], F32, 2, "ss", st),
                            xs=kb.slots([128, 1024], F32, 1, "xs", st))
                xts = kb.slots([128, 1024], F32, 2, "xt", st)
                xnTs = kb.slots([128, 8, 128], BF16, 2, "xnT", st)
                qks = kb.slots([128, 16, 64], F32, 2, "qk", st)
                qkrs = kb.slots([128, 16, 64], F32, 2, "qkr", st)
                tmps = kb.slots([128, 16, 32], F32, 4, "ropetmp", st)
                vbs = kb.slots([128, 512], BF16, 2, "vb", st)
                sgs = kb.slots([128, 512], F32, 2, "sg", st)
                qTs = kb.slots([128, 8, 128], BF16, 2, "qTz", st)
                for (q_, bq_) in qTs.items:
                    kb.op(dve, lambda q_=q_: V.memset(q_[:], 0.0), writes=[bq_])
                kTs = kb.slots([128, 4, 128], BF16, 2, "kT", st)
                qfTs = kb.slots([128, 4, 128], BF16, 2, "qfT", st)
                qbTs = kb.slots([128, 4, 128], BF16, 2, "qbT", st)
                kdfs = kb.slots([128, 8, 64], BF16, 2, "kdf", st)
                kdbs = kb.slots([128, 8, 64], BF16, 2, "kdb", st)
                sDs = kb.slots([128, 8, 128], BF16, 2, "sD", st)
                yps = kb.slots([128, 512], F32, 2, "yp", st)
                bbs = kb.slots([128, 4, 64], F32, 2, "bb", st)
                sqs = kb.slots([128, 5, 128], F32, 2, "sq", st)
                rsbs = kb.slots([128, 2, 128], F32, 2, "rsb", st)
                cns = kb.slots([128, 5, 128], BF16, 2, "cn", st)
                cqts = kb.slots([96, 128], F32, 2, "cqt", st)
                sqts = kb.slots([96, 128], F32, 2, "sqt", st)
                ckts = kb.slots([32, 128], F32, 2, "ckt", st)
                skts = kb.slots([32, 128], F32, 2, "skt", st)
                t1s = kb.slots([96, 8, 128], F32, 1, "t1", st)
                t2s = kb.slots([96, 8, 128], F32, 1, "t2", st)
                QTts = kb.slots([96, 8, 128], BF16, 2, "QTt", st)
                KNts = kb.slots([64, 8, 128], BF16, 2, "KNt", st)
                KRts = kb.slots([32, 128], BF16, 2, "KRt", st)
                kr1s = kb.slots([32, 128], F32, 2, "kr1", st)
                kr2s = kb.slots([32, 128], F32, 2, "kr2", st)
                Vts = kb.slots([128, 8, 65], BF16, 2, "Vt", st)
                for (vt_, bv_) in Vts.items:
                    kb.op(dve, lambda vt_=vt_: V.memset(vt_[:, :, 64:65], 1.0), writes=[bv_])
                sq2s = kb.slots([96, 1024], F32, 1, "sq2", st)
                sq3s = kb.slots([32, 128], F32, 1, "sq3", st)
                nrows = kb.slots([1, 2, 1024], F32, 1, "nrow", st)
                nbrs = kb.slots([1, 1024], BF16, 2, "nbr", st)
                one8, b_one8 = kb.sbb([1, 1024], BF16, "one8", st)
                kb.op(dve, lambda: V.memset(one8[:], 1.0), writes=[b_one8])
                krn = kb.slots([1, 128], F32, 2, "krn", st)

                for t in range(nt):
                    r0 = t * 128
                    xt, bx = xts.next()
                    kb.dma(sp, xt[:], X[r0:r0 + 128, :], bx, writes=[bx])
                    xnT, bxnT = xnTs.next()
                    norm_transpose(xt[:], bx, nmw, b_nmw, xnT, bxnT, 0, work)
                    pq, bq = kb.bank(1)
                    pk, bk = kb.bank(1)
                    pv, bv = kb.bank(1)
                    pg, bg = kb.bank(1)
                    for n, (pp, bpp) in enumerate(((pq, bq), (pk, bk), (pv, bv), (pg, bg))):
                        for kc in range(8):
                            kb.op(pe, lambda pp=pp, n=n, kc=kc: PE.matmul(pp, lhsT=xnT[:, kc, :], rhs=Win[:, kc, n * 512:(n + 1) * 512],
                                                                        start=(kc == 0), stop=(kc == 7)),
                                  reads=[bxnT, b_Win], writes=bpp)
                    qk, bqk = qks.next()
                    qkf = qk[:].rearrange("p a d -> p (a d)")
                    kb.op(act, lambda: ACT.copy(out=qkf[:, 0:512], in_=pq), reads=bq, writes=[bqk])
                    kb.op(act, lambda: ACT.activation(out=qkf[:, 512:1024], in_=pk, func=AF.Copy, scale=0.125), reads=bk, writes=[bqk])
                    vb, bvb = vbs.next()
                    kb.op(dve, lambda: V.tensor_copy(out=vb[:], in_=pv), reads=bv, writes=[bvb])
                    sg, bsg = sgs.next()
                    kb.op(act, lambda: ACT.activation(out=sg[:], in_=pg, func=AF.Silu), reads=bg, writes=[bsg])
                    kb.dma(sp, Z["SG"][r0:r0 + 128, :], sg[:], bsg, reads=[bsg], writes=[ZB["SG"]])
                    qkr, bqkr = qkrs.next()
                    cosr, b_cosr = cosrs.next()
                    sinr, b_sinr = sinrs.next()
                    kb.dma(sp, cosr[:], A["c_cosr"][r0:r0 + 128, :], b_cosr, writes=[b_cosr])
                    kb.dma(sp, sinr[:], A["c_sinr"][r0:r0 + 128, :], b_sinr, writes=[b_sinr])
                    cb = cosr[:, :].unsqueeze(1).to_broadcast([128, 16, 32])
                    sbc = sinr[:, :].unsqueeze(1).to_broadcast([128, 16, 32])
                    ta, bta = tmps.next()
                    tb, btb = tmps.next()
                    tc_, btc = tmps.next()
                    td, btd = tmps.next()
                    kb.op(dve, lambda: V.tensor_tensor(out=ta[:], in0=qk[:, :, 0:32], in1=cb, op=ALU.mult), reads=[bqk, b_cosr], writes=[bta])
                    kb.op(dve, lambda: V.tensor_tensor(out=tb[:], in0=qk[:, :, 32:64], in1=sbc, op=ALU.mult), reads=[bqk, b_sinr], writes=[btb])
                    kb.op(dve, lambda: V.tensor_tensor(out=qkr[:, :, 0:32], in0=ta[:], in1=tb[:], op=ALU.subtract), reads=[bta, btb], writes=[bqkr])
                    kb.op(pool, lambda: G.tensor_tensor(out=tc_[:], in0=qk[:, :, 0:32], in1=sbc, op=ALU.mult), reads=[bqk, b_sinr], writes=[btc])
                    kb.op(pool, lambda: G.tensor_tensor(out=td[:], in0=qk[:, :, 32:64], in1=cb, op=ALU.mult), reads=[bqk, b_cosr], writes=[btd])
                    kb.op(pool, lambda: G.tensor_tensor(out=qkr[:, :, 32:64], in0=tc_[:], in1=td[:], op=ALU.add), reads=[btc, btd], writes=[bqkr])
                    qkrf = qkr[:].rearrange("p a d -> p (a d)")
                    pT, bpT = kb.bank(2)
                    for j in range(8):
                        kb.op(pe, lambda j=j: PE.transpose(pT[:, j * 128:(j + 1) * 128], qkrf[:, j * 128:(j + 1) * 128], ident[:]),
                              reads=[bqkr, b_ident], writes=[bpT[j // 4]])
                    pTv = pT.rearrange("p (j t) -> p j t", j=8)
                    qT, bqT = qTs.next()
                    kT, bkT = kTs.next()
                    qfT, bqfT = qfTs.next()
                    qbT, bqbT = qbTs.next()
                    kb.op(act, lambda: ACT.copy(out=qT[0:64, 0:8:2, :], in_=pTv[0:64, 0:4, :]), reads=[bpT[0]], writes=[bqT])
                    kb.op(act, lambda: ACT.copy(out=qT[64:128, 1:8:2, :], in_=pTv[64:128, 0:4, :]), reads=[bpT[0]], writes=[bqT])
                    kb.op(act, lambda: ACT.copy(out=kT[:], in_=pTv[:, 4:8, :]), reads=[bpT[1]], writes=[bkT])
                    kb.op(dve, lambda: V.tensor_tensor(out=qfT[:], in0=pTv[:, 0:4, :], in1=QF[:], op=ALU.mult), reads=[bpT[0], b_QF], writes=[bqfT])
                    kb.op(dve, lambda: V.tensor_tensor(out=qbT[:], in0=pTv[:, 0:4, :], in1=QBt[:], op=ALU.mult), reads=[bpT[0], b_QBt], writes=[bqbT])
                    kb.dma(sp, Z["QB"][t], qbT[:].rearrange("p a t -> p (a t)"), bqbT, reads=[bqbT], writes=[ZB["QB"]])
                    kdf, bkdf = kdfs.next()
                    kdb, bkdb = kdbs.next()
                    kb.op(dve, lambda: V.tensor_tensor(out=kdf[:], in0=qkr[:, 8:16, :], in1=tk[:, 0:8].unsqueeze(2).to_broadcast([128, 8, 64]), op=ALU.mult),
                          reads=[bqkr, b_tk], writes=[bkdf])
                    kb.op(dve, lambda: V.tensor_tensor(out=kdb[:], in0=qkr[:, 8:16, :], in1=tk[:, 8:16].unsqueeze(2).to_broadcast([128, 8, 64]), op=ALU.mult),
                          reads=[bqkr, b_tk], writes=[bkdb])
                    pS, bpS = kb.bank(2)
                    for h in range(8):
                        a, off = h // 2, (h % 2) * 64
                        kb.op(pe, lambda h=h, a=a, off=off: PE.matmul(pS[:, h * 128:(h + 1) * 128], lhsT=kT[:, a, :], rhs=qT[:, h, :],
                                                                      start=True, stop=True),
                              reads=[bkT, bqT], writes=[bpS[h // 4]])
                    sD, bsD = sDs.next()
                    pSv = pS.rearrange("p (h t) -> p h t", h=8)
                    kb.op(dve, lambda: V.tensor_tensor(out=sD[:, 0:4, :], in0=pSv[:, 0:4, :], in1=DT[:, 0:4, :], op=ALU.mult), reads=[bpS[0], b_DT], writes=[bsD])
                    kb.op(dve, lambda: V.tensor_tensor(out=sD[:, 4:8, :], in0=pSv[:, 4:8, :], in1=DT[:, 4:8, :], op=ALU.mult), reads=[bpS[1], b_DT], writes=[bsD])
                    pY, bpY = kb.bank(1)
                    for a in range(4):
                        kb.op(pe, lambda a=a: PE.matmul(pY[:, a * 128:(a + 1) * 128], lhsT=qfT[:, a, :], rhs=Rfb[:, a, :], start=True, stop=False),
                              reads=[bqfT, b_Rfb], writes=bpY)
                        for h in (2 * a, 2 * a + 1):
                            kb.op(pe, lambda h=h: PE.matmul(pY[:, h * 64:(h + 1) * 64], lhsT=sD[:, h, :], rhs=vb[:, h * 64:(h + 1) * 64], start=False, stop=(h % 2 == 1)),
                                  reads=[bsD, bvb], writes=bpY)
                    yp, byp = yps.next()
                    kb.op(act, lambda: ACT.copy(out=yp[:], in_=pY), reads=bpY, writes=[byp])
                    kb.dma(sp, Z["YP"][r0:r0 + 128, :], yp[:], byp, reads=[byp], writes=[ZB["YP"]])
                    pBf, bpBf = kb.bank(1)
                    pBb, bpBb = kb.bank(1)
                    kdf2 = kdf[:].rearrange("p h d -> p (h d)")
                    kdb2 = kdb[:].rearrange("p h d -> p (h d)")
                    for a in range(4):
                        kb.op(pe, lambda a=a: PE.matmul(pBf[:, a * 128:(a + 1) * 128], lhsT=kdf2[:, a * 128:(a + 1) * 128], rhs=vb[:, a * 128:(a + 1) * 128],
                                                        start=True, stop=True), reads=[bkdf, bvb], writes=bpBf)
                        kb.op(pe, lambda a=a: PE.matmul(pBb[:, a * 128:(a + 1) * 128], lhsT=kdb2[:, a * 128:(a + 1) * 128], rhs=vb[:, a * 128:(a + 1) * 128],
                                                        start=True, stop=True), reads=[bkdb, bvb], writes=bpBb)
                    pBfv = pBf.rearrange("p (a x) -> p a x", a=4)
                    pBbv = pBb.rearrange("p (a x) -> p a x", a=4)
                    kb.op(dve, lambda: V.tensor_tensor(out=Rf[:], in0=Rf[:], in1=GC[:, :, 0:1].to_broadcast([128, 4, 64]), op=ALU.mult),
                          reads=[b_Rf, b_GC], writes=[b_Rf])
                    kb.op(dve, lambda: V.tensor_tensor(out=Rf[0:64], in0=Rf[0:64], in1=pBfv[0:64, :, 0:64], op=ALU.add), reads=[b_Rf] + bpBf, writes=[b_Rf])
                    kb.op(dve, lambda: V.tensor_tensor(out=Rf[64:128], in0=Rf[64:128], in1=pBfv[64:128, :, 64:128], op=ALU.add), reads=[b_Rf] + bpBf, writes=[b_Rf])
                    kb.op(dve, lambda: V.tensor_copy(out=Rfb[0:64, :, 0:64], in_=Rf[0:64]), reads=[b_Rf], writes=[b_Rfb])
                    kb.op(dve, lambda: V.tensor_copy(out=Rfb[64:128, :, 64:128], in_=Rf[64:128]), reads=[b_Rf], writes=[b_Rfb])
                    bb, bbb = bbs.next()
                    kb.op(act, lambda: ACT.copy(out=bb[0:64], in_=pBbv[0:64, :, 0:64]), reads=bpBb, writes=[bbb])
                    kb.op(act, lambda: ACT.copy(out=bb[64:128], in_=pBbv[64:128, :, 64:128]), reads=bpBb, writes=[bbb])
                    kb.dma(sp, Z["BB"][t], bb[:].rearrange("p a v -> p (a v)"), bbb, reads=[bbb], writes=[ZB["BB"]])

                    pC, bpC = kb.bank(2)
                    for r in range(5):
                        for kc in range(8):
                            kb.op(pe, lambda r=r, kc=kc: PE.matmul(pC[:, r * 128:(r + 1) * 128], lhsT=Win[:, kc, 2048 + r * 128:2048 + (r + 1) * 128],
                                                                   rhs=xnT[:, kc, :], start=(kc == 0), stop=(kc == 7)),
                                  reads=[b_Win, bxnT], writes=[bpC[r // 4]])
                    sq, bsq = sqs.next()
                    kb.op(act, lambda: ACT.activation(out=sq[:, 0:4, :].rearrange("p r t -> p (r t)"), in_=pC[:, 0:512], func=AF.Square), reads=[bpC[0]], writes=[bsq])
                    kb.op(act, lambda: ACT.activation(out=sq[:, 4, :], in_=pC[:, 512:640], func=AF.Square), reads=[bpC[1]], writes=[bsq])
                    pN, bpN = kb.bank(1)
                    for r in range(3):
                        kb.op(pe, lambda r=r: PE.matmul(pN[:, 0:128], lhsT=onesf[:], rhs=sq[:, r, :], start=(r == 0), stop=(r == 2)), reads=[b_onesf, bsq], writes=bpN)
                    for r in range(2):
                        kb.op(pe, lambda r=r: PE.matmul(pN[:, 128:256], lhsT=onesf[:], rhs=sq[:, 3 + r, :], start=(r == 0), stop=(r == 1)), reads=[b_onesf, bsq], writes=bpN)
                    rsb, brsb = rsbs.next()
                    kb.op(act, lambda: ACT.activation(out=rsb[:, 0, :], in_=pN[:, 0:128], func=AF.Sqrt, scale=1.0 / 384, bias=epsn[:, 0:1]), reads=bpN + [b_epsn], writes=[brsb])
                    kb.op(act, lambda: ACT.activation(out=rsb[:, 1, :], in_=pN[:, 128:256], func=AF.Sqrt, scale=1.0 / 256, bias=epsn[:, 0:1]), reads=bpN + [b_epsn], writes=[brsb])
                    kb.op(dve, lambda: V.reciprocal(out=rsb[:], in_=rsb[:]), reads=[brsb], writes=[brsb])
                    cn, bcn = cns.next()
                    for r in range(5):
                        kb.op(dve, lambda r=r: V.scalar_tensor_tensor(out=cn[:, r, :], in0=pC[:, r * 128:(r + 1) * 128], scalar=nw5[:, r:r + 1],
                                                                      in1=rsb[:, 0 if r < 3 else 1, :], op0=ALU.mult, op1=ALU.mult),
                              reads=[bpC[r // 4], b_nw5, brsb], writes=[bcn])
                    pQ, bpQ = kb.bank(2)
                    pQr, bpQr = kb.bank(2)
                    for h in range(8):
                        for r in range(3):
                            kb.op(pe, lambda h=h, r=r: PE.matmul(pQ[0:96, h * 128:(h + 1) * 128], lhsT=Wuq[:, r, h * 96:(h + 1) * 96], rhs=cn[:, r, :],
                                                                 start=(r == 0), stop=(r == 2)), reads=[b_Wuq, bcn], writes=[bpQ[h // 4]])
                        for r in range(3):
                            kb.op(pe, lambda h=h, r=r: PE.matmul(pQr[0:96, h * 128:(h + 1) * 128], lhsT=Wuqr[:, r, h * 96:(h + 1) * 96], rhs=cn[:, r, :],
                                                                 start=(r == 0), stop=(r == 2)), reads=[b_Wuqr, bcn], writes=[bpQr[h // 4]])
                    cqt, bcqt = cqts.next()
                    sqt, bsqt = sqts.next()
                    kb.dma(sp, cqt[:], A["c_cq"][:, r0:r0 + 128], bcqt, writes=[bcqt])
                    kb.dma(sp, sqt[:], A["c_sq"][:, r0:r0 + 128], bsqt, writes=[bsqt])
                    t1, bt1 = t1s.next()
                    t2, bt2 = t2s.next()
                    kb.op(dve, lambda: V.tensor_tensor(out=t1[:], in0=pQ[0:96, :].rearrange("p (h t) -> p h t", h=8),
                                                       in1=cqt[:].unsqueeze(1).to_broadcast([96, 8, 128]), op=ALU.mult), reads=bpQ + [bcqt], writes=[bt1])
                    kb.op(dve, lambda: V.tensor_tensor(out=t2[:], in0=pQr[0:96, :].rearrange("p (h t) -> p h t", h=8),
                                                       in1=sqt[:].unsqueeze(1).to_broadcast([96, 8, 128]), op=ALU.mult), reads=bpQr + [bsqt], writes=[bt2])
                    QTt, bQTt = QTts.next()
                    kb.op(dve, lambda: V.tensor_tensor(out=QTt[:], in0=t1[:], in1=t2[:], op=ALU.add), reads=[bt1, bt2], writes=[bQTt])
                    kb.dma(sp, Z["QT"][:, 0:96, r0:r0 + 128].rearrange("h f t -> f h t"), QTt[:], bQTt, reads=[bQTt], writes=[ZB["QT"]])
                    pKN, bpKN = kb.bank(2)
                    for h in range(8):
                        for c in range(2):
                            kb.op(pe, lambda h=h, c=c: PE.matmul(pKN[0:64, h * 128:(h + 1) * 128], lhsT=Wukv[:, c, 0, h, :], rhs=cn[:, 3 + c, :],
                                                                 start=(c == 0), stop=(c == 1)), reads=[b_Wukv, bcn], writes=[bpKN[h // 4]])
                    KNt, bKNt = KNts.next()
                    kb.op(act, lambda: ACT.copy(out=KNt[:].rearrange("p h t -> p (h t)"), in_=pKN[0:64, :]), reads=bpKN, writes=[bKNt])
                    kb.dma(sp, Z["KT"][:, 0:64, r0:r0 + 128].rearrange("h f t -> f h t"), KNt[:], bKNt, reads=[bKNt], writes=[ZB["KT"]])
                    pKR, bpKR = kb.bank(1)
                    for kc in range(8):
                        kb.op(pe, lambda kc=kc: PE.matmul(pKR[0:32, 0:128], lhsT=Win[:, kc, 2688:2720], rhs=xnT[:, kc, :], start=(kc == 0), stop=(kc == 7)),
                              reads=[b_Win, bxnT], writes=bpKR)
                    for kc in range(8):
                        kb.op(pe, lambda kc=kc: PE.matmul(pKR[0:32, 128:256], lhsT=Wkrr[:, kc, :], rhs=xnT[:, kc, :], start=(kc == 0), stop=(kc == 7)),
                              reads=[b_Wkrr, bxnT], writes=bpKR)
                    ckt, bckt = ckts.next()
                    skt, bskt = skts.next()
                    kb.dma(sp, ckt[:], A["c_ck"][:, r0:r0 + 128], bckt, writes=[bckt])
                    kb.dma(sp, skt[:], A["c_sk"][:, r0:r0 + 128], bskt, writes=[bskt])
                    kr1, bkr1 = kr1s.next()
                    kr2, bkr2 = kr2s.next()
                    kb.op(dve, lambda: V.tensor_tensor(out=kr1[:], in0=pKR[0:32, 0:128], in1=ckt[:], op=ALU.mult), reads=bpKR + [bckt], writes=[bkr1])
                    kb.op(dve, lambda: V.tensor_tensor(out=kr2[:], in0=pKR[0:32, 128:256], in1=skt[:], op=ALU.mult), reads=bpKR + [bskt], writes=[bkr2])
                    KRt, bKRt = KRts.next()
                    kb.op(dve, lambda: V.tensor_tensor(out=KRt[:], in0=kr1[:], in1=kr2[:], op=ALU.add), reads=[bkr1, bkr2], writes=[bKRt])
                    for h in range(8):
                        kb.dma(sp, Z["KT"][h, 64:96, r0:r0 + 128], KRt[:], bKRt, reads=[bKRt], writes=[ZB["KT"]])
                    pV, bpV = kb.bank(1)
                    for c in range(2):
                        kb.op(pe, lambda c=c: PE.matmul(pV, lhsT=cn[:, 3 + c, :], rhs=Wukv[:, c, 1, :, :].rearrange("p h d -> p (h d)"),
                                                        start=(c == 0), stop=(c == 1)), reads=[bcn, b_Wukv], writes=bpV)
                    Vt, bVt = Vts.next()
                    kb.op(act, lambda: ACT.copy(out=Vt[:, :, 0:64], in_=pV.rearrange("p (h d) -> p h d", h=8)), reads=bpV, writes=[bVt])
                    kb.dma(sp, Z["VS"][r0:r0 + 128, :], Vt[:].rearrange("p h d -> p (h d)"), bVt, reads=[bVt], writes=[ZB["VS"]])
                    sq2, bsq2 = sq2s.next()
                    nrow, bnrow = nrows.next()
                    kb.op(act, lambda: ACT.activation(out=sq2[:], in_=QTt[:].rearrange("p h t -> p (h t)"), func=AF.Square), reads=[bQTt], writes=[bsq2])
                    pR, bpR = kb.bank(2)
                    for n in range(2):
                        kb.op(pe, lambda n=n: PE.matmul(pR[0:1, n * 512:(n + 1) * 512], lhsT=onesf[0:96, 0:1], rhs=sq2[:, n * 512:(n + 1) * 512], start=True, stop=True),
                              reads=[b_onesf, bsq2], writes=[bpR[n]])
                    kb.op(dve, lambda: V.tensor_copy(out=nrow[:, 0, :], in_=pR[0:1, :]), reads=bpR, writes=[bnrow])
                    nbr, bnbr = nbrs.next()
                    kb.op(act, lambda: ACT.activation(out=nrow[:, 0, :], in_=nrow[:, 0, :], func=AF.Sqrt), reads=[bnrow], writes=[bnrow])
                    kb.op(dve, lambda: V.tensor_scalar(out=nbr[:], in0=nrow[:, 0, :], scalar1=-1.0, scalar2=None, op0=ALU.mult), reads=[bnrow], writes=[bnbr])
                    kb.dma(sp, Z["QT"][:, 96:97, r0:r0 + 128].rearrange("h o t -> o h t"), nbr[:].rearrange("o (h t) -> o h t", h=8), bnbr, reads=[bnbr], writes=[ZB["QT"]])
                    kb.dma(sp, Z["KT"][:, 96:97, r0:r0 + 128].rearrange("h o t -> o h t"), one8[:].rearrange("o (h t) -> o h t", h=8), b_one8, reads=[b_one8], writes=[ZB["KT"]])
                    kb.op(act, lambda: ACT.activation(out=sq2[0:64, :], in_=KNt[:].rearrange("p h t -> p (h t)"), func=AF.Square), reads=[bKNt, bsq2], writes=[bsq2])
                    sq3, bsq3 = sq3s.next()
                    kb.op(act, lambda: ACT.activation(out=sq3[:], in_=KRt[:], func=AF.Square), reads=[bKRt], writes=[bsq3])
                    pR2, bpR2 = kb.bank(2)
                    for n in range(2):
                        kb.op(pe, lambda n=n: PE.matmul(pR2[0:1, n * 512:(n + 1) * 512], lhsT=onesf[0:64, 0:1], rhs=sq2[0:64, n * 512:(n + 1) * 512], start=True, stop=True),
                              reads=[b_onesf, bsq2], writes=[bpR2[n]])
                    pR3, bpR3 = kb.bank(1)
                    kb.op(pe, lambda: PE.matmul(pR3[0:1, 0:128], lhsT=onesf[0:32, 0:1], rhs=sq3[:], start=True, stop=True), reads=[b_onesf, bsq3], writes=bpR3)
                    kr_, bkr_ = krn.next()
                    kb.op(dve, lambda: V.tensor_copy(out=kr_[:], in_=pR3[0:1, 0:128]), reads=bpR3, writes=[bkr_])
                    kb.op(dve, lambda: V.tensor_tensor(out=nrow[:, 1, :].rearrange("o (h t) -> o h t", h=8), in0=pR2[0:1, :].rearrange("o (h t) -> o h t", h=8),
                                                       in1=kr_[:].unsqueeze(1).to_broadcast([1, 8, 128]), op=ALU.add), reads=bpR2 + [bkr_], writes=[bnrow])
                    kb.dma(sp, Z["KN"][:, r0:r0 + 128].rearrange("(o h) t -> o h t", o=1), nrow[:, 1, :].rearrange("o (h t) -> o h t", h=8), bnrow, reads=[bnrow], writes=[ZB["KN"]])
                kb.barrier()

        def phase2(si, S):
            nkt = S // 128
            QBLK = 512 if S >= 512 else S
            nqb = S // QBLK
            with ExitStack() as st:
                KThs = kb.slots([97, S], BF16, 2, "KTh", st)
                QThs = kb.slots([97, S], BF16, 2, "QTh", st)
                Vhs = kb.slots([128, nkt, 65], BF16, 2, "Vh", st)
                kn2, b_kn2 = kb.sbb([128, S // 128], F32, "kn2", st)
                kst, b_kst = kb.sbb([128, 4], F32, "kst", st)
                krow, b_krow = kb.sbb([1, 132], F32, "krow", st)
                kmx, b_kmx = kb.sbb([128, 1], F32, "kmx", st)
                PTs = kb.slots([128, QBLK], BF16, 5, "PT", st)
                osbs = kb.slots([65, QBLK], F32, 2, "osb", st)
                rds = kb.slots([64, QBLK], F32, 2, "rd", st)
                mos = kb.slots([64, QBLK], BF16, 2, "mo", st)
                Esel, b_Esel = kb.sbb([65, 64], F32, "Esel", st)
                kb.op(dve, lambda: V.memset(Esel[:], 0.0), writes=[b_Esel])
                kb.op(dve, lambda: V.memset(Esel[64:65, :], 1.0), writes=[b_Esel])
                kb.set_banks(range(2, 8))
                for h in range(8):
                    KTh, bKTh = KThs.next()
                    QTh, bQTh = QThs.next()
                    Vh, bVh = Vhs.next()
                    kb.dma(sp, KTh[:, :], Z["KT"][h, :, 0:S], bKTh, reads=[ZB["KT"]], writes=[bKTh])
                    kb.dma(sp, QTh[:, :], Z["QT"][h, :, 0:S], bQTh, reads=[ZB["QT"]], writes=[bQTh])
                    with nc.allow_non_contiguous_dma(reason="per-head V rows (130B)"):
                        kb.dma(sp, Vh[:], Z["VS"][0:S, h * 65:(h + 1) * 65].rearrange("(kt p) c -> p kt c", p=128), bVh, reads=[ZB["VS"]], writes=[bVh])
                    kb.dma(sp, kn2[:], Z["KN"][h, 0:S].rearrange("(p f) -> p f", p=128), b_kn2, reads=[ZB["KN"]], writes=[b_kn2])
                    kb.op(dve, lambda: V.tensor_reduce(out=kst[:, 0:1], in_=kn2[:], axis=AX.X, op=ALU.max), reads=[b_kn2], writes=[b_kst])
                    pK1, bpK1 = kb.bank(1)
                    kb.op(pe, lambda pK1=pK1: PE.transpose(pK1[0:1, 0:128], kst[:, 0:1], ident[:]), reads=[b_kst, b_ident], writes=bpK1)
                    kb.op(dve, lambda pK1=pK1: V.tensor_copy(out=krow[:, 0:128], in_=pK1[0:1, 0:128]), reads=bpK1, writes=[b_krow])
                    kb.op(dve, lambda: V.tensor_reduce(out=krow[:, 128:129], in_=krow[:, 0:128], axis=AX.X, op=ALU.max), reads=[b_krow], writes=[b_krow])
                    pK2, bpK2 = kb.bank(1)
                    kb.op(pe, lambda pK2=pK2: PE.matmul(pK2[:, 0:1], lhsT=onesf[0:1, :], rhs=krow[:, 128:129], start=True, stop=True), reads=[b_onesf, b_krow], writes=bpK2)
                    kb.op(act, lambda pK2=pK2: ACT.activation(out=kmx[:], in_=pK2[:, 0:1], func=AF.Sqrt), reads=bpK2, writes=[b_kmx])
                    kb.op(dve, lambda QTh=QTh: V.tensor_scalar(out=QTh[96:97, :], in0=QTh[96:97, :], scalar1=kmx[96:97, 0:1], scalar2=None, op0=ALU.mult),
                          reads=[bQTh, b_kmx], writes=[bQTh])
                    for qb in range(nqb):
                        q0 = qb * QBLK
                        pO, bpO = kb.fixed_bank(qb % 2, 1)
                        LA = 2
                        pend = []

                        def emit_s(kt):
                            pS, bpS = kb.bank(1)
                            kb.op(pe, lambda: PE.matmul(pS[:, 0:QBLK], lhsT=KTh[:, kt * 128:(kt + 1) * 128], rhs=QTh[:, q0:q0 + QBLK], start=True, stop=True),
                                  reads=[bKTh, bQTh], writes=bpS)
                            PT, bPT = PTs.next()
                            kb.op(act, lambda: ACT.activation(out=PT[:], in_=pS[:, 0:QBLK], func=AF.Exp), reads=bpS, writes=[bPT])
                            pend.append((kt, PT, bPT))

                        def emit_pv():
                            pkt, pPT, pbPT = pend.pop(0)
                            kb.op(pe, lambda: PE.matmul(pO[0:65, 0:QBLK], lhsT=Vh[:, pkt, :], rhs=pPT[:], start=(pkt == 0), stop=(pkt == nkt - 1)),
                                  reads=[bVh, pbPT], writes=bpO)
                        for kt in range(nkt):
                            emit_s(kt)
                            if len(pend) > LA:
                                emit_pv()
                        while pend:
                            emit_pv()
                        osb, bosb = osbs.next()
                        kb.op(dve, lambda: V.tensor_copy(out=osb[:], in_=pO[0:65, 0:QBLK]), reads=bpO, writes=[bosb])
                        pD, bpD = kb.bank(1)
                        kb.op(pe, lambda: PE.matmul(pD[0:64, 0:QBLK], lhsT=Esel[:], rhs=osb[:], start=True, stop=True), reads=[b_Esel, bosb], writes=bpD)
                        rd, brd = rds.next()
                        kb.op(dve, lambda: V.reciprocal(out=rd[:], in_=pD[0:64, 0:QBLK]), reads=bpD, writes=[brd])
                        mo, bmo = mos.next()
                        kb.op(dve, lambda: V.tensor_tensor(out=mo[:], in0=osb[0:64, :], in1=rd[:], op=ALU.mult), reads=[bosb, brd], writes=[bmo])
                        kb.dma(sp, Z["MO"][h, :, q0:q0 + QBLK], mo[:], bmo, reads=[bmo], writes=[ZB["MO"]])
                kb.barrier()

        def phase3a(si, S):
            nt = S // 128
            X = A[f"x{si}"]
            MEM = A[f"mem{si}"]
            kb.set_banks(range(8))
            with ExitStack() as st:
                work = dict(junk=kb.slots([128, 1024], BF16, 1, "junk", st), ss=kb.slots([128, 4], F32, 2, "ss", st),
                            xs=kb.slots([128, 1024], F32, 2, "xs", st))
                KmT, b_KmT = kb.sbb([128, 4, 2, 256], BF16, "KmT", st)
                Vm, b_Vm = kb.sbb([128, 2, 1024], BF16, "Vm", st)
                ncw, b_ncw = load_cols("norm_ca_w", 8, st)
                with ExitStack() as st2:
                    Wkv, b_Wkv = kb.sbb([128, 8, 2048], BF16, "Wkv", st2)
                    for kc in range(8):
                        load_cast(Wkv[:, kc, :], b_Wkv, A["ca_wkv"][0, kc * 128:(kc + 1) * 128, :])
                    nmm, b_nmm = load_cols("norm_mem_w", 8, st2)
                    memT, b_memT = kb.sbb([128, 8, 256], BF16, "memT", st2)
                    mts = kb.slots([128, 1024], F32, 2, "memt", st2)
                    for m in range(2):
                        mt, bmt = mts.next()
                        kb.dma(sp, mt[:], MEM[m * 128:(m + 1) * 128, :], bmt, writes=[bmt])
                        norm_transpose(mt[:], bmt, nmm, b_nmm, memT, b_memT, m * 128, work)
                    for j in range(8):
                        pK, bpK = kb.bank(1)
                        for kc in range(8):
                            kb.op(pe, lambda j=j, kc=kc, pK=pK: PE.matmul(pK[:, 0:256], lhsT=Wkv[:, kc, j * 128:(j + 1) * 128], rhs=memT[:, kc, :], start=(kc == 0), stop=(kc == 7)),
                                  reads=[b_Wkv, b_memT], writes=bpK)
                        kb.op(act, lambda j=j, pK=pK: ACT.copy(out=KmT[:, j // 2, j % 2, :], in_=pK[:, 0:256]), reads=bpK, writes=[b_KmT])
                    for mc in range(2):
                        for n in range(2):
                            pVm, bpVm = kb.bank(1)
                            for kc in range(8):
                                kb.op(pe, lambda mc=mc, n=n, kc=kc, pVm=pVm: PE.matmul(pVm, lhsT=memT[:, kc, mc * 128:(mc + 1) * 128], rhs=Wkv[:, kc, 1024 + n * 512:1024 + (n + 1) * 512],
                                                                                  start=(kc == 0), stop=(kc == 7)), reads=[b_memT, b_Wkv], writes=bpVm)
                            kb.op(act, lambda mc=mc, n=n, pVm=pVm: ACT.copy(out=Vm[:, mc, n * 512:(n + 1) * 512], in_=pVm), reads=bpVm, writes=[b_Vm])
                    kb.barrier()
                Wor, b_Wor = kb.sbb([128, 4, 1024], BF16, "Wor", st)
                for r in range(4):
                    load_cast(Wor[:, r, :], b_Wor, A["w_out"][0, r * 128:(r + 1) * 128, :])
                Wom, b_Wom = kb.sbb([64, 8, 1024], BF16, "Wom", st)
                for h in range(8):
                    load_cast(Wom[:, h, :], b_Wom, A["w_out"][0, 512 + h * 64:512 + (h + 1) * 64, :])
                Wq, b_Wq = kb.sbb([128, 8, 1024], BF16, "Wq", st)
                Wo, b_Wo = kb.sbb([128, 8, 1024], BF16, "Wo", st)
                for kc in range(8):
                    load_cast(Wq[:, kc, :], b_Wq, A["ca_wq"][0, kc * 128:(kc + 1) * 128, :])
                    load_cast(Wo[:, kc, :], b_Wo, A["ca_wo"][0, kc * 128:(kc + 1) * 128, :])
                gnw, b_gnw = kb.sbb([128, 512], F32, "gnw", st)
                kb.dma(sp, gnw[:], A["ret_gn_w"][0, :].partition_broadcast(128), b_gnw, writes=[b_gnw])
                dec, b_dec = kb.sbb([128, 8], F32, "dec3", st)
                kb.dma(sp, dec[:], A["ret_decay_bwd"][0, :].partition_broadcast(128), b_dec, writes=[b_dec])
                kb.op(act, lambda: ACT.activation(out=dec[:], in_=dec[:], func=AF.Exp, scale=-1.0), reads=[b_dec], writes=[b_dec])
                kb.op(act, lambda: ACT.activation(out=dec[:], in_=dec[:], func=AF.Ln, bias=1.0), reads=[b_dec], writes=[b_dec])
                kb.op(act, lambda: ACT.activation(out=dec[:], in_=dec[:], func=AF.Exp, scale=-128.0), reads=[b_dec], writes=[b_dec])
                GCb, b_GCb = kb.sbb([128, 4], F32, "GCb", st)
                decv = dec[:].rearrange("p (a two) -> p a two", two=2)
                kb.op(dve, lambda: V.tensor_copy(out=GCb[0:64, :], in_=decv[0:64, :, 0]), reads=[b_dec], writes=[b_GCb])
                kb.op(dve, lambda: V.tensor_copy(out=GCb[64:128, :], in_=decv[64:128, :, 1]), reads=[b_dec], writes=[b_GCb])
                Rb, b_Rb = kb.sbb([128, 4, 64], F32, "Rb", st)
                Rbb, b_Rbb = kb.sbb([128, 4, 128], BF16, "Rbb", st)
                kb.op(dve, lambda: V.memset(Rb[:], 0.0), writes=[b_Rb])
                kb.op(dve, lambda: V.memset(Rbb[:], 0.0), writes=[b_Rbb])
                yps = kb.slots([128, 512], F32, 2, "yp3", st)
                sgs = kb.slots([128, 512], F32, 2, "sg3", st)
                qbTs = kb.slots([128, 4, 128], BF16, 2, "qbT3", st)
                bbs = kb.slots([128, 4, 64], F32, 2, "bb3", st)
                xts = kb.slots([128, 1024], F32, 2, "xt3", st)
                mots = kb.slots([64, 8, 128], BF16, 2, "mot", st)
                ys = kb.slots([128, 8, 64], F32, 2, "y3", st)
                ycs = kb.slots([128, 8, 64], F32, 2, "yc3", st)
                y2s = kb.slots([128, 8, 64], F32, 2, "ysq3", st)
                sts = kb.slots([128, 8, 4], F32, 2, "st3", st)
                ros = kb.slots([128, 512], F32, 2, "ro3", st)
                roTs = kb.slots([128, 4, 128], BF16, 2, "roT", st)
                x1s = kb.slots([128, 1024], F32, 2, "x1", st)
                hnTs = kb.slots([128, 8, 128], BF16, 2, "hnT", st)
                qcTs = kb.slots([128, 8, 128], BF16, 2, "qcT", st)
                mxs = kb.slots([128, 4, 4], F32, 2, "mx3", st)
                Ps = kb.slots([128, 4, 256], F32, 2, "P3", st)
                PnTs = kb.slots([128, 8, 128], BF16, 2, "PnT", st)
                oTs = kb.slots([128, 8, 128], BF16, 2, "oT", st)
                x2s = kb.slots([128, 1024], F32, 2, "x2", st)
                for t in reversed(range(nt)):
                    r0 = t * 128
                    yp, byp = yps.next()
                    sg, bsg = sgs.next()
                    qbT, bqbT = qbTs.next()
                    bb, bbb = bbs.next()
                    xt, bx = xts.next()
                    mot, bmot = mots.next()
                    kb.dma(sp, yp[:], Z["YP"][r0:r0 + 128, :], byp, reads=[ZB["YP"]], writes=[byp])
                    kb.dma(sp, sg[:], Z["SG"][r0:r0 + 128, :], bsg, reads=[ZB["SG"]], writes=[bsg])
                    kb.dma(sp, qbT[:].rearrange("p a t -> p (a t)"), Z["QB"][t], bqbT, reads=[ZB["QB"]], writes=[bqbT])
                    kb.dma(sp, bb[:].rearrange("p a v -> p (a v)"), Z["BB"][t], bbb, reads=[ZB["BB"]], writes=[bbb])
                    kb.dma(sp, xt[:], X[r0:r0 + 128, :], bx, writes=[bx])
                    kb.dma(sp, mot[:], Z["MO"][:, :, r0:r0 + 128].rearrange("h v t -> v h t"), bmot, reads=[ZB["MO"]], writes=[bmot])
                    pY, bpY = kb.bank(1)
                    for a in range(4):
                        kb.op(pe, lambda a=a: PE.matmul(pY[:, a * 128:(a + 1) * 128], lhsT=qbT[:, a, :], rhs=Rbb[:, a, :], start=True, stop=True),
                              reads=[bqbT, b_Rbb], writes=bpY)
                    y, by = ys.next()
                    kb.op(dve, lambda: V.tensor_tensor(out=y[:].rearrange("p h d -> p (h d)"), in0=pY, in1=yp[:], op=ALU.add), reads=bpY + [byp], writes=[by])
                    kb.op(dve, lambda: V.tensor_tensor(out=Rb[:], in0=Rb[:], in1=GCb[:, :].unsqueeze(2).to_broadcast([128, 4, 64]), op=ALU.mult), reads=[b_Rb, b_GCb], writes=[b_Rb])
                    kb.op(dve, lambda: V.tensor_tensor(out=Rb[:], in0=Rb[:], in1=bb[:], op=ALU.add), reads=[b_Rb, bbb], writes=[b_Rb])
                    kb.op(dve, lambda: V.tensor_copy(out=Rbb[0:64, :, 0:64], in_=Rb[0:64]), reads=[b_Rb], writes=[b_Rbb])
                    kb.op(dve, lambda: V.tensor_copy(out=Rbb[64:128, :, 64:128], in_=Rb[64:128]), reads=[b_Rb], writes=[b_Rbb])
                    stt, bst = sts.next()
                    yc, byc = ycs.next()
                    ysq, bysq = y2s.next()
                    kb.op(dve, lambda: V.tensor_reduce(out=stt[:, :, 0], in_=y[:], axis=AX.X, op=ALU.add), reads=[by], writes=[bst])
                    kb.op(dve, lambda: V.tensor_scalar(out=stt[:, :, 1], in0=stt[:, :, 0], scalar1=1.0 / 64, scalar2=None, op0=ALU.mult), reads=[bst], writes=[bst])
                    kb.op(dve, lambda: V.tensor_tensor(out=yc[:], in0=y[:], in1=stt[:, :, 1:2].to_broadcast([128, 8, 64]), op=ALU.subtract), reads=[by, bst], writes=[byc])
                    kb.op(pool, lambda: G.tensor_tensor(out=ysq[:], in0=yc[:], in1=yc[:], op=ALU.mult), reads=[byc], writes=[bysq])
                    kb.op(dve, lambda: V.tensor_reduce(out=stt[:, :, 2], in_=ysq[:], axis=AX.X, op=ALU.add), reads=[bysq], writes=[bst])
                    kb.op(act, lambda: ACT.activation(out=stt[:, :, 3], in_=stt[:, :, 2], func=AF.Sqrt, scale=1.0 / 64, bias=epsg[:, 0:1]), reads=[bst, b_epsg], writes=[bst])
                    kb.op(dve, lambda: V.reciprocal(out=stt[:, :, 3], in_=stt[:, :, 3]), reads=[bst], writes=[bst])
                    kb.op(dve, lambda: V.tensor_tensor(out=yc[:], in0=yc[:], in1=stt[:, :, 3:4].to_broadcast([128, 8, 64]), op=ALU.mult), reads=[byc, bst], writes=[byc])
                    ro, bro = ros.next()
                    kb.op(pool, lambda: G.tensor_tensor(out=ro[:], in0=yc[:].rearrange("p h d -> p (h d)"), in1=gnw[:], op=ALU.mult), reads=[byc, b_gnw], writes=[bro])
                    kb.op(pool, lambda: G.tensor_tensor(out=ro[:], in0=ro[:], in1=sg[:], op=ALU.mult), reads=[bro, bsg], writes=[bro])
                    pT, bpT = kb.bank(1)
                    for r in range(4):
                        kb.op(pe, lambda r=r: PE.transpose(pT[:, r * 128:(r + 1) * 128], ro[:, r * 128:(r + 1) * 128], ident[:]), reads=[bro, b_ident], writes=bpT)
                    roT, broT = roTs.next()
                    kb.op(act, lambda: ACT.copy(out=roT[:].rearrange("p r t -> p (r t)"), in_=pT), reads=bpT, writes=[broT])
                    pM, bpM = kb.bank(2)
                    for n in range(2):
                        for r in range(4):
                            kb.op(pe, lambda n=n, r=r: PE.matmul(pM[:, n * 512:(n + 1) * 512], lhsT=roT[:, r, :], rhs=Wor[:, r, n * 512:(n + 1) * 512], start=(r == 0), stop=False),
                                  reads=[broT, b_Wor], writes=[bpM[n]])
                        for h in range(8):
                            kb.op(pe, lambda n=n, h=h: PE.matmul(pM[:, n * 512:(n + 1) * 512], lhsT=mot[:, h, :], rhs=Wom[:, h, n * 512:(n + 1) * 512], start=False, stop=(h == 7)),
                                  reads=[bmot, b_Wom], writes=[bpM[n]])
                    x1, bx1 = x1s.next()
                    kb.op(dve, lambda: V.tensor_tensor(out=x1[:], in0=pM, in1=xt[:], op=ALU.add), reads=bpM + [bx], writes=[bx1])
                    hnT, bhnT = hnTs.next()
                    norm_transpose(x1[:], bx1, ncw, b_ncw, hnT, bhnT, 0, work)
                    pQ, bpQ = kb.bank(2)
                    for j in range(8):
                        for kc in range(8):
                            kb.op(pe, lambda j=j, kc=kc: PE.matmul(pQ[:, j * 128:(j + 1) * 128], lhsT=Wq[:, kc, j * 128:(j + 1) * 128], rhs=hnT[:, kc, :], start=(kc == 0), stop=(kc == 7)),
                                  reads=[b_Wq, bhnT], writes=[bpQ[j // 4]])
                    qcT, bqcT = qcTs.next()
                    kb.op(act, lambda: ACT.activation(out=qcT[:].rearrange("p j t -> p (j t)"), in_=pQ, func=AF.Copy, scale=1.0 / 16), reads=bpQ, writes=[bqcT])
                    pS, bpS = kb.bank(2)
                    for hd in range(4):
                        for dc in range(2):
                            kb.op(pe, lambda hd=hd, dc=dc: PE.matmul(pS[:, hd * 256:(hd + 1) * 256], lhsT=qcT[:, hd * 2 + dc, :], rhs=KmT[:, hd, dc, :], start=(dc == 0), stop=(dc == 1)),
                                  reads=[bqcT, b_KmT], writes=[bpS[hd // 2]])
                    mx, bmx = mxs.next()
                    kb.op(dve, lambda: V.tensor_reduce(out=mx[:, :, 0], in_=pS.rearrange("p (h m) -> p h m", h=4), axis=AX.X, op=ALU.max), reads=bpS, writes=[bmx])
                    kb.op(dve, lambda: V.tensor_scalar(out=mx[:, :, 1], in0=mx[:, :, 0], scalar1=-1.0, scalar2=None, op0=ALU.mult), reads=[bmx], writes=[bmx])
                    P_, bP = Ps.next()
                    for hd in range(4):
                        kb.op(act, lambda hd=hd: ACT.activation(out=P_[:, hd, :], in_=pS[:, hd * 256:(hd + 1) * 256], func=AF.Exp, bias=mx[:, hd, 1:2], accum_out=mx[:, hd, 2:3]),
                              reads=[bpS[hd // 2], bmx], writes=[bP, bmx])
                    kb.op(dve, lambda: V.reciprocal(out=mx[:, :, 3], in_=mx[:, :, 2]), reads=[bmx], writes=[bmx])
                    kb.op(dve, lambda: V.tensor_tensor(out=P_[:], in0=P_[:], in1=mx[:, :, 3:4].to_broadcast([128, 4, 256]), op=ALU.mult), reads=[bP, bmx], writes=[bP])
                    pPT, bpPT = kb.bank(2)
                    Pf = P_[:].rearrange("p h m -> p (h m)")
                    for j in range(8):
                        kb.op(pe, lambda j=j: PE.transpose(pPT[:, j * 128:(j + 1) * 128], Pf[:, j * 128:(j + 1) * 128], ident[:]), reads=[bP, b_ident], writes=[bpPT[j // 4]])
                    PnT, bPnT = PnTs.next()
                    kb.op(act, lambda: ACT.copy(out=PnT[:].rearrange("p j t -> p (j t)"), in_=pPT), reads=bpPT, writes=[bPnT])
                    pO, bpO = kb.bank(2)
                    for hd in range(4):
                        for dd in range(2):
                            for mc in range(2):
                                kb.op(pe, lambda hd=hd, dd=dd, mc=mc: PE.matmul(pO[:, (hd * 2 + dd) * 128:(hd * 2 + dd + 1) * 128],
                                                                                lhsT=Vm[:, mc, hd * 256 + dd * 128:hd * 256 + (dd + 1) * 128], rhs=PnT[:, hd * 2 + mc, :],
                                                                                start=(mc == 0), stop=(mc == 1)), reads=[b_Vm, bPnT], writes=[bpO[(hd * 2 + dd) // 4]])
                    oT, boT = oTs.next()
                    kb.op(act, lambda: ACT.copy(out=oT[:].rearrange("p j t -> p (j t)"), in_=pO), reads=bpO, writes=[boT])
                    pC, bpC = kb.bank(2)
                    for n in range(2):
                        for j in range(8):
                            kb.op(pe, lambda n=n, j=j: PE.matmul(pC[:, n * 512:(n + 1) * 512], lhsT=oT[:, j, :], rhs=Wo[:, j, n * 512:(n + 1) * 512], start=(j == 0), stop=(j == 7)),
                                  reads=[boT, b_Wo], writes=[bpC[n]])
                    x2, bx2 = x2s.next()
                    kb.op(dve, lambda: V.tensor_tensor(out=x2[:], in0=pC, in1=x1[:], op=ALU.add), reads=bpC + [bx1], writes=[bx2])
                    kb.dma(sp, Z["X2"][r0:r0 + 128, :], x2[:], bx2, reads=[bx2], writes=[ZB["X2"]])
                kb.barrier()

        def phase3b(si, S):
            TB = 256
            nb = S // TB
            Y = A[f"y{si}"]
            with ExitStack() as st:
                work = dict(junk=kb.slots([128, 1024], BF16, 1, "junk", st), ss=kb.slots([128, 4], F32, 2, "ss", st),
                            xs=kb.slots([128, 1024], F32, 1, "xs", st))
                kb.set_banks(range(8))
                Wpq, b_Wpq = kb.sbb([128, 8, 2048], BF16, "Wpq", st)
                for kc in range(8):
                    load_cast(Wpq[:, kc, :], b_Wpq, A["peer_wq"][0, kc * 128:(kc + 1) * 128, :])
                nfw, b_nfw = load_cols("norm_ffn_w", 8, st)
                fnw, b_fnw = kb.sbb([128, 1024], F32, "fnw", st)
                kb.dma(sp, fnw[:], A["final_norm_w"].partition_broadcast(128), b_fnw, writes=[b_fnw])
                SKT, b_SKT = kb.sbb([128, 16, 128], BF16, "SKT", st)
                sks = kb.slots([128, 128], F32, 2, "skld", st)
                for g in range(16):
                    h, half = g // 2, g % 2
                    skt, bskt = sks.next()
                    kb.dma(sp, skt[:], A["peer_sub_keys"][0, half, h, :, :], bskt, writes=[bskt])
                    pT, bpT = kb.bank(1)
                    kb.op(pe, lambda skt=skt, pT=pT: PE.transpose(pT[:, 0:128], skt[:], ident[:]), reads=[bskt, b_ident], writes=bpT)
                    kb.op(act, lambda g=g, pT=pT: ACT.copy(out=SKT[:, g, :], in_=pT[:, 0:128]), reads=bpT, writes=[b_SKT])
                iota16, b_iota16 = kb.sbb([128, 16], F32, "iota16", st)
                kb.op(dve, lambda: V.tensor_copy(out=iota16[:], in_=iota[:, 0:16]), reads=[b_iota], writes=[b_iota16])
                x2ts = kb.slots([128, 2, 1024], F32, 2, "x2b", st)
                xn3Ts = kb.slots([128, 8, TB], BF16, 2, "xn3T", st)
                qpTs = kb.slots([128, 16, TB], BF16, 1, "qpT", st)
                ssbs = kb.slots([128, 16, 128], F32, 1, "ssb", st)
                reps = kb.slots([128, 256], F32, 2, "rep", st)
                v16s = kb.slots([128, 16, 16], F32, 1, "v16", st)
                ix16s = kb.slots([128, 16, 16], U32, 1, "ix16", st)
                ixfs = kb.slots([128, 16, 16], F32, 1, "ixf", st)
                cands = kb.slots([128, 8, 256], F32, 1, "cand", st)
                m2s = kb.slots([128, 8, 16], F32, 1, "m2", st)
                p2s = kb.slots([128, 8, 16], U32, 1, "p2", st)
                abs_ = kb.slots([128, 2, 128], U32, 1, "abu", st)
                abfs = kb.slots([128, 2, 128], F32, 1, "abf", st)
                ohs = kb.slots([128, 8, 16, 16], BF16, 1, "oh", st)
                sel3s = kb.slots([128, 3, 128], F32, 1, "sel3", st)
                zs = kb.slots([128, 8, 2], F32, 1, "z", st)
                T3s = kb.slots([128, 3, 128], F32, 2, "T3", st)
                Rs = kb.slots([128, 16, 128], BF16, 1, "Roh", st)
                L0s = kb.slots([128, 16, 128], BF16, 1, "L0oh", st)
                Gm, b_Gm = kb.sbb([128, TB, 128], BF16, "Gm", st)
                UTs = kb.slots([128, 8, 128], BF16, 3, "UTi", st)
                Vis = kb.slots([128, 1024], BF16, 3, "Vi", st)
                gas = kb.slots([128, TB], F32, 3, "ga", st)
                Wds = kb.slots([128, TB], BF16, 4, "Wd", st)
                x3s = kb.slots([128, 1024], F32, 1, "x3", st)

                def mkpool(idx):
                    return {"idx": list(idx), "i": 0}

                def pbank(pool, n=1):
                    L = len(pool["idx"])
                    while True:
                        i = pool["i"] % L
                        sel = pool["idx"][i:i + n]
                        if len(sel) == n and all(sel[k] == sel[0] + k for k in range(n)):
                            break
                        pool["i"] += 1
                    pool["i"] += n
                    b0 = sel[0]
                    return kb.PS[:, b0 * 512:(b0 + n) * 512], [kb.bank_bufs[bb] for bb in sel]
                poolR, poolM, poolG = mkpool([4, 5]), mkpool([6, 7]), mkpool([4, 5, 6, 7])
                bankR = lambda n=1: pbank(poolR, n)
                bankM = lambda n=1: pbank(poolM, n)
                bankG = lambda n=1: pbank(poolG, n)
                routed = {}

                def routing_gen(b):
                    r0 = b * TB
                    x2b, bx2b = x2ts.next()
                    kb.dma(sp, x2b[:], Z["X2"][r0:r0 + TB, :].rearrange("(t p) d -> p t d", p=128), bx2b, reads=[ZB["X2"]], writes=[bx2b])
                    xn3T, bxn3T = xn3Ts.next()
                    for tt in range(2):
                        norm_transpose(x2b[:, tt, :], bx2b, nfw, b_nfw, xn3T, bxn3T, tt * 128, work, bankfn=bankR)
                        yield
                    qpT, bqpT = qpTs.next()
                    for g in range(16):
                        pQ, bpQ = bankR(1)
                        for kc in range(8):
                            kb.op(pe, lambda: PE.matmul(pQ[:, 0:TB], lhsT=Wpq[:, kc, g * 128:(g + 1) * 128], rhs=xn3T[:, kc, :], start=(kc == 0), stop=(kc == 7)),
                                  reads=[b_Wpq, bxn3T], writes=bpQ)
                        kb.op(act, lambda: ACT.copy(out=qpT[:, g, :], in_=pQ[:, 0:TB]), reads=bpQ, writes=[bqpT])
                        yield
                    T3l = []
                    for tt in range(2):
                        ssb, bssb = ssbs.next()
                        for q4 in range(4):
                            pS, bpS = bankR(1)
                            for gg in range(4):
                                g = q4 * 4 + gg
                                kb.op(pe, lambda: PE.matmul(pS[:, gg * 128:(gg + 1) * 128], lhsT=qpT[:, g, tt * 128:(tt + 1) * 128], rhs=SKT[:, g, :], start=True, stop=True),
                                      reads=[bqpT, b_SKT], writes=bpS)
                            kb.op(act, lambda: ACT.copy(out=ssb[:, q4 * 4:(q4 + 1) * 4, :].rearrange("p g k -> p (g k)"), in_=pS), reads=bpS, writes=[bssb])
                            yield
                        v16, bv16 = v16s.next()
                        ix16, bix16 = ix16s.next()
                        for g in range(16):
                            src = ssb[:, g, :]
                            rep, brep = reps.next()
                            kb.op(dve, lambda: V.max(out=v16[:, g, 0:8], in_=src), reads=[bssb], writes=[bv16])
                            kb.op(dve, lambda: V.match_replace(out=rep[:, 0:128], in_to_replace=v16[:, g, 0:8], in_values=src, imm_value=NEG), reads=[bssb, bv16], writes=[brep])
                            kb.op(dve, lambda: V.max(out=v16[:, g, 8:16], in_=rep[:, 0:128]), reads=[brep], writes=[bv16])
                            kb.op(dve, lambda: V.max_index(out=ix16[:, g, 0:8], in_max=v16[:, g, 0:8], in_values=src), reads=[bssb, bv16], writes=[bix16])
                            kb.op(dve, lambda: V.max_index(out=ix16[:, g, 8:16], in_max=v16[:, g, 8:16], in_values=rep[:, 0:128]), reads=[brep, bv16], writes=[bix16])
                            yield
                        ixf, bixf = ixfs.next()
                        kb.op(dve, lambda: V.tensor_copy(out=ixf[:], in_=ix16[:]), reads=[bix16], writes=[bixf])
                        cand, bcand = cands.next()
                        v4 = v16[:].rearrange("p (h two) k -> p h two k", two=2)
                        kb.op(dve, lambda: V.tensor_tensor(out=cand[:].rearrange("p h (a b) -> p h a b", a=16),
                                                           in0=v4[:, :, 0, :].unsqueeze(3).to_broadcast([128, 8, 16, 16]),
                                                           in1=v4[:, :, 1, :].unsqueeze(2).to_broadcast([128, 8, 16, 16]), op=ALU.add), reads=[bv16], writes=[bcand])
                        yield
                        m2, bm2 = m2s.next()
                        p2, bp2 = p2s.next()
                        for h in range(8):
                            rep, brep = reps.next()
                            kb.op(dve, lambda: V.max(out=m2[:, h, 0:8], in_=cand[:, h, :]), reads=[bcand], writes=[bm2])
                            kb.op(dve, lambda: V.match_replace(out=rep[:], in_to_replace=m2[:, h, 0:8], in_values=cand[:, h, :], imm_value=NEG), reads=[bcand, bm2], writes=[brep])
                            kb.op(dve, lambda: V.max(out=m2[:, h, 8:16], in_=rep[:]), reads=[brep], writes=[bm2])
                            kb.op(dve, lambda: V.max_index(out=p2[:, h, 0:8], in_max=m2[:, h, 0:8], in_values=cand[:, h, :]), reads=[bcand, bm2], writes=[bp2])
                            kb.op(dve, lambda: V.max_index(out=p2[:, h, 8:16], in_max=m2[:, h, 8:16], in_values=rep[:]), reads=[brep, bm2], writes=[bp2])
                            yield
                        abu, babu = abs_.next()
                        abf, babf = abfs.next()
                        p2f = p2[:].rearrange("p h k -> p (h k)")
                        kb.op(dve, lambda: V.tensor_single_scalar(out=abu[:, 0, :], in_=p2f, scalar=4, op=ALU.logical_shift_right), reads=[bp2], writes=[babu])
                        kb.op(dve, lambda: V.tensor_single_scalar(out=abu[:, 1, :], in_=p2f, scalar=15, op=ALU.bitwise_and), reads=[bp2], writes=[babu])
                        kb.op(dve, lambda: V.tensor_copy(out=abf[:], in_=abu[:]), reads=[babu], writes=[babf])
                        yield
                        sel3, bsel3 = sel3s.next()
                        ix4 = ixf[:].rearrange("p (h two) k -> p h two k", two=2)
                        for w in range(2):
                            oh, boh = ohs.next()
                            kb.op(dve, lambda: V.tensor_tensor(out=oh[:], in0=abf[:, w, :].rearrange("p (h k) -> p h k", h=8).unsqueeze(3).to_broadcast([128, 8, 16, 16]),
                                                               in1=iota16[:].unsqueeze(1).unsqueeze(1).to_broadcast([128, 8, 16, 16]), op=ALU.is_equal),
                                  reads=[babf, b_iota16], writes=[boh])
                            yield
                            kb.op(dve, lambda: V.tensor_tensor(out=oh[:], in0=oh[:], in1=ix4[:, :, w, :].unsqueeze(2).to_broadcast([128, 8, 16, 16]), op=ALU.mult),
                                  reads=[boh, bixf], writes=[boh])
                            yield
                            kb.op(dve, lambda: V.tensor_reduce(out=sel3[:, w, :].rearrange("p (h k) -> p h k", h=8), in_=oh[:], axis=AX.X, op=ALU.add),
                                  reads=[boh], writes=[bsel3])
                            yield
                        z, bz = zs.next()
                        g3 = sel3[:, 2, :].rearrange("p (h k) -> p h k", h=8)
                        kb.op(dve, lambda: V.tensor_tensor(out=g3, in0=m2[:], in1=m2[:, :, 0:1].to_broadcast([128, 8, 16]), op=ALU.subtract), reads=[bm2], writes=[bsel3])
                        kb.op(act, lambda: ACT.activation(out=sel3[:, 2, :], in_=sel3[:, 2, :], func=AF.Exp), reads=[bsel3], writes=[bsel3])
                        kb.op(dve, lambda: V.tensor_reduce(out=z[:, :, 0], in_=g3, axis=AX.X, op=ALU.add), reads=[bsel3], writes=[bz])
                        kb.op(dve, lambda: V.reciprocal(out=z[:, :, 1], in_=z[:, :, 0]), reads=[bz], writes=[bz])
                        kb.op(dve, lambda: V.tensor_tensor(out=g3, in0=g3, in1=z[:, :, 1:2].to_broadcast([128, 8, 16]), op=ALU.mult), reads=[bsel3, bz], writes=[bsel3])
                        yield
                        pT, bpT = bankR(1)
                        for w in range(3):
                            kb.op(pe, lambda: PE.transpose(pT[:, w * 128:(w + 1) * 128], sel3[:, w, :], ident[:]), reads=[bsel3, b_ident], writes=bpT)
                        T3, bT3 = T3s.next()
                        kb.op(act, lambda: ACT.copy(out=T3[:].rearrange("p w t -> p (w t)"), in_=pT[:, 0:384]), reads=bpT, writes=[bT3])
                        T3l.append((T3, bT3))
                        yield
                    routed[b] = (x2b, bx2b, xn3T, bxn3T, T3l)

                def exhaust(gen):
                    if gen is not None:
                        for _ in gen:
                            pass

                pO0, bpO0 = kb.fixed_bank(0, 2)
                pO1, bpO1 = kb.fixed_bank(2, 2)
                pOs = ((pO0, bpO0), (pO1, bpO1))
                exhaust(routing_gen(0))
                for b in range(nb):
                    r0 = b * TB
                    x2b, bx2b, xn3T, bxn3T, T3l = routed.pop(b)
                    for tt in range(2):
                        T3, bT3 = T3l[tt]
                        for hf in range(8):
                            t0 = hf * 16
                            R_, bR = Rs.next()
                            L_, bL = L0s.next()
                            io_b = iota[:].unsqueeze(1).to_broadcast([128, 16, 128])
                            kb.op(dve, lambda: V.tensor_tensor(out=R_[:], in0=io_b, in1=T3[:, 1, t0:t0 + 16].unsqueeze(2).to_broadcast([128, 16, 128]), op=ALU.is_equal),
                                  reads=[b_iota, bT3], writes=[bR])
                            kb.op(dve, lambda: V.tensor_tensor(out=L_[:], in0=io_b, in1=T3[:, 0, t0:t0 + 16].unsqueeze(2).to_broadcast([128, 16, 128]), op=ALU.is_equal),
                                  reads=[b_iota, bT3], writes=[bL])
                            kb.op(dve, lambda: V.tensor_tensor(out=L_[:], in0=L_[:], in1=T3[:, 2, t0:t0 + 16].unsqueeze(2).to_broadcast([128, 16, 128]), op=ALU.mult),
                                  reads=[bL, bT3], writes=[bL])
                            for q in range(4):
                                pG, bpG = bankG(1)
                                for u in range(4):
                                    tl = q * 4 + u
                                    kb.op(pe, lambda: PE.matmul(pG[:, u * 128:(u + 1) * 128], lhsT=R_[:, tl, :], rhs=L_[:, tl, :], start=True, stop=True),
                                          reads=[bR, bL], writes=bpG)
                                tg = tt * 128 + t0 + q * 4
                                kb.op(act, lambda: ACT.copy(out=Gm[:, tg:tg + 4, :].rearrange("p t i -> p (t i)"), in_=pG), reads=bpG, writes=[b_Gm])
                    gen = routing_gen(b + 1) if b + 1 < nb else None

                    def emit_A(i):
                        UTi, bUTi = UTs.next()
                        Vi, bVi = Vis.next()
                        kb.dma(sp, UTi[:].rearrange("p k j -> p (k j)"), Z["UT"][i], bUTi, reads=[ZB["UT"]], writes=[bUTi])
                        kb.dma(sp, Vi[:], Z["VB"][i], bVi, reads=[ZB["VB"]], writes=[bVi])
                        pA, bpA = bankM(1)
                        for kc in range(8):
                            kb.op(pe, lambda: PE.matmul(pA[:, 0:TB], lhsT=UTi[:, kc, :], rhs=xn3T[:, kc, :], start=(kc == 0), stop=(kc == 7)),
                                  reads=[bUTi, bxn3T], writes=bpA)
                        ga, bga = gas.next()
                        kb.op(act, lambda: ACT.activation(out=ga[:], in_=pA[:, 0:TB], func=AF.Gelu), reads=bpA, writes=[bga])
                        Wd, bWd = Wds.next()
                        kb.op(dve, lambda: V.tensor_tensor(out=Wd[:], in0=ga[:], in1=Gm[:, :, i], op=ALU.mult), reads=[bga, b_Gm], writes=[bWd])
                        return (Wd, bWd, Vi, bVi)

                    def emit_O(i, cur):
                        Wd, bWd, Vi, bVi = cur
                        for tt in range(2):
                            pO, bpO = pOs[tt]
                            for n in range(2):
                                kb.op(pe, lambda: PE.matmul(pO[:, n * 512:(n + 1) * 512], lhsT=Wd[:, tt * 128:(tt + 1) * 128], rhs=Vi[:, n * 512:(n + 1) * 512],
                                                            start=(i == 0), stop=(i == NE - 1)), reads=[bWd, bVi], writes=[bpO[n]])
                    LAP = 2
                    pendA = []
                    for i in range(NE):
                        pendA.append((i, emit_A(i)))
                        if len(pendA) > LAP:
                            j, cur = pendA.pop(0)
                            emit_O(j, cur)
                        if gen is not None:
                            next(gen, None)
                    while pendA:
                        j, cur = pendA.pop(0)
                        emit_O(j, cur)
                    exhaust(gen)
                    for tt in range(2):
                        pO, bpO = pOs[tt]
                        x3, bx3 = x3s.next()
                        kb.op(dve, lambda: V.tensor_tensor(out=x3[:], in0=pO, in1=x2b[:, tt, :], op=ALU.add), reads=bpO + [bx2b], writes=[bx3])
                        junk, b_junk = work["junk"].next()
                        ss, b_ss = work["ss"].next()
                        kb.op(act, lambda: ACT.activation(out=junk[:], in_=x3[:], func=AF.Square, accum_out=ss[:, 0:1]), reads=[bx3], writes=[b_junk, b_ss])
                        kb.op(act, lambda: ACT.activation(out=ss[:, 1:2], in_=ss[:, 0:1], func=AF.Sqrt, scale=1.0 / D, bias=epsn[:, 0:1]), reads=[b_ss, b_epsn], writes=[b_ss])
                        kb.op(dve, lambda: V.reciprocal(out=ss[:, 2:3], in_=ss[:, 1:2]), reads=[b_ss], writes=[b_ss])
                        kb.op(dve, lambda: V.scalar_tensor_tensor(out=x3[:], in0=x3[:], scalar=ss[:, 2:3], in1=fnw[:], op0=ALU.mult, op1=ALU.mult),
                              reads=[bx3, b_ss, b_fnw], writes=[bx3])
                        kb.dma(sp, Y[r0 + tt * 128:r0 + (tt + 1) * 128, :], x3[:], bx3, reads=[bx3])
                kb.barrier()

        if prepass:
            kb.begin_phase()
            prepass_peer()
            kb.end_phase()
        nph = 0
        for si, S in enumerate(S_list):
            for ph in (phase1, phase2, phase3a, phase3b):
                if nph >= upto:
                    break
                nph += 1
                kb.set_banks(range(8))
                kb.begin_phase()
                ph(si, S)
                kb.end_phase()
        kb.barrier()
        kb.stats = {E.name: E.n for E in kb.engs}
        kb.stats["nsem"] = kb.nsem
        kb.stats["nops"] = kb.nops
        build.last_stats = kb.stats
    return nc


_CACHE = {}


def kernel(**inputs):
    ncores = 8
    S0 = inputs["x_prompt"].shape[1]
    S1 = inputs["x_sample"].shape[1]
    key = (S0, S1)
    if key not in _CACHE:
        _CACHE[key] = build([S0, S1])
    nc = _CACHE[key]
    consts = host_consts(max(S0, S1))
    in_maps = []
    for c in range(ncores):
        m = {"x0": np.ascontiguousarray(inputs["x_prompt"][c], dtype=np.float32),
             "x1": np.ascontiguousarray(inputs["x_sample"][c], dtype=np.float32),
             "mem0": np.ascontiguousarray(inputs["mem_prompt"][c], dtype=np.float32),
             "mem1": np.ascontiguousarray(inputs["mem_sample"][c], dtype=np.float32)}
        for n in W_NAMES:
            m[n] = np.ascontiguousarray(inputs[n], dtype=np.float32)
        m.update(consts)
        in_maps.append(m)
    res = run_bass_kernel_spmd(nc, in_maps, core_ids=list(range(ncores)))
    y0 = np.stack([np.asarray(res.results[c]["y0"], dtype=np.float32) for c in range(ncores)], 0)
    y1 = np.stack([np.asarray(res.results[c]["y1"], dtype=np.float32) for c in range(ncores)], 0)
    return (y0, y1)
```

```python
from contextlib import ExitStack
import numpy as np
import ml_dtypes
import concourse.bass as bass
import concourse.mybir as mybir
from concourse.bass_utils import run_bass_kernel_spmd

F32 = mybir.dt.float32
BF16 = mybir.dt.bfloat16
U32 = mybir.dt.uint32
AF = mybir.ActivationFunctionType
ALU = mybir.AluOpType
AX = mybir.AxisListType

SEM_LIMIT = 30000
D = 1024
NEG = -1.0e30


class Buf:
    __slots__ = ("name", "w", "r", "multi", "dsem", "dcnt", "excl")

    def __init__(self, name, multi=False, excl=False):
        self.name = name
        self.excl = excl
        self.w = {}
        self.r = {}
        self.multi = multi
        self.dsem = None
        self.dcnt = 0


class Eng:
    def __init__(self, kb, name, eng, own_wait=True):
        self.kb = kb
        self.name = name
        self.eng = eng
        self.sem = None
        self.val = 0
        self.seen = {}
        self.own_wait = own_wait
        self.n = 0

    def next_event(self):
        if self.sem is None or self.val >= SEM_LIMIT:
            self.sem = self.kb.new_sem(self.name)
            self.val = 0
        self.val += 1
        return (self.sem, self.val)


class Slots:
    def __init__(self, items):
        self.items = items
        self.i = 0

    def next(self):
        it = self.items[self.i % len(self.items)]
        self.i += 1
        return it


class KB:
    def __init__(self, nc, stack):
        self.nc = nc
        self.stack = stack
        self.nsem = 0
        self.pe = Eng(self, "pe", nc.tensor, own_wait=False)
        self.dve = Eng(self, "dve", nc.vector)
        self.act = Eng(self, "act", nc.scalar)
        self.pool = Eng(self, "pool", nc.gpsimd)
        self.sp = Eng(self, "sp", nc.sync)
        self.engs = [self.pe, self.dve, self.act, self.pool, self.sp]
        self.anchors = []
        self.free_sems = []
        self.nops = 0
        import os
        self.limit = int(os.environ.get("OPLIMIT", "1000000000"))
        self.phase_mark = 0
        self.persist = []
        self.uid = 0
        self.PS = None
        self.bank_bufs = None
        self.bank_pool = list(range(8))
        self.bank_i = 0

    def new_sem(self, name):
        self.nsem += 1
        return self.stack.enter_context(self.nc.semaphore(f"s{self.nsem}_{name}"))

    def sb(self, shape, dtype, name="sb", stack=None):
        self.uid += 1
        st = stack or self.stack
        return st.enter_context(self.nc.sbuf_tensor(f"{name}_{self.uid}", list(shape), dtype))

    def sbb(self, shape, dtype, name="sb", stack=None):
        return self.sb(shape, dtype, name, stack), Buf(name)

    def slots(self, shape, dtype, n, name, stack=None):
        return Slots([self.sbb(shape, dtype, f"{name}{i}", stack) for i in range(n)])

    def init_psum(self):
        self.PS = self.stack.enter_context(self.nc.psum_tensor("PSALL", [128, 8 * 512], F32))
        self.bank_bufs = [Buf(f"bank{i}", excl=True) for i in range(8)]

    def set_banks(self, pool):
        self.bank_pool = list(pool)
        self.bank_i = 0

    def bank(self, n=1):
        L = len(self.bank_pool)
        while True:
            i = self.bank_i % L
            idx = self.bank_pool[i:i + n]
            if len(idx) == n and all(idx[k] == idx[0] + k for k in range(n)):
                break
            self.bank_i += 1
        self.bank_i += n
        b0 = idx[0]
        return self.PS[:, b0 * 512:(b0 + n) * 512], [self.bank_bufs[b] for b in idx]

    def fixed_bank(self, b0, n=1):
        return self.PS[:, b0 * 512:(b0 + n) * 512], [self.bank_bufs[b] for b in range(b0, b0 + n)]

    def _waits(self, E, reads, writes):
        need = {}

        def add(d):
            for k, ev in d.items():
                if k not in need or need[k][1] < ev[1]:
                    need[k] = ev
        own = id(E.sem) if E.sem is not None else None
        for b in reads:
            add(b.w)
            if b.excl:
                add({k: ev for k, ev in b.r.items() if k != own})
        for b in writes:
            if not b.multi:
                add(b.w)
            add(b.r)
        for k, (sem, val) in need.items():
            if (not E.own_wait) and E.sem is not None and k == id(E.sem):
                continue
            if E.seen.get(k, 0) >= val:
                continue
            E.eng.wait_ge(sem, val)
            E.seen[k] = val

    def _record(self, ev, reads, writes):
        k = id(ev[0])
        for b in reads:
            if k not in b.r or b.r[k][1] < ev[1]:
                b.r[k] = ev
        for b in writes:
            if b.multi:
                if k not in b.w or b.w[k][1] < ev[1]:
                    b.w[k] = ev
            else:
                b.w = {k: ev}
            b.r = {}

    def op(self, E, fn, reads=(), writes=()):
        self.nops += 1
        if self.nops > self.limit:
            return None
        self._waits(E, reads, writes)
        inst = fn()
        ev = E.next_event()
        inst.then_inc(ev[0], 1)
        self._record(ev, reads, writes)
        E.n += 1
        return inst

    def dma(self, Q, out, in_, anchor, reads=(), writes=(), **kw):
        self.nops += 1
        if self.nops > self.limit:
            return None
        self._waits(Q, reads, writes)
        if anchor.dsem is None:
            if self.free_sems:
                anchor.dsem, anchor.dcnt = self.free_sems.pop()
            else:
                anchor.dsem, anchor.dcnt = self.new_sem("d_" + anchor.name), 0
            self.anchors.append(anchor)
        inst = Q.eng.dma_start(out=out, in_=in_, **kw)
        anchor.dcnt += 16
        ev = (anchor.dsem, anchor.dcnt)
        inst.then_inc(anchor.dsem, 16)
        self._record(ev, reads, writes)
        Q.n += 1
        return inst

    def barrier(self):
        evs = {}
        for E in self.engs:
            if E.sem is not None:
                evs[id(E.sem)] = (E.sem, E.val)
        for a in self.anchors:
            evs[id(a.dsem)] = (a.dsem, a.dcnt)
        S = self.sp
        for k, (sem, val) in evs.items():
            if S.seen.get(k, 0) >= val:
                continue
            S.eng.wait_ge(sem, val)
            S.seen[k] = val
        inst = S.eng.nop()
        ev = S.next_event()
        inst.then_inc(ev[0], 1)
        for E in self.engs:
            if E is S:
                continue
            E.eng.wait_ge(ev[0], ev[1])
            E.seen[id(ev[0])] = ev[1]
            for k, (sem, val) in evs.items():
                if E.seen.get(k, 0) < val:
                    E.seen[k] = val
        S.seen[id(ev[0])] = ev[1]


    def begin_phase(self):
        self.phase_mark = len(self.anchors)

    def end_phase(self):
        self.barrier()
        dead = self.anchors[self.phase_mark:]
        self.anchors = self.anchors[:self.phase_mark]
        for a in dead:
            if a.dcnt <= 20000:
                self.free_sems.append((a.dsem, a.dcnt))
            a.dsem = None
        for b in self.persist:
            b.w = {}
            b.r = {}


W_NAMES = ["norm_mix_w", "w_in", "ret_decay_fwd", "ret_decay_bwd", "ret_gn_w", "mla_q_norm_w", "mla_w_uq",
           "mla_kv_norm_w", "mla_w_ukv", "w_out", "norm_ca_w", "norm_mem_w", "ca_wq", "ca_wkv", "ca_wo",
           "norm_ffn_w", "peer_wq", "peer_sub_keys", "peer_u", "peer_v", "final_norm_w"]
W_SHAPES = {
    "norm_mix_w": [1, 1024], "w_in": [1, 1024, 2720], "ret_decay_fwd": [1, 8], "ret_decay_bwd": [1, 8],
    "ret_gn_w": [1, 512], "mla_q_norm_w": [1, 384], "mla_w_uq": [1, 384, 768], "mla_kv_norm_w": [1, 256],
    "mla_w_ukv": [1, 256, 1024], "w_out": [1, 1024, 1024], "norm_ca_w": [1, 1024], "norm_mem_w": [1, 1024],
    "ca_wq": [1, 1024, 1024], "ca_wkv": [1, 1024, 2048], "ca_wo": [1, 1024, 1024], "norm_ffn_w": [1, 1024],
    "peer_wq": [1, 1024, 2048], "peer_sub_keys": [1, 2, 8, 128, 128], "peer_u": [1, 16384, 1024],
    "peer_v": [1, 16384, 1024], "final_norm_w": [1024],
}


def host_consts(smax):
    pos = np.arange(smax, dtype=np.float32)
    c = {}
    invf = (1.0 / (np.float32(10000.0) ** (np.arange(0, 64, 2, dtype=np.float32) / np.float32(64)))).astype(np.float32)
    ang = (pos[:, None] * invf[None, :]).astype(np.float32)
    c["c_cosr"] = np.cos(ang).astype(np.float32)
    c["c_sinr"] = np.sin(ang).astype(np.float32)
    invm = (1.0 / (np.float32(10000.0) ** (np.arange(0, 32, 2, dtype=np.float32) / np.float32(32)))).astype(np.float32)
    angm = (pos[:, None] * invm[None, :]).astype(np.float32)
    cm = np.cos(angm).astype(np.float32).T
    sm = np.sin(angm).astype(np.float32).T
    ck = np.concatenate([cm, cm], 0)
    sk = np.concatenate([sm, sm], 0)
    sc = np.float32(96.0 ** -0.5)
    c["c_cq"] = np.concatenate([np.full((64, smax), sc, np.float32), ck * sc], 0).astype(np.float32)
    c["c_sq"] = np.concatenate([np.zeros((64, smax), np.float32), sk * sc], 0).astype(np.float32)
    c["c_ck"] = np.ascontiguousarray(ck)
    c["c_sk"] = np.ascontiguousarray(sk)
    io = np.arange(128, dtype=np.float32)
    c["c_iota"] = np.tile(io[None, :], (128, 1)).astype(np.float32)
    c["c_diff"] = (io[None, :] - io[:, None]).astype(np.float32)
    c["c_pcol"] = np.stack([127.0 - io, io], 1).astype(np.float32)
    c["c_ident"] = np.eye(128, dtype=np.float32)
    return c


CONST_SHAPES = lambda smax: {"c_cosr": [smax, 32], "c_sinr": [smax, 32], "c_cq": [96, smax], "c_sq": [96, smax],
                             "c_ck": [32, smax], "c_sk": [32, smax], "c_iota": [128, 128], "c_diff": [128, 128],
                             "c_pcol": [128, 2], "c_ident": [128, 128]}


def build(S_list, n_exp_chunks=128, debug=False, upto=99, prepass=True):
    nc = bass.Bass("TRN2", target_bir_lowering=False)
    smax = max(S_list)
    NE = n_exp_chunks
    A = {}
    for i, S in enumerate(S_list):
        A[f"x{i}"] = nc.dram_tensor(f"x{i}", [S, D], F32, kind="ExternalInput").ap()
        A[f"mem{i}"] = nc.dram_tensor(f"mem{i}", [256, D], F32, kind="ExternalInput").ap()
        A[f"y{i}"] = nc.dram_tensor(f"y{i}", [S, D], F32, kind="ExternalOutput").ap()
    for n in W_NAMES:
        A[n] = nc.dram_tensor(n, W_SHAPES[n], F32, kind="ExternalInput").ap()
    for n, shp in CONST_SHAPES(smax).items():
        A[n] = nc.dram_tensor(n, shp, F32, kind="ExternalInput").ap()
    skind = "ExternalOutput" if debug else "Internal"
    nt_max = smax // 128

    def scr(name, shape, dt):
        return nc.dram_tensor("z_" + name, shape, dt, kind=skind).ap()
    Z = dict(
        SG=scr("SG", [smax, 512], F32), YP=scr("YP", [smax, 512], F32),
        QB=scr("QB", [nt_max, 128, 512], BF16), BB=scr("BB", [nt_max, 128, 256], F32),
        QT=scr("QT", [8, 97, smax], BF16), KT=scr("KT", [8, 97, smax], BF16),
        VS=scr("VS", [smax, 8 * 65], BF16), KN=scr("KN", [8, smax], F32),
        MO=scr("MO", [8, 64, smax], BF16), X2=scr("X2", [smax, D], F32),
        UT=scr("UT", [128, 128, 1024], BF16), VB=scr("VB", [128, 128, 1024], BF16),
    )
    ZB = {k: Buf("z" + k, multi=True) for k in Z}

    with ExitStack() as top:
        kb = KB(nc, top)
        kb.init_psum()
        pe, dve, act, pool, sp = kb.pe, kb.dve, kb.act, kb.pool, kb.sp
        V = nc.vector
        ACT = nc.scalar
        G = nc.gpsimd
        PE = nc.tensor

        ident, b_ident = kb.sbb([128, 128], F32, "ident")
        iota, b_iota = kb.sbb([128, 128], F32, "iota")
        onesf, b_onesf = kb.sbb([128, 128], F32, "onesf")
        epsn, b_epsn = kb.sbb([128, 1], F32, "epsn")
        epsg, b_epsg = kb.sbb([128, 1], F32, "epsg")
        kb.dma(sp, ident[:], A["c_ident"][:, :], b_ident, writes=[b_ident])
        kb.dma(sp, iota[:], A["c_iota"][:, :], b_iota, writes=[b_iota])
        kb.op(dve, lambda: V.memset(onesf[:], 1.0), writes=[b_onesf])
        kb.op(dve, lambda: V.memset(epsn[:], 1e-6), writes=[b_epsn])
        kb.op(dve, lambda: V.memset(epsg[:], 1e-5), writes=[b_epsg])
        kb.persist = list(ZB.values()) + [b_ident, b_iota, b_onesf, b_epsn, b_epsg]
        kb.barrier()

        def load_cast(dst, dst_buf, src, stack=None):
            kb.dma(pool, dst, src, dst_buf, writes=[dst_buf])

        def norm_transpose(xt, bx, wcol, b_wcol, outT, b_outT, col0, work, bankfn=None):
            junk, b_junk = work["junk"].next()
            ss, b_ss = work["ss"].next()
            xs, b_xs = work["xs"].next()
            kb.op(act, lambda: ACT.activation(out=junk[:], in_=xt, func=AF.Square, accum_out=ss[:, 0:1]),
                  reads=[bx], writes=[b_junk, b_ss])
            kb.op(act, lambda: ACT.activation(out=ss[:, 1:2], in_=ss[:, 0:1], func=AF.Sqrt, scale=1.0 / D, bias=epsn[:, 0:1]),
                  reads=[b_ss, b_epsn], writes=[b_ss])
            kb.op(dve, lambda: V.reciprocal(out=ss[:, 2:3], in_=ss[:, 1:2]), reads=[b_ss], writes=[b_ss])
            kb.op(pool, lambda: G.tensor_scalar(out=xs[:], in0=xt, scalar1=ss[:, 2:3], scalar2=None, op0=ALU.mult),
                  reads=[bx, b_ss], writes=[b_xs])
            pT, bpT = (bankfn or kb.bank)(2)
            for kc in range(8):
                kb.op(pe, lambda kc=kc: PE.transpose(pT[:, kc * 128:(kc + 1) * 128], xs[:, kc * 128:(kc + 1) * 128], ident[:]),
                      reads=[b_xs, b_ident], writes=[bpT[kc // 4]])
            pTv = pT.rearrange("p (k t) -> p k t", k=8)
            kb.op(dve, lambda: V.tensor_tensor(out=outT[:, :, col0:col0 + 128], in0=pTv,
                                               in1=wcol[:, :].unsqueeze(2).to_broadcast([128, 8, 128]), op=ALU.mult),
                  reads=bpT + [b_wcol], writes=[b_outT])
            return ss, b_ss

        def load_cols(name, ncols, stack):
            t, b = kb.sbb([128, ncols], F32, name, stack)
            src = A[name].rearrange("o (c p) -> p (o c)", p=128) if len(W_SHAPES[name]) == 2 else A[name].rearrange("(c p) -> p c", p=128)
            with nc.allow_non_contiguous_dma(reason="tiny per-feature vector"):
                kb.dma(sp, t[:], src, b, writes=[b])
            return t, b

        def prepass_peer():
            with ExitStack() as st:
                us = kb.slots([128, 1024], F32, 2, "pp_u", st)
                uts = kb.slots([128, 1024], BF16, 2, "pp_ut", st)
                vs = kb.slots([128, 1024], BF16, 2, "pp_v", st)
                for i in range(NE):
                    ut, bu = us.next()
                    kb.dma(sp, ut[:], A["peer_u"][0, i * 128:(i + 1) * 128, :], bu, writes=[bu])
                    pT, bpT = kb.bank(2)
                    for kc in range(8):
                        kb.op(pe, lambda kc=kc: PE.transpose(pT[:, kc * 128:(kc + 1) * 128], ut[:, kc * 128:(kc + 1) * 128], ident[:]),
                              reads=[bu, b_ident], writes=[bpT[kc // 4]])
                    utt, butt = uts.next()
                    kb.op(act, lambda: ACT.copy(out=utt[:, 0:512], in_=pT[:, 0:512]), reads=[bpT[0]], writes=[butt])
                    kb.op(dve, lambda: V.tensor_copy(out=utt[:, 512:1024], in_=pT[:, 512:1024]), reads=[bpT[1]], writes=[butt])
                    kb.dma(sp, Z["UT"][i], utt[:], butt, reads=[butt], writes=[ZB["UT"]])
                    vt, bv = vs.next()
                    load_cast(vt[:], bv, A["peer_v"][0, i * 128:(i + 1) * 128, :])
                    kb.dma(sp, Z["VB"][i], vt[:], bv, reads=[bv], writes=[ZB["VB"]])
                kb.barrier()

        def phase1(si, S):
            nt = S // 128
            X = A[f"x{si}"]
            with ExitStack() as st:
                Win, b_Win = kb.sbb([128, 8, 2720], BF16, "Win", st)
                for kc in range(8):
                    for (c0, c1) in ((0, 1360), (1360, 2720)):
                        load_cast(Win[:, kc, c0:c1], b_Win, A["w_in"][0, kc * 128:(kc + 1) * 128, c0:c1])
                Wuq, b_Wuq = kb.sbb([128, 3, 768], BF16, "Wuq", st)
                load_cast(Wuq[:], b_Wuq, A["mla_w_uq"][0].rearrange("(r p) n -> p r n", p=128))
                Wuqr, b_Wuqr = kb.sbb([128, 3, 768], BF16, "Wuqr", st)
                Wukv, b_Wukv = kb.sbb([128, 2, 2, 8, 64], BF16, "Wukv", st)
                for c in range(2):
                    for t in range(2):
                        load_cast(Wukv[:, c, t, :, :], b_Wukv,
                                  A["mla_w_ukv"][0, c * 128:(c + 1) * 128, :].rearrange("p (h t d) -> p t h d", h=8, t=2)[:, t, :, :])
                Wkrr, b_Wkrr = kb.sbb([128, 8, 32], BF16, "Wkrr", st)
                kb.op(dve, lambda: V.memset(Wuqr[:], 0.0), writes=[b_Wuqr])
                Wuq4 = Wuq[:].rearrange("p r (h f) -> p r h f", h=8)
                Wuqr4 = Wuqr[:].rearrange("p r (h f) -> p r h f", h=8)
                for r in range(3):
                    kb.op(dve, lambda r=r: V.tensor_scalar(out=Wuqr4[:, r, :, 64:80], in0=Wuq4[:, r, :, 80:96], scalar1=-1.0, scalar2=None, op0=ALU.mult),
                          reads=[b_Wuq], writes=[b_Wuqr])
                    kb.op(dve, lambda r=r: V.tensor_copy(out=Wuqr4[:, r, :, 80:96], in_=Wuq4[:, r, :, 64:80]),
                          reads=[b_Wuq], writes=[b_Wuqr])
                kb.op(dve, lambda: V.tensor_scalar(out=Wkrr[:, :, 0:16], in0=Win[:, :, 2704:2720], scalar1=-1.0, scalar2=None, op0=ALU.mult),
                      reads=[b_Win], writes=[b_Wkrr])
                kb.op(dve, lambda: V.tensor_copy(out=Wkrr[:, :, 16:32], in_=Win[:, :, 2688:2704]), reads=[b_Win], writes=[b_Wkrr])
                nmw, b_nmw = load_cols("norm_mix_w", 8, st)
                nw5, b_nw5 = kb.sbb([128, 5], F32, "nw5", st)
                with nc.allow_non_contiguous_dma(reason="tiny per-feature vector"):
                    kb.dma(sp, nw5[:, 0:3], A["mla_q_norm_w"].rearrange("o (c p) -> p (o c)", p=128), b_nw5, writes=[b_nw5])
                    kb.dma(sp, nw5[:, 3:5], A["mla_kv_norm_w"].rearrange("o (c p) -> p (o c)", p=128), b_nw5, writes=[b_nw5])
                cosrs = kb.slots([128, 32], F32, 2, "cosr", st)
                sinrs = kb.slots([128, 32], F32, 2, "sinr", st)
                dec, b_dec = kb.sbb([128, 16], F32, "dec", st)
                kb.dma(sp, dec[:, 0:8], A["ret_decay_fwd"][0, :].partition_broadcast(128), b_dec, writes=[b_dec])
                kb.dma(sp, dec[:, 8:16], A["ret_decay_bwd"][0, :].partition_broadcast(128), b_dec, writes=[b_dec])
                lg, b_lg = kb.sbb([128, 16], F32, "lg", st)
                kb.op(act, lambda: ACT.activation(out=lg[:], in_=dec[:], func=AF.Exp, scale=-1.0), reads=[b_dec], writes=[b_lg])
                kb.op(act, lambda: ACT.activation(out=lg[:], in_=lg[:], func=AF.Ln, bias=1.0), reads=[b_lg], writes=[b_lg])
                kb.op(dve, lambda: V.tensor_scalar(out=lg[:], in0=lg[:], scalar1=-1.0, scalar2=None, op0=ALU.mult), reads=[b_lg], writes=[b_lg])
                lgp, b_lgp = kb.sbb([128, 4, 2], F32, "lgp", st)
                lgv = lg[:].rearrange("p (d a two) -> p a d two", d=2, two=2)
                kb.op(dve, lambda: V.tensor_copy(out=lgp[0:64, :, :], in_=lgv[0:64, :, :, 0]), reads=[b_lg], writes=[b_lgp])
                kb.op(dve, lambda: V.tensor_copy(out=lgp[64:128, :, :], in_=lgv[64:128, :, :, 1]), reads=[b_lg], writes=[b_lgp])
                diff, b_diff = kb.sbb([128, 128], F32, "diff", st)
                kb.dma(sp, diff[:], A["c_diff"][:, :], b_diff, writes=[b_diff])
                pcol, b_pcol = kb.sbb([128, 2], F32, "pcol", st)
                kb.dma(sp, pcol[:], A["c_pcol"][:, :], b_pcol, writes=[b_pcol])
                dpos, b_dpos = kb.sbb([128, 128], F32, "dpos", st)
                dneg, b_dneg = kb.sbb([128, 128], F32, "dneg", st)
                kb.op(dve, lambda: V.tensor_scalar(out=dpos[:], in0=diff[:], scalar1=0.0, scalar2=None, op0=ALU.max), reads=[b_diff], writes=[b_dpos])
                kb.op(dve, lambda: V.tensor_scalar(out=dneg[:], in0=diff[:], scalar1=-1.0, scalar2=0.0, op0=ALU.mult, op1=ALU.max), reads=[b_diff], writes=[b_dneg])
                DT, b_DT = kb.sbb([128, 8, 128], F32, "DT", st)
                tmpE, b_tmpE = kb.sbb([128, 128], F32, "tmpE", st)
                for h in range(8):
                    kb.op(act, lambda h=h: ACT.activation(out=DT[:, h, :], in_=dpos[:], func=AF.Exp, scale=lg[:, h:h + 1]),
                          reads=[b_dpos, b_lg], writes=[b_DT])
                    kb.op(pool, lambda h=h: G.affine_select(out=DT[:, h, :], in_=DT[:, h, :], pattern=[[1, 128]], compare_op=ALU.is_ge,
                                                            fill=0.0, base=0, channel_multiplier=-1), reads=[b_DT], writes=[b_DT])
                    kb.op(act, lambda h=h: ACT.activation(out=tmpE[:], in_=dneg[:], func=AF.Exp, scale=lg[:, 8 + h:9 + h]),
                          reads=[b_dneg, b_lg], writes=[b_tmpE])
                    kb.op(pool, lambda: G.affine_select(out=tmpE[:], in_=tmpE[:], pattern=[[-1, 128]], compare_op=ALU.is_gt,
                                                        fill=0.0, base=0, channel_multiplier=1), reads=[b_tmpE], writes=[b_tmpE])
                    kb.op(dve, lambda h=h: V.tensor_tensor(out=DT[:, h, :], in0=DT[:, h, :], in1=tmpE[:], op=ALU.add),
                          reads=[b_DT, b_tmpE], writes=[b_DT])
                ip1, b_ip1 = kb.sbb([128, 128], F32, "ip1", st)
                cmi, b_cmi = kb.sbb([128, 128], F32, "cmi", st)
                kb.op(dve, lambda: V.tensor_scalar(out=ip1[:], in0=iota[:], scalar1=1.0, scalar2=None, op0=ALU.add), reads=[b_iota], writes=[b_ip1])
                kb.op(dve, lambda: V.tensor_scalar(out=cmi[:], in0=iota[:], scalar1=-1.0, scalar2=128.0, op0=ALU.mult, op1=ALU.add), reads=[b_iota], writes=[b_cmi])
                QF, b_QF = kb.sbb([128, 4, 128], F32, "QF", st)
                QBt, b_QBt = kb.sbb([128, 4, 128], F32, "QBt", st)
                for a in range(4):
                    kb.op(act, lambda a=a: ACT.activation(out=QF[:, a, :], in_=ip1[:], func=AF.Exp, scale=lgp[:, a, 0:1]), reads=[b_ip1, b_lgp], writes=[b_QF])
                    kb.op(act, lambda a=a: ACT.activation(out=QBt[:, a, :], in_=cmi[:], func=AF.Exp, scale=lgp[:, a, 1:2]), reads=[b_cmi, b_lgp], writes=[b_QBt])
                tk, b_tk = kb.sbb([128, 16], F32, "tk", st)
                kb.op(act, lambda: ACT.activation(out=tk[:, 0:8], in_=lg[:, 0:8], func=AF.Exp, scale=pcol[:, 0:1]), reads=[b_lg, b_pcol], writes=[b_tk])
                kb.op(act, lambda: ACT.activation(out=tk[:, 8:16], in_=lg[:, 8:16], func=AF.Exp, scale=pcol[:, 1:2]), reads=[b_lg, b_pcol], writes=[b_tk])
                GC, b_GC = kb.sbb([128, 4, 2], F32, "GC", st)
                kb.op(act, lambda: ACT.activation(out=GC[:], in_=lgp[:], func=AF.Exp, scale=128.0), reads=[b_lgp], writes=[b_GC])
                Rf, b_Rf = kb.sbb([128, 4, 64], F32, "Rf", st)
                Rfb, b_Rfb = kb.sbb([128, 4, 128], BF16, "Rfb", st)
                kb.op(dve, lambda: V.memset(Rf[:], 0.0), writes=[b_Rf])
                kb.op(dve, lambda: V.memset(Rfb[:], 0.0), writes=[b_Rfb])
                work = dict(junk=kb.slots([128, 1024], BF16, 1, "junk", st), ss=kb.slots([128, 4], F32, 2, "ss", st),
                            xs=kb.slots([128, 1024], F32, 1, "xs", st))
                xts = kb.slots([128, 1024], F32, 2, "xt", st)
                xnTs = kb.slots([128, 8, 128], BF16, 2, "xnT", st)
                qks = kb.slots([128, 16, 64], F32, 2, "qk", st)
                qkrs = kb.slots([128, 16, 64], F32, 2, "qkr", st)
                tmps = kb.slots([128, 16, 32], F32, 4, "ropetmp", st)
                vbs = kb.slots([128, 512], BF16, 2, "vb", st)
                sgs = kb.slots([128, 512], F32, 2, "sg", st)
                qTs = kb.slots([128, 8, 128], BF16, 2, "qTz", st)
                for (q_, bq_) in qTs.items:
                    kb.op(dve, lambda q_=q_: V.memset(q_[:], 0.0), writes=[bq_])
                kTs = kb.slots([128, 4, 128], BF16, 2, "kT", st)
                qfTs = kb.slots([128, 4, 128], BF16, 2, "qfT", st)
                qbTs = kb.slots([128, 4, 128], BF16, 2, "qbT", st)
                kdfs = kb.slots([128, 8, 64], BF16, 2, "kdf", st)
                kdbs = kb.slots([128, 8, 64], BF16, 2, "kdb", st)
                sDs = kb.slots([128, 8, 128], BF16, 2, "sD", st)
                yps = kb.slots([128, 512], F32, 2, "yp", st)
                bbs = kb.slots([128, 4, 64], F32, 2, "bb", st)
                sqs = kb.slots([128, 5, 128], F32, 2, "sq", st)
                rsbs = kb.slots([128, 2, 128], F32, 2, "rsb", st)
                cns = kb.slots([128, 5, 128], BF16, 2, "cn", st)
                cqts = kb.slots([96, 128], F32, 2, "cqt", st)
                sqts = kb.slots([96, 128], F32, 2, "sqt", st)
                ckts = kb.slots([32, 128], F32, 2, "ckt", st)
                skts = kb.slots([32, 128], F32, 2, "skt", st)
                t1s = kb.slots([96, 8, 128], F32, 1, "t1", st)
                t2s = kb.slots([96, 8, 128], F32, 1, "t2", st)
                QTts = kb.slots([96, 8, 128], BF16, 2, "QTt", st)
                KNts = kb.slots([64, 8, 128], BF16, 2, "KNt", st)
                KRts = kb.slots([32, 128], BF16, 2, "KRt", st)
                kr1s = kb.slots([32, 128], F32, 2, "kr1", st)
                kr2s = kb.slots([32, 128], F32, 2, "kr2", st)
                Vts = kb.slots([128, 8, 65], BF16, 2, "Vt", st)
                for (vt_, bv_) in Vts.items:
                    kb.op(dve, lambda vt_=vt_: V.memset(vt_[:, :, 64:65], 1.0), writes=[bv_])
                sq2s = kb.slots([96, 1024], F32, 1, "sq2", st)
                sq3s = kb.slots([32, 128], F32, 1, "sq3", st)
                nrows = kb.slots([1, 2, 1024], F32, 1, "nrow", st)
                nbrs = kb.slots([1, 1024], BF16, 2, "nbr", st)
                one8, b_one8 = kb.sbb([1, 1024], BF16, "one8", st)
                kb.op(dve, lambda: V.memset(one8[:], 1.0), writes=[b_one8])
                krn = kb.slots([1, 128], F32, 2, "krn", st)

                def tile_gen(t):
                        r0 = t * 128
                        xt, bx = xts.next()
                        kb.dma(sp, xt[:], X[r0:r0 + 128, :], bx, writes=[bx])
                        xnT, bxnT = xnTs.next()
                        norm_transpose(xt[:], bx, nmw, b_nmw, xnT, bxnT, 0, work)
                        pq, bq = kb.bank(1)
                        pk, bk = kb.bank(1)
                        pv, bv = kb.bank(1)
                        pg, bg = kb.bank(1)
                        for n, (pp, bpp) in enumerate(((pq, bq), (pk, bk), (pv, bv), (pg, bg))):
                            for kc in range(8):
                                kb.op(pe, lambda pp=pp, n=n, kc=kc: PE.matmul(pp, lhsT=xnT[:, kc, :], rhs=Win[:, kc, n * 512:(n + 1) * 512],
                                                                            start=(kc == 0), stop=(kc == 7)),
                                      reads=[bxnT, b_Win], writes=bpp)
                        qk, bqk = qks.next()
                        qkf = qk[:].rearrange("p a d -> p (a d)")
                        kb.op(act, lambda: ACT.copy(out=qkf[:, 0:512], in_=pq), reads=bq, writes=[bqk])
                        kb.op(act, lambda: ACT.activation(out=qkf[:, 512:1024], in_=pk, func=AF.Copy, scale=0.125), reads=bk, writes=[bqk])
                        vb, bvb = vbs.next()
                        kb.op(dve, lambda: V.tensor_copy(out=vb[:], in_=pv), reads=bv, writes=[bvb])
                        sg, bsg = sgs.next()
                        kb.op(act, lambda: ACT.activation(out=sg[:], in_=pg, func=AF.Silu), reads=bg, writes=[bsg])
                        kb.dma(sp, Z["SG"][r0:r0 + 128, :], sg[:], bsg, reads=[bsg], writes=[ZB["SG"]])
                        qkr, bqkr = qkrs.next()
                        cosr, b_cosr = cosrs.next()
                        sinr, b_sinr = sinrs.next()
                        kb.dma(sp, cosr[:], A["c_cosr"][r0:r0 + 128, :], b_cosr, writes=[b_cosr])
                        kb.dma(sp, sinr[:], A["c_sinr"][r0:r0 + 128, :], b_sinr, writes=[b_sinr])
                        cb = cosr[:, :].unsqueeze(1).to_broadcast([128, 16, 32])
                        sbc = sinr[:, :].unsqueeze(1).to_broadcast([128, 16, 32])
                        ta, bta = tmps.next()
                        tb, btb = tmps.next()
                        tc_, btc = tmps.next()
                        td, btd = tmps.next()
                        kb.op(dve, lambda: V.tensor_tensor(out=ta[:], in0=qk[:, :, 0:32], in1=cb, op=ALU.mult), reads=[bqk, b_cosr], writes=[bta])
                        kb.op(dve, lambda: V.tensor_tensor(out=tb[:], in0=qk[:, :, 32:64], in1=sbc, op=ALU.mult), reads=[bqk, b_sinr], writes=[btb])
                        kb.op(dve, lambda: V.tensor_tensor(out=qkr[:, :, 0:32], in0=ta[:], in1=tb[:], op=ALU.subtract), reads=[bta, btb], writes=[bqkr])
                        kb.op(pool, lambda: G.tensor_tensor(out=tc_[:], in0=qk[:, :, 0:32], in1=sbc, op=ALU.mult), reads=[bqk, b_sinr], writes=[btc])
                        kb.op(pool, lambda: G.tensor_tensor(out=td[:], in0=qk[:, :, 32:64], in1=cb, op=ALU.mult), reads=[bqk, b_cosr], writes=[btd])
                        kb.op(pool, lambda: G.tensor_tensor(out=qkr[:, :, 32:64], in0=tc_[:], in1=td[:], op=ALU.add), reads=[btc, btd], writes=[bqkr])
                        qkrf = qkr[:].rearrange("p a d -> p (a d)")
                        pT, bpT = kb.bank(2)
                        for j in range(8):
                            kb.op(pe, lambda j=j: PE.transpose(pT[:, j * 128:(j + 1) * 128], qkrf[:, j * 128:(j + 1) * 128], ident[:]),
                                  reads=[bqkr, b_ident], writes=[bpT[j // 4]])
                        pTv = pT.rearrange("p (j t) -> p j t", j=8)
                        qT, bqT = qTs.next()
                        kT, bkT = kTs.next()
                        qfT, bqfT = qfTs.next()
                        qbT, bqbT = qbTs.next()
                        kb.op(act, lambda: ACT.copy(out=qT[0:64, 0:8:2, :], in_=pTv[0:64, 0:4, :]), reads=[bpT[0]], writes=[bqT])
                        kb.op(act, lambda: ACT.copy(out=qT[64:128, 1:8:2, :], in_=pTv[64:128, 0:4, :]), reads=[bpT[0]], writes=[bqT])
                        kb.op(act, lambda: ACT.copy(out=kT[:], in_=pTv[:, 4:8, :]), reads=[bpT[1]], writes=[bkT])
                        kb.op(dve, lambda: V.tensor_tensor(out=qfT[:], in0=pTv[:, 0:4, :], in1=QF[:], op=ALU.mult), reads=[bpT[0], b_QF], writes=[bqfT])
                        kb.op(dve, lambda: V.tensor_tensor(out=qbT[:], in0=pTv[:, 0:4, :], in1=QBt[:], op=ALU.mult), reads=[bpT[0], b_QBt], writes=[bqbT])
                        kb.dma(sp, Z["QB"][t], qbT[:].rearrange("p a t -> p (a t)"), bqbT, reads=[bqbT], writes=[ZB["QB"]])
                        kdf, bkdf = kdfs.next()
                        kdb, bkdb = kdbs.next()
                        kb.op(dve, lambda: V.tensor_tensor(out=kdf[:], in0=qkr[:, 8:16, :], in1=tk[:, 0:8].unsqueeze(2).to_broadcast([128, 8, 64]), op=ALU.mult),
                              reads=[bqkr, b_tk], writes=[bkdf])
                        kb.op(dve, lambda: V.tensor_tensor(out=kdb[:], in0=qkr[:, 8:16, :], in1=tk[:, 8:16].unsqueeze(2).to_broadcast([128, 8, 64]), op=ALU.mult),
                              reads=[bqkr, b_tk], writes=[bkdb])
                        yield
                        pS, bpS = kb.bank(2)
                        for h in range(8):
                            a, off = h // 2, (h % 2) * 64
                            kb.op(pe, lambda h=h, a=a, off=off: PE.matmul(pS[:, h * 128:(h + 1) * 128], lhsT=kT[:, a, :], rhs=qT[:, h, :],
                                                                          start=True, stop=True),
                                  reads=[bkT, bqT], writes=[bpS[h // 4]])
                        sD, bsD = sDs.next()
                        pSv = pS.rearrange("p (h t) -> p h t", h=8)
                        kb.op(dve, lambda: V.tensor_tensor(out=sD[:, 0:4, :], in0=pSv[:, 0:4, :], in1=DT[:, 0:4, :], op=ALU.mult), reads=[bpS[0], b_DT], writes=[bsD])
                        kb.op(dve, lambda: V.tensor_tensor(out=sD[:, 4:8, :], in0=pSv[:, 4:8, :], in1=DT[:, 4:8, :], op=ALU.mult), reads=[bpS[1], b_DT], writes=[bsD])
                        pY, bpY = kb.bank(1)
                        for a in range(4):
                            kb.op(pe, lambda a=a: PE.matmul(pY[:, a * 128:(a + 1) * 128], lhsT=qfT[:, a, :], rhs=Rfb[:, a, :], start=True, stop=False),
                                  reads=[bqfT, b_Rfb], writes=bpY)
                            for h in (2 * a, 2 * a + 1):
                                kb.op(pe, lambda h=h: PE.matmul(pY[:, h * 64:(h + 1) * 64], lhsT=sD[:, h, :], rhs=vb[:, h * 64:(h + 1) * 64], start=False, stop=(h % 2 == 1)),
                                      reads=[bsD, bvb], writes=bpY)
                        yp, byp = yps.next()
                        kb.op(act, lambda: ACT.copy(out=yp[:], in_=pY), reads=bpY, writes=[byp])
                        kb.dma(sp, Z["YP"][r0:r0 + 128, :], yp[:], byp, reads=[byp], writes=[ZB["YP"]])
                        pBf, bpBf = kb.bank(1)
                        pBb, bpBb = kb.bank(1)
                        kdf2 = kdf[:].rearrange("p h d -> p (h d)")
                        kdb2 = kdb[:].rearrange("p h d -> p (h d)")
                        for a in range(4):
                            kb.op(pe, lambda a=a: PE.matmul(pBf[:, a * 128:(a + 1) * 128], lhsT=kdf2[:, a * 128:(a + 1) * 128], rhs=vb[:, a * 128:(a + 1) * 128],
                                                            start=True, stop=True), reads=[bkdf, bvb], writes=bpBf)
                            kb.op(pe, lambda a=a: PE.matmul(pBb[:, a * 128:(a + 1) * 128], lhsT=kdb2[:, a * 128:(a + 1) * 128], rhs=vb[:, a * 128:(a + 1) * 128],
                                                            start=True, stop=True), reads=[bkdb, bvb], writes=bpBb)
                        pBfv = pBf.rearrange("p (a x) -> p a x", a=4)
                        pBbv = pBb.rearrange("p (a x) -> p a x", a=4)
                        kb.op(dve, lambda: V.tensor_tensor(out=Rf[:], in0=Rf[:], in1=GC[:, :, 0:1].to_broadcast([128, 4, 64]), op=ALU.mult),
                              reads=[b_Rf, b_GC], writes=[b_Rf])
                        kb.op(dve, lambda: V.tensor_tensor(out=Rf[0:64], in0=Rf[0:64], in1=pBfv[0:64, :, 0:64], op=ALU.add), reads=[b_Rf] + bpBf, writes=[b_Rf])
                        kb.op(dve, lambda: V.tensor_tensor(out=Rf[64:128], in0=Rf[64:128], in1=pBfv[64:128, :, 64:128], op=ALU.add), reads=[b_Rf] + bpBf, writes=[b_Rf])
                        kb.op(dve, lambda: V.tensor_copy(out=Rfb[0:64, :, 0:64], in_=Rf[0:64]), reads=[b_Rf], writes=[b_Rfb])
                        kb.op(dve, lambda: V.tensor_copy(out=Rfb[64:128, :, 64:128], in_=Rf[64:128]), reads=[b_Rf], writes=[b_Rfb])
                        bb, bbb = bbs.next()
                        kb.op(act, lambda: ACT.copy(out=bb[0:64], in_=pBbv[0:64, :, 0:64]), reads=bpBb, writes=[bbb])
                        kb.op(act, lambda: ACT.copy(out=bb[64:128], in_=pBbv[64:128, :, 64:128]), reads=bpBb, writes=[bbb])
                        kb.dma(sp, Z["BB"][t], bb[:].rearrange("p a v -> p (a v)"), bbb, reads=[bbb], writes=[ZB["BB"]])

                        pC, bpC = kb.bank(2)
                        for r in range(5):
                            for kc in range(8):
                                kb.op(pe, lambda r=r, kc=kc: PE.matmul(pC[:, r * 128:(r + 1) * 128], lhsT=Win[:, kc, 2048 + r * 128:2048 + (r + 1) * 128],
                                                                       rhs=xnT[:, kc, :], start=(kc == 0), stop=(kc == 7)),
                                      reads=[b_Win, bxnT], writes=[bpC[r // 4]])
                        sq, bsq = sqs.next()
                        kb.op(act, lambda: ACT.activation(out=sq[:, 0:4, :].rearrange("p r t -> p (r t)"), in_=pC[:, 0:512], func=AF.Square), reads=[bpC[0]], writes=[bsq])
                        kb.op(act, lambda: ACT.activation(out=sq[:, 4, :], in_=pC[:, 512:640], func=AF.Square), reads=[bpC[1]], writes=[bsq])
                        pN, bpN = kb.bank(1)
                        for r in range(3):
                            kb.op(pe, lambda r=r: PE.matmul(pN[:, 0:128], lhsT=onesf[:], rhs=sq[:, r, :], start=(r == 0), stop=(r == 2)), reads=[b_onesf, bsq], writes=bpN)
                        for r in range(2):
                            kb.op(pe, lambda r=r: PE.matmul(pN[:, 128:256], lhsT=onesf[:], rhs=sq[:, 3 + r, :], start=(r == 0), stop=(r == 1)), reads=[b_onesf, bsq], writes=bpN)
                        rsb, brsb = rsbs.next()
                        kb.op(act, lambda: ACT.activation(out=rsb[:, 0, :], in_=pN[:, 0:128], func=AF.Sqrt, scale=1.0 / 384, bias=epsn[:, 0:1]), reads=bpN + [b_epsn], writes=[brsb])
                        kb.op(act, lambda: ACT.activation(out=rsb[:, 1, :], in_=pN[:, 128:256], func=AF.Sqrt, scale=1.0 / 256, bias=epsn[:, 0:1]), reads=bpN + [b_epsn], writes=[brsb])
                        kb.op(dve, lambda: V.reciprocal(out=rsb[:], in_=rsb[:]), reads=[brsb], writes=[brsb])
                        cn, bcn = cns.next()
                        for r in range(5):
                            kb.op(dve, lambda r=r: V.scalar_tensor_tensor(out=cn[:, r, :], in0=pC[:, r * 128:(r + 1) * 128], scalar=nw5[:, r:r + 1],
                                                                          in1=rsb[:, 0 if r < 3 else 1, :], op0=ALU.mult, op1=ALU.mult),
                                  reads=[bpC[r // 4], b_nw5, brsb], writes=[bcn])
                        pQ, bpQ = kb.bank(2)
                        pQr, bpQr = kb.bank(2)
                        for h in range(8):
                            for r in range(3):
                                kb.op(pe, lambda h=h, r=r: PE.matmul(pQ[0:96, h * 128:(h + 1) * 128], lhsT=Wuq[:, r, h * 96:(h + 1) * 96], rhs=cn[:, r, :],
                                                                     start=(r == 0), stop=(r == 2)), reads=[b_Wuq, bcn], writes=[bpQ[h // 4]])
                            for r in range(3):
                                kb.op(pe, lambda h=h, r=r: PE.matmul(pQr[0:96, h * 128:(h + 1) * 128], lhsT=Wuqr[:, r, h * 96:(h + 1) * 96], rhs=cn[:, r, :],
                                                                     start=(r == 0), stop=(r == 2)), reads=[b_Wuqr, bcn], writes=[bpQr[h // 4]])
                        cqt, bcqt = cqts.next()
                        sqt, bsqt = sqts.next()
                        kb.dma(sp, cqt[:], A["c_cq"][:, r0:r0 + 128], bcqt, writes=[bcqt])
                        kb.dma(sp, sqt[:], A["c_sq"][:, r0:r0 + 128], bsqt, writes=[bsqt])
                        t1, bt1 = t1s.next()
                        t2, bt2 = t2s.next()
                        kb.op(dve, lambda: V.tensor_tensor(out=t1[:], in0=pQ[0:96, :].rearrange("p (h t) -> p h t", h=8),
                                                           in1=cqt[:].unsqueeze(1).to_broadcast([96, 8, 128]), op=ALU.mult), reads=bpQ + [bcqt], writes=[bt1])
                        kb.op(dve, lambda: V.tensor_tensor(out=t2[:], in0=pQr[0:96, :].rearrange("p (h t) -> p h t", h=8),
                                                           in1=sqt[:].unsqueeze(1).to_broadcast([96, 8, 128]), op=ALU.mult), reads=bpQr + [bsqt], writes=[bt2])
                        QTt, bQTt = QTts.next()
                        kb.op(dve, lambda: V.tensor_tensor(out=QTt[:], in0=t1[:], in1=t2[:], op=ALU.add), reads=[bt1, bt2], writes=[bQTt])
                        kb.dma(sp, Z["QT"][:, 0:96, r0:r0 + 128].rearrange("h f t -> f h t"), QTt[:], bQTt, reads=[bQTt], writes=[ZB["QT"]])
                        pKN, bpKN = kb.bank(2)
                        for h in range(8):
                            for c in range(2):
                                kb.op(pe, lambda h=h, c=c: PE.matmul(pKN[0:64, h * 128:(h + 1) * 128], lhsT=Wukv[:, c, 0, h, :], rhs=cn[:, 3 + c, :],
                                                                     start=(c == 0), stop=(c == 1)), reads=[b_Wukv, bcn], writes=[bpKN[h // 4]])
                        KNt, bKNt = KNts.next()
                        kb.op(act, lambda: ACT.copy(out=KNt[:].rearrange("p h t -> p (h t)"), in_=pKN[0:64, :]), reads=bpKN, writes=[bKNt])
                        kb.dma(sp, Z["KT"][:, 0:64, r0:r0 + 128].rearrange("h f t -> f h t"), KNt[:], bKNt, reads=[bKNt], writes=[ZB["KT"]])
                        pKR, bpKR = kb.bank(1)
                        for kc in range(8):
                            kb.op(pe, lambda kc=kc: PE.matmul(pKR[0:32, 0:128], lhsT=Win[:, kc, 2688:2720], rhs=xnT[:, kc, :], start=(kc == 0), stop=(kc == 7)),
                                  reads=[b_Win, bxnT], writes=bpKR)
                        for kc in range(8):
                            kb.op(pe, lambda kc=kc: PE.matmul(pKR[0:32, 128:256], lhsT=Wkrr[:, kc, :], rhs=xnT[:, kc, :], start=(kc == 0), stop=(kc == 7)),
                                  reads=[b_Wkrr, bxnT], writes=bpKR)
                        ckt, bckt = ckts.next()
                        skt, bskt = skts.next()
                        kb.dma(sp, ckt[:], A["c_ck"][:, r0:r0 + 128], bckt, writes=[bckt])
                        kb.dma(sp, skt[:], A["c_sk"][:, r0:r0 + 128], bskt, writes=[bskt])
                        kr1, bkr1 = kr1s.next()
                        kr2, bkr2 = kr2s.next()
                        kb.op(dve, lambda: V.tensor_tensor(out=kr1[:], in0=pKR[0:32, 0:128], in1=ckt[:], op=ALU.mult), reads=bpKR + [bckt], writes=[bkr1])
                        kb.op(dve, lambda: V.tensor_tensor(out=kr2[:], in0=pKR[0:32, 128:256], in1=skt[:], op=ALU.mult), reads=bpKR + [bskt], writes=[bkr2])
                        KRt, bKRt = KRts.next()
                        kb.op(dve, lambda: V.tensor_tensor(out=KRt[:], in0=kr1[:], in1=kr2[:], op=ALU.add), reads=[bkr1, bkr2], writes=[bKRt])
                        for h in range(8):
                            kb.dma(sp, Z["KT"][h, 64:96, r0:r0 + 128], KRt[:], bKRt, reads=[bKRt], writes=[ZB["KT"]])
                        pV, bpV = kb.bank(1)
                        for c in range(2):
                            kb.op(pe, lambda c=c: PE.matmul(pV, lhsT=cn[:, 3 + c, :], rhs=Wukv[:, c, 1, :, :].rearrange("p h d -> p (h d)"),
                                                            start=(c == 0), stop=(c == 1)), reads=[bcn, b_Wukv], writes=bpV)
                        Vt, bVt = Vts.next()
                        kb.op(act, lambda: ACT.copy(out=Vt[:, :, 0:64], in_=pV.rearrange("p (h d) -> p h d", h=8)), reads=bpV, writes=[bVt])
                        kb.dma(sp, Z["VS"][r0:r0 + 128, :], Vt[:].rearrange("p h d -> p (h d)"), bVt, reads=[bVt], writes=[ZB["VS"]])
                        sq2, bsq2 = sq2s.next()
                        nrow, bnrow = nrows.next()
                        kb.op(act, lambda: ACT.activation(out=sq2[:], in_=QTt[:].rearrange("p h t -> p (h t)"), func=AF.Square), reads=[bQTt], writes=[bsq2])
                        pR, bpR = kb.bank(2)
                        for n in range(2):
                            kb.op(pe, lambda n=n: PE.matmul(pR[0:1, n * 512:(n + 1) * 512], lhsT=onesf[0:96, 0:1], rhs=sq2[:, n * 512:(n + 1) * 512], start=True, stop=True),
                                  reads=[b_onesf, bsq2], writes=[bpR[n]])
                        kb.op(dve, lambda: V.tensor_copy(out=nrow[:, 0, :], in_=pR[0:1, :]), reads=bpR, writes=[bnrow])
                        nbr, bnbr = nbrs.next()
                        kb.op(act, lambda: ACT.activation(out=nrow[:, 0, :], in_=nrow[:, 0, :], func=AF.Sqrt), reads=[bnrow], writes=[bnrow])
                        kb.op(dve, lambda: V.tensor_scalar(out=nbr[:], in0=nrow[:, 0, :], scalar1=-1.0, scalar2=None, op0=ALU.mult), reads=[bnrow], writes=[bnbr])
                        kb.dma(sp, Z["QT"][:, 96:97, r0:r0 + 128].rearrange("h o t -> o h t"), nbr[:].rearrange("o (h t) -> o h t", h=8), bnbr, reads=[bnbr], writes=[ZB["QT"]])
                        kb.dma(sp, Z["KT"][:, 96:97, r0:r0 + 128].rearrange("h o t -> o h t"), one8[:].rearrange("o (h t) -> o h t", h=8), b_one8, reads=[b_one8], writes=[ZB["KT"]])
                        kb.op(act, lambda: ACT.activation(out=sq2[0:64, :], in_=KNt[:].rearrange("p h t -> p (h t)"), func=AF.Square), reads=[bKNt, bsq2], writes=[bsq2])
                        sq3, bsq3 = sq3s.next()
                        kb.op(act, lambda: ACT.activation(out=sq3[:], in_=KRt[:], func=AF.Square), reads=[bKRt], writes=[bsq3])
                        pR2, bpR2 = kb.bank(2)
                        for n in range(2):
                            kb.op(pe, lambda n=n: PE.matmul(pR2[0:1, n * 512:(n + 1) * 512], lhsT=onesf[0:64, 0:1], rhs=sq2[0:64, n * 512:(n + 1) * 512], start=True, stop=True),
                                  reads=[b_onesf, bsq2], writes=[bpR2[n]])
                        pR3, bpR3 = kb.bank(1)
                        kb.op(pe, lambda: PE.matmul(pR3[0:1, 0:128], lhsT=onesf[0:32, 0:1], rhs=sq3[:], start=True, stop=True), reads=[b_onesf, bsq3], writes=bpR3)
                        kr_, bkr_ = krn.next()
                        kb.op(dve, lambda: V.tensor_copy(out=kr_[:], in_=pR3[0:1, 0:128]), reads=bpR3, writes=[bkr_])
                        kb.op(dve, lambda: V.tensor_tensor(out=nrow[:, 1, :].rearrange("o (h t) -> o h t", h=8), in0=pR2[0:1, :].rearrange("o (h t) -> o h t", h=8),
                                                           in1=kr_[:].unsqueeze(1).to_broadcast([1, 8, 128]), op=ALU.add), reads=bpR2 + [bkr_], writes=[bnrow])
                        kb.dma(sp, Z["KN"][:, r0:r0 + 128].rearrange("(o h) t -> o h t", o=1), nrow[:, 1, :].rearrange("o (h t) -> o h t", h=8), bnrow, reads=[bnrow], writes=[ZB["KN"]])

                def exhaust(gen):
                    if gen is not None:
                        for _ in gen:
                            pass
                g_prev = None
                for t in range(nt):
                    g_cur = tile_gen(t)
                    next(g_cur)
                    exhaust(g_prev)
                    g_prev = g_cur
                exhaust(g_prev)
                kb.barrier()

        def phase2(si, S):
            nkt = S // 128
            QBLK = 512 if S >= 512 else S
            nqb = S // QBLK
            with ExitStack() as st:
                KThs = kb.slots([97, S], BF16, 2, "KTh", st)
                QThs = kb.slots([97, S], BF16, 2, "QTh", st)
                Vhs = kb.slots([128, nkt, 65], BF16, 2, "Vh", st)
                kn2, b_kn2 = kb.sbb([128, S // 128], F32, "kn2", st)
                kst, b_kst = kb.sbb([128, 4], F32, "kst", st)
                krow, b_krow = kb.sbb([1, 132], F32, "krow", st)
                kmx, b_kmx = kb.sbb([128, 1], F32, "kmx", st)
                PTs = kb.slots([128, QBLK], BF16, 5, "PT", st)
                osbs = kb.slots([65, QBLK], F32, 2, "osb", st)
                rds = kb.slots([64, QBLK], F32, 2, "rd", st)
                mos = kb.slots([64, QBLK], BF16, 2, "mo", st)
                Esel, b_Esel = kb.sbb([65, 64], F32, "Esel", st)
                kb.op(dve, lambda: V.memset(Esel[:], 0.0), writes=[b_Esel])
                kb.op(dve, lambda: V.memset(Esel[64:65, :], 1.0), writes=[b_Esel])
                kb.set_banks(range(2, 8))
                for h in range(8):
                    KTh, bKTh = KThs.next()
                    QTh, bQTh = QThs.next()
                    Vh, bVh = Vhs.next()
                    kb.dma(sp, KTh[:, :], Z["KT"][h, :, 0:S], bKTh, reads=[ZB["KT"]], writes=[bKTh])
                    kb.dma(sp, QTh[:, :], Z["QT"][h, :, 0:S], bQTh, reads=[ZB["QT"]], writes=[bQTh])
                    with nc.allow_non_contiguous_dma(reason="per-head V rows (130B)"):
                        kb.dma(sp, Vh[:], Z["VS"][0:S, h * 65:(h + 1) * 65].rearrange("(kt p) c -> p kt c", p=128), bVh, reads=[ZB["VS"]], writes=[bVh])
                    kb.dma(sp, kn2[:], Z["KN"][h, 0:S].rearrange("(p f) -> p f", p=128), b_kn2, reads=[ZB["KN"]], writes=[b_kn2])
                    kb.op(dve, lambda: V.tensor_reduce(out=kst[:, 0:1], in_=kn2[:], axis=AX.X, op=ALU.max), reads=[b_kn2], writes=[b_kst])
                    pK1, bpK1 = kb.bank(1)
                    kb.op(pe, lambda pK1=pK1: PE.transpose(pK1[0:1, 0:128], kst[:, 0:1], ident[:]), reads=[b_kst, b_ident], writes=bpK1)
                    kb.op(dve, lambda pK1=pK1: V.tensor_copy(out=krow[:, 0:128], in_=pK1[0:1, 0:128]), reads=bpK1, writes=[b_krow])
                    kb.op(dve, lambda: V.tensor_reduce(out=krow[:, 128:129], in_=krow[:, 0:128], axis=AX.X, op=ALU.max), reads=[b_krow], writes=[b_krow])
                    pK2, bpK2 = kb.bank(1)
                    kb.op(pe, lambda pK2=pK2: PE.matmul(pK2[:, 0:1], lhsT=onesf[0:1, :], rhs=krow[:, 128:129], start=True, stop=True), reads=[b_onesf, b_krow], writes=bpK2)
                    kb.op(act, lambda pK2=pK2: ACT.activation(out=kmx[:], in_=pK2[:, 0:1], func=AF.Sqrt), reads=bpK2, writes=[b_kmx])
                    kb.op(dve, lambda QTh=QTh: V.tensor_scalar(out=QTh[96:97, :], in0=QTh[96:97, :], scalar1=kmx[96:97, 0:1], scalar2=None, op0=ALU.mult),
                          reads=[bQTh, b_kmx], writes=[bQTh])
                    for qb in range(nqb):
                        q0 = qb * QBLK
                        pO, bpO = kb.fixed_bank(qb % 2, 1)
                        LA = 2
                        pend = []

                        def emit_s(kt):
                            pS, bpS = kb.bank(1)
                            kb.op(pe, lambda: PE.matmul(pS[:, 0:QBLK], lhsT=KTh[:, kt * 128:(kt + 1) * 128], rhs=QTh[:, q0:q0 + QBLK], start=True, stop=True),
                                  reads=[bKTh, bQTh], writes=bpS)
                            PT, bPT = PTs.next()
                            kb.op(act, lambda: ACT.activation(out=PT[:], in_=pS[:, 0:QBLK], func=AF.Exp), reads=bpS, writes=[bPT])
                            pend.append((kt, PT, bPT))

                        def emit_pv():
                            pkt, pPT, pbPT = pend.pop(0)
                            kb.op(pe, lambda: PE.matmul(pO[0:65, 0:QBLK], lhsT=Vh[:, pkt, :], rhs=pPT[:], start=(pkt == 0), stop=(pkt == nkt - 1)),
                                  reads=[bVh, pbPT], writes=bpO)
                        for kt in range(nkt):
                            emit_s(kt)
                            if len(pend) > LA:
                                emit_pv()
                        while pend:
                            emit_pv()
                        osb, bosb = osbs.next()
                        kb.op(dve, lambda: V.tensor_copy(out=osb[:], in_=pO[0:65, 0:QBLK]), reads=bpO, writes=[bosb])
                        pD, bpD = kb.bank(1)
                        kb.op(pe, lambda: PE.matmul(pD[0:64, 0:QBLK], lhsT=Esel[:], rhs=osb[:], start=True, stop=True), reads=[b_Esel, bosb], writes=bpD)
                        rd, brd = rds.next()
                        kb.op(dve, lambda: V.reciprocal(out=rd[:], in_=pD[0:64, 0:QBLK]), reads=bpD, writes=[brd])
                        mo, bmo = mos.next()
                        kb.op(dve, lambda: V.tensor_tensor(out=mo[:], in0=osb[0:64, :], in1=rd[:], op=ALU.mult), reads=[bosb, brd], writes=[bmo])
                        kb.dma(sp, Z["MO"][h, :, q0:q0 + QBLK], mo[:], bmo, reads=[bmo], writes=[ZB["MO"]])
                kb.barrier()

        def phase3a(si, S):
            nt = S // 128
            X = A[f"x{si}"]
            MEM = A[f"mem{si}"]
            kb.set_banks(range(8))
            with ExitStack() as st:
                work = dict(junk=kb.slots([128, 1024], BF16, 1, "junk", st), ss=kb.slots([128, 4], F32, 2, "ss", st),
                            xs=kb.slots([128, 1024], F32, 2, "xs", st))
                KmT, b_KmT = kb.sbb([128, 4, 2, 256], BF16, "KmT", st)
                Vm, b_Vm = kb.sbb([128, 2, 1024], BF16, "Vm", st)
                ncw, b_ncw = load_cols("norm_ca_w", 8, st)
                with ExitStack() as st2:
                    Wkv, b_Wkv = kb.sbb([128, 8, 2048], BF16, "Wkv", st2)
                    for kc in range(8):
                        load_cast(Wkv[:, kc, :], b_Wkv, A["ca_wkv"][0, kc * 128:(kc + 1) * 128, :])
                    nmm, b_nmm = load_cols("norm_mem_w", 8, st2)
                    memT, b_memT = kb.sbb([128, 8, 256], BF16, "memT", st2)
                    mts = kb.slots([128, 1024], F32, 2, "memt", st2)
                    for m in range(2):
                        mt, bmt = mts.next()
                        kb.dma(sp, mt[:], MEM[m * 128:(m + 1) * 128, :], bmt, writes=[bmt])
                        norm_transpose(mt[:], bmt, nmm, b_nmm, memT, b_memT, m * 128, work)
                    for j in range(8):
                        pK, bpK = kb.bank(1)
                        for kc in range(8):
                            kb.op(pe, lambda j=j, kc=kc, pK=pK: PE.matmul(pK[:, 0:256], lhsT=Wkv[:, kc, j * 128:(j + 1) * 128], rhs=memT[:, kc, :], start=(kc == 0), stop=(kc == 7)),
                                  reads=[b_Wkv, b_memT], writes=bpK)
                        kb.op(act, lambda j=j, pK=pK: ACT.copy(out=KmT[:, j // 2, j % 2, :], in_=pK[:, 0:256]), reads=bpK, writes=[b_KmT])
                    for mc in range(2):
                        for n in range(2):
                            pVm, bpVm = kb.bank(1)
                            for kc in range(8):
                                kb.op(pe, lambda mc=mc, n=n, kc=kc, pVm=pVm: PE.matmul(pVm, lhsT=memT[:, kc, mc * 128:(mc + 1) * 128], rhs=Wkv[:, kc, 1024 + n * 512:1024 + (n + 1) * 512],
                                                                                  start=(kc == 0), stop=(kc == 7)), reads=[b_memT, b_Wkv], writes=bpVm)
                            kb.op(act, lambda mc=mc, n=n, pVm=pVm: ACT.copy(out=Vm[:, mc, n * 512:(n + 1) * 512], in_=pVm), reads=bpVm, writes=[b_Vm])
                    kb.barrier()
                Wor, b_Wor = kb.sbb([128, 4, 1024], BF16, "Wor", st)
                for r in range(4):
                    load_cast(Wor[:, r, :], b_Wor, A["w_out"][0, r * 128:(r + 1) * 128, :])
                Wom, b_Wom = kb.sbb([64, 8, 1024], BF16, "Wom", st)
                for h in range(8):
                    load_cast(Wom[:, h, :], b_Wom, A["w_out"][0, 512 + h * 64:512 + (h + 1) * 64, :])
                Wq, b_Wq = kb.sbb([128, 8, 1024], BF16, "Wq", st)
                Wo, b_Wo = kb.sbb([128, 8, 1024], BF16, "Wo", st)
                for kc in range(8):
                    load_cast(Wq[:, kc, :], b_Wq, A["ca_wq"][0, kc * 128:(kc + 1) * 128, :])
                    load_cast(Wo[:, kc, :], b_Wo, A["ca_wo"][0, kc * 128:(kc + 1) * 128, :])
                gnw, b_gnw = kb.sbb([128, 512], F32, "gnw", st)
                kb.dma(sp, gnw[:], A["ret_gn_w"][0, :].partition_broadcast(128), b_gnw, writes=[b_gnw])
                dec, b_dec = kb.sbb([128, 8], F32, "dec3", st)
                kb.dma(sp, dec[:], A["ret_decay_bwd"][0, :].partition_broadcast(128), b_dec, writes=[b_dec])
                kb.op(act, lambda: ACT.activation(out=dec[:], in_=dec[:], func=AF.Exp, scale=-1.0), reads=[b_dec], writes=[b_dec])
                kb.op(act, lambda: ACT.activation(out=dec[:], in_=dec[:], func=AF.Ln, bias=1.0), reads=[b_dec], writes=[b_dec])
                kb.op(act, lambda: ACT.activation(out=dec[:], in_=dec[:], func=AF.Exp, scale=-128.0), reads=[b_dec], writes=[b_dec])
                GCb, b_GCb = kb.sbb([128, 4], F32, "GCb", st)
                decv = dec[:].rearrange("p (a two) -> p a two", two=2)
                kb.op(dve, lambda: V.tensor_copy(out=GCb[0:64, :], in_=decv[0:64, :, 0]), reads=[b_dec], writes=[b_GCb])
                kb.op(dve, lambda: V.tensor_copy(out=GCb[64:128, :], in_=decv[64:128, :, 1]), reads=[b_dec], writes=[b_GCb])
                Rb, b_Rb = kb.sbb([128, 4, 64], F32, "Rb", st)
                Rbb, b_Rbb = kb.sbb([128, 4, 128], BF16, "Rbb", st)
                kb.op(dve, lambda: V.memset(Rb[:], 0.0), writes=[b_Rb])
                kb.op(dve, lambda: V.memset(Rbb[:], 0.0), writes=[b_Rbb])
                yps = kb.slots([128, 512], F32, 2, "yp3", st)
                sgs = kb.slots([128, 512], F32, 2, "sg3", st)
                qbTs = kb.slots([128, 4, 128], BF16, 2, "qbT3", st)
                bbs = kb.slots([128, 4, 64], F32, 2, "bb3", st)
                xts = kb.slots([128, 1024], F32, 2, "xt3", st)
                mots = kb.slots([64, 8, 128], BF16, 2, "mot", st)
                ys = kb.slots([128, 8, 64], F32, 2, "y3", st)
                ycs = kb.slots([128, 8, 64], F32, 2, "yc3", st)
                y2s = kb.slots([128, 8, 64], F32, 2, "ysq3", st)
                sts = kb.slots([128, 8, 4], F32, 2, "st3", st)
                ros = kb.slots([128, 512], F32, 2, "ro3", st)
                roTs = kb.slots([128, 4, 128], BF16, 2, "roT", st)
                x1s = kb.slots([128, 1024], F32, 2, "x1", st)
                hnTs = kb.slots([128, 8, 128], BF16, 2, "hnT", st)
                qcTs = kb.slots([128, 8, 128], BF16, 2, "qcT", st)
                mxs = kb.slots([128, 4, 4], F32, 2, "mx3", st)
                Ps = kb.slots([128, 4, 256], F32, 2, "P3", st)
                PnTs = kb.slots([128, 8, 128], BF16, 2, "PnT", st)
                oTs = kb.slots([128, 8, 128], BF16, 2, "oT", st)
                x2s = kb.slots([128, 1024], F32, 2, "x2", st)
                def tile_gen(t):
                        r0 = t * 128
                        yp, byp = yps.next()
                        sg, bsg = sgs.next()
                        qbT, bqbT = qbTs.next()
                        bb, bbb = bbs.next()
                        xt, bx = xts.next()
                        mot, bmot = mots.next()
                        kb.dma(sp, yp[:], Z["YP"][r0:r0 + 128, :], byp, reads=[ZB["YP"]], writes=[byp])
                        kb.dma(sp, sg[:], Z["SG"][r0:r0 + 128, :], bsg, reads=[ZB["SG"]], writes=[bsg])
                        kb.dma(sp, qbT[:].rearrange("p a t -> p (a t)"), Z["QB"][t], bqbT, reads=[ZB["QB"]], writes=[bqbT])
                        kb.dma(sp, bb[:].rearrange("p a v -> p (a v)"), Z["BB"][t], bbb, reads=[ZB["BB"]], writes=[bbb])
                        kb.dma(sp, xt[:], X[r0:r0 + 128, :], bx, writes=[bx])
                        kb.dma(sp, mot[:], Z["MO"][:, :, r0:r0 + 128].rearrange("h v t -> v h t"), bmot, reads=[ZB["MO"]], writes=[bmot])
                        pY, bpY = kb.bank(1)
                        for a in range(4):
                            kb.op(pe, lambda a=a: PE.matmul(pY[:, a * 128:(a + 1) * 128], lhsT=qbT[:, a, :], rhs=Rbb[:, a, :], start=True, stop=True),
                                  reads=[bqbT, b_Rbb], writes=bpY)
                        y, by = ys.next()
                        kb.op(dve, lambda: V.tensor_tensor(out=y[:].rearrange("p h d -> p (h d)"), in0=pY, in1=yp[:], op=ALU.add), reads=bpY + [byp], writes=[by])
                        kb.op(dve, lambda: V.tensor_tensor(out=Rb[:], in0=Rb[:], in1=GCb[:, :].unsqueeze(2).to_broadcast([128, 4, 64]), op=ALU.mult), reads=[b_Rb, b_GCb], writes=[b_Rb])
                        kb.op(dve, lambda: V.tensor_tensor(out=Rb[:], in0=Rb[:], in1=bb[:], op=ALU.add), reads=[b_Rb, bbb], writes=[b_Rb])
                        kb.op(dve, lambda: V.tensor_copy(out=Rbb[0:64, :, 0:64], in_=Rb[0:64]), reads=[b_Rb], writes=[b_Rbb])
                        kb.op(dve, lambda: V.tensor_copy(out=Rbb[64:128, :, 64:128], in_=Rb[64:128]), reads=[b_Rb], writes=[b_Rbb])
                        stt, bst = sts.next()
                        yc, byc = ycs.next()
                        ysq, bysq = y2s.next()
                        kb.op(dve, lambda: V.tensor_reduce(out=stt[:, :, 0], in_=y[:], axis=AX.X, op=ALU.add), reads=[by], writes=[bst])
                        kb.op(dve, lambda: V.tensor_scalar(out=stt[:, :, 1], in0=stt[:, :, 0], scalar1=1.0 / 64, scalar2=None, op0=ALU.mult), reads=[bst], writes=[bst])
                        kb.op(dve, lambda: V.tensor_tensor(out=yc[:], in0=y[:], in1=stt[:, :, 1:2].to_broadcast([128, 8, 64]), op=ALU.subtract), reads=[by, bst], writes=[byc])
                        kb.op(pool, lambda: G.tensor_tensor(out=ysq[:], in0=yc[:], in1=yc[:], op=ALU.mult), reads=[byc], writes=[bysq])
                        kb.op(dve, lambda: V.tensor_reduce(out=stt[:, :, 2], in_=ysq[:], axis=AX.X, op=ALU.add), reads=[bysq], writes=[bst])
                        kb.op(act, lambda: ACT.activation(out=stt[:, :, 3], in_=stt[:, :, 2], func=AF.Sqrt, scale=1.0 / 64, bias=epsg[:, 0:1]), reads=[bst, b_epsg], writes=[bst])
                        kb.op(dve, lambda: V.reciprocal(out=stt[:, :, 3], in_=stt[:, :, 3]), reads=[bst], writes=[bst])
                        kb.op(dve, lambda: V.tensor_tensor(out=yc[:], in0=yc[:], in1=stt[:, :, 3:4].to_broadcast([128, 8, 64]), op=ALU.mult), reads=[byc, bst], writes=[byc])
                        ro, bro = ros.next()
                        kb.op(pool, lambda: G.tensor_tensor(out=ro[:], in0=yc[:].rearrange("p h d -> p (h d)"), in1=gnw[:], op=ALU.mult), reads=[byc, b_gnw], writes=[bro])
                        kb.op(pool, lambda: G.tensor_tensor(out=ro[:], in0=ro[:], in1=sg[:], op=ALU.mult), reads=[bro, bsg], writes=[bro])
                        pT, bpT = kb.bank(1)
                        for r in range(4):
                            kb.op(pe, lambda r=r: PE.transpose(pT[:, r * 128:(r + 1) * 128], ro[:, r * 128:(r + 1) * 128], ident[:]), reads=[bro, b_ident], writes=bpT)
                        roT, broT = roTs.next()
                        kb.op(act, lambda: ACT.copy(out=roT[:].rearrange("p r t -> p (r t)"), in_=pT), reads=bpT, writes=[broT])
                        pM, bpM = kb.bank(2)
                        for n in range(2):
                            for r in range(4):
                                kb.op(pe, lambda n=n, r=r: PE.matmul(pM[:, n * 512:(n + 1) * 512], lhsT=roT[:, r, :], rhs=Wor[:, r, n * 512:(n + 1) * 512], start=(r == 0), stop=False),
                                      reads=[broT, b_Wor], writes=[bpM[n]])
                            for h in range(8):
                                kb.op(pe, lambda n=n, h=h: PE.matmul(pM[:, n * 512:(n + 1) * 512], lhsT=mot[:, h, :], rhs=Wom[:, h, n * 512:(n + 1) * 512], start=False, stop=(h == 7)),
                                      reads=[bmot, b_Wom], writes=[bpM[n]])
                        x1, bx1 = x1s.next()
                        kb.op(dve, lambda: V.tensor_tensor(out=x1[:], in0=pM, in1=xt[:], op=ALU.add), reads=bpM + [bx], writes=[bx1])
                        yield
                        hnT, bhnT = hnTs.next()
                        norm_transpose(x1[:], bx1, ncw, b_ncw, hnT, bhnT, 0, work)
                        pQ, bpQ = kb.bank(2)
                        for j in range(8):
                            for kc in range(8):
                                kb.op(pe, lambda j=j, kc=kc: PE.matmul(pQ[:, j * 128:(j + 1) * 128], lhsT=Wq[:, kc, j * 128:(j + 1) * 128], rhs=hnT[:, kc, :], start=(kc == 0), stop=(kc == 7)),
                                      reads=[b_Wq, bhnT], writes=[bpQ[j // 4]])
                        qcT, bqcT = qcTs.next()
                        kb.op(act, lambda: ACT.activation(out=qcT[:].rearrange("p j t -> p (j t)"), in_=pQ, func=AF.Copy, scale=1.0 / 16), reads=bpQ, writes=[bqcT])
                        pS, bpS = kb.bank(2)
                        for hd in range(4):
                            for dc in range(2):
                                kb.op(pe, lambda hd=hd, dc=dc: PE.matmul(pS[:, hd * 256:(hd + 1) * 256], lhsT=qcT[:, hd * 2 + dc, :], rhs=KmT[:, hd, dc, :], start=(dc == 0), stop=(dc == 1)),
                                      reads=[bqcT, b_KmT], writes=[bpS[hd // 2]])
                        mx, bmx = mxs.next()
                        kb.op(dve, lambda: V.tensor_reduce(out=mx[:, :, 0], in_=pS.rearrange("p (h m) -> p h m", h=4), axis=AX.X, op=ALU.max), reads=bpS, writes=[bmx])
                        kb.op(dve, lambda: V.tensor_scalar(out=mx[:, :, 1], in0=mx[:, :, 0], scalar1=-1.0, scalar2=None, op0=ALU.mult), reads=[bmx], writes=[bmx])
                        P_, bP = Ps.next()
                        for hd in range(4):
                            kb.op(act, lambda hd=hd: ACT.activation(out=P_[:, hd, :], in_=pS[:, hd * 256:(hd + 1) * 256], func=AF.Exp, bias=mx[:, hd, 1:2], accum_out=mx[:, hd, 2:3]),
                                  reads=[bpS[hd // 2], bmx], writes=[bP, bmx])
                        kb.op(dve, lambda: V.reciprocal(out=mx[:, :, 3], in_=mx[:, :, 2]), reads=[bmx], writes=[bmx])
                        kb.op(dve, lambda: V.tensor_tensor(out=P_[:], in0=P_[:], in1=mx[:, :, 3:4].to_broadcast([128, 4, 256]), op=ALU.mult), reads=[bP, bmx], writes=[bP])
                        pPT, bpPT = kb.bank(2)
                        Pf = P_[:].rearrange("p h m -> p (h m)")
                        for j in range(8):
                            kb.op(pe, lambda j=j: PE.transpose(pPT[:, j * 128:(j + 1) * 128], Pf[:, j * 128:(j + 1) * 128], ident[:]), reads=[bP, b_ident], writes=[bpPT[j // 4]])
                        PnT, bPnT = PnTs.next()
                        kb.op(act, lambda: ACT.copy(out=PnT[:].rearrange("p j t -> p (j t)"), in_=pPT), reads=bpPT, writes=[bPnT])
                        pO, bpO = kb.bank(2)
                        for hd in range(4):
                            for dd in range(2):
                                for mc in range(2):
                                    kb.op(pe, lambda hd=hd, dd=dd, mc=mc: PE.matmul(pO[:, (hd * 2 + dd) * 128:(hd * 2 + dd + 1) * 128],
                                                                                    lhsT=Vm[:, mc, hd * 256 + dd * 128:hd * 256 + (dd + 1) * 128], rhs=PnT[:, hd * 2 + mc, :],
                                                                                    start=(mc == 0), stop=(mc == 1)), reads=[b_Vm, bPnT], writes=[bpO[(hd * 2 + dd) // 4]])
                        oT, boT = oTs.next()
                        kb.op(act, lambda: ACT.copy(out=oT[:].rearrange("p j t -> p (j t)"), in_=pO), reads=bpO, writes=[boT])
                        pC, bpC = kb.bank(2)
                        for n in range(2):
                            for j in range(8):
                                kb.op(pe, lambda n=n, j=j: PE.matmul(pC[:, n * 512:(n + 1) * 512], lhsT=oT[:, j, :], rhs=Wo[:, j, n * 512:(n + 1) * 512], start=(j == 0), stop=(j == 7)),
                                      reads=[boT, b_Wo], writes=[bpC[n]])
                        x2, bx2 = x2s.next()
                        kb.op(dve, lambda: V.tensor_tensor(out=x2[:], in0=pC, in1=x1[:], op=ALU.add), reads=bpC + [bx1], writes=[bx2])
                        kb.dma(sp, Z["X2"][r0:r0 + 128, :], x2[:], bx2, reads=[bx2], writes=[ZB["X2"]])

                def exhaust(gen):
                    if gen is not None:
                        for _ in gen:
                            pass
                g_prev = None
                for t in reversed(range(nt)):
                    g_cur = tile_gen(t)
                    next(g_cur)
                    exhaust(g_prev)
                    g_prev = g_cur
                exhaust(g_prev)
                kb.barrier()

        def phase3b(si, S):
            TB = 256
            nb = S // TB
            Y = A[f"y{si}"]
            with ExitStack() as st:
                work = dict(junk=kb.slots([128, 1024], BF16, 1, "junk", st), ss=kb.slots([128, 4], F32, 2, "ss", st),
                            xs=kb.slots([128, 1024], F32, 1, "xs", st))
                kb.set_banks(range(8))
                Wpq, b_Wpq = kb.sbb([128, 8, 2048], BF16, "Wpq", st)
                for kc in range(8):
                    load_cast(Wpq[:, kc, :], b_Wpq, A["peer_wq"][0, kc * 128:(kc + 1) * 128, :])
                nfw, b_nfw = load_cols("norm_ffn_w", 8, st)
                fnw, b_fnw = kb.sbb([128, 1024], F32, "fnw", st)
                kb.dma(sp, fnw[:], A["final_norm_w"].partition_broadcast(128), b_fnw, writes=[b_fnw])
                SKT, b_SKT = kb.sbb([128, 16, 128], BF16, "SKT", st)
                sks = kb.slots([128, 128], F32, 2, "skld", st)
                for g in range(16):
                    h, half = g // 2, g % 2
                    skt, bskt = sks.next()
                    kb.dma(sp, skt[:], A["peer_sub_keys"][0, half, h, :, :], bskt, writes=[bskt])
                    pT, bpT = kb.bank(1)
                    kb.op(pe, lambda skt=skt, pT=pT: PE.transpose(pT[:, 0:128], skt[:], ident[:]), reads=[bskt, b_ident], writes=bpT)
                    kb.op(act, lambda g=g, pT=pT: ACT.copy(out=SKT[:, g, :], in_=pT[:, 0:128]), reads=bpT, writes=[b_SKT])
                iota16, b_iota16 = kb.sbb([128, 16], F32, "iota16", st)
                kb.op(dve, lambda: V.tensor_copy(out=iota16[:], in_=iota[:, 0:16]), reads=[b_iota], writes=[b_iota16])
                x2ts = kb.slots([128, 2, 1024], F32, 2, "x2b", st)
                xn3Ts = kb.slots([128, 8, TB], BF16, 2, "xn3T", st)
                qpTs = kb.slots([128, 16, TB], BF16, 1, "qpT", st)
                ssbs = kb.slots([128, 16, 128], F32, 1, "ssb", st)
                reps = kb.slots([128, 256], F32, 2, "rep", st)
                v16s = kb.slots([128, 16, 16], F32, 1, "v16", st)
                ix16s = kb.slots([128, 16, 16], U32, 1, "ix16", st)
                ixfs = kb.slots([128, 16, 16], F32, 1, "ixf", st)
                cands = kb.slots([128, 8, 256], F32, 1, "cand", st)
                m2s = kb.slots([128, 8, 16], F32, 1, "m2", st)
                p2s = kb.slots([128, 8, 16], U32, 1, "p2", st)
                abs_ = kb.slots([128, 2, 128], U32, 1, "abu", st)
                abfs = kb.slots([128, 2, 128], F32, 1, "abf", st)
                ohs = kb.slots([128, 8, 16, 16], BF16, 1, "oh", st)
                sel3s = kb.slots([128, 3, 128], F32, 1, "sel3", st)
                zs = kb.slots([128, 8, 2], F32, 1, "z", st)
                T3s = kb.slots([128, 3, 128], F32, 2, "T3", st)
                Rs = kb.slots([128, 8, 128], BF16, 2, "Roh", st)
                L0s = kb.slots([128, 8, 128], BF16, 2, "L0oh", st)
                Gm, b_Gm = kb.sbb([128, TB, 128], BF16, "Gm", st)
                UTs = kb.slots([128, 8, 128], BF16, 3, "UTi", st)
                Vis = kb.slots([128, 1024], BF16, 3, "Vi", st)
                gas = kb.slots([128, TB], F32, 3, "ga", st)
                Wds = kb.slots([128, TB], BF16, 4, "Wd", st)
                x3s = kb.slots([128, 1024], F32, 1, "x3", st)

                def mkpool(idx):
                    return {"idx": list(idx), "i": 0}

                def pbank(pool, n=1):
                    L = len(pool["idx"])
                    while True:
                        i = pool["i"] % L
                        sel = pool["idx"][i:i + n]
                        if len(sel) == n and all(sel[k] == sel[0] + k for k in range(n)):
                            break
                        pool["i"] += 1
                    pool["i"] += n
                    b0 = sel[0]
                    return kb.PS[:, b0 * 512:(b0 + n) * 512], [kb.bank_bufs[bb] for bb in sel]
                poolR, poolM, poolG = mkpool([4, 5]), mkpool([6, 7]), mkpool([4, 5, 6, 7])
                bankR = lambda n=1: pbank(poolR, n)
                bankM = lambda n=1: pbank(poolM, n)
                bankG = lambda n=1: pbank(poolG, n)
                routed = {}

                def routing_gen(b):
                    r0 = b * TB
                    x2b, bx2b = x2ts.next()
                    kb.dma(sp, x2b[:], Z["X2"][r0:r0 + TB, :].rearrange("(t p) d -> p t d", p=128), bx2b, reads=[ZB["X2"]], writes=[bx2b])
                    xn3T, bxn3T = xn3Ts.next()
                    for tt in range(2):
                        norm_transpose(x2b[:, tt, :], bx2b, nfw, b_nfw, xn3T, bxn3T, tt * 128, work, bankfn=bankR)
                        yield
                    qpT, bqpT = qpTs.next()
                    for g in range(16):
                        pQ, bpQ = bankR(1)
                        for kc in range(8):
                            kb.op(pe, lambda: PE.matmul(pQ[:, 0:TB], lhsT=Wpq[:, kc, g * 128:(g + 1) * 128], rhs=xn3T[:, kc, :], start=(kc == 0), stop=(kc == 7)),
                                  reads=[b_Wpq, bxn3T], writes=bpQ)
                        kb.op(act, lambda: ACT.copy(out=qpT[:, g, :], in_=pQ[:, 0:TB]), reads=bpQ, writes=[bqpT])
                        yield
                    T3l = []
                    for tt in range(2):
                        ssb, bssb = ssbs.next()
                        for q4 in range(4):
                            pS, bpS = bankR(1)
                            for gg in range(4):
                                g = q4 * 4 + gg
                                kb.op(pe, lambda: PE.matmul(pS[:, gg * 128:(gg + 1) * 128], lhsT=qpT[:, g, tt * 128:(tt + 1) * 128], rhs=SKT[:, g, :], start=True, stop=True),
                                      reads=[bqpT, b_SKT], writes=bpS)
                            kb.op(act, lambda: ACT.copy(out=ssb[:, q4 * 4:(q4 + 1) * 4, :].rearrange("p g k -> p (g k)"), in_=pS), reads=bpS, writes=[bssb])
                            yield
                        v16, bv16 = v16s.next()
                        ix16, bix16 = ix16s.next()
                        for g in range(16):
                            src = ssb[:, g, :]
                            rep, brep = reps.next()
                            kb.op(dve, lambda: V.max(out=v16[:, g, 0:8], in_=src), reads=[bssb], writes=[bv16])
                            kb.op(dve, lambda: V.match_replace(out=rep[:, 0:128], in_to_replace=v16[:, g, 0:8], in_values=src, imm_value=NEG), reads=[bssb, bv16], writes=[brep])
                            kb.op(dve, lambda: V.max(out=v16[:, g, 8:16], in_=rep[:, 0:128]), reads=[brep], writes=[bv16])
                            kb.op(dve, lambda: V.max_index(out=ix16[:, g, 0:8], in_max=v16[:, g, 0:8], in_values=src), reads=[bssb, bv16], writes=[bix16])
                            kb.op(dve, lambda: V.max_index(out=ix16[:, g, 8:16], in_max=v16[:, g, 8:16], in_values=rep[:, 0:128]), reads=[brep, bv16], writes=[bix16])
                            yield
                        ixf, bixf = ixfs.next()
                        kb.op(dve, lambda: V.tensor_copy(out=ixf[:], in_=ix16[:]), reads=[bix16], writes=[bixf])
                        cand, bcand = cands.next()
                        v4 = v16[:].rearrange("p (h two) k -> p h two k", two=2)
                        kb.op(dve, lambda: V.tensor_tensor(out=cand[:].rearrange("p h (a b) -> p h a b", a=16),
                                                           in0=v4[:, :, 0, :].unsqueeze(3).to_broadcast([128, 8, 16, 16]),
                                                           in1=v4[:, :, 1, :].unsqueeze(2).to_broadcast([128, 8, 16, 16]), op=ALU.add), reads=[bv16], writes=[bcand])
                        yield
                        m2, bm2 = m2s.next()
                        p2, bp2 = p2s.next()
                        for h in range(8):
                            rep, brep = reps.next()
                            kb.op(dve, lambda: V.max(out=m2[:, h, 0:8], in_=cand[:, h, :]), reads=[bcand], writes=[bm2])
                            kb.op(dve, lambda: V.match_replace(out=rep[:], in_to_replace=m2[:, h, 0:8], in_values=cand[:, h, :], imm_value=NEG), reads=[bcand, bm2], writes=[brep])
                            kb.op(dve, lambda: V.max(out=m2[:, h, 8:16], in_=rep[:]), reads=[brep], writes=[bm2])
                            kb.op(dve, lambda: V.max_index(out=p2[:, h, 0:8], in_max=m2[:, h, 0:8], in_values=cand[:, h, :]), reads=[bcand, bm2], writes=[bp2])
                            kb.op(dve, lambda: V.max_index(out=p2[:, h, 8:16], in_max=m2[:, h, 8:16], in_values=rep[:]), reads=[brep, bm2], writes=[bp2])
                            yield
                        abu, babu = abs_.next()
                        abf, babf = abfs.next()
                        p2f = p2[:].rearrange("p h k -> p (h k)")
                        kb.op(dve, lambda: V.tensor_single_scalar(out=abu[:, 0, :], in_=p2f, scalar=4, op=ALU.logical_shift_right), reads=[bp2], writes=[babu])
                        kb.op(dve, lambda: V.tensor_single_scalar(out=abu[:, 1, :], in_=p2f, scalar=15, op=ALU.bitwise_and), reads=[bp2], writes=[babu])
                        kb.op(dve, lambda: V.tensor_copy(out=abf[:], in_=abu[:]), reads=[babu], writes=[babf])
                        yield
                        sel3, bsel3 = sel3s.next()
                        ix4 = ixf[:].rearrange("p (h two) k -> p h two k", two=2)
                        for w in range(2):
                            oh, boh = ohs.next()
                            kb.op(dve, lambda: V.tensor_tensor(out=oh[:], in0=abf[:, w, :].rearrange("p (h k) -> p h k", h=8).unsqueeze(3).to_broadcast([128, 8, 16, 16]),
                                                               in1=iota16[:].unsqueeze(1).unsqueeze(1).to_broadcast([128, 8, 16, 16]), op=ALU.is_equal),
                                  reads=[babf, b_iota16], writes=[boh])
                            yield
                            kb.op(dve, lambda: V.tensor_tensor(out=oh[:], in0=oh[:], in1=ix4[:, :, w, :].unsqueeze(2).to_broadcast([128, 8, 16, 16]), op=ALU.mult),
                                  reads=[boh, bixf], writes=[boh])
                            yield
                            kb.op(dve, lambda: V.tensor_reduce(out=sel3[:, w, :].rearrange("p (h k) -> p h k", h=8), in_=oh[:], axis=AX.X, op=ALU.add),
                                  reads=[boh], writes=[bsel3])
                            yield
                        z, bz = zs.next()
                        g3 = sel3[:, 2, :].rearrange("p (h k) -> p h k", h=8)
                        kb.op(dve, lambda: V.tensor_tensor(out=g3, in0=m2[:], in1=m2[:, :, 0:1].to_broadcast([128, 8, 16]), op=ALU.subtract), reads=[bm2], writes=[bsel3])
                        kb.op(act, lambda: ACT.activation(out=sel3[:, 2, :], in_=sel3[:, 2, :], func=AF.Exp), reads=[bsel3], writes=[bsel3])
                        kb.op(dve, lambda: V.tensor_reduce(out=z[:, :, 0], in_=g3, axis=AX.X, op=ALU.add), reads=[bsel3], writes=[bz])
                        kb.op(dve, lambda: V.reciprocal(out=z[:, :, 1], in_=z[:, :, 0]), reads=[bz], writes=[bz])
                        kb.op(dve, lambda: V.tensor_tensor(out=g3, in0=g3, in1=z[:, :, 1:2].to_broadcast([128, 8, 16]), op=ALU.mult), reads=[bsel3, bz], writes=[bsel3])
                        yield
                        pT, bpT = bankR(1)
                        for w in range(3):
                            kb.op(pe, lambda: PE.transpose(pT[:, w * 128:(w + 1) * 128], sel3[:, w, :], ident[:]), reads=[bsel3, b_ident], writes=bpT)
                        T3, bT3 = T3s.next()
                        kb.op(act, lambda: ACT.copy(out=T3[:].rearrange("p w t -> p (w t)"), in_=pT[:, 0:384]), reads=bpT, writes=[bT3])
                        T3l.append((T3, bT3))
                        yield
                    routed[b] = (x2b, bx2b, xn3T, bxn3T, T3l)

                def exhaust(gen):
                    if gen is not None:
                        for _ in gen:
                            pass

                pO0, bpO0 = kb.fixed_bank(0, 2)
                pO1, bpO1 = kb.fixed_bank(2, 2)
                pOs = ((pO0, bpO0), (pO1, bpO1))
                exhaust(routing_gen(0))
                for b in range(nb):
                    r0 = b * TB
                    x2b, bx2b, xn3T, bxn3T, T3l = routed.pop(b)
                    for tt in range(2):
                        T3, bT3 = T3l[tt]
                        for hf in range(16):
                            t0 = hf * 8
                            R_, bR = Rs.next()
                            L_, bL = L0s.next()
                            io_b = iota[:].unsqueeze(1).to_broadcast([128, 8, 128])
                            kb.op(dve, lambda: V.tensor_tensor(out=R_[:], in0=io_b, in1=T3[:, 1, t0:t0 + 8].unsqueeze(2).to_broadcast([128, 8, 128]), op=ALU.is_equal),
                                  reads=[b_iota, bT3], writes=[bR])
                            kb.op(dve, lambda: V.tensor_tensor(out=L_[:], in0=io_b, in1=T3[:, 0, t0:t0 + 8].unsqueeze(2).to_broadcast([128, 8, 128]), op=ALU.is_equal),
                                  reads=[b_iota, bT3], writes=[bL])
                            kb.op(dve, lambda: V.tensor_tensor(out=L_[:], in0=L_[:], in1=T3[:, 2, t0:t0 + 8].unsqueeze(2).to_broadcast([128, 8, 128]), op=ALU.mult),
                                  reads=[bL, bT3], writes=[bL])
                            for q in range(2):
                                pG, bpG = bankG(1)
                                for u in range(4):
                                    tl = q * 4 + u
                                    kb.op(pe, lambda: PE.matmul(pG[:, u * 128:(u + 1) * 128], lhsT=R_[:, tl, :], rhs=L_[:, tl, :], start=True, stop=True),
                                          reads=[bR, bL], writes=bpG)
                                tg = tt * 128 + t0 + q * 4
                                kb.op(act, lambda: ACT.copy(out=Gm[:, tg:tg + 4, :].rearrange("p t i -> p (t i)"), in_=pG), reads=bpG, writes=[b_Gm])
                    gen = routing_gen(b + 1) if b + 1 < nb else None

                    def emit_A(i):
                        UTi, bUTi = UTs.next()
                        Vi, bVi = Vis.next()
                        kb.dma(sp, UTi[:].rearrange("p k j -> p (k j)"), Z["UT"][i], bUTi, reads=[ZB["UT"]], writes=[bUTi])
                        kb.dma(sp, Vi[:], Z["VB"][i], bVi, reads=[ZB["VB"]], writes=[bVi])
                        pA, bpA = bankM(1)
                        for kc in range(8):
                            kb.op(pe, lambda: PE.matmul(pA[:, 0:TB], lhsT=UTi[:, kc, :], rhs=xn3T[:, kc, :], start=(kc == 0), stop=(kc == 7)),
                                  reads=[bUTi, bxn3T], writes=bpA)
                        ga, bga = gas.next()
                        kb.op(act, lambda: ACT.activation(out=ga[:], in_=pA[:, 0:TB], func=AF.Gelu), reads=bpA, writes=[bga])
                        Wd, bWd = Wds.next()
                        kb.op(dve, lambda: V.tensor_tensor(out=Wd[:], in0=ga[:], in1=Gm[:, :, i], op=ALU.mult), reads=[bga, b_Gm], writes=[bWd])
                        return (Wd, bWd, Vi, bVi)

                    def emit_O(i, cur):
                        Wd, bWd, Vi, bVi = cur
                        for tt in range(2):
                            pO, bpO = pOs[tt]
                            for n in range(2):
                                kb.op(pe, lambda: PE.matmul(pO[:, n * 512:(n + 1) * 512], lhsT=Wd[:, tt * 128:(tt + 1) * 128], rhs=Vi[:, n * 512:(n + 1) * 512],
                                                            start=(i == 0), stop=(i == NE - 1)), reads=[bWd, bVi], writes=[bpO[n]])
                    LAP = 2
                    pendA = []
                    for i in range(NE):
                        pendA.append((i, emit_A(i)))
                        if len(pendA) > LAP:
                            j, cur = pendA.pop(0)
                            emit_O(j, cur)
                        if gen is not None:
                            next(gen, None)
                    while pendA:
                        j, cur = pendA.pop(0)
                        emit_O(j, cur)
                    exhaust(gen)
                    for tt in range(2):
                        pO, bpO = pOs[tt]
                        x3, bx3 = x3s.next()
                        kb.op(dve, lambda: V.tensor_tensor(out=x3[:], in0=pO, in1=x2b[:, tt, :], op=ALU.add), reads=bpO + [bx2b], writes=[bx3])
                        junk, b_junk = work["junk"].next()
                        ss, b_ss = work["ss"].next()
                        kb.op(act, lambda: ACT.activation(out=junk[:], in_=x3[:], func=AF.Square, accum_out=ss[:, 0:1]), reads=[bx3], writes=[b_junk, b_ss])
                        kb.op(act, lambda: ACT.activation(out=ss[:, 1:2], in_=ss[:, 0:1], func=AF.Sqrt, scale=1.0 / D, bias=epsn[:, 0:1]), reads=[b_ss, b_epsn], writes=[b_ss])
                        kb.op(dve, lambda: V.reciprocal(out=ss[:, 2:3], in_=ss[:, 1:2]), reads=[b_ss], writes=[b_ss])
                        kb.op(dve, lambda: V.scalar_tensor_tensor(out=x3[:], in0=x3[:], scalar=ss[:, 2:3], in1=fnw[:], op0=ALU.mult, op1=ALU.mult),
                              reads=[bx3, b_ss, b_fnw], writes=[bx3])
                        kb.dma(sp, Y[r0 + tt * 128:r0 + (tt + 1) * 128, :], x3[:], bx3, reads=[bx3])
                kb.barrier()

        if prepass:
            kb.begin_phase()
            prepass_peer()
            kb.end_phase()
        nph = 0
        for si, S in enumerate(S_list):
            for ph in (phase1, phase2, phase3a, phase3b):
                if nph >= upto:
                    break
                nph += 1
                kb.set_banks(range(8))
                kb.begin_phase()
                ph(si, S)
                kb.end_phase()
        kb.barrier()
        kb.stats = {E.name: E.n for E in kb.engs}
        kb.stats["nsem"] = kb.nsem
        kb.stats["nops"] = kb.nops
        build.last_stats = kb.stats
    return nc


_CACHE = {}


def kernel(**inputs):
    ncores = 8
    S0 = inputs["x_prompt"].shape[1]
    S1 = inputs["x_sample"].shape[1]
    key = (S0, S1)
    if key not in _CACHE:
        _CACHE[key] = build([S0, S1])
    nc = _CACHE[key]
    consts = host_consts(max(S0, S1))
    in_maps = []
    for c in range(ncores):
        m = {"x0": np.ascontiguousarray(inputs["x_prompt"][c], dtype=np.float32),
             "x1": np.ascontiguousarray(inputs["x_sample"][c], dtype=np.float32),
             "mem0": np.ascontiguousarray(inputs["mem_prompt"][c], dtype=np.float32),
             "mem1": np.ascontiguousarray(inputs["mem_sample"][c], dtype=np.float32)}
        for n in W_NAMES:
            m[n] = np.ascontiguousarray(inputs[n], dtype=np.float32)
        m.update(consts)
        in_maps.append(m)
    res = run_bass_kernel_spmd(nc, in_maps, core_ids=list(range(ncores)))
    y0 = np.stack([np.asarray(res.results[c]["y0"], dtype=np.float32) for c in range(ncores)], 0)
    y1 = np.stack([np.asarray(res.results[c]["y1"], dtype=np.float32) for c in range(ncores)], 0)
    return (y0, y1)
```

```python
from contextlib import ExitStack
import numpy as np
import ml_dtypes
import concourse.bass as bass
import concourse.mybir as mybir
from concourse.bass_utils import run_bass_kernel_spmd

F32 = mybir.dt.float32
BF16 = mybir.dt.bfloat16
U32 = mybir.dt.uint32
AF = mybir.ActivationFunctionType
ALU = mybir.AluOpType
AX = mybir.AxisListType

SEM_LIMIT = 30000
D = 1024
NEG = -1.0e30


class Buf:
    __slots__ = ("name", "w", "r", "multi", "dsem", "dcnt", "excl")

    def __init__(self, name, multi=False, excl=False):
        self.name = name
        self.excl = excl
        self.w = {}
        self.r = {}
        self.multi = multi
        self.dsem = None
        self.dcnt = 0


class Eng:
    def __init__(self, kb, name, eng, own_wait=True):
        self.kb = kb
        self.name = name
        self.eng = eng
        self.sem = None
        self.val = 0
        self.seen = {}
        self.own_wait = own_wait
        self.n = 0

    def next_event(self):
        if self.sem is None or self.val >= SEM_LIMIT:
            self.sem = self.kb.new_sem(self.name)
            self.val = 0
        self.val += 1
        return (self.sem, self.val)


class Slots:
    def __init__(self, items):
        self.items = items
        self.i = 0

    def next(self):
        it = self.items[self.i % len(self.items)]
        self.i += 1
        return it


class KB:
    def __init__(self, nc, stack):
        self.nc = nc
        self.stack = stack
        self.nsem = 0
        self.pe = Eng(self, "pe", nc.tensor, own_wait=False)
        self.dve = Eng(self, "dve", nc.vector)
        self.act = Eng(self, "act", nc.scalar)
        self.pool = Eng(self, "pool", nc.gpsimd)
        self.sp = Eng(self, "sp", nc.sync)
        self.engs = [self.pe, self.dve, self.act, self.pool, self.sp]
        self.anchors = []
        self.free_sems = []
        self.nops = 0
        import os
        self.limit = int(os.environ.get("OPLIMIT", "1000000000"))
        self.phase_mark = 0
        self.persist = []
        self.uid = 0
        self.PS = None
        self.bank_bufs = None
        self.bank_pool = list(range(8))
        self.bank_i = 0

    def new_sem(self, name):
        self.nsem += 1
        return self.stack.enter_context(self.nc.semaphore(f"s{self.nsem}_{name}"))

    def sb(self, shape, dtype, name="sb", stack=None):
        self.uid += 1
        st = stack or self.stack
        return st.enter_context(self.nc.sbuf_tensor(f"{name}_{self.uid}", list(shape), dtype))

    def sbb(self, shape, dtype, name="sb", stack=None):
        return self.sb(shape, dtype, name, stack), Buf(name)

    def slots(self, shape, dtype, n, name, stack=None):
        return Slots([self.sbb(shape, dtype, f"{name}{i}", stack) for i in range(n)])

    def init_psum(self):
        self.PS = self.stack.enter_context(self.nc.psum_tensor("PSALL", [128, 8 * 512], F32))
        self.bank_bufs = [Buf(f"bank{i}", excl=True) for i in range(8)]

    def set_banks(self, pool):
        self.bank_pool = list(pool)
        self.bank_i = 0

    def bank(self, n=1):
        L = len(self.bank_pool)
        while True:
            i = self.bank_i % L
            idx = self.bank_pool[i:i + n]
            if len(idx) == n and all(idx[k] == idx[0] + k for k in range(n)):
                break
            self.bank_i += 1
        self.bank_i += n
        b0 = idx[0]
        return self.PS[:, b0 * 512:(b0 + n) * 512], [self.bank_bufs[b] for b in idx]

    def fixed_bank(self, b0, n=1):
        return self.PS[:, b0 * 512:(b0 + n) * 512], [self.bank_bufs[b] for b in range(b0, b0 + n)]

    def _waits(self, E, reads, writes):
        need = {}

        def add(d):
            for k, ev in d.items():
                if k not in need or need[k][1] < ev[1]:
                    need[k] = ev
        own = id(E.sem) if E.sem is not None else None
        for b in reads:
            add(b.w)
            if b.excl:
                add({k: ev for k, ev in b.r.items() if k != own})
        for b in writes:
            if not b.multi:
                add(b.w)
            add(b.r)
        for k, (sem, val) in need.items():
            if (not E.own_wait) and E.sem is not None and k == id(E.sem):
                continue
            if E.seen.get(k, 0) >= val:
                continue
            E.eng.wait_ge(sem, val)
            E.seen[k] = val

    def _record(self, ev, reads, writes):
        k = id(ev[0])
        for b in reads:
            if k not in b.r or b.r[k][1] < ev[1]:
                b.r[k] = ev
        for b in writes:
            if b.multi:
                if k not in b.w or b.w[k][1] < ev[1]:
                    b.w[k] = ev
            else:
                b.w = {k: ev}
            b.r = {}

    def op(self, E, fn, reads=(), writes=()):
        self.nops += 1
        if self.nops > self.limit:
            return None
        self._waits(E, reads, writes)
        inst = fn()
        ev = E.next_event()
        inst.then_inc(ev[0], 1)
        self._record(ev, reads, writes)
        E.n += 1
        return inst

    def dma(self, Q, out, in_, anchor, reads=(), writes=(), **kw):
        self.nops += 1
        if self.nops > self.limit:
            return None
        self._waits(Q, reads, writes)
        if anchor.dsem is None:
            if self.free_sems:
                anchor.dsem, anchor.dcnt = self.free_sems.pop()
            else:
                anchor.dsem, anchor.dcnt = self.new_sem("d_" + anchor.name), 0
            self.anchors.append(anchor)
        inst = Q.eng.dma_start(out=out, in_=in_, **kw)
        anchor.dcnt += 16
        ev = (anchor.dsem, anchor.dcnt)
        inst.then_inc(anchor.dsem, 16)
        self._record(ev, reads, writes)
        Q.n += 1
        return inst

    def barrier(self):
        evs = {}
        for E in self.engs:
            if E.sem is not None:
                evs[id(E.sem)] = (E.sem, E.val)
        for a in self.anchors:
            evs[id(a.dsem)] = (a.dsem, a.dcnt)
        S = self.sp
        for k, (sem, val) in evs.items():
            if S.seen.get(k, 0) >= val:
                continue
            S.eng.wait_ge(sem, val)
            S.seen[k] = val
        inst = S.eng.nop()
        ev = S.next_event()
        inst.then_inc(ev[0], 1)
        for E in self.engs:
            if E is S:
                continue
            E.eng.wait_ge(ev[0], ev[1])
            E.seen[id(ev[0])] = ev[1]
            for k, (sem, val) in evs.items():
                if E.seen.get(k, 0) < val:
                    E.seen[k] = val
        S.seen[id(ev[0])] = ev[1]


    def begin_phase(self):
        self.phase_mark = len(self.anchors)

    def end_phase(self):
        self.barrier()
        dead = self.anchors[self.phase_mark:]
        self.anchors = self.anchors[:self.phase_mark]
        for a in dead:
            if a.dcnt <= 20000:
                self.free_sems.append((a.dsem, a.dcnt))
            a.dsem = None
        for b in self.persist:
            b.w = {}
            b.r = {}


W_NAMES = ["norm_mix_w", "w_in", "ret_decay_fwd", "ret_decay_bwd", "ret_gn_w", "mla_q_norm_w", "mla_w_uq",
           "mla_kv_norm_w", "mla_w_ukv", "w_out", "norm_ca_w", "norm_mem_w", "ca_wq", "ca_wkv", "ca_wo",
           "norm_ffn_w", "peer_wq", "peer_sub_keys", "peer_u", "peer_v", "final_norm_w"]
W_SHAPES = {
    "norm_mix_w": [1, 1024], "w_in": [1, 1024, 2720], "ret_decay_fwd": [1, 8], "ret_decay_bwd": [1, 8],
    "ret_gn_w": [1, 512], "mla_q_norm_w": [1, 384], "mla_w_uq": [1, 384, 768], "mla_kv_norm_w": [1, 256],
    "mla_w_ukv": [1, 256, 1024], "w_out": [1, 1024, 1024], "norm_ca_w": [1, 1024], "norm_mem_w": [1, 1024],
    "ca_wq": [1, 1024, 1024], "ca_wkv": [1, 1024, 2048], "ca_wo": [1, 1024, 1024], "norm_ffn_w": [1, 1024],
    "peer_wq": [1, 1024, 2048], "peer_sub_keys": [1, 2, 8, 128, 128], "peer_u": [1, 16384, 1024],
    "peer_v": [1, 16384, 1024], "final_norm_w": [1024],
}


def host_consts(smax):
    pos = np.arange(smax, dtype=np.float32)
    c = {}
    invf = (1.0 / (np.float32(10000.0) ** (np.arange(0, 64, 2, dtype=np.float32) / np.float32(64)))).astype(np.float32)
    ang = (pos[:, None] * invf[None, :]).astype(np.float32)
    c["c_cosr"] = np.cos(ang).astype(np.float32)
    c["c_sinr"] = np.sin(ang).astype(np.float32)
    invm = (1.0 / (np.float32(10000.0) ** (np.arange(0, 32, 2, dtype=np.float32) / np.float32(32)))).astype(np.float32)
    angm = (pos[:, None] * invm[None, :]).astype(np.float32)
    cm = np.cos(angm).astype(np.float32).T
    sm = np.sin(angm).astype(np.float32).T
    ck = np.concatenate([cm, cm], 0)
    sk = np.concatenate([sm, sm], 0)
    sc = np.float32(96.0 ** -0.5)
    c["c_cq"] = np.concatenate([np.full((64, smax), sc, np.float32), ck * sc], 0).astype(np.float32)
    c["c_sq"] = np.concatenate([np.zeros((64, smax), np.float32), sk * sc], 0).astype(np.float32)
    c["c_ck"] = np.ascontiguousarray(ck)
    c["c_sk"] = np.ascontiguousarray(sk)
    io = np.arange(128, dtype=np.float32)
    c["c_iota"] = np.tile(io[None, :], (128, 1)).astype(np.float32)
    c["c_diff"] = (io[None, :] - io[:, None]).astype(np.float32)
    c["c_pcol"] = np.stack([127.0 - io, io], 1).astype(np.float32)
    c["c_ident"] = np.eye(128, dtype=np.float32)
    return c


CONST_SHAPES = lambda smax: {"c_cosr": [smax, 32], "c_sinr": [smax, 32], "c_cq": [96, smax], "c_sq": [96, smax],
                             "c_ck": [32, smax], "c_sk": [32, smax], "c_iota": [128, 128], "c_diff": [128, 128],
                             "c_pcol": [128, 2], "c_ident": [128, 128]}


def build(S_list, n_exp_chunks=128, debug=False, upto=99, prepass=True):
    nc = bass.Bass("TRN2", target_bir_lowering=False)
    smax = max(S_list)
    NE = n_exp_chunks
    A = {}
    for i, S in enumerate(S_list):
        A[f"x{i}"] = nc.dram_tensor(f"x{i}", [S, D], F32, kind="ExternalInput").ap()
        A[f"mem{i}"] = nc.dram_tensor(f"mem{i}", [256, D], F32, kind="ExternalInput").ap()
        A[f"y{i}"] = nc.dram_tensor(f"y{i}", [S, D], F32, kind="ExternalOutput").ap()
    for n in W_NAMES:
        A[n] = nc.dram_tensor(n, W_SHAPES[n], F32, kind="ExternalInput").ap()
    for n, shp in CONST_SHAPES(smax).items():
        A[n] = nc.dram_tensor(n, shp, F32, kind="ExternalInput").ap()
    skind = "ExternalOutput" if debug else "Internal"
    nt_max = smax // 128

    def scr(name, shape, dt):
        return nc.dram_tensor("z_" + name, shape, dt, kind=skind).ap()
    Z = dict(
        SG=scr("SG", [smax, 512], F32), YP=scr("YP", [smax, 512], F32),
        QB=scr("QB", [nt_max, 128, 512], BF16), BB=scr("BB", [nt_max, 128, 256], F32),
        QT=scr("QT", [8, 97, smax], BF16), KT=scr("KT", [8, 97, smax], BF16),
        VS=scr("VS", [smax, 8 * 65], BF16), KN=scr("KN", [8, smax], F32),
        MO=scr("MO", [8, 64, smax], BF16), X2=scr("X2", [smax, D], F32),
        UT=scr("UT", [128, 128, 1024], BF16), VB=scr("VB", [128, 128, 1024], BF16),
    )
    ZB = {k: Buf("z" + k, multi=True) for k in Z}

    with ExitStack() as top:
        kb = KB(nc, top)
        kb.init_psum()
        pe, dve, act, pool, sp = kb.pe, kb.dve, kb.act, kb.pool, kb.sp
        V = nc.vector
        ACT = nc.scalar
        G = nc.gpsimd
        PE = nc.tensor

        ident, b_ident = kb.sbb([128, 128], F32, "ident")
        iota, b_iota = kb.sbb([128, 128], F32, "iota")
        onesf, b_onesf = kb.sbb([128, 128], F32, "onesf")
        epsn, b_epsn = kb.sbb([128, 1], F32, "epsn")
        epsg, b_epsg = kb.sbb([128, 1], F32, "epsg")
        kb.dma(sp, ident[:], A["c_ident"][:, :], b_ident, writes=[b_ident])
        kb.dma(sp, iota[:], A["c_iota"][:, :], b_iota, writes=[b_iota])
        kb.op(dve, lambda: V.memset(onesf[:], 1.0), writes=[b_onesf])
        kb.op(dve, lambda: V.memset(epsn[:], 1e-6), writes=[b_epsn])
        kb.op(dve, lambda: V.memset(epsg[:], 1e-5), writes=[b_epsg])
        kb.persist = list(ZB.values()) + [b_ident, b_iota, b_onesf, b_epsn, b_epsg]
        kb.barrier()

        def load_cast(dst, dst_buf, src, stack=None):
            kb.dma(pool, dst, src, dst_buf, writes=[dst_buf])

        def norm_transpose(xt, bx, wcol, b_wcol, outT, b_outT, col0, work, bankfn=None):
            junk, b_junk = work["junk"].next()
            ss, b_ss = work["ss"].next()
            xs, b_xs = work["xs"].next()
            kb.op(act, lambda: ACT.activation(out=junk[:], in_=xt, func=AF.Square, accum_out=ss[:, 0:1]),
                  reads=[bx], writes=[b_junk, b_ss])
            kb.op(act, lambda: ACT.activation(out=ss[:, 1:2], in_=ss[:, 0:1], func=AF.Sqrt, scale=1.0 / D, bias=epsn[:, 0:1]),
                  reads=[b_ss, b_epsn], writes=[b_ss])
            kb.op(dve, lambda: V.reciprocal(out=ss[:, 2:3], in_=ss[:, 1:2]), reads=[b_ss], writes=[b_ss])
            kb.op(act, lambda: ACT.activation(out=xs[:], in_=xt, func=AF.Copy, scale=ss[:, 2:3]),
                  reads=[bx, b_ss], writes=[b_xs])
            pT, bpT = (bankfn or kb.bank)(2)
            for kc in range(8):
                kb.op(pe, lambda kc=kc: PE.transpose(pT[:, kc * 128:(kc + 1) * 128], xs[:, kc * 128:(kc + 1) * 128], ident[:]),
                      reads=[b_xs, b_ident], writes=[bpT[kc // 4]])
            pTv = pT.rearrange("p (k t) -> p k t", k=8)
            kb.op(dve, lambda: V.tensor_tensor(out=outT[:, :, col0:col0 + 128], in0=pTv,
                                               in1=wcol[:, :].unsqueeze(2).to_broadcast([128, 8, 128]), op=ALU.mult),
                  reads=bpT + [b_wcol], writes=[b_outT])
            return ss, b_ss

        def load_cols(name, ncols, stack):
            t, b = kb.sbb([128, ncols], F32, name, stack)
            src = A[name].rearrange("o (c p) -> p (o c)", p=128) if len(W_SHAPES[name]) == 2 else A[name].rearrange("(c p) -> p c", p=128)
            with nc.allow_non_contiguous_dma(reason="tiny per-feature vector"):
                kb.dma(sp, t[:], src, b, writes=[b])
            return t, b

        def prepass_peer():
            with ExitStack() as st:
                us = kb.slots([128, 1024], F32, 2, "pp_u", st)
                uts = kb.slots([128, 1024], BF16, 2, "pp_ut", st)
                vs = kb.slots([128, 1024], BF16, 2, "pp_v", st)
                for i in range(NE):
                    ut, bu = us.next()
                    kb.dma(sp, ut[:], A["peer_u"][0, i * 128:(i + 1) * 128, :], bu, writes=[bu])
                    pT, bpT = kb.bank(2)
                    for kc in range(8):
                        kb.op(pe, lambda kc=kc: PE.transpose(pT[:, kc * 128:(kc + 1) * 128], ut[:, kc * 128:(kc + 1) * 128], ident[:]),
                              reads=[bu, b_ident], writes=[bpT[kc // 4]])
                    utt, butt = uts.next()
                    kb.op(act, lambda: ACT.copy(out=utt[:, 0:512], in_=pT[:, 0:512]), reads=[bpT[0]], writes=[butt])
                    kb.op(dve, lambda: V.tensor_copy(out=utt[:, 512:1024], in_=pT[:, 512:1024]), reads=[bpT[1]], writes=[butt])
                    kb.dma(pool, Z["UT"][i], utt[:], butt, reads=[butt], writes=[ZB["UT"]])
                    vt, bv = vs.next()
                    load_cast(vt[:], bv, A["peer_v"][0, i * 128:(i + 1) * 128, :])
                    kb.dma(pool, Z["VB"][i], vt[:], bv, reads=[bv], writes=[ZB["VB"]])
                kb.barrier()

        def phase1(si, S):
            nt = S // 128
            X = A[f"x{si}"]
            with ExitStack() as st:
                Win, b_Win = kb.sbb([128, 8, 2720], BF16, "Win", st)
                for kc in range(8):
                    for (c0, c1) in ((0, 1360), (1360, 2720)):
                        load_cast(Win[:, kc, c0:c1], b_Win, A["w_in"][0, kc * 128:(kc + 1) * 128, c0:c1])
                Wuq, b_Wuq = kb.sbb([128, 3, 768], BF16, "Wuq", st)
                load_cast(Wuq[:], b_Wuq, A["mla_w_uq"][0].rearrange("(r p) n -> p r n", p=128))
                Wuqr, b_Wuqr = kb.sbb([128, 3, 768], BF16, "Wuqr", st)
                Wukv, b_Wukv = kb.sbb([128, 2, 2, 8, 64], BF16, "Wukv", st)
                for c in range(2):
                    for t in range(2):
                        load_cast(Wukv[:, c, t, :, :], b_Wukv,
                                  A["mla_w_ukv"][0, c * 128:(c + 1) * 128, :].rearrange("p (h t d) -> p t h d", h=8, t=2)[:, t, :, :])
                Wkrr, b_Wkrr = kb.sbb([128, 8, 32], BF16, "Wkrr", st)
                kb.op(dve, lambda: V.memset(Wuqr[:], 0.0), writes=[b_Wuqr])
                Wuq4 = Wuq[:].rearrange("p r (h f) -> p r h f", h=8)
                Wuqr4 = Wuqr[:].rearrange("p r (h f) -> p r h f", h=8)
                for r in range(3):
                    kb.op(dve, lambda r=r: V.tensor_scalar(out=Wuqr4[:, r, :, 64:80], in0=Wuq4[:, r, :, 80:96], scalar1=-1.0, scalar2=None, op0=ALU.mult),
                          reads=[b_Wuq], writes=[b_Wuqr])
                    kb.op(dve, lambda r=r: V.tensor_copy(out=Wuqr4[:, r, :, 80:96], in_=Wuq4[:, r, :, 64:80]),
                          reads=[b_Wuq], writes=[b_Wuqr])
                kb.op(dve, lambda: V.tensor_scalar(out=Wkrr[:, :, 0:16], in0=Win[:, :, 2704:2720], scalar1=-1.0, scalar2=None, op0=ALU.mult),
                      reads=[b_Win], writes=[b_Wkrr])
                kb.op(dve, lambda: V.tensor_copy(out=Wkrr[:, :, 16:32], in_=Win[:, :, 2688:2704]), reads=[b_Win], writes=[b_Wkrr])
                nmw, b_nmw = load_cols("norm_mix_w", 8, st)
                nw5, b_nw5 = kb.sbb([128, 5], F32, "nw5", st)
                with nc.allow_non_contiguous_dma(reason="tiny per-feature vector"):
                    kb.dma(sp, nw5[:, 0:3], A["mla_q_norm_w"].rearrange("o (c p) -> p (o c)", p=128), b_nw5, writes=[b_nw5])
                    kb.dma(sp, nw5[:, 3:5], A["mla_kv_norm_w"].rearrange("o (c p) -> p (o c)", p=128), b_nw5, writes=[b_nw5])
                cosrs = kb.slots([128, 32], F32, 2, "cosr", st)
                sinrs = kb.slots([128, 32], F32, 2, "sinr", st)
                dec, b_dec = kb.sbb([128, 16], F32, "dec", st)
                kb.dma(sp, dec[:, 0:8], A["ret_decay_fwd"][0, :].partition_broadcast(128), b_dec, writes=[b_dec])
                kb.dma(sp, dec[:, 8:16], A["ret_decay_bwd"][0, :].partition_broadcast(128), b_dec, writes=[b_dec])
                lg, b_lg = kb.sbb([128, 16], F32, "lg", st)
                kb.op(act, lambda: ACT.activation(out=lg[:], in_=dec[:], func=AF.Exp, scale=-1.0), reads=[b_dec], writes=[b_lg])
                kb.op(act, lambda: ACT.activation(out=lg[:], in_=lg[:], func=AF.Ln, bias=1.0), reads=[b_lg], writes=[b_lg])
                kb.op(dve, lambda: V.tensor_scalar(out=lg[:], in0=lg[:], scalar1=-1.0, scalar2=None, op0=ALU.mult), reads=[b_lg], writes=[b_lg])
                lgp, b_lgp = kb.sbb([128, 4, 2], F32, "lgp", st)
                lgv = lg[:].rearrange("p (d a two) -> p a d two", d=2, two=2)
                kb.op(dve, lambda: V.tensor_copy(out=lgp[0:64, :, :], in_=lgv[0:64, :, :, 0]), reads=[b_lg], writes=[b_lgp])
                kb.op(dve, lambda: V.tensor_copy(out=lgp[64:128, :, :], in_=lgv[64:128, :, :, 1]), reads=[b_lg], writes=[b_lgp])
                diff, b_diff = kb.sbb([128, 128], F32, "diff", st)
                kb.dma(sp, diff[:], A["c_diff"][:, :], b_diff, writes=[b_diff])
                pcol, b_pcol = kb.sbb([128, 2], F32, "pcol", st)
                kb.dma(sp, pcol[:], A["c_pcol"][:, :], b_pcol, writes=[b_pcol])
                dpos, b_dpos = kb.sbb([128, 128], F32, "dpos", st)
                dneg, b_dneg = kb.sbb([128, 128], F32, "dneg", st)
                kb.op(dve, lambda: V.tensor_scalar(out=dpos[:], in0=diff[:], scalar1=0.0, scalar2=None, op0=ALU.max), reads=[b_diff], writes=[b_dpos])
                kb.op(dve, lambda: V.tensor_scalar(out=dneg[:], in0=diff[:], scalar1=-1.0, scalar2=0.0, op0=ALU.mult, op1=ALU.max), reads=[b_diff], writes=[b_dneg])
                DT, b_DT = kb.sbb([128, 8, 128], F32, "DT", st)
                tmpE, b_tmpE = kb.sbb([128, 128], F32, "tmpE", st)
                for h in range(8):
                    kb.op(act, lambda h=h: ACT.activation(out=DT[:, h, :], in_=dpos[:], func=AF.Exp, scale=lg[:, h:h + 1]),
                          reads=[b_dpos, b_lg], writes=[b_DT])
                    kb.op(pool, lambda h=h: G.affine_select(out=DT[:, h, :], in_=DT[:, h, :], pattern=[[1, 128]], compare_op=ALU.is_ge,
                                                            fill=0.0, base=0, channel_multiplier=-1), reads=[b_DT], writes=[b_DT])
                    kb.op(act, lambda h=h: ACT.activation(out=tmpE[:], in_=dneg[:], func=AF.Exp, scale=lg[:, 8 + h:9 + h]),
                          reads=[b_dneg, b_lg], writes=[b_tmpE])
                    kb.op(pool, lambda: G.affine_select(out=tmpE[:], in_=tmpE[:], pattern=[[-1, 128]], compare_op=ALU.is_gt,
                                                        fill=0.0, base=0, channel_multiplier=1), reads=[b_tmpE], writes=[b_tmpE])
                    kb.op(dve, lambda h=h: V.tensor_tensor(out=DT[:, h, :], in0=DT[:, h, :], in1=tmpE[:], op=ALU.add),
                          reads=[b_DT, b_tmpE], writes=[b_DT])
                ip1, b_ip1 = kb.sbb([128, 128], F32, "ip1", st)
                cmi, b_cmi = kb.sbb([128, 128], F32, "cmi", st)
                kb.op(dve, lambda: V.tensor_scalar(out=ip1[:], in0=iota[:], scalar1=1.0, scalar2=None, op0=ALU.add), reads=[b_iota], writes=[b_ip1])
                kb.op(dve, lambda: V.tensor_scalar(out=cmi[:], in0=iota[:], scalar1=-1.0, scalar2=128.0, op0=ALU.mult, op1=ALU.add), reads=[b_iota], writes=[b_cmi])
                QF, b_QF = kb.sbb([128, 4, 128], F32, "QF", st)
                QBt, b_QBt = kb.sbb([128, 4, 128], F32, "QBt", st)
                for a in range(4):
                    kb.op(act, lambda a=a: ACT.activation(out=QF[:, a, :], in_=ip1[:], func=AF.Exp, scale=lgp[:, a, 0:1]), reads=[b_ip1, b_lgp], writes=[b_QF])
                    kb.op(act, lambda a=a: ACT.activation(out=QBt[:, a, :], in_=cmi[:], func=AF.Exp, scale=lgp[:, a, 1:2]), reads=[b_cmi, b_lgp], writes=[b_QBt])
                tk, b_tk = kb.sbb([128, 16], F32, "tk", st)
                kb.op(act, lambda: ACT.activation(out=tk[:, 0:8], in_=lg[:, 0:8], func=AF.Exp, scale=pcol[:, 0:1]), reads=[b_lg, b_pcol], writes=[b_tk])
                kb.op(act, lambda: ACT.activation(out=tk[:, 8:16], in_=lg[:, 8:16], func=AF.Exp, scale=pcol[:, 1:2]), reads=[b_lg, b_pcol], writes=[b_tk])
                GC, b_GC = kb.sbb([128, 4, 2], F32, "GC", st)
                kb.op(act, lambda: ACT.activation(out=GC[:], in_=lgp[:], func=AF.Exp, scale=128.0), reads=[b_lgp], writes=[b_GC])
                Rf, b_Rf = kb.sbb([128, 4, 64], F32, "Rf", st)
                Rfb, b_Rfb = kb.sbb([128, 4, 128], BF16, "Rfb", st)
                kb.op(dve, lambda: V.memset(Rf[:], 0.0), writes=[b_Rf])
                kb.op(dve, lambda: V.memset(Rfb[:], 0.0), writes=[b_Rfb])
                work = dict(junk=kb.slots([128, 1024], BF16, 1, "junk", st), ss=kb.slots([128, 4], F32, 2, "ss", st),
                            xs=kb.slots([128, 1024], F32, 1, "xs", st))
                xts = kb.slots([128, 1024], F32, 2, "xt", st)
                xnTs = kb.slots([128, 8, 128], BF16, 2, "xnT", st)
                qks = kb.slots([128, 16, 64], F32, 2, "qk", st)
                qkrs = kb.slots([128, 16, 64], F32, 2, "qkr", st)
                tmps = kb.slots([128, 16, 32], F32, 4, "ropetmp", st)
                vbs = kb.slots([128, 512], BF16, 2, "vb", st)
                sgs = kb.slots([128, 512], F32, 2, "sg", st)
                qTs = kb.slots([128, 8, 128], BF16, 2, "qTz", st)
                for (q_, bq_) in qTs.items:
                    kb.op(dve, lambda q_=q_: V.memset(q_[:], 0.0), writes=[bq_])
                kTs = kb.slots([128, 4, 128], BF16, 2, "kT", st)
                qfTs = kb.slots([128, 4, 128], BF16, 2, "qfT", st)
                qbTs = kb.slots([128, 4, 128], BF16, 2, "qbT", st)
                kdfs = kb.slots([128, 8, 64], BF16, 2, "kdf", st)
                kdbs = kb.slots([128, 8, 64], BF16, 2, "kdb", st)
                sDs = kb.slots([128, 8, 128], BF16, 2, "sD", st)
                yps = kb.slots([128, 512], F32, 2, "yp", st)
                bbs = kb.slots([128, 4, 64], F32, 2, "bb", st)
                sqs = kb.slots([128, 5, 128], F32, 2, "sq", st)
                rsbs = kb.slots([128, 2, 128], F32, 2, "rsb", st)
                cns = kb.slots([128, 5, 128], BF16, 2, "cn", st)
                cqts = kb.slots([96, 128], F32, 2, "cqt", st)
                sqts = kb.slots([96, 128], F32, 2, "sqt", st)
                ckts = kb.slots([32, 128], F32, 2, "ckt", st)
                skts = kb.slots([32, 128], F32, 2, "skt", st)
                t1s = kb.slots([96, 8, 128], F32, 1, "t1", st)
                t2s = kb.slots([96, 8, 128], F32, 1, "t2", st)
                QTts = kb.slots([96, 8, 128], BF16, 2, "QTt", st)
                KNts = kb.slots([64, 8, 128], BF16, 2, "KNt", st)
                KRts = kb.slots([32, 128], BF16, 2, "KRt", st)
                kr1s = kb.slots([32, 128], F32, 2, "kr1", st)
                kr2s = kb.slots([32, 128], F32, 2, "kr2", st)
                Vts = kb.slots([128, 8, 65], BF16, 2, "Vt", st)
                for (vt_, bv_) in Vts.items:
                    kb.op(dve, lambda vt_=vt_: V.memset(vt_[:, :, 64:65], 1.0), writes=[bv_])
                sq2s = kb.slots([96, 1024], F32, 1, "sq2", st)
                sq3s = kb.slots([32, 128], F32, 1, "sq3", st)
                nrows = kb.slots([1, 2, 1024], F32, 1, "nrow", st)
                nbrs = kb.slots([1, 1024], BF16, 2, "nbr", st)
                one8, b_one8 = kb.sbb([1, 1024], BF16, "one8", st)
                kb.op(dve, lambda: V.memset(one8[:], 1.0), writes=[b_one8])
                krn = kb.slots([1, 128], F32, 2, "krn", st)

                def tile_gen(t):
                        r0 = t * 128
                        xt, bx = xts.next()
                        kb.dma(sp, xt[:], X[r0:r0 + 128, :], bx, writes=[bx])
                        cqt, bcqt = cqts.next()
                        sqt, bsqt = sqts.next()
                        kb.dma(sp, cqt[:], A["c_cq"][:, r0:r0 + 128], bcqt, writes=[bcqt])
                        kb.dma(sp, sqt[:], A["c_sq"][:, r0:r0 + 128], bsqt, writes=[bsqt])
                        ckt, bckt = ckts.next()
                        skt, bskt = skts.next()
                        kb.dma(sp, ckt[:], A["c_ck"][:, r0:r0 + 128], bckt, writes=[bckt])
                        kb.dma(sp, skt[:], A["c_sk"][:, r0:r0 + 128], bskt, writes=[bskt])
                        xnT, bxnT = xnTs.next()
                        norm_transpose(xt[:], bx, nmw, b_nmw, xnT, bxnT, 0, work)
                        pq, bq = kb.bank(1)
                        pk, bk = kb.bank(1)
                        pv, bv = kb.bank(1)
                        pg, bg = kb.bank(1)
                        for n, (pp, bpp) in enumerate(((pq, bq), (pk, bk), (pv, bv), (pg, bg))):
                            for kc in range(8):
                                kb.op(pe, lambda pp=pp, n=n, kc=kc: PE.matmul(pp, lhsT=xnT[:, kc, :], rhs=Win[:, kc, n * 512:(n + 1) * 512],
                                                                            start=(kc == 0), stop=(kc == 7)),
                                      reads=[bxnT, b_Win], writes=bpp)
                        qk, bqk = qks.next()
                        qkf = qk[:].rearrange("p a d -> p (a d)")
                        kb.op(act, lambda: ACT.copy(out=qkf[:, 0:512], in_=pq), reads=bq, writes=[bqk])
                        kb.op(act, lambda: ACT.activation(out=qkf[:, 512:1024], in_=pk, func=AF.Copy, scale=0.125), reads=bk, writes=[bqk])
                        vb, bvb = vbs.next()
                        kb.op(dve, lambda: V.tensor_copy(out=vb[:], in_=pv), reads=bv, writes=[bvb])
                        sg, bsg = sgs.next()
                        kb.op(act, lambda: ACT.activation(out=sg[:], in_=pg, func=AF.Silu), reads=bg, writes=[bsg])
                        kb.dma(pool, Z["SG"][r0:r0 + 128, :], sg[:], bsg, reads=[bsg], writes=[ZB["SG"]])
                        qkr, bqkr = qkrs.next()
                        cosr, b_cosr = cosrs.next()
                        sinr, b_sinr = sinrs.next()
                        kb.dma(sp, cosr[:], A["c_cosr"][r0:r0 + 128, :], b_cosr, writes=[b_cosr])
                        kb.dma(sp, sinr[:], A["c_sinr"][r0:r0 + 128, :], b_sinr, writes=[b_sinr])
                        cb = cosr[:, :].unsqueeze(1).to_broadcast([128, 16, 32])
                        sbc = sinr[:, :].unsqueeze(1).to_broadcast([128, 16, 32])
                        ta, bta = tmps.next()
                        tb, btb = tmps.next()
                        tc_, btc = tmps.next()
                        td, btd = tmps.next()
                        kb.op(dve, lambda: V.tensor_tensor(out=ta[:], in0=qk[:, :, 0:32], in1=cb, op=ALU.mult), reads=[bqk, b_cosr], writes=[bta])
                        kb.op(dve, lambda: V.tensor_tensor(out=tb[:], in0=qk[:, :, 32:64], in1=sbc, op=ALU.mult), reads=[bqk, b_sinr], writes=[btb])
                        kb.op(dve, lambda: V.tensor_tensor(out=qkr[:, :, 0:32], in0=ta[:], in1=tb[:], op=ALU.subtract), reads=[bta, btb], writes=[bqkr])
                        kb.op(dve, lambda: V.tensor_tensor(out=tc_[:], in0=qk[:, :, 0:32], in1=sbc, op=ALU.mult), reads=[bqk, b_sinr], writes=[btc])
                        kb.op(dve, lambda: V.tensor_tensor(out=td[:], in0=qk[:, :, 32:64], in1=cb, op=ALU.mult), reads=[bqk, b_cosr], writes=[btd])
                        kb.op(dve, lambda: V.tensor_tensor(out=qkr[:, :, 32:64], in0=tc_[:], in1=td[:], op=ALU.add), reads=[btc, btd], writes=[bqkr])
                        qkrf = qkr[:].rearrange("p a d -> p (a d)")
                        pT, bpT = kb.bank(2)
                        for j in range(8):
                            kb.op(pe, lambda j=j: PE.transpose(pT[:, j * 128:(j + 1) * 128], qkrf[:, j * 128:(j + 1) * 128], ident[:]),
                                  reads=[bqkr, b_ident], writes=[bpT[j // 4]])
                        pTv = pT.rearrange("p (j t) -> p j t", j=8)
                        qT, bqT = qTs.next()
                        kT, bkT = kTs.next()
                        qfT, bqfT = qfTs.next()
                        qbT, bqbT = qbTs.next()
                        kb.op(act, lambda: ACT.copy(out=qT[0:64, 0:8:2, :], in_=pTv[0:64, 0:4, :]), reads=[bpT[0]], writes=[bqT])
                        kb.op(act, lambda: ACT.copy(out=qT[64:128, 1:8:2, :], in_=pTv[64:128, 0:4, :]), reads=[bpT[0]], writes=[bqT])
                        kb.op(act, lambda: ACT.copy(out=kT[:], in_=pTv[:, 4:8, :]), reads=[bpT[1]], writes=[bkT])
                        kb.op(dve, lambda: V.tensor_tensor(out=qfT[:], in0=pTv[:, 0:4, :], in1=QF[:], op=ALU.mult), reads=[bpT[0], b_QF], writes=[bqfT])
                        kb.op(dve, lambda: V.tensor_tensor(out=qbT[:], in0=pTv[:, 0:4, :], in1=QBt[:], op=ALU.mult), reads=[bpT[0], b_QBt], writes=[bqbT])
                        kb.dma(pool, Z["QB"][t], qbT[:].rearrange("p a t -> p (a t)"), bqbT, reads=[bqbT], writes=[ZB["QB"]])
                        kdf, bkdf = kdfs.next()
                        kdb, bkdb = kdbs.next()
                        kb.op(dve, lambda: V.tensor_tensor(out=kdf[:], in0=qkr[:, 8:16, :], in1=tk[:, 0:8].unsqueeze(2).to_broadcast([128, 8, 64]), op=ALU.mult),
                              reads=[bqkr, b_tk], writes=[bkdf])
                        kb.op(dve, lambda: V.tensor_tensor(out=kdb[:], in0=qkr[:, 8:16, :], in1=tk[:, 8:16].unsqueeze(2).to_broadcast([128, 8, 64]), op=ALU.mult),
                              reads=[bqkr, b_tk], writes=[bkdb])
                        yield
                        pS, bpS = kb.bank(2)
                        for h in range(8):
                            a, off = h // 2, (h % 2) * 64
                            kb.op(pe, lambda h=h, a=a, off=off: PE.matmul(pS[:, h * 128:(h + 1) * 128], lhsT=kT[:, a, :], rhs=qT[:, h, :],
                                                                          start=True, stop=True),
                                  reads=[bkT, bqT], writes=[bpS[h // 4]])
                        sD, bsD = sDs.next()
                        pSv = pS.rearrange("p (h t) -> p h t", h=8)
                        kb.op(dve, lambda: V.tensor_tensor(out=sD[:, 0:4, :], in0=pSv[:, 0:4, :], in1=DT[:, 0:4, :], op=ALU.mult), reads=[bpS[0], b_DT], writes=[bsD])
                        kb.op(dve, lambda: V.tensor_tensor(out=sD[:, 4:8, :], in0=pSv[:, 4:8, :], in1=DT[:, 4:8, :], op=ALU.mult), reads=[bpS[1], b_DT], writes=[bsD])
                        pY, bpY = kb.bank(1)
                        for a in range(4):
                            kb.op(pe, lambda a=a: PE.matmul(pY[:, a * 128:(a + 1) * 128], lhsT=qfT[:, a, :], rhs=Rfb[:, a, :], start=True, stop=False),
                                  reads=[bqfT, b_Rfb], writes=bpY)
                            for h in (2 * a, 2 * a + 1):
                                kb.op(pe, lambda h=h: PE.matmul(pY[:, h * 64:(h + 1) * 64], lhsT=sD[:, h, :], rhs=vb[:, h * 64:(h + 1) * 64], start=False, stop=(h % 2 == 1)),
                                      reads=[bsD, bvb], writes=bpY)
                        yp, byp = yps.next()
                        kb.op(act, lambda: ACT.copy(out=yp[:], in_=pY), reads=bpY, writes=[byp])
                        kb.dma(pool, Z["YP"][r0:r0 + 128, :], yp[:], byp, reads=[byp], writes=[ZB["YP"]])
                        pBf, bpBf = kb.bank(1)
                        pBb, bpBb = kb.bank(1)
                        kdf2 = kdf[:].rearrange("p h d -> p (h d)")
                        kdb2 = kdb[:].rearrange("p h d -> p (h d)")
                        for a in range(4):
                            kb.op(pe, lambda a=a: PE.matmul(pBf[:, a * 128:(a + 1) * 128], lhsT=kdf2[:, a * 128:(a + 1) * 128], rhs=vb[:, a * 128:(a + 1) * 128],
                                                            start=True, stop=True), reads=[bkdf, bvb], writes=bpBf)
                            kb.op(pe, lambda a=a: PE.matmul(pBb[:, a * 128:(a + 1) * 128], lhsT=kdb2[:, a * 128:(a + 1) * 128], rhs=vb[:, a * 128:(a + 1) * 128],
                                                            start=True, stop=True), reads=[bkdb, bvb], writes=bpBb)
                        pBfv = pBf.rearrange("p (a x) -> p a x", a=4)
                        pBbv = pBb.rearrange("p (a x) -> p a x", a=4)
                        kb.op(dve, lambda: V.tensor_tensor(out=Rf[:], in0=Rf[:], in1=GC[:, :, 0:1].to_broadcast([128, 4, 64]), op=ALU.mult),
                              reads=[b_Rf, b_GC], writes=[b_Rf])
                        kb.op(dve, lambda: V.tensor_tensor(out=Rf[0:64], in0=Rf[0:64], in1=pBfv[0:64, :, 0:64], op=ALU.add), reads=[b_Rf] + bpBf, writes=[b_Rf])
                        kb.op(dve, lambda: V.tensor_tensor(out=Rf[64:128], in0=Rf[64:128], in1=pBfv[64:128, :, 64:128], op=ALU.add), reads=[b_Rf] + bpBf, writes=[b_Rf])
                        kb.op(dve, lambda: V.tensor_copy(out=Rfb[0:64, :, 0:64], in_=Rf[0:64]), reads=[b_Rf], writes=[b_Rfb])
                        kb.op(dve, lambda: V.tensor_copy(out=Rfb[64:128, :, 64:128], in_=Rf[64:128]), reads=[b_Rf], writes=[b_Rfb])
                        bb, bbb = bbs.next()
                        kb.op(act, lambda: ACT.copy(out=bb[0:64], in_=pBbv[0:64, :, 0:64]), reads=bpBb, writes=[bbb])
                        kb.op(act, lambda: ACT.copy(out=bb[64:128], in_=pBbv[64:128, :, 64:128]), reads=bpBb, writes=[bbb])
                        kb.dma(pool, Z["BB"][t], bb[:].rearrange("p a v -> p (a v)"), bbb, reads=[bbb], writes=[ZB["BB"]])

                        pC, bpC = kb.bank(2)
                        for r in range(5):
                            for kc in range(8):
                                kb.op(pe, lambda r=r, kc=kc: PE.matmul(pC[:, r * 128:(r + 1) * 128], lhsT=Win[:, kc, 2048 + r * 128:2048 + (r + 1) * 128],
                                                                       rhs=xnT[:, kc, :], start=(kc == 0), stop=(kc == 7)),
                                      reads=[b_Win, bxnT], writes=[bpC[r // 4]])
                        sq, bsq = sqs.next()
                        kb.op(act, lambda: ACT.activation(out=sq[:, 0:4, :].rearrange("p r t -> p (r t)"), in_=pC[:, 0:512], func=AF.Square), reads=[bpC[0]], writes=[bsq])
                        kb.op(act, lambda: ACT.activation(out=sq[:, 4, :], in_=pC[:, 512:640], func=AF.Square), reads=[bpC[1]], writes=[bsq])
                        pN, bpN = kb.bank(1)
                        for r in range(3):
                            kb.op(pe, lambda r=r: PE.matmul(pN[:, 0:128], lhsT=onesf[:], rhs=sq[:, r, :], start=(r == 0), stop=(r == 2)), reads=[b_onesf, bsq], writes=bpN)
                        for r in range(2):
                            kb.op(pe, lambda r=r: PE.matmul(pN[:, 128:256], lhsT=onesf[:], rhs=sq[:, 3 + r, :], start=(r == 0), stop=(r == 1)), reads=[b_onesf, bsq], writes=bpN)
                        rsb, brsb = rsbs.next()
                        kb.op(act, lambda: ACT.activation(out=rsb[:, 0, :], in_=pN[:, 0:128], func=AF.Sqrt, scale=1.0 / 384, bias=epsn[:, 0:1]), reads=bpN + [b_epsn], writes=[brsb])
                        kb.op(act, lambda: ACT.activation(out=rsb[:, 1, :], in_=pN[:, 128:256], func=AF.Sqrt, scale=1.0 / 256, bias=epsn[:, 0:1]), reads=bpN + [b_epsn], writes=[brsb])
                        kb.op(dve, lambda: V.reciprocal(out=rsb[:], in_=rsb[:]), reads=[brsb], writes=[brsb])
                        cn, bcn = cns.next()
                        for r in range(5):
                            kb.op(dve, lambda r=r: V.scalar_tensor_tensor(out=cn[:, r, :], in0=pC[:, r * 128:(r + 1) * 128], scalar=nw5[:, r:r + 1],
                                                                          in1=rsb[:, 0 if r < 3 else 1, :], op0=ALU.mult, op1=ALU.mult),
                                  reads=[bpC[r // 4], b_nw5, brsb], writes=[bcn])
                        pQ, bpQ = kb.bank(2)
                        pQr, bpQr = kb.bank(2)
                        for h in range(8):
                            for r in range(3):
                                kb.op(pe, lambda h=h, r=r: PE.matmul(pQ[0:96, h * 128:(h + 1) * 128], lhsT=Wuq[:, r, h * 96:(h + 1) * 96], rhs=cn[:, r, :],
                                                                     start=(r == 0), stop=(r == 2)), reads=[b_Wuq, bcn], writes=[bpQ[h // 4]])
                            for r in range(3):
                                kb.op(pe, lambda h=h, r=r: PE.matmul(pQr[0:96, h * 128:(h + 1) * 128], lhsT=Wuqr[:, r, h * 96:(h + 1) * 96], rhs=cn[:, r, :],
                                                                     start=(r == 0), stop=(r == 2)), reads=[b_Wuqr, bcn], writes=[bpQr[h // 4]])
                        t1, bt1 = t1s.next()
                        t2, bt2 = t2s.next()
                        kb.op(dve, lambda: V.tensor_tensor(out=t1[:], in0=pQ[0:96, :].rearrange("p (h t) -> p h t", h=8),
                                                           in1=cqt[:].unsqueeze(1).to_broadcast([96, 8, 128]), op=ALU.mult), reads=bpQ + [bcqt], writes=[bt1])
                        kb.op(dve, lambda: V.tensor_tensor(out=t2[:], in0=pQr[0:96, :].rearrange("p (h t) -> p h t", h=8),
                                                           in1=sqt[:].unsqueeze(1).to_broadcast([96, 8, 128]), op=ALU.mult), reads=bpQr + [bsqt], writes=[bt2])
                        QTt, bQTt = QTts.next()
                        kb.op(dve, lambda: V.tensor_tensor(out=QTt[:], in0=t1[:], in1=t2[:], op=ALU.add), reads=[bt1, bt2], writes=[bQTt])
                        kb.dma(pool, Z["QT"][:, 0:96, r0:r0 + 128].rearrange("h f t -> f h t"), QTt[:], bQTt, reads=[bQTt], writes=[ZB["QT"]])
                        pKN, bpKN = kb.bank(2)
                        for h in range(8):
                            for c in range(2):
                                kb.op(pe, lambda h=h, c=c: PE.matmul(pKN[0:64, h * 128:(h + 1) * 128], lhsT=Wukv[:, c, 0, h, :], rhs=cn[:, 3 + c, :],
                                                                     start=(c == 0), stop=(c == 1)), reads=[b_Wukv, bcn], writes=[bpKN[h // 4]])
                        KNt, bKNt = KNts.next()
                        kb.op(act, lambda: ACT.copy(out=KNt[:].rearrange("p h t -> p (h t)"), in_=pKN[0:64, :]), reads=bpKN, writes=[bKNt])
                        kb.dma(pool, Z["KT"][:, 0:64, r0:r0 + 128].rearrange("h f t -> f h t"), KNt[:], bKNt, reads=[bKNt], writes=[ZB["KT"]])
                        pKR, bpKR = kb.bank(1)
                        for kc in range(8):
                            kb.op(pe, lambda kc=kc: PE.matmul(pKR[0:32, 0:128], lhsT=Win[:, kc, 2688:2720], rhs=xnT[:, kc, :], start=(kc == 0), stop=(kc == 7)),
                                  reads=[b_Win, bxnT], writes=bpKR)
                        for kc in range(8):
                            kb.op(pe, lambda kc=kc: PE.matmul(pKR[0:32, 128:256], lhsT=Wkrr[:, kc, :], rhs=xnT[:, kc, :], start=(kc == 0), stop=(kc == 7)),
                                  reads=[b_Wkrr, bxnT], writes=bpKR)
                        kr1, bkr1 = kr1s.next()
                        kr2, bkr2 = kr2s.next()
                        kb.op(dve, lambda: V.tensor_tensor(out=kr1[:], in0=pKR[0:32, 0:128], in1=ckt[:], op=ALU.mult), reads=bpKR + [bckt], writes=[bkr1])
                        kb.op(dve, lambda: V.tensor_tensor(out=kr2[:], in0=pKR[0:32, 128:256], in1=skt[:], op=ALU.mult), reads=bpKR + [bskt], writes=[bkr2])
                        KRt, bKRt = KRts.next()
                        kb.op(dve, lambda: V.tensor_tensor(out=KRt[:], in0=kr1[:], in1=kr2[:], op=ALU.add), reads=[bkr1, bkr2], writes=[bKRt])
                        for h in range(8):
                            kb.dma(pool, Z["KT"][h, 64:96, r0:r0 + 128], KRt[:], bKRt, reads=[bKRt], writes=[ZB["KT"]])
                        pV, bpV = kb.bank(1)
                        for c in range(2):
                            kb.op(pe, lambda c=c: PE.matmul(pV, lhsT=cn[:, 3 + c, :], rhs=Wukv[:, c, 1, :, :].rearrange("p h d -> p (h d)"),
                                                            start=(c == 0), stop=(c == 1)), reads=[bcn, b_Wukv], writes=bpV)
                        Vt, bVt = Vts.next()
                        kb.op(act, lambda: ACT.copy(out=Vt[:, :, 0:64], in_=pV.rearrange("p (h d) -> p h d", h=8)), reads=bpV, writes=[bVt])
                        kb.dma(pool, Z["VS"][r0:r0 + 128, :], Vt[:].rearrange("p h d -> p (h d)"), bVt, reads=[bVt], writes=[ZB["VS"]])
                        sq2, bsq2 = sq2s.next()
                        nrow, bnrow = nrows.next()
                        kb.op(act, lambda: ACT.activation(out=sq2[:], in_=QTt[:].rearrange("p h t -> p (h t)"), func=AF.Square), reads=[bQTt], writes=[bsq2])
                        pR, bpR = kb.bank(2)
                        for n in range(2):
                            kb.op(pe, lambda n=n: PE.matmul(pR[0:1, n * 512:(n + 1) * 512], lhsT=onesf[0:96, 0:1], rhs=sq2[:, n * 512:(n + 1) * 512], start=True, stop=True),
                                  reads=[b_onesf, bsq2], writes=[bpR[n]])
                        kb.op(dve, lambda: V.tensor_copy(out=nrow[:, 0, :], in_=pR[0:1, :]), reads=bpR, writes=[bnrow])
                        nbr, bnbr = nbrs.next()
                        kb.op(act, lambda: ACT.activation(out=nrow[:, 0, :], in_=nrow[:, 0, :], func=AF.Sqrt), reads=[bnrow], writes=[bnrow])
                        kb.op(dve, lambda: V.tensor_scalar(out=nbr[:], in0=nrow[:, 0, :], scalar1=-1.0, scalar2=None, op0=ALU.mult), reads=[bnrow], writes=[bnbr])
                        kb.dma(pool, Z["QT"][:, 96:97, r0:r0 + 128].rearrange("h o t -> o h t"), nbr[:].rearrange("o (h t) -> o h t", h=8), bnbr, reads=[bnbr], writes=[ZB["QT"]])
                        kb.dma(pool, Z["KT"][:, 96:97, r0:r0 + 128].rearrange("h o t -> o h t"), one8[:].rearrange("o (h t) -> o h t", h=8), b_one8, reads=[b_one8], writes=[ZB["KT"]])
                        kb.op(act, lambda: ACT.activation(out=sq2[0:64, :], in_=KNt[:].rearrange("p h t -> p (h t)"), func=AF.Square), reads=[bKNt, bsq2], writes=[bsq2])
                        sq3, bsq3 = sq3s.next()
                        kb.op(act, lambda: ACT.activation(out=sq3[:], in_=KRt[:], func=AF.Square), reads=[bKRt], writes=[bsq3])
                        pR2, bpR2 = kb.bank(2)
                        for n in range(2):
                            kb.op(pe, lambda n=n: PE.matmul(pR2[0:1, n * 512:(n + 1) * 512], lhsT=onesf[0:64, 0:1], rhs=sq2[0:64, n * 512:(n + 1) * 512], start=True, stop=True),
                                  reads=[b_onesf, bsq2], writes=[bpR2[n]])
                        pR3, bpR3 = kb.bank(1)
                        kb.op(pe, lambda: PE.matmul(pR3[0:1, 0:128], lhsT=onesf[0:32, 0:1], rhs=sq3[:], start=True, stop=True), reads=[b_onesf, bsq3], writes=bpR3)
                        kr_, bkr_ = krn.next()
                        kb.op(dve, lambda: V.tensor_copy(out=kr_[:], in_=pR3[0:1, 0:128]), reads=bpR3, writes=[bkr_])
                        kb.op(dve, lambda: V.tensor_tensor(out=nrow[:, 1, :].rearrange("o (h t) -> o h t", h=8), in0=pR2[0:1, :].rearrange("o (h t) -> o h t", h=8),
                                                           in1=kr_[:].unsqueeze(1).to_broadcast([1, 8, 128]), op=ALU.add), reads=bpR2 + [bkr_], writes=[bnrow])
                        kb.dma(pool, Z["KN"][:, r0:r0 + 128].rearrange("(o h) t -> o h t", o=1), nrow[:, 1, :].rearrange("o (h t) -> o h t", h=8), bnrow, reads=[bnrow], writes=[ZB["KN"]])

                def exhaust(gen):
                    if gen is not None:
                        for _ in gen:
                            pass
                g_prev = None
                for t in range(nt):
                    g_cur = tile_gen(t)
                    next(g_cur)
                    exhaust(g_prev)
                    g_prev = g_cur
                exhaust(g_prev)
                kb.barrier()

        def phase2(si, S):
            nkt = S // 128
            QBLK = 512 if S >= 512 else S
            nqb = S // QBLK
            with ExitStack() as st:
                KThs = kb.slots([97, S], BF16, 2, "KTh", st)
                QThs = kb.slots([97, S], BF16, 2, "QTh", st)
                Vhs = kb.slots([128, nkt, 65], BF16, 2, "Vh", st)
                kn2, b_kn2 = kb.sbb([128, S // 128], F32, "kn2", st)
                kst, b_kst = kb.sbb([128, 4], F32, "kst", st)
                krow, b_krow = kb.sbb([1, 132], F32, "krow", st)
                kmx, b_kmx = kb.sbb([128, 1], F32, "kmx", st)
                PTs = kb.slots([128, QBLK], BF16, 5, "PT", st)
                osbs = kb.slots([65, QBLK], F32, 2, "osb", st)
                rds = kb.slots([64, QBLK], F32, 2, "rd", st)
                mos = kb.slots([64, QBLK], BF16, 2, "mo", st)
                Esel, b_Esel = kb.sbb([65, 64], F32, "Esel", st)
                kb.op(dve, lambda: V.memset(Esel[:], 0.0), writes=[b_Esel])
                kb.op(dve, lambda: V.memset(Esel[64:65, :], 1.0), writes=[b_Esel])
                kb.set_banks(range(2, 8))
                for h in range(8):
                    KTh, bKTh = KThs.next()
                    QTh, bQTh = QThs.next()
                    Vh, bVh = Vhs.next()
                    kb.dma(sp, KTh[:, :], Z["KT"][h, :, 0:S], bKTh, reads=[ZB["KT"]], writes=[bKTh])
                    kb.dma(sp, QTh[:, :], Z["QT"][h, :, 0:S], bQTh, reads=[ZB["QT"]], writes=[bQTh])
                    with nc.allow_non_contiguous_dma(reason="per-head V rows (130B)"):
                        kb.dma(sp, Vh[:], Z["VS"][0:S, h * 65:(h + 1) * 65].rearrange("(kt p) c -> p kt c", p=128), bVh, reads=[ZB["VS"]], writes=[bVh])
                    kb.dma(sp, kn2[:], Z["KN"][h, 0:S].rearrange("(p f) -> p f", p=128), b_kn2, reads=[ZB["KN"]], writes=[b_kn2])
                    kb.op(dve, lambda: V.tensor_reduce(out=kst[:, 0:1], in_=kn2[:], axis=AX.X, op=ALU.max), reads=[b_kn2], writes=[b_kst])
                    pK1, bpK1 = kb.bank(1)
                    kb.op(pe, lambda pK1=pK1: PE.transpose(pK1[0:1, 0:128], kst[:, 0:1], ident[:]), reads=[b_kst, b_ident], writes=bpK1)
                    kb.op(dve, lambda pK1=pK1: V.tensor_copy(out=krow[:, 0:128], in_=pK1[0:1, 0:128]), reads=bpK1, writes=[b_krow])
                    kb.op(dve, lambda: V.tensor_reduce(out=krow[:, 128:129], in_=krow[:, 0:128], axis=AX.X, op=ALU.max), reads=[b_krow], writes=[b_krow])
                    pK2, bpK2 = kb.bank(1)
                    kb.op(pe, lambda pK2=pK2: PE.matmul(pK2[:, 0:1], lhsT=onesf[0:1, :], rhs=krow[:, 128:129], start=True, stop=True), reads=[b_onesf, b_krow], writes=bpK2)
                    kb.op(act, lambda pK2=pK2: ACT.activation(out=kmx[:], in_=pK2[:, 0:1], func=AF.Sqrt), reads=bpK2, writes=[b_kmx])
                    kb.op(dve, lambda QTh=QTh: V.tensor_scalar(out=QTh[96:97, :], in0=QTh[96:97, :], scalar1=kmx[96:97, 0:1], scalar2=None, op0=ALU.mult),
                          reads=[bQTh, b_kmx], writes=[bQTh])
                    for qb in range(nqb):
                        q0 = qb * QBLK
                        pO, bpO = kb.fixed_bank(qb % 2, 1)
                        LA = 2
                        pend = []

                        def emit_s(kt):
                            pS, bpS = kb.bank(1)
                            kb.op(pe, lambda: PE.matmul(pS[:, 0:QBLK], lhsT=KTh[:, kt * 128:(kt + 1) * 128], rhs=QTh[:, q0:q0 + QBLK], start=True, stop=True),
                                  reads=[bKTh, bQTh], writes=bpS)
                            PT, bPT = PTs.next()
                            kb.op(act, lambda: ACT.activation(out=PT[:], in_=pS[:, 0:QBLK], func=AF.Exp), reads=bpS, writes=[bPT])
                            pend.append((kt, PT, bPT))

                        def emit_pv():
                            pkt, pPT, pbPT = pend.pop(0)
                            kb.op(pe, lambda: PE.matmul(pO[0:65, 0:QBLK], lhsT=Vh[:, pkt, :], rhs=pPT[:], start=(pkt == 0), stop=(pkt == nkt - 1)),
                                  reads=[bVh, pbPT], writes=bpO)
                        for kt in range(nkt):
                            emit_s(kt)
                            if len(pend) > LA:
                                emit_pv()
                        while pend:
                            emit_pv()
                        osb, bosb = osbs.next()
                        kb.op(dve, lambda: V.tensor_copy(out=osb[:], in_=pO[0:65, 0:QBLK]), reads=bpO, writes=[bosb])
                        pD, bpD = kb.bank(1)
                        kb.op(pe, lambda: PE.matmul(pD[0:64, 0:QBLK], lhsT=Esel[:], rhs=osb[:], start=True, stop=True), reads=[b_Esel, bosb], writes=bpD)
                        rd, brd = rds.next()
                        kb.op(dve, lambda: V.reciprocal(out=rd[:], in_=pD[0:64, 0:QBLK]), reads=bpD, writes=[brd])
                        mo, bmo = mos.next()
                        kb.op(dve, lambda: V.tensor_tensor(out=mo[:], in0=osb[0:64, :], in1=rd[:], op=ALU.mult), reads=[bosb, brd], writes=[bmo])
                        kb.dma(pool, Z["MO"][h, :, q0:q0 + QBLK], mo[:], bmo, reads=[bmo], writes=[ZB["MO"]])
                kb.barrier()

        def phase3a(si, S):
            nt = S // 128
            X = A[f"x{si}"]
            MEM = A[f"mem{si}"]
            kb.set_banks(range(8))
            with ExitStack() as st:
                work = dict(junk=kb.slots([128, 1024], BF16, 1, "junk", st), ss=kb.slots([128, 4], F32, 2, "ss", st),
                            xs=kb.slots([128, 1024], F32, 2, "xs", st))
                KmT, b_KmT = kb.sbb([128, 4, 2, 256], BF16, "KmT", st)
                Vm, b_Vm = kb.sbb([128, 2, 1024], BF16, "Vm", st)
                ncw, b_ncw = load_cols("norm_ca_w", 8, st)
                with ExitStack() as st2:
                    Wkv, b_Wkv = kb.sbb([128, 8, 2048], BF16, "Wkv", st2)
                    for kc in range(8):
                        load_cast(Wkv[:, kc, :], b_Wkv, A["ca_wkv"][0, kc * 128:(kc + 1) * 128, :])
                    nmm, b_nmm = load_cols("norm_mem_w", 8, st2)
                    memT, b_memT = kb.sbb([128, 8, 256], BF16, "memT", st2)
                    mts = kb.slots([128, 1024], F32, 2, "memt", st2)
                    for m in range(2):
                        mt, bmt = mts.next()
                        kb.dma(sp, mt[:], MEM[m * 128:(m + 1) * 128, :], bmt, writes=[bmt])
                        norm_transpose(mt[:], bmt, nmm, b_nmm, memT, b_memT, m * 128, work)
                    for j in range(8):
                        pK, bpK = kb.bank(1)
                        for kc in range(8):
                            kb.op(pe, lambda j=j, kc=kc, pK=pK: PE.matmul(pK[:, 0:256], lhsT=Wkv[:, kc, j * 128:(j + 1) * 128], rhs=memT[:, kc, :], start=(kc == 0), stop=(kc == 7)),
                                  reads=[b_Wkv, b_memT], writes=bpK)
                        kb.op(act, lambda j=j, pK=pK: ACT.copy(out=KmT[:, j // 2, j % 2, :], in_=pK[:, 0:256]), reads=bpK, writes=[b_KmT])
                    for mc in range(2):
                        for n in range(2):
                            pVm, bpVm = kb.bank(1)
                            for kc in range(8):
                                kb.op(pe, lambda mc=mc, n=n, kc=kc, pVm=pVm: PE.matmul(pVm, lhsT=memT[:, kc, mc * 128:(mc + 1) * 128], rhs=Wkv[:, kc, 1024 + n * 512:1024 + (n + 1) * 512],
                                                                                  start=(kc == 0), stop=(kc == 7)), reads=[b_memT, b_Wkv], writes=bpVm)
                            kb.op(act, lambda mc=mc, n=n, pVm=pVm: ACT.copy(out=Vm[:, mc, n * 512:(n + 1) * 512], in_=pVm), reads=bpVm, writes=[b_Vm])
                    kb.barrier()
                Wor, b_Wor = kb.sbb([128, 4, 1024], BF16, "Wor", st)
                for r in range(4):
                    load_cast(Wor[:, r, :], b_Wor, A["w_out"][0, r * 128:(r + 1) * 128, :])
                Wom, b_Wom = kb.sbb([64, 8, 1024], BF16, "Wom", st)
                for h in range(8):
                    load_cast(Wom[:, h, :], b_Wom, A["w_out"][0, 512 + h * 64:512 + (h + 1) * 64, :])
                Wq, b_Wq = kb.sbb([128, 8, 1024], BF16, "Wq", st)
                Wo, b_Wo = kb.sbb([128, 8, 1024], BF16, "Wo", st)
                for kc in range(8):
                    load_cast(Wq[:, kc, :], b_Wq, A["ca_wq"][0, kc * 128:(kc + 1) * 128, :])
                    load_cast(Wo[:, kc, :], b_Wo, A["ca_wo"][0, kc * 128:(kc + 1) * 128, :])
                gnw, b_gnw = kb.sbb([128, 512], F32, "gnw", st)
                kb.dma(sp, gnw[:], A["ret_gn_w"][0, :].partition_broadcast(128), b_gnw, writes=[b_gnw])
                dec, b_dec = kb.sbb([128, 8], F32, "dec3", st)
                kb.dma(sp, dec[:], A["ret_decay_bwd"][0, :].partition_broadcast(128), b_dec, writes=[b_dec])
                kb.op(act, lambda: ACT.activation(out=dec[:], in_=dec[:], func=AF.Exp, scale=-1.0), reads=[b_dec], writes=[b_dec])
                kb.op(act, lambda: ACT.activation(out=dec[:], in_=dec[:], func=AF.Ln, bias=1.0), reads=[b_dec], writes=[b_dec])
                kb.op(act, lambda: ACT.activation(out=dec[:], in_=dec[:], func=AF.Exp, scale=-128.0), reads=[b_dec], writes=[b_dec])
                GCb, b_GCb = kb.sbb([128, 4], F32, "GCb", st)
                decv = dec[:].rearrange("p (a two) -> p a two", two=2)
                kb.op(dve, lambda: V.tensor_copy(out=GCb[0:64, :], in_=decv[0:64, :, 0]), reads=[b_dec], writes=[b_GCb])
                kb.op(dve, lambda: V.tensor_copy(out=GCb[64:128, :], in_=decv[64:128, :, 1]), reads=[b_dec], writes=[b_GCb])
                Rb, b_Rb = kb.sbb([128, 4, 64], F32, "Rb", st)
                Rbb, b_Rbb = kb.sbb([128, 4, 128], BF16, "Rbb", st)
                kb.op(dve, lambda: V.memset(Rb[:], 0.0), writes=[b_Rb])
                kb.op(dve, lambda: V.memset(Rbb[:], 0.0), writes=[b_Rbb])
                yps = kb.slots([128, 512], F32, 2, "yp3", st)
                sgs = kb.slots([128, 512], F32, 2, "sg3", st)
                qbTs = kb.slots([128, 4, 128], BF16, 2, "qbT3", st)
                bbs = kb.slots([128, 4, 64], F32, 2, "bb3", st)
                xts = kb.slots([128, 1024], F32, 2, "xt3", st)
                mots = kb.slots([64, 8, 128], BF16, 2, "mot", st)
                ys = kb.slots([128, 8, 64], F32, 2, "y3", st)
                ycs = kb.slots([128, 8, 64], F32, 2, "yc3", st)
                y2s = kb.slots([128, 8, 64], F32, 2, "ysq3", st)
                sts = kb.slots([128, 8, 4], F32, 2, "st3", st)
                ros = kb.slots([128, 512], F32, 2, "ro3", st)
                roTs = kb.slots([128, 4, 128], BF16, 2, "roT", st)
                x1s = kb.slots([128, 1024], F32, 2, "x1", st)
                hnTs = kb.slots([128, 8, 128], BF16, 2, "hnT", st)
                qcTs = kb.slots([128, 8, 128], BF16, 2, "qcT", st)
                mxs = kb.slots([128, 4, 4], F32, 2, "mx3", st)
                Ps = kb.slots([128, 4, 256], F32, 2, "P3", st)
                PnTs = kb.slots([128, 8, 128], BF16, 2, "PnT", st)
                oTs = kb.slots([128, 8, 128], BF16, 2, "oT", st)
                x2s = kb.slots([128, 1024], F32, 2, "x2", st)
                def tile_gen(t):
                        r0 = t * 128
                        yp, byp = yps.next()
                        sg, bsg = sgs.next()
                        qbT, bqbT = qbTs.next()
                        bb, bbb = bbs.next()
                        xt, bx = xts.next()
                        mot, bmot = mots.next()
                        kb.dma(sp, yp[:], Z["YP"][r0:r0 + 128, :], byp, reads=[ZB["YP"]], writes=[byp])
                        kb.dma(sp, sg[:], Z["SG"][r0:r0 + 128, :], bsg, reads=[ZB["SG"]], writes=[bsg])
                        kb.dma(sp, qbT[:].rearrange("p a t -> p (a t)"), Z["QB"][t], bqbT, reads=[ZB["QB"]], writes=[bqbT])
                        kb.dma(sp, bb[:].rearrange("p a v -> p (a v)"), Z["BB"][t], bbb, reads=[ZB["BB"]], writes=[bbb])
                        kb.dma(sp, xt[:], X[r0:r0 + 128, :], bx, writes=[bx])
                        kb.dma(sp, mot[:], Z["MO"][:, :, r0:r0 + 128].rearrange("h v t -> v h t"), bmot, reads=[ZB["MO"]], writes=[bmot])
                        pY, bpY = kb.bank(1)
                        for a in range(4):
                            kb.op(pe, lambda a=a: PE.matmul(pY[:, a * 128:(a + 1) * 128], lhsT=qbT[:, a, :], rhs=Rbb[:, a, :], start=True, stop=True),
                                  reads=[bqbT, b_Rbb], writes=bpY)
                        y, by = ys.next()
                        kb.op(dve, lambda: V.tensor_tensor(out=y[:].rearrange("p h d -> p (h d)"), in0=pY, in1=yp[:], op=ALU.add), reads=bpY + [byp], writes=[by])
                        kb.op(dve, lambda: V.tensor_tensor(out=Rb[:], in0=Rb[:], in1=GCb[:, :].unsqueeze(2).to_broadcast([128, 4, 64]), op=ALU.mult), reads=[b_Rb, b_GCb], writes=[b_Rb])
                        kb.op(dve, lambda: V.tensor_tensor(out=Rb[:], in0=Rb[:], in1=bb[:], op=ALU.add), reads=[b_Rb, bbb], writes=[b_Rb])
                        kb.op(dve, lambda: V.tensor_copy(out=Rbb[0:64, :, 0:64], in_=Rb[0:64]), reads=[b_Rb], writes=[b_Rbb])
                        kb.op(dve, lambda: V.tensor_copy(out=Rbb[64:128, :, 64:128], in_=Rb[64:128]), reads=[b_Rb], writes=[b_Rbb])
                        stt, bst = sts.next()
                        yc, byc = ycs.next()
                        ysq, bysq = y2s.next()
                        kb.op(dve, lambda: V.tensor_reduce(out=stt[:, :, 0], in_=y[:], axis=AX.X, op=ALU.add), reads=[by], writes=[bst])
                        kb.op(dve, lambda: V.tensor_scalar(out=stt[:, :, 1], in0=stt[:, :, 0], scalar1=1.0 / 64, scalar2=None, op0=ALU.mult), reads=[bst], writes=[bst])
                        kb.op(dve, lambda: V.tensor_tensor(out=yc[:], in0=y[:], in1=stt[:, :, 1:2].to_broadcast([128, 8, 64]), op=ALU.subtract), reads=[by, bst], writes=[byc])
                        kb.op(dve, lambda: V.tensor_tensor(out=ysq[:], in0=yc[:], in1=yc[:], op=ALU.mult), reads=[byc], writes=[bysq])
                        kb.op(dve, lambda: V.tensor_reduce(out=stt[:, :, 2], in_=ysq[:], axis=AX.X, op=ALU.add), reads=[bysq], writes=[bst])
                        kb.op(act, lambda: ACT.activation(out=stt[:, :, 3], in_=stt[:, :, 2], func=AF.Sqrt, scale=1.0 / 64, bias=epsg[:, 0:1]), reads=[bst, b_epsg], writes=[bst])
                        kb.op(dve, lambda: V.reciprocal(out=stt[:, :, 3], in_=stt[:, :, 3]), reads=[bst], writes=[bst])
                        kb.op(dve, lambda: V.tensor_tensor(out=yc[:], in0=yc[:], in1=stt[:, :, 3:4].to_broadcast([128, 8, 64]), op=ALU.mult), reads=[byc, bst], writes=[byc])
                        ro, bro = ros.next()
                        kb.op(dve, lambda: V.tensor_tensor(out=ro[:], in0=yc[:].rearrange("p h d -> p (h d)"), in1=gnw[:], op=ALU.mult), reads=[byc, b_gnw], writes=[bro])
                        kb.op(dve, lambda: V.tensor_tensor(out=ro[:], in0=ro[:], in1=sg[:], op=ALU.mult), reads=[bro, bsg], writes=[bro])
                        pT, bpT = kb.bank(1)
                        for r in range(4):
                            kb.op(pe, lambda r=r: PE.transpose(pT[:, r * 128:(r + 1) * 128], ro[:, r * 128:(r + 1) * 128], ident[:]), reads=[bro, b_ident], writes=bpT)
                        roT, broT = roTs.next()
                        kb.op(act, lambda: ACT.copy(out=roT[:].rearrange("p r t -> p (r t)"), in_=pT), reads=bpT, writes=[broT])
                        pM, bpM = kb.bank(2)
                        for n in range(2):
                            for r in range(4):
                                kb.op(pe, lambda n=n, r=r: PE.matmul(pM[:, n * 512:(n + 1) * 512], lhsT=roT[:, r, :], rhs=Wor[:, r, n * 512:(n + 1) * 512], start=(r == 0), stop=False),
                                      reads=[broT, b_Wor], writes=[bpM[n]])
                            for h in range(8):
                                kb.op(pe, lambda n=n, h=h: PE.matmul(pM[:, n * 512:(n + 1) * 512], lhsT=mot[:, h, :], rhs=Wom[:, h, n * 512:(n + 1) * 512], start=False, stop=(h == 7)),
                                      reads=[bmot, b_Wom], writes=[bpM[n]])
                        x1, bx1 = x1s.next()
                        kb.op(dve, lambda: V.tensor_tensor(out=x1[:], in0=pM, in1=xt[:], op=ALU.add), reads=bpM + [bx], writes=[bx1])
                        yield
                        hnT, bhnT = hnTs.next()
                        norm_transpose(x1[:], bx1, ncw, b_ncw, hnT, bhnT, 0, work)
                        pQ, bpQ = kb.bank(2)
                        for j in range(8):
                            for kc in range(8):
                                kb.op(pe, lambda j=j, kc=kc: PE.matmul(pQ[:, j * 128:(j + 1) * 128], lhsT=Wq[:, kc, j * 128:(j + 1) * 128], rhs=hnT[:, kc, :], start=(kc == 0), stop=(kc == 7)),
                                      reads=[b_Wq, bhnT], writes=[bpQ[j // 4]])
                        qcT, bqcT = qcTs.next()
                        kb.op(act, lambda: ACT.activation(out=qcT[:].rearrange("p j t -> p (j t)"), in_=pQ, func=AF.Copy, scale=1.0 / 16), reads=bpQ, writes=[bqcT])
                        pS, bpS = kb.bank(2)
                        for hd in range(4):
                            for dc in range(2):
                                kb.op(pe, lambda hd=hd, dc=dc: PE.matmul(pS[:, hd * 256:(hd + 1) * 256], lhsT=qcT[:, hd * 2 + dc, :], rhs=KmT[:, hd, dc, :], start=(dc == 0), stop=(dc == 1)),
                                      reads=[bqcT, b_KmT], writes=[bpS[hd // 2]])
                        mx, bmx = mxs.next()
                        kb.op(dve, lambda: V.tensor_reduce(out=mx[:, :, 0], in_=pS.rearrange("p (h m) -> p h m", h=4), axis=AX.X, op=ALU.max), reads=bpS, writes=[bmx])
                        kb.op(dve, lambda: V.tensor_scalar(out=mx[:, :, 1], in0=mx[:, :, 0], scalar1=-1.0, scalar2=None, op0=ALU.mult), reads=[bmx], writes=[bmx])
                        P_, bP = Ps.next()
                        for hd in range(4):
                            kb.op(act, lambda hd=hd: ACT.activation(out=P_[:, hd, :], in_=pS[:, hd * 256:(hd + 1) * 256], func=AF.Exp, bias=mx[:, hd, 1:2], accum_out=mx[:, hd, 2:3]),
                                  reads=[bpS[hd // 2], bmx], writes=[bP, bmx])
                        kb.op(dve, lambda: V.reciprocal(out=mx[:, :, 3], in_=mx[:, :, 2]), reads=[bmx], writes=[bmx])
                        kb.op(dve, lambda: V.tensor_tensor(out=P_[:], in0=P_[:], in1=mx[:, :, 3:4].to_broadcast([128, 4, 256]), op=ALU.mult), reads=[bP, bmx], writes=[bP])
                        pPT, bpPT = kb.bank(2)
                        Pf = P_[:].rearrange("p h m -> p (h m)")
                        for j in range(8):
                            kb.op(pe, lambda j=j: PE.transpose(pPT[:, j * 128:(j + 1) * 128], Pf[:, j * 128:(j + 1) * 128], ident[:]), reads=[bP, b_ident], writes=[bpPT[j // 4]])
                        PnT, bPnT = PnTs.next()
                        kb.op(act, lambda: ACT.copy(out=PnT[:].rearrange("p j t -> p (j t)"), in_=pPT), reads=bpPT, writes=[bPnT])
                        pO, bpO = kb.bank(2)
                        for hd in range(4):
                            for dd in range(2):
                                for mc in range(2):
                                    kb.op(pe, lambda hd=hd, dd=dd, mc=mc: PE.matmul(pO[:, (hd * 2 + dd) * 128:(hd * 2 + dd + 1) * 128],
                                                                                    lhsT=Vm[:, mc, hd * 256 + dd * 128:hd * 256 + (dd + 1) * 128], rhs=PnT[:, hd * 2 + mc, :],
                                                                                    start=(mc == 0), stop=(mc == 1)), reads=[b_Vm, bPnT], writes=[bpO[(hd * 2 + dd) // 4]])
                        oT, boT = oTs.next()
                        kb.op(act, lambda: ACT.copy(out=oT[:].rearrange("p j t -> p (j t)"), in_=pO), reads=bpO, writes=[boT])
                        pC, bpC = kb.bank(2)
                        for n in range(2):
                            for j in range(8):
                                kb.op(pe, lambda n=n, j=j: PE.matmul(pC[:, n * 512:(n + 1) * 512], lhsT=oT[:, j, :], rhs=Wo[:, j, n * 512:(n + 1) * 512], start=(j == 0), stop=(j == 7)),
                                      reads=[boT, b_Wo], writes=[bpC[n]])
                        x2, bx2 = x2s.next()
                        kb.op(dve, lambda: V.tensor_tensor(out=x2[:], in0=pC, in1=x1[:], op=ALU.add), reads=bpC + [bx1], writes=[bx2])
                        kb.dma(pool, Z["X2"][r0:r0 + 128, :], x2[:], bx2, reads=[bx2], writes=[ZB["X2"]])

                def exhaust(gen):
                    if gen is not None:
                        for _ in gen:
                            pass
                g_prev = None
                for t in reversed(range(nt)):
                    g_cur = tile_gen(t)
                    next(g_cur)
                    exhaust(g_prev)
                    g_prev = g_cur
                exhaust(g_prev)
                kb.barrier()

        def phase3b(si, S):
            TB = 256
            nb = S // TB
            Y = A[f"y{si}"]
            with ExitStack() as st:
                work = dict(junk=kb.slots([128, 1024], BF16, 1, "junk", st), ss=kb.slots([128, 4], F32, 2, "ss", st),
                            xs=kb.slots([128, 1024], F32, 1, "xs", st))
                kb.set_banks(range(8))
                Wpq, b_Wpq = kb.sbb([128, 8, 2048], BF16, "Wpq", st)
                for kc in range(8):
                    load_cast(Wpq[:, kc, :], b_Wpq, A["peer_wq"][0, kc * 128:(kc + 1) * 128, :])
                nfw, b_nfw = load_cols("norm_ffn_w", 8, st)
                fnw, b_fnw = kb.sbb([128, 1024], F32, "fnw", st)
                kb.dma(sp, fnw[:], A["final_norm_w"].partition_broadcast(128), b_fnw, writes=[b_fnw])
                SKT, b_SKT = kb.sbb([128, 16, 128], BF16, "SKT", st)
                sks = kb.slots([128, 128], F32, 2, "skld", st)
                for g in range(16):
                    h, half = g // 2, g % 2
                    skt, bskt = sks.next()
                    kb.dma(sp, skt[:], A["peer_sub_keys"][0, half, h, :, :], bskt, writes=[bskt])
                    pT, bpT = kb.bank(1)
                    kb.op(pe, lambda skt=skt, pT=pT: PE.transpose(pT[:, 0:128], skt[:], ident[:]), reads=[bskt, b_ident], writes=bpT)
                    kb.op(act, lambda g=g, pT=pT: ACT.copy(out=SKT[:, g, :], in_=pT[:, 0:128]), reads=bpT, writes=[b_SKT])
                iota16, b_iota16 = kb.sbb([128, 16], F32, "iota16", st)
                kb.op(dve, lambda: V.tensor_copy(out=iota16[:], in_=iota[:, 0:16]), reads=[b_iota], writes=[b_iota16])
                x2ts = kb.slots([128, 2, 1024], F32, 2, "x2b", st)
                xn3Ts = kb.slots([128, 8, TB], BF16, 2, "xn3T", st)
                qpTs = kb.slots([128, 16, TB], BF16, 1, "qpT", st)
                ssbs = kb.slots([128, 16, 128], F32, 1, "ssb", st)
                reps = kb.slots([128, 256], F32, 2, "rep", st)
                v16s = kb.slots([128, 16, 16], F32, 1, "v16", st)
                ix16s = kb.slots([128, 16, 16], U32, 1, "ix16", st)
                ixfs = kb.slots([128, 16, 16], F32, 1, "ixf", st)
                cands = kb.slots([128, 8, 256], F32, 1, "cand", st)
                m2s = kb.slots([128, 8, 16], F32, 1, "m2", st)
                p2s = kb.slots([128, 8, 16], U32, 1, "p2", st)
                abs_ = kb.slots([128, 2, 128], U32, 1, "abu", st)
                abfs = kb.slots([128, 2, 128], F32, 1, "abf", st)
                ohs = kb.slots([128, 8, 16, 16], BF16, 1, "oh", st)
                sel3s = kb.slots([128, 3, 128], F32, 1, "sel3", st)
                zs = kb.slots([128, 8, 2], F32, 1, "z", st)
                T3s = kb.slots([128, 3, 128], F32, 2, "T3", st)
                Rs = kb.slots([128, 8, 128], BF16, 2, "Roh", st)
                L0s = kb.slots([128, 8, 128], BF16, 2, "L0oh", st)
                Gm, b_Gm = kb.sbb([128, TB, 128], BF16, "Gm", st)
                UTs = kb.slots([128, 8, 128], BF16, 3, "UTi", st)
                Vis = kb.slots([128, 1024], BF16, 3, "Vi", st)
                gas = kb.slots([128, TB], F32, 3, "ga", st)
                Wds = kb.slots([128, TB], BF16, 4, "Wd", st)
                x3s = kb.slots([128, 1024], F32, 1, "x3", st)

                def mkpool(idx):
                    return {"idx": list(idx), "i": 0}

                def pbank(pool, n=1):
                    L = len(pool["idx"])
                    while True:
                        i = pool["i"] % L
                        sel = pool["idx"][i:i + n]
                        if len(sel) == n and all(sel[k] == sel[0] + k for k in range(n)):
                            break
                        pool["i"] += 1
                    pool["i"] += n
                    b0 = sel[0]
                    return kb.PS[:, b0 * 512:(b0 + n) * 512], [kb.bank_bufs[bb] for bb in sel]
                poolR, poolM, poolG = mkpool([4, 5]), mkpool([6, 7]), mkpool([4, 5, 6, 7])
                bankR = lambda n=1: pbank(poolR, n)
                bankM = lambda n=1: pbank(poolM, n)
                bankG = lambda n=1: pbank(poolG, n)
                routed = {}

                def routing_gen(b):
                    r0 = b * TB
                    x2b, bx2b = x2ts.next()
                    kb.dma(sp, x2b[:], Z["X2"][r0:r0 + TB, :].rearrange("(t p) d -> p t d", p=128), bx2b, reads=[ZB["X2"]], writes=[bx2b])
                    xn3T, bxn3T = xn3Ts.next()
                    for tt in range(2):
                        norm_transpose(x2b[:, tt, :], bx2b, nfw, b_nfw, xn3T, bxn3T, tt * 128, work, bankfn=bankR)
                        yield
                    qpT, bqpT = qpTs.next()
                    for g in range(16):
                        pQ, bpQ = bankR(1)
                        for kc in range(8):
                            kb.op(pe, lambda: PE.matmul(pQ[:, 0:TB], lhsT=Wpq[:, kc, g * 128:(g + 1) * 128], rhs=xn3T[:, kc, :], start=(kc == 0), stop=(kc == 7)),
                                  reads=[b_Wpq, bxn3T], writes=bpQ)
                        kb.op(act, lambda: ACT.copy(out=qpT[:, g, :], in_=pQ[:, 0:TB]), reads=bpQ, writes=[bqpT])
                        yield
                    T3l = []
                    for tt in range(2):
                        ssb, bssb = ssbs.next()
                        for q4 in range(4):
                            pS, bpS = bankR(1)
                            for gg in range(4):
                                g = q4 * 4 + gg
                                kb.op(pe, lambda: PE.matmul(pS[:, gg * 128:(gg + 1) * 128], lhsT=qpT[:, g, tt * 128:(tt + 1) * 128], rhs=SKT[:, g, :], start=True, stop=True),
                                      reads=[bqpT, b_SKT], writes=bpS)
                            kb.op(act, lambda: ACT.copy(out=ssb[:, q4 * 4:(q4 + 1) * 4, :].rearrange("p g k -> p (g k)"), in_=pS), reads=bpS, writes=[bssb])
                            yield
                        v16, bv16 = v16s.next()
                        ix16, bix16 = ix16s.next()
                        for g in range(16):
                            src = ssb[:, g, :]
                            rep, brep = reps.next()
                            kb.op(dve, lambda: V.max(out=v16[:, g, 0:8], in_=src), reads=[bssb], writes=[bv16])
                            kb.op(dve, lambda: V.match_replace(out=rep[:, 0:128], in_to_replace=v16[:, g, 0:8], in_values=src, imm_value=NEG), reads=[bssb, bv16], writes=[brep])
                            kb.op(dve, lambda: V.max(out=v16[:, g, 8:16], in_=rep[:, 0:128]), reads=[brep], writes=[bv16])
                            kb.op(dve, lambda: V.max_index(out=ix16[:, g, 0:8], in_max=v16[:, g, 0:8], in_values=src), reads=[bssb, bv16], writes=[bix16])
                            kb.op(dve, lambda: V.max_index(out=ix16[:, g, 8:16], in_max=v16[:, g, 8:16], in_values=rep[:, 0:128]), reads=[brep, bv16], writes=[bix16])
                            yield
                        ixf, bixf = ixfs.next()
                        kb.op(dve, lambda: V.tensor_copy(out=ixf[:], in_=ix16[:]), reads=[bix16], writes=[bixf])
                        cand, bcand = cands.next()
                        v4 = v16[:].rearrange("p (h two) k -> p h two k", two=2)
                        kb.op(dve, lambda: V.tensor_tensor(out=cand[:].rearrange("p h (a b) -> p h a b", a=16),
                                                           in0=v4[:, :, 0, :].unsqueeze(3).to_broadcast([128, 8, 16, 16]),
                                                           in1=v4[:, :, 1, :].unsqueeze(2).to_broadcast([128, 8, 16, 16]), op=ALU.add), reads=[bv16], writes=[bcand])
                        yield
                        m2, bm2 = m2s.next()
                        p2, bp2 = p2s.next()
                        for h in range(8):
                            rep, brep = reps.next()
                            kb.op(dve, lambda: V.max(out=m2[:, h, 0:8], in_=cand[:, h, :]), reads=[bcand], writes=[bm2])
                            kb.op(dve, lambda: V.match_replace(out=rep[:], in_to_replace=m2[:, h, 0:8], in_values=cand[:, h, :], imm_value=NEG), reads=[bcand, bm2], writes=[brep])
                            kb.op(dve, lambda: V.max(out=m2[:, h, 8:16], in_=rep[:]), reads=[brep], writes=[bm2])
                            kb.op(dve, lambda: V.max_index(out=p2[:, h, 0:8], in_max=m2[:, h, 0:8], in_values=cand[:, h, :]), reads=[bcand, bm2], writes=[bp2])
                            kb.op(dve, lambda: V.max_index(out=p2[:, h, 8:16], in_max=m2[:, h, 8:16], in_values=rep[:]), reads=[brep, bm2], writes=[bp2])
                            yield
                        abu, babu = abs_.next()
                        abf, babf = abfs.next()
                        p2f = p2[:].rearrange("p h k -> p (h k)")
                        kb.op(dve, lambda: V.tensor_single_scalar(out=abu[:, 0, :], in_=p2f, scalar=4, op=ALU.logical_shift_right), reads=[bp2], writes=[babu])
                        kb.op(dve, lambda: V.tensor_single_scalar(out=abu[:, 1, :], in_=p2f, scalar=15, op=ALU.bitwise_and), reads=[bp2], writes=[babu])
                        kb.op(dve, lambda: V.tensor_copy(out=abf[:], in_=abu[:]), reads=[babu], writes=[babf])
                        yield
                        sel3, bsel3 = sel3s.next()
                        ix4 = ixf[:].rearrange("p (h two) k -> p h two k", two=2)
                        for w in range(2):
                            oh, boh = ohs.next()
                            kb.op(dve, lambda: V.tensor_tensor(out=oh[:], in0=abf[:, w, :].rearrange("p (h k) -> p h k", h=8).unsqueeze(3).to_broadcast([128, 8, 16, 16]),
                                                               in1=iota16[:].unsqueeze(1).unsqueeze(1).to_broadcast([128, 8, 16, 16]), op=ALU.is_equal),
                                  reads=[babf, b_iota16], writes=[boh])
                            yield
                            kb.op(dve, lambda: V.tensor_tensor(out=oh[:], in0=oh[:], in1=ix4[:, :, w, :].unsqueeze(2).to_broadcast([128, 8, 16, 16]), op=ALU.mult),
                                  reads=[boh, bixf], writes=[boh])
                            yield
                            kb.op(dve, lambda: V.tensor_reduce(out=sel3[:, w, :].rearrange("p (h k) -> p h k", h=8), in_=oh[:], axis=AX.X, op=ALU.add),
                                  reads=[boh], writes=[bsel3])
                            yield
                        z, bz = zs.next()
                        g3 = sel3[:, 2, :].rearrange("p (h k) -> p h k", h=8)
                        kb.op(dve, lambda: V.tensor_tensor(out=g3, in0=m2[:], in1=m2[:, :, 0:1].to_broadcast([128, 8, 16]), op=ALU.subtract), reads=[bm2], writes=[bsel3])
                        kb.op(act, lambda: ACT.activation(out=sel3[:, 2, :], in_=sel3[:, 2, :], func=AF.Exp), reads=[bsel3], writes=[bsel3])
                        kb.op(dve, lambda: V.tensor_reduce(out=z[:, :, 0], in_=g3, axis=AX.X, op=ALU.add), reads=[bsel3], writes=[bz])
                        kb.op(dve, lambda: V.reciprocal(out=z[:, :, 1], in_=z[:, :, 0]), reads=[bz], writes=[bz])
                        kb.op(dve, lambda: V.tensor_tensor(out=g3, in0=g3, in1=z[:, :, 1:2].to_broadcast([128, 8, 16]), op=ALU.mult), reads=[bsel3, bz], writes=[bsel3])
                        yield
                        pT, bpT = bankR(1)
                        for w in range(3):
                            kb.op(pe, lambda: PE.transpose(pT[:, w * 128:(w + 1) * 128], sel3[:, w, :], ident[:]), reads=[bsel3, b_ident], writes=bpT)
                        T3, bT3 = T3s.next()
                        kb.op(act, lambda: ACT.copy(out=T3[:].rearrange("p w t -> p (w t)"), in_=pT[:, 0:384]), reads=bpT, writes=[bT3])
                        T3l.append((T3, bT3))
                        yield
                    routed[b] = (x2b, bx2b, xn3T, bxn3T, T3l)

                def exhaust(gen):
                    if gen is not None:
                        for _ in gen:
                            pass

                pO0, bpO0 = kb.fixed_bank(0, 2)
                pO1, bpO1 = kb.fixed_bank(2, 2)
                pOs = ((pO0, bpO0), (pO1, bpO1))
                exhaust(routing_gen(0))
                for b in range(nb):
                    r0 = b * TB
                    x2b, bx2b, xn3T, bxn3T, T3l = routed.pop(b)
                    for tt in range(2):
                        T3, bT3 = T3l[tt]
                        for hf in range(16):
                            t0 = hf * 8
                            R_, bR = Rs.next()
                            L_, bL = L0s.next()
                            io_b = iota[:].unsqueeze(1).to_broadcast([128, 8, 128])
                            kb.op(dve, lambda: V.tensor_tensor(out=R_[:], in0=io_b, in1=T3[:, 1, t0:t0 + 8].unsqueeze(2).to_broadcast([128, 8, 128]), op=ALU.is_equal),
                                  reads=[b_iota, bT3], writes=[bR])
                            kb.op(dve, lambda: V.tensor_tensor(out=L_[:], in0=io_b, in1=T3[:, 0, t0:t0 + 8].unsqueeze(2).to_broadcast([128, 8, 128]), op=ALU.is_equal),
                                  reads=[b_iota, bT3], writes=[bL])
                            kb.op(dve, lambda: V.tensor_tensor(out=L_[:], in0=L_[:], in1=T3[:, 2, t0:t0 + 8].unsqueeze(2).to_broadcast([128, 8, 128]), op=ALU.mult),
                                  reads=[bL, bT3], writes=[bL])
                            for q in range(2):
                                pG, bpG = bankG(1)
                                for u in range(4):
                                    tl = q * 4 + u
                                    kb.op(pe, lambda: PE.matmul(pG[:, u * 128:(u + 1) * 128], lhsT=R_[:, tl, :], rhs=L_[:, tl, :], start=True, stop=True),
                                          reads=[bR, bL], writes=bpG)
                                tg = tt * 128 + t0 + q * 4
                                kb.op(act, lambda: ACT.copy(out=Gm[:, tg:tg + 4, :].rearrange("p t i -> p (t i)"), in_=pG), reads=bpG, writes=[b_Gm])
                    gen = routing_gen(b + 1) if b + 1 < nb else None

                    def emit_A(i):
                        UTi, bUTi = UTs.next()
                        Vi, bVi = Vis.next()
                        kb.dma(sp, UTi[:].rearrange("p k j -> p (k j)"), Z["UT"][i], bUTi, reads=[ZB["UT"]], writes=[bUTi])
                        kb.dma(sp, Vi[:], Z["VB"][i], bVi, reads=[ZB["VB"]], writes=[bVi])
                        pA, bpA = bankM(1)
                        for kc in range(8):
                            kb.op(pe, lambda: PE.matmul(pA[:, 0:TB], lhsT=UTi[:, kc, :], rhs=xn3T[:, kc, :], start=(kc == 0), stop=(kc == 7)),
                                  reads=[bUTi, bxn3T], writes=bpA)
                        ga, bga = gas.next()
                        kb.op(act, lambda: ACT.activation(out=ga[:], in_=pA[:, 0:TB], func=AF.Gelu), reads=bpA, writes=[bga])
                        Wd, bWd = Wds.next()
                        kb.op(dve, lambda: V.tensor_tensor(out=Wd[:], in0=ga[:], in1=Gm[:, :, i], op=ALU.mult), reads=[bga, b_Gm], writes=[bWd])
                        return (Wd, bWd, Vi, bVi)

                    def emit_O(i, cur):
                        Wd, bWd, Vi, bVi = cur
                        for tt in range(2):
                            pO, bpO = pOs[tt]
                            for n in range(2):
                                kb.op(pe, lambda: PE.matmul(pO[:, n * 512:(n + 1) * 512], lhsT=Wd[:, tt * 128:(tt + 1) * 128], rhs=Vi[:, n * 512:(n + 1) * 512],
                                                            start=(i == 0), stop=(i == NE - 1)), reads=[bWd, bVi], writes=[bpO[n]])
                    LAP = 2
                    pendA = []
                    for i in range(NE):
                        pendA.append((i, emit_A(i)))
                        if len(pendA) > LAP:
                            j, cur = pendA.pop(0)
                            emit_O(j, cur)
                        if gen is not None:
                            next(gen, None)
                    while pendA:
                        j, cur = pendA.pop(0)
                        emit_O(j, cur)
                    exhaust(gen)
                    for tt in range(2):
                        pO, bpO = pOs[tt]
                        x3, bx3 = x3s.next()
                        kb.op(dve, lambda: V.tensor_tensor(out=x3[:], in0=pO, in1=x2b[:, tt, :], op=ALU.add), reads=bpO + [bx2b], writes=[bx3])
                        junk, b_junk = work["junk"].next()
                        ss, b_ss = work["ss"].next()
                        kb.op(act, lambda: ACT.activation(out=junk[:], in_=x3[:], func=AF.Square, accum_out=ss[:, 0:1]), reads=[bx3], writes=[b_junk, b_ss])
                        kb.op(act, lambda: ACT.activation(out=ss[:, 1:2], in_=ss[:, 0:1], func=AF.Sqrt, scale=1.0 / D, bias=epsn[:, 0:1]), reads=[b_ss, b_epsn], writes=[b_ss])
                        kb.op(dve, lambda: V.reciprocal(out=ss[:, 2:3], in_=ss[:, 1:2]), reads=[b_ss], writes=[b_ss])
                        kb.op(dve, lambda: V.scalar_tensor_tensor(out=x3[:], in0=x3[:], scalar=ss[:, 2:3], in1=fnw[:], op0=ALU.mult, op1=ALU.mult),
                              reads=[bx3, b_ss, b_fnw], writes=[bx3])
                        kb.dma(pool, Y[r0 + tt * 128:r0 + (tt + 1) * 128, :], x3[:], bx3, reads=[bx3])
                kb.barrier()

        if prepass:
            kb.begin_phase()
            prepass_peer()
            kb.end_phase()
        nph = 0
        for si, S in enumerate(S_list):
            for ph in (phase1, phase2, phase3a, phase3b):
                if nph >= upto:
                    break
                nph += 1
                kb.set_banks(range(8))
                kb.begin_phase()
                ph(si, S)
                kb.end_phase()
        kb.barrier()
        kb.stats = {E.name: E.n for E in kb.engs}
        kb.stats["nsem"] = kb.nsem
        kb.stats["nops"] = kb.nops
        build.last_stats = kb.stats
    return nc


_CACHE = {}


def kernel(**inputs):
    ncores = 8
    S0 = inputs["x_prompt"].shape[1]
    S1 = inputs["x_sample"].shape[1]
    key = (S0, S1)
    if key not in _CACHE:
        _CACHE[key] = build([S0, S1])
    nc = _CACHE[key]
    consts = host_consts(max(S0, S1))
    in_maps = []
    for c in range(ncores):
        m = {"x0": np.ascontiguousarray(inputs["x_prompt"][c], dtype=np.float32),
             "x1": np.ascontiguousarray(inputs["x_sample"][c], dtype=np.float32),
             "mem0": np.ascontiguousarray(inputs["mem_prompt"][c], dtype=np.float32),
             "mem1": np.ascontiguousarray(inputs["mem_sample"][c], dtype=np.float32)}
        for n in W_NAMES:
            m[n] = np.ascontiguousarray(inputs[n], dtype=np.float32)
        m.update(consts)
        in_maps.append(m)
    res = run_bass_kernel_spmd(nc, in_maps, core_ids=list(range(ncores)))
    y0 = np.stack([np.asarray(res.results[c]["y0"], dtype=np.float32) for c in range(ncores)], 0)
    y1 = np.stack([np.asarray(res.results[c]["y1"], dtype=np.float32) for c in range(ncores)], 0)
    return (y0, y1)
```

```python
from contextlib import ExitStack
import numpy as np
import ml_dtypes
import concourse.bass as bass
import concourse.mybir as mybir
from concourse.bass_utils import run_bass_kernel_spmd

F32 = mybir.dt.float32
BF16 = mybir.dt.bfloat16
U32 = mybir.dt.uint32
AF = mybir.ActivationFunctionType
ALU = mybir.AluOpType
AX = mybir.AxisListType

SEM_LIMIT = 30000
D = 1024
NEG = -1.0e30


class Buf:
    __slots__ = ("name", "w", "r", "multi", "dk", "excl")

    def __init__(self, name, multi=False, excl=False):
        self.name = name
        self.excl = excl
        self.w = {}
        self.r = {}
        self.multi = multi
        self.dk = {}


class Eng:
    def __init__(self, kb, name, eng, own_wait=True):
        self.kb = kb
        self.name = name
        self.eng = eng
        self.sem = None
        self.val = 0
        self.seen = {}
        self.own_wait = own_wait
        self.n = 0

    def next_event(self):
        if self.sem is None or self.val >= SEM_LIMIT:
            self.sem = self.kb.new_sem(self.name)
            self.val = 0
        self.val += 1
        return (self.sem, self.val)


class Slots:
    def __init__(self, items):
        self.items = items
        self.i = 0

    def next(self):
        it = self.items[self.i % len(self.items)]
        self.i += 1
        return it


class KB:
    def __init__(self, nc, stack):
        self.nc = nc
        self.stack = stack
        self.nsem = 0
        self.pe = Eng(self, "pe", nc.tensor, own_wait=False)
        self.dve = Eng(self, "dve", nc.vector)
        self.act = Eng(self, "act", nc.scalar)
        self.pool = Eng(self, "pool", nc.gpsimd)
        self.sp = Eng(self, "sp", nc.sync)
        self.engs = [self.pe, self.dve, self.act, self.pool, self.sp]
        self.anchors = []
        self.free_sems = {"hw": [], "sw": []}
        self.nops = 0
        import os
        self.limit = int(os.environ.get("OPLIMIT", "1000000000"))
        self.phase_mark = 0
        self.persist = []
        self.uid = 0
        self.PS = None
        self.bank_bufs = None
        self.bank_pool = list(range(8))
        self.bank_i = 0

    def new_sem(self, name):
        self.nsem += 1
        return self.stack.enter_context(self.nc.semaphore(f"s{self.nsem}_{name}"))

    def sb(self, shape, dtype, name="sb", stack=None):
        self.uid += 1
        st = stack or self.stack
        return st.enter_context(self.nc.sbuf_tensor(f"{name}_{self.uid}", list(shape), dtype))

    def sbb(self, shape, dtype, name="sb", stack=None):
        return self.sb(shape, dtype, name, stack), Buf(name)

    def slots(self, shape, dtype, n, name, stack=None):
        return Slots([self.sbb(shape, dtype, f"{name}{i}", stack) for i in range(n)])

    def init_psum(self):
        self.PS = self.stack.enter_context(self.nc.psum_tensor("PSALL", [128, 8 * 512], F32))
        self.bank_bufs = [Buf(f"bank{i}", excl=True) for i in range(8)]

    def set_banks(self, pool):
        self.bank_pool = list(pool)
        self.bank_i = 0

    def bank(self, n=1):
        L = len(self.bank_pool)
        while True:
            i = self.bank_i % L
            idx = self.bank_pool[i:i + n]
            if len(idx) == n and all(idx[k] == idx[0] + k for k in range(n)):
                break
            self.bank_i += 1
        self.bank_i += n
        b0 = idx[0]
        return self.PS[:, b0 * 512:(b0 + n) * 512], [self.bank_bufs[b] for b in idx]

    def fixed_bank(self, b0, n=1):
        return self.PS[:, b0 * 512:(b0 + n) * 512], [self.bank_bufs[b] for b in range(b0, b0 + n)]

    def _waits(self, E, reads, writes):
        need = {}

        def add(d):
            for k, ev in d.items():
                if k not in need or need[k][1] < ev[1]:
                    need[k] = ev
        own = id(E.sem) if E.sem is not None else None
        for b in reads:
            add(b.w)
            if b.excl:
                add({k: ev for k, ev in b.r.items() if k != own})
        for b in writes:
            if not b.multi:
                add(b.w)
            add(b.r)
        for k, (sem, val) in need.items():
            if (not E.own_wait) and E.sem is not None and k == id(E.sem):
                continue
            if E.seen.get(k, 0) >= val:
                continue
            E.eng.wait_ge(sem, val)
            E.seen[k] = val

    def _record(self, ev, reads, writes):
        k = id(ev[0])
        for b in reads:
            if k not in b.r or b.r[k][1] < ev[1]:
                b.r[k] = ev
        for b in writes:
            if b.multi:
                if k not in b.w or b.w[k][1] < ev[1]:
                    b.w[k] = ev
            else:
                b.w = {k: ev}
            b.r = {}

    def op(self, E, fn, reads=(), writes=()):
        self.nops += 1
        if self.nops > self.limit:
            return None
        self._waits(E, reads, writes)
        inst = fn()
        ev = E.next_event()
        inst.then_inc(ev[0], 1)
        self._record(ev, reads, writes)
        E.n += 1
        return inst

    def dma(self, Q, out, in_, anchor, reads=(), writes=(), **kw):
        self.nops += 1
        if self.nops > self.limit:
            return None
        self._waits(Q, reads, writes)
        kind = "sw" if Q is self.pool else "hw"
        d = anchor.dk.get(kind)
        if d is None:
            fl = self.free_sems[kind]
            d = list(fl.pop()) if fl else [self.new_sem("d_" + anchor.name), 0]
            anchor.dk[kind] = d
            if anchor not in self.anchors:
                self.anchors.append(anchor)
        inst = Q.eng.dma_start(out=out, in_=in_, **kw)
        d[1] += 16
        ev = (d[0], d[1])
        inst.then_inc(d[0], 16)
        self._record(ev, reads, writes)
        Q.n += 1
        return inst

    def barrier(self):
        evs = {}
        for E in self.engs:
            if E.sem is not None:
                evs[id(E.sem)] = (E.sem, E.val)
        for a in self.anchors:
            for d in a.dk.values():
                evs[id(d[0])] = (d[0], d[1])
        S = self.sp
        for k, (sem, val) in evs.items():
            if S.seen.get(k, 0) >= val:
                continue
            S.eng.wait_ge(sem, val)
            S.seen[k] = val
        inst = S.eng.nop()
        ev = S.next_event()
        inst.then_inc(ev[0], 1)
        for E in self.engs:
            if E is S:
                continue
            E.eng.wait_ge(ev[0], ev[1])
            E.seen[id(ev[0])] = ev[1]
            for k, (sem, val) in evs.items():
                if E.seen.get(k, 0) < val:
                    E.seen[k] = val
        S.seen[id(ev[0])] = ev[1]


    def begin_phase(self):
        self.phase_mark = len(self.anchors)

    def end_phase(self):
        self.barrier()
        dead = self.anchors[self.phase_mark:]
        self.anchors = self.anchors[:self.phase_mark]
        for a in dead:
            for kind, d in a.dk.items():
                if d[1] <= 20000:
                    self.free_sems[kind].append((d[0], d[1]))
            a.dk = {}
        for b in self.persist:
            b.w = {}
            b.r = {}


W_NAMES = ["norm_mix_w", "w_in", "ret_decay_fwd", "ret_decay_bwd", "ret_gn_w", "mla_q_norm_w", "mla_w_uq",
           "mla_kv_norm_w", "mla_w_ukv", "w_out", "norm_ca_w", "norm_mem_w", "ca_wq", "ca_wkv", "ca_wo",
           "norm_ffn_w", "peer_wq", "peer_sub_keys", "peer_u", "peer_v", "final_norm_w"]
W_SHAPES = {
    "norm_mix_w": [1, 1024], "w_in": [1, 1024, 2720], "ret_decay_fwd": [1, 8], "ret_decay_bwd": [1, 8],
    "ret_gn_w": [1, 512], "mla_q_norm_w": [1, 384], "mla_w_uq": [1, 384, 768], "mla_kv_norm_w": [1, 256],
    "mla_w_ukv": [1, 256, 1024], "w_out": [1, 1024, 1024], "norm_ca_w": [1, 1024], "norm_mem_w": [1, 1024],
    "ca_wq": [1, 1024, 1024], "ca_wkv": [1, 1024, 2048], "ca_wo": [1, 1024, 1024], "norm_ffn_w": [1, 1024],
    "peer_wq": [1, 1024, 2048], "peer_sub_keys": [1, 2, 8, 128, 128], "peer_u": [1, 16384, 1024],
    "peer_v": [1, 16384, 1024], "final_norm_w": [1024],
}


def host_consts(smax):
    pos = np.arange(smax, dtype=np.float32)
    c = {}
    invf = (1.0 / (np.float32(10000.0) ** (np.arange(0, 64, 2, dtype=np.float32) / np.float32(64)))).astype(np.float32)
    ang = (pos[:, None] * invf[None, :]).astype(np.float32)
    c["c_cosr"] = np.cos(ang).astype(np.float32)
    c["c_sinr"] = np.sin(ang).astype(np.float32)
    invm = (1.0 / (np.float32(10000.0) ** (np.arange(0, 32, 2, dtype=np.float32) / np.float32(32)))).astype(np.float32)
    angm = (pos[:, None] * invm[None, :]).astype(np.float32)
    cm = np.cos(angm).astype(np.float32).T
    sm = np.sin(angm).astype(np.float32).T
    ck = np.concatenate([cm, cm], 0)
    sk = np.concatenate([sm, sm], 0)
    sc = np.float32(96.0 ** -0.5)
    c["c_cq"] = np.concatenate([np.full((64, smax), sc, np.float32), ck * sc], 0).astype(np.float32)
    c["c_sq"] = np.concatenate([np.zeros((64, smax), np.float32), sk * sc], 0).astype(np.float32)
    c["c_ck"] = np.ascontiguousarray(ck)
    c["c_sk"] = np.ascontiguousarray(sk)
    io = np.arange(128, dtype=np.float32)
    c["c_iota"] = np.tile(io[None, :], (128, 1)).astype(np.float32)
    c["c_diff"] = (io[None, :] - io[:, None]).astype(np.float32)
    c["c_pcol"] = np.stack([127.0 - io, io], 1).astype(np.float32)
    c["c_ident"] = np.eye(128, dtype=np.float32)
    return c


CONST_SHAPES = lambda smax: {"c_cosr": [smax, 32], "c_sinr": [smax, 32], "c_cq": [96, smax], "c_sq": [96, smax],
                             "c_ck": [32, smax], "c_sk": [32, smax], "c_iota": [128, 128], "c_diff": [128, 128],
                             "c_pcol": [128, 2], "c_ident": [128, 128]}


def build(S_list, n_exp_chunks=128, debug=False, upto=99, prepass=True):
    nc = bass.Bass("TRN2", target_bir_lowering=False)
    smax = max(S_list)
    NE = n_exp_chunks
    A = {}
    for i, S in enumerate(S_list):
        A[f"x{i}"] = nc.dram_tensor(f"x{i}", [S, D], F32, kind="ExternalInput").ap()
        A[f"mem{i}"] = nc.dram_tensor(f"mem{i}", [256, D], F32, kind="ExternalInput").ap()
        A[f"y{i}"] = nc.dram_tensor(f"y{i}", [S, D], F32, kind="ExternalOutput").ap()
    for n in W_NAMES:
        A[n] = nc.dram_tensor(n, W_SHAPES[n], F32, kind="ExternalInput").ap()
    for n, shp in CONST_SHAPES(smax).items():
        A[n] = nc.dram_tensor(n, shp, F32, kind="ExternalInput").ap()
    skind = "ExternalOutput" if debug else "Internal"
    nt_max = smax // 128

    def scr(name, shape, dt):
        return nc.dram_tensor("z_" + name, shape, dt, kind=skind).ap()
    Z = dict(
        SG=scr("SG", [smax, 512], F32), YP=scr("YP", [smax, 512], F32),
        QB=scr("QB", [nt_max, 128, 512], BF16), BB=scr("BB", [nt_max, 128, 256], F32),
        QT=scr("QT", [8, 97, smax], BF16), KT=scr("KT", [8, 97, smax], BF16),
        VS=scr("VS", [smax, 8 * 65], BF16), KN=scr("KN", [8, smax], F32),
        MO=scr("MO", [8, 64, smax], BF16), X2=scr("X2", [smax, D], F32),
        UT=scr("UT", [128, 128, 1024], BF16), VB=scr("VB", [128, 128, 1024], BF16),
    )
    ZB = {k: Buf("z" + k, multi=True) for k in Z}

    with ExitStack() as top:
        kb = KB(nc, top)
        kb.init_psum()
        pe, dve, act, pool, sp = kb.pe, kb.dve, kb.act, kb.pool, kb.sp
        V = nc.vector
        ACT = nc.scalar
        G = nc.gpsimd
        PE = nc.tensor

        ident, b_ident = kb.sbb([128, 128], F32, "ident")
        iota, b_iota = kb.sbb([128, 128], F32, "iota")
        onesf, b_onesf = kb.sbb([128, 128], F32, "onesf")
        epsn, b_epsn = kb.sbb([128, 1], F32, "epsn")
        epsg, b_epsg = kb.sbb([128, 1], F32, "epsg")
        kb.dma(sp, ident[:], A["c_ident"][:, :], b_ident, writes=[b_ident])
        kb.dma(sp, iota[:], A["c_iota"][:, :], b_iota, writes=[b_iota])
        kb.op(dve, lambda: V.memset(onesf[:], 1.0), writes=[b_onesf])
        kb.op(dve, lambda: V.memset(epsn[:], 1e-6), writes=[b_epsn])
        kb.op(dve, lambda: V.memset(epsg[:], 1e-5), writes=[b_epsg])
        kb.persist = list(ZB.values()) + [b_ident, b_iota, b_onesf, b_epsn, b_epsg]
        kb.barrier()

        def load_cast(dst, dst_buf, src, stack=None):
            kb.dma(pool, dst, src, dst_buf, writes=[dst_buf])

        def norm_transpose(xt, bx, wcol, b_wcol, outT, b_outT, col0, work, bankfn=None):
            junk, b_junk = work["junk"].next()
            ss, b_ss = work["ss"].next()
            xs, b_xs = work["xs"].next()
            kb.op(act, lambda: ACT.activation(out=junk[:], in_=xt, func=AF.Square, accum_out=ss[:, 0:1]),
                  reads=[bx], writes=[b_junk, b_ss])
            kb.op(act, lambda: ACT.activation(out=ss[:, 1:2], in_=ss[:, 0:1], func=AF.Sqrt, scale=1.0 / D, bias=epsn[:, 0:1]),
                  reads=[b_ss, b_epsn], writes=[b_ss])
            kb.op(dve, lambda: V.reciprocal(out=ss[:, 2:3], in_=ss[:, 1:2]), reads=[b_ss], writes=[b_ss])
            kb.op(act, lambda: ACT.activation(out=xs[:], in_=xt, func=AF.Copy, scale=ss[:, 2:3]),
                  reads=[bx, b_ss], writes=[b_xs])
            pT, bpT = (bankfn or kb.bank)(2)
            for kc in range(8):
                kb.op(pe, lambda kc=kc: PE.transpose(pT[:, kc * 128:(kc + 1) * 128], xs[:, kc * 128:(kc + 1) * 128], ident[:]),
                      reads=[b_xs, b_ident], writes=[bpT[kc // 4]])
            pTv = pT.rearrange("p (k t) -> p k t", k=8)
            kb.op(dve, lambda: V.tensor_tensor(out=outT[:, :, col0:col0 + 128], in0=pTv,
                                               in1=wcol[:, :].unsqueeze(2).to_broadcast([128, 8, 128]), op=ALU.mult),
                  reads=bpT + [b_wcol], writes=[b_outT])
            return ss, b_ss

        def load_cols(name, ncols, stack):
            t, b = kb.sbb([128, ncols], F32, name, stack)
            src = A[name].rearrange("o (c p) -> p (o c)", p=128) if len(W_SHAPES[name]) == 2 else A[name].rearrange("(c p) -> p c", p=128)
            with nc.allow_non_contiguous_dma(reason="tiny per-feature vector"):
                kb.dma(sp, t[:], src, b, writes=[b])
            return t, b

        def prepass_peer():
            with ExitStack() as st:
                us = kb.slots([128, 1024], F32, 2, "pp_u", st)
                uts = kb.slots([128, 1024], BF16, 2, "pp_ut", st)
                vs = kb.slots([128, 1024], BF16, 2, "pp_v", st)
                for i in range(NE):
                    ut, bu = us.next()
                    kb.dma(sp, ut[:], A["peer_u"][0, i * 128:(i + 1) * 128, :], bu, writes=[bu])
                    pT, bpT = kb.bank(2)
                    for kc in range(8):
                        kb.op(pe, lambda kc=kc: PE.transpose(pT[:, kc * 128:(kc + 1) * 128], ut[:, kc * 128:(kc + 1) * 128], ident[:]),
                              reads=[bu, b_ident], writes=[bpT[kc // 4]])
                    utt, butt = uts.next()
                    kb.op(act, lambda: ACT.copy(out=utt[:, 0:512], in_=pT[:, 0:512]), reads=[bpT[0]], writes=[butt])
                    kb.op(dve, lambda: V.tensor_copy(out=utt[:, 512:1024], in_=pT[:, 512:1024]), reads=[bpT[1]], writes=[butt])
                    kb.dma(pool, Z["UT"][i], utt[:], butt, reads=[butt], writes=[ZB["UT"]])
                    vt, bv = vs.next()
                    load_cast(vt[:], bv, A["peer_v"][0, i * 128:(i + 1) * 128, :])
                    kb.dma(pool, Z["VB"][i], vt[:], bv, reads=[bv], writes=[ZB["VB"]])
                kb.barrier()

        def phase1(si, S):
            nt = S // 128
            X = A[f"x{si}"]
            with ExitStack() as st:
                Win, b_Win = kb.sbb([128, 8, 2720], BF16, "Win", st)
                for kc in range(8):
                    for (c0, c1) in ((0, 1360), (1360, 2720)):
                        load_cast(Win[:, kc, c0:c1], b_Win, A["w_in"][0, kc * 128:(kc + 1) * 128, c0:c1])
                Wuq, b_Wuq = kb.sbb([128, 3, 768], BF16, "Wuq", st)
                load_cast(Wuq[:], b_Wuq, A["mla_w_uq"][0].rearrange("(r p) n -> p r n", p=128))
                Wuqr, b_Wuqr = kb.sbb([128, 3, 768], BF16, "Wuqr", st)
                Wukv, b_Wukv = kb.sbb([128, 2, 2, 8, 64], BF16, "Wukv", st)
                for c in range(2):
                    for t in range(2):
                        load_cast(Wukv[:, c, t, :, :], b_Wukv,
                                  A["mla_w_ukv"][0, c * 128:(c + 1) * 128, :].rearrange("p (h t d) -> p t h d", h=8, t=2)[:, t, :, :])
                Wkrr, b_Wkrr = kb.sbb([128, 8, 32], BF16, "Wkrr", st)
                kb.op(dve, lambda: V.memset(Wuqr[:], 0.0), writes=[b_Wuqr])
                Wuq4 = Wuq[:].rearrange("p r (h f) -> p r h f", h=8)
                Wuqr4 = Wuqr[:].rearrange("p r (h f) -> p r h f", h=8)
                for r in range(3):
                    kb.op(dve, lambda r=r: V.tensor_scalar(out=Wuqr4[:, r, :, 64:80], in0=Wuq4[:, r, :, 80:96], scalar1=-1.0, scalar2=None, op0=ALU.mult),
                          reads=[b_Wuq], writes=[b_Wuqr])
                    kb.op(dve, lambda r=r: V.tensor_copy(out=Wuqr4[:, r, :, 80:96], in_=Wuq4[:, r, :, 64:80]),
                          reads=[b_Wuq], writes=[b_Wuqr])
                kb.op(dve, lambda: V.tensor_scalar(out=Wkrr[:, :, 0:16], in0=Win[:, :, 2704:2720], scalar1=-1.0, scalar2=None, op0=ALU.mult),
                      reads=[b_Win], writes=[b_Wkrr])
                kb.op(dve, lambda: V.tensor_copy(out=Wkrr[:, :, 16:32], in_=Win[:, :, 2688:2704]), reads=[b_Win], writes=[b_Wkrr])
                nmw, b_nmw = load_cols("norm_mix_w", 8, st)
                nw5, b_nw5 = kb.sbb([128, 5], F32, "nw5", st)
                with nc.allow_non_contiguous_dma(reason="tiny per-feature vector"):
                    kb.dma(sp, nw5[:, 0:3], A["mla_q_norm_w"].rearrange("o (c p) -> p (o c)", p=128), b_nw5, writes=[b_nw5])
                    kb.dma(sp, nw5[:, 3:5], A["mla_kv_norm_w"].rearrange("o (c p) -> p (o c)", p=128), b_nw5, writes=[b_nw5])
                cosrs = kb.slots([128, 32], F32, 2, "cosr", st)
                sinrs = kb.slots([128, 32], F32, 2, "sinr", st)
                dec, b_dec = kb.sbb([128, 16], F32, "dec", st)
                kb.dma(sp, dec[:, 0:8], A["ret_decay_fwd"][0, :].partition_broadcast(128), b_dec, writes=[b_dec])
                kb.dma(sp, dec[:, 8:16], A["ret_decay_bwd"][0, :].partition_broadcast(128), b_dec, writes=[b_dec])
                lg, b_lg = kb.sbb([128, 16], F32, "lg", st)
                kb.op(act, lambda: ACT.activation(out=lg[:], in_=dec[:], func=AF.Exp, scale=-1.0), reads=[b_dec], writes=[b_lg])
                kb.op(act, lambda: ACT.activation(out=lg[:], in_=lg[:], func=AF.Ln, bias=1.0), reads=[b_lg], writes=[b_lg])
                kb.op(dve, lambda: V.tensor_scalar(out=lg[:], in0=lg[:], scalar1=-1.0, scalar2=None, op0=ALU.mult), reads=[b_lg], writes=[b_lg])
                lgp, b_lgp = kb.sbb([128, 4, 2], F32, "lgp", st)
                lgv = lg[:].rearrange("p (d a two) -> p a d two", d=2, two=2)
                kb.op(dve, lambda: V.tensor_copy(out=lgp[0:64, :, :], in_=lgv[0:64, :, :, 0]), reads=[b_lg], writes=[b_lgp])
                kb.op(dve, lambda: V.tensor_copy(out=lgp[64:128, :, :], in_=lgv[64:128, :, :, 1]), reads=[b_lg], writes=[b_lgp])
                diff, b_diff = kb.sbb([128, 128], F32, "diff", st)
                kb.dma(sp, diff[:], A["c_diff"][:, :], b_diff, writes=[b_diff])
                pcol, b_pcol = kb.sbb([128, 2], F32, "pcol", st)
                kb.dma(sp, pcol[:], A["c_pcol"][:, :], b_pcol, writes=[b_pcol])
                dpos, b_dpos = kb.sbb([128, 128], F32, "dpos", st)
                dneg, b_dneg = kb.sbb([128, 128], F32, "dneg", st)
                kb.op(dve, lambda: V.tensor_scalar(out=dpos[:], in0=diff[:], scalar1=0.0, scalar2=None, op0=ALU.max), reads=[b_diff], writes=[b_dpos])
                kb.op(dve, lambda: V.tensor_scalar(out=dneg[:], in0=diff[:], scalar1=-1.0, scalar2=0.0, op0=ALU.mult, op1=ALU.max), reads=[b_diff], writes=[b_dneg])
                DT, b_DT = kb.sbb([128, 8, 128], F32, "DT", st)
                tmpE, b_tmpE = kb.sbb([128, 128], F32, "tmpE", st)
                for h in range(8):
                    kb.op(act, lambda h=h: ACT.activation(out=DT[:, h, :], in_=dpos[:], func=AF.Exp, scale=lg[:, h:h + 1]),
                          reads=[b_dpos, b_lg], writes=[b_DT])
                    kb.op(pool, lambda h=h: G.affine_select(out=DT[:, h, :], in_=DT[:, h, :], pattern=[[1, 128]], compare_op=ALU.is_ge,
                                                            fill=0.0, base=0, channel_multiplier=-1), reads=[b_DT], writes=[b_DT])
                    kb.op(act, lambda h=h: ACT.activation(out=tmpE[:], in_=dneg[:], func=AF.Exp, scale=lg[:, 8 + h:9 + h]),
                          reads=[b_dneg, b_lg], writes=[b_tmpE])
                    kb.op(pool, lambda: G.affine_select(out=tmpE[:], in_=tmpE[:], pattern=[[-1, 128]], compare_op=ALU.is_gt,
                                                        fill=0.0, base=0, channel_multiplier=1), reads=[b_tmpE], writes=[b_tmpE])
                    kb.op(dve, lambda h=h: V.tensor_tensor(out=DT[:, h, :], in0=DT[:, h, :], in1=tmpE[:], op=ALU.add),
                          reads=[b_DT, b_tmpE], writes=[b_DT])
                ip1, b_ip1 = kb.sbb([128, 128], F32, "ip1", st)
                cmi, b_cmi = kb.sbb([128, 128], F32, "cmi", st)
                kb.op(dve, lambda: V.tensor_scalar(out=ip1[:], in0=iota[:], scalar1=1.0, scalar2=None, op0=ALU.add), reads=[b_iota], writes=[b_ip1])
                kb.op(dve, lambda: V.tensor_scalar(out=cmi[:], in0=iota[:], scalar1=-1.0, scalar2=128.0, op0=ALU.mult, op1=ALU.add), reads=[b_iota], writes=[b_cmi])
                QF, b_QF = kb.sbb([128, 4, 128], F32, "QF", st)
                QBt, b_QBt = kb.sbb([128, 4, 128], F32, "QBt", st)
                for a in range(4):
                    kb.op(act, lambda a=a: ACT.activation(out=QF[:, a, :], in_=ip1[:], func=AF.Exp, scale=lgp[:, a, 0:1]), reads=[b_ip1, b_lgp], writes=[b_QF])
                    kb.op(act, lambda a=a: ACT.activation(out=QBt[:, a, :], in_=cmi[:], func=AF.Exp, scale=lgp[:, a, 1:2]), reads=[b_cmi, b_lgp], writes=[b_QBt])
                tk, b_tk = kb.sbb([128, 16], F32, "tk", st)
                kb.op(act, lambda: ACT.activation(out=tk[:, 0:8], in_=lg[:, 0:8], func=AF.Exp, scale=pcol[:, 0:1]), reads=[b_lg, b_pcol], writes=[b_tk])
                kb.op(act, lambda: ACT.activation(out=tk[:, 8:16], in_=lg[:, 8:16], func=AF.Exp, scale=pcol[:, 1:2]), reads=[b_lg, b_pcol], writes=[b_tk])
                GC, b_GC = kb.sbb([128, 4, 2], F32, "GC", st)
                kb.op(act, lambda: ACT.activation(out=GC[:], in_=lgp[:], func=AF.Exp, scale=128.0), reads=[b_lgp], writes=[b_GC])
                Rf, b_Rf = kb.sbb([128, 4, 64], F32, "Rf", st)
                Rfb, b_Rfb = kb.sbb([128, 4, 128], BF16, "Rfb", st)
                kb.op(dve, lambda: V.memset(Rf[:], 0.0), writes=[b_Rf])
                kb.op(dve, lambda: V.memset(Rfb[:], 0.0), writes=[b_Rfb])
                work = dict(junk=kb.slots([128, 1024], BF16, 1, "junk", st), ss=kb.slots([128, 4], F32, 2, "ss", st),
                            xs=kb.slots([128, 1024], F32, 1, "xs", st))
                xts = kb.slots([128, 1024], F32, 2, "xt", st)
                xnTs = kb.slots([128, 8, 128], BF16, 2, "xnT", st)
                qks = kb.slots([128, 16, 64], F32, 2, "qk", st)
                qkrs = kb.slots([128, 16, 64], F32, 2, "qkr", st)
                tmps = kb.slots([128, 16, 32], F32, 4, "ropetmp", st)
                vbs = kb.slots([128, 512], BF16, 2, "vb", st)
                sgs = kb.slots([128, 512], F32, 2, "sg", st)
                qTs = kb.slots([128, 8, 128], BF16, 2, "qTz", st)
                for (q_, bq_) in qTs.items:
                    kb.op(dve, lambda q_=q_: V.memset(q_[:], 0.0), writes=[bq_])
                kTs = kb.slots([128, 4, 128], BF16, 2, "kT", st)
                qfTs = kb.slots([128, 4, 128], BF16, 2, "qfT", st)
                qbTs = kb.slots([128, 4, 128], BF16, 2, "qbT", st)
                kdfs = kb.slots([128, 8, 64], BF16, 2, "kdf", st)
                kdbs = kb.slots([128, 8, 64], BF16, 2, "kdb", st)
                sDs = kb.slots([128, 8, 128], BF16, 2, "sD", st)
                yps = kb.slots([128, 512], F32, 2, "yp", st)
                bbs = kb.slots([128, 4, 64], F32, 2, "bb", st)
                sqs = kb.slots([128, 5, 128], F32, 2, "sq", st)
                rsbs = kb.slots([128, 2, 128], F32, 2, "rsb", st)
                cns = kb.slots([128, 5, 128], BF16, 2, "cn", st)
                cqts = kb.slots([96, 128], F32, 2, "cqt", st)
                sqts = kb.slots([96, 128], F32, 2, "sqt", st)
                ckts = kb.slots([32, 128], F32, 2, "ckt", st)
                skts = kb.slots([32, 128], F32, 2, "skt", st)
                t1s = kb.slots([96, 8, 128], F32, 1, "t1", st)
                t2s = kb.slots([96, 8, 128], F32, 1, "t2", st)
                QTts = kb.slots([96, 8, 128], BF16, 2, "QTt", st)
                KNts = kb.slots([64, 8, 128], BF16, 2, "KNt", st)
                KRts = kb.slots([32, 128], BF16, 2, "KRt", st)
                kr1s = kb.slots([32, 128], F32, 2, "kr1", st)
                kr2s = kb.slots([32, 128], F32, 2, "kr2", st)
                Vts = kb.slots([128, 8, 65], BF16, 2, "Vt", st)
                for (vt_, bv_) in Vts.items:
                    kb.op(dve, lambda vt_=vt_: V.memset(vt_[:, :, 64:65], 1.0), writes=[bv_])
                sq2s = kb.slots([96, 1024], F32, 1, "sq2", st)
                sq3s = kb.slots([32, 128], F32, 1, "sq3", st)
                nrows = kb.slots([1, 2, 1024], F32, 1, "nrow", st)
                nbrs = kb.slots([1, 1024], BF16, 2, "nbr", st)
                one8, b_one8 = kb.sbb([1, 1024], BF16, "one8", st)
                kb.op(dve, lambda: V.memset(one8[:], 1.0), writes=[b_one8])
                krn = kb.slots([1, 128], F32, 2, "krn", st)

                def tile_gen(t):
                        r0 = t * 128
                        xt, bx = xts.next()
                        kb.dma(sp, xt[:], X[r0:r0 + 128, :], bx, writes=[bx])
                        cqt, bcqt = cqts.next()
                        sqt, bsqt = sqts.next()
                        kb.dma(sp, cqt[:], A["c_cq"][:, r0:r0 + 128], bcqt, writes=[bcqt])
                        kb.dma(sp, sqt[:], A["c_sq"][:, r0:r0 + 128], bsqt, writes=[bsqt])
                        ckt, bckt = ckts.next()
                        skt, bskt = skts.next()
                        kb.dma(sp, ckt[:], A["c_ck"][:, r0:r0 + 128], bckt, writes=[bckt])
                        kb.dma(sp, skt[:], A["c_sk"][:, r0:r0 + 128], bskt, writes=[bskt])
                        xnT, bxnT = xnTs.next()
                        norm_transpose(xt[:], bx, nmw, b_nmw, xnT, bxnT, 0, work)
                        pq, bq = kb.bank(1)
                        pk, bk = kb.bank(1)
                        pv, bv = kb.bank(1)
                        pg, bg = kb.bank(1)
                        for n, (pp, bpp) in enumerate(((pq, bq), (pk, bk), (pv, bv), (pg, bg))):
                            for kc in range(8):
                                kb.op(pe, lambda pp=pp, n=n, kc=kc: PE.matmul(pp, lhsT=xnT[:, kc, :], rhs=Win[:, kc, n * 512:(n + 1) * 512],
                                                                            start=(kc == 0), stop=(kc == 7)),
                                      reads=[bxnT, b_Win], writes=bpp)
                        qk, bqk = qks.next()
                        qkf = qk[:].rearrange("p a d -> p (a d)")
                        kb.op(act, lambda: ACT.copy(out=qkf[:, 0:512], in_=pq), reads=bq, writes=[bqk])
                        kb.op(act, lambda: ACT.activation(out=qkf[:, 512:1024], in_=pk, func=AF.Copy, scale=0.125), reads=bk, writes=[bqk])
                        vb, bvb = vbs.next()
                        kb.op(dve, lambda: V.tensor_copy(out=vb[:], in_=pv), reads=bv, writes=[bvb])
                        sg, bsg = sgs.next()
                        kb.op(act, lambda: ACT.activation(out=sg[:], in_=pg, func=AF.Silu), reads=bg, writes=[bsg])
                        kb.dma(pool, Z["SG"][r0:r0 + 128, :], sg[:], bsg, reads=[bsg], writes=[ZB["SG"]])
                        qkr, bqkr = qkrs.next()
                        cosr, b_cosr = cosrs.next()
                        sinr, b_sinr = sinrs.next()
                        kb.dma(sp, cosr[:], A["c_cosr"][r0:r0 + 128, :], b_cosr, writes=[b_cosr])
                        kb.dma(sp, sinr[:], A["c_sinr"][r0:r0 + 128, :], b_sinr, writes=[b_sinr])
                        cb = cosr[:, :].unsqueeze(1).to_broadcast([128, 16, 32])
                        sbc = sinr[:, :].unsqueeze(1).to_broadcast([128, 16, 32])
                        ta, bta = tmps.next()
                        tb, btb = tmps.next()
                        tc_, btc = tmps.next()
                        td, btd = tmps.next()
                        kb.op(dve, lambda: V.tensor_tensor(out=ta[:], in0=qk[:, :, 0:32], in1=cb, op=ALU.mult), reads=[bqk, b_cosr], writes=[bta])
                        kb.op(dve, lambda: V.tensor_tensor(out=tb[:], in0=qk[:, :, 32:64], in1=sbc, op=ALU.mult), reads=[bqk, b_sinr], writes=[btb])
                        kb.op(dve, lambda: V.tensor_tensor(out=qkr[:, :, 0:32], in0=ta[:], in1=tb[:], op=ALU.subtract), reads=[bta, btb], writes=[bqkr])
                        kb.op(dve, lambda: V.tensor_tensor(out=tc_[:], in0=qk[:, :, 0:32], in1=sbc, op=ALU.mult), reads=[bqk, b_sinr], writes=[btc])
                        kb.op(dve, lambda: V.tensor_tensor(out=td[:], in0=qk[:, :, 32:64], in1=cb, op=ALU.mult), reads=[bqk, b_cosr], writes=[btd])
                        kb.op(dve, lambda: V.tensor_tensor(out=qkr[:, :, 32:64], in0=tc_[:], in1=td[:], op=ALU.add), reads=[btc, btd], writes=[bqkr])
                        qkrf = qkr[:].rearrange("p a d -> p (a d)")
                        pT, bpT = kb.bank(2)
                        for j in range(8):
                            kb.op(pe, lambda j=j: PE.transpose(pT[:, j * 128:(j + 1) * 128], qkrf[:, j * 128:(j + 1) * 128], ident[:]),
                                  reads=[bqkr, b_ident], writes=[bpT[j // 4]])
                        pTv = pT.rearrange("p (j t) -> p j t", j=8)
                        qT, bqT = qTs.next()
                        kT, bkT = kTs.next()
                        qfT, bqfT = qfTs.next()
                        qbT, bqbT = qbTs.next()
                        kb.op(act, lambda: ACT.copy(out=qT[0:64, 0:8:2, :], in_=pTv[0:64, 0:4, :]), reads=[bpT[0]], writes=[bqT])
                        kb.op(act, lambda: ACT.copy(out=qT[64:128, 1:8:2, :], in_=pTv[64:128, 0:4, :]), reads=[bpT[0]], writes=[bqT])
                        kb.op(act, lambda: ACT.copy(out=kT[:], in_=pTv[:, 4:8, :]), reads=[bpT[1]], writes=[bkT])
                        kb.op(dve, lambda: V.tensor_tensor(out=qfT[:], in0=pTv[:, 0:4, :], in1=QF[:], op=ALU.mult), reads=[bpT[0], b_QF], writes=[bqfT])
                        kb.op(dve, lambda: V.tensor_tensor(out=qbT[:], in0=pTv[:, 0:4, :], in1=QBt[:], op=ALU.mult), reads=[bpT[0], b_QBt], writes=[bqbT])
                        kb.dma(pool, Z["QB"][t], qbT[:].rearrange("p a t -> p (a t)"), bqbT, reads=[bqbT], writes=[ZB["QB"]])
                        kdf, bkdf = kdfs.next()
                        kdb, bkdb = kdbs.next()
                        kb.op(dve, lambda: V.tensor_tensor(out=kdf[:], in0=qkr[:, 8:16, :], in1=tk[:, 0:8].unsqueeze(2).to_broadcast([128, 8, 64]), op=ALU.mult),
                              reads=[bqkr, b_tk], writes=[bkdf])
                        kb.op(dve, lambda: V.tensor_tensor(out=kdb[:], in0=qkr[:, 8:16, :], in1=tk[:, 8:16].unsqueeze(2).to_broadcast([128, 8, 64]), op=ALU.mult),
                              reads=[bqkr, b_tk], writes=[bkdb])
                        yield
                        pS, bpS = kb.bank(2)
                        for h in range(8):
                            a, off = h // 2, (h % 2) * 64
                            kb.op(pe, lambda h=h, a=a, off=off: PE.matmul(pS[:, h * 128:(h + 1) * 128], lhsT=kT[:, a, :], rhs=qT[:, h, :],
                                                                          start=True, stop=True),
                                  reads=[bkT, bqT], writes=[bpS[h // 4]])
                        sD, bsD = sDs.next()
                        pSv = pS.rearrange("p (h t) -> p h t", h=8)
                        kb.op(dve, lambda: V.tensor_tensor(out=sD[:, 0:4, :], in0=pSv[:, 0:4, :], in1=DT[:, 0:4, :], op=ALU.mult), reads=[bpS[0], b_DT], writes=[bsD])
                        kb.op(dve, lambda: V.tensor_tensor(out=sD[:, 4:8, :], in0=pSv[:, 4:8, :], in1=DT[:, 4:8, :], op=ALU.mult), reads=[bpS[1], b_DT], writes=[bsD])
                        pY, bpY = kb.bank(1)
                        for a in range(4):
                            kb.op(pe, lambda a=a: PE.matmul(pY[:, a * 128:(a + 1) * 128], lhsT=qfT[:, a, :], rhs=Rfb[:, a, :], start=True, stop=False),
                                  reads=[bqfT, b_Rfb], writes=bpY)
                            for h in (2 * a, 2 * a + 1):
                                kb.op(pe, lambda h=h: PE.matmul(pY[:, h * 64:(h + 1) * 64], lhsT=sD[:, h, :], rhs=vb[:, h * 64:(h + 1) * 64], start=False, stop=(h % 2 == 1)),
                                      reads=[bsD, bvb], writes=bpY)
                        yp, byp = yps.next()
                        kb.op(act, lambda: ACT.copy(out=yp[:], in_=pY), reads=bpY, writes=[byp])
                        kb.dma(pool, Z["YP"][r0:r0 + 128, :], yp[:], byp, reads=[byp], writes=[ZB["YP"]])
                        pBf, bpBf = kb.bank(1)
                        pBb, bpBb = kb.bank(1)
                        kdf2 = kdf[:].rearrange("p h d -> p (h d)")
                        kdb2 = kdb[:].rearrange("p h d -> p (h d)")
                        for a in range(4):
                            kb.op(pe, lambda a=a: PE.matmul(pBf[:, a * 128:(a + 1) * 128], lhsT=kdf2[:, a * 128:(a + 1) * 128], rhs=vb[:, a * 128:(a + 1) * 128],
                                                            start=True, stop=True), reads=[bkdf, bvb], writes=bpBf)
                            kb.op(pe, lambda a=a: PE.matmul(pBb[:, a * 128:(a + 1) * 128], lhsT=kdb2[:, a * 128:(a + 1) * 128], rhs=vb[:, a * 128:(a + 1) * 128],
                                                            start=True, stop=True), reads=[bkdb, bvb], writes=bpBb)
                        pBfv = pBf.rearrange("p (a x) -> p a x", a=4)
                        pBbv = pBb.rearrange("p (a x) -> p a x", a=4)
                        kb.op(dve, lambda: V.tensor_tensor(out=Rf[:], in0=Rf[:], in1=GC[:, :, 0:1].to_broadcast([128, 4, 64]), op=ALU.mult),
                              reads=[b_Rf, b_GC], writes=[b_Rf])
                        kb.op(dve, lambda: V.tensor_tensor(out=Rf[0:64], in0=Rf[0:64], in1=pBfv[0:64, :, 0:64], op=ALU.add), reads=[b_Rf] + bpBf, writes=[b_Rf])
                        kb.op(dve, lambda: V.tensor_tensor(out=Rf[64:128], in0=Rf[64:128], in1=pBfv[64:128, :, 64:128], op=ALU.add), reads=[b_Rf] + bpBf, writes=[b_Rf])
                        kb.op(dve, lambda: V.tensor_copy(out=Rfb[0:64, :, 0:64], in_=Rf[0:64]), reads=[b_Rf], writes=[b_Rfb])
                        kb.op(dve, lambda: V.tensor_copy(out=Rfb[64:128, :, 64:128], in_=Rf[64:128]), reads=[b_Rf], writes=[b_Rfb])
                        bb, bbb = bbs.next()
                        kb.op(act, lambda: ACT.copy(out=bb[0:64], in_=pBbv[0:64, :, 0:64]), reads=bpBb, writes=[bbb])
                        kb.op(act, lambda: ACT.copy(out=bb[64:128], in_=pBbv[64:128, :, 64:128]), reads=bpBb, writes=[bbb])
                        kb.dma(pool, Z["BB"][t], bb[:].rearrange("p a v -> p (a v)"), bbb, reads=[bbb], writes=[ZB["BB"]])

                        pC, bpC = kb.bank(2)
                        for r in range(5):
                            for kc in range(8):
                                kb.op(pe, lambda r=r, kc=kc: PE.matmul(pC[:, r * 128:(r + 1) * 128], lhsT=Win[:, kc, 2048 + r * 128:2048 + (r + 1) * 128],
                                                                       rhs=xnT[:, kc, :], start=(kc == 0), stop=(kc == 7)),
                                      reads=[b_Win, bxnT], writes=[bpC[r // 4]])
                        sq, bsq = sqs.next()
                        kb.op(act, lambda: ACT.activation(out=sq[:, 0:4, :].rearrange("p r t -> p (r t)"), in_=pC[:, 0:512], func=AF.Square), reads=[bpC[0]], writes=[bsq])
                        kb.op(act, lambda: ACT.activation(out=sq[:, 4, :], in_=pC[:, 512:640], func=AF.Square), reads=[bpC[1]], writes=[bsq])
                        pN, bpN = kb.bank(1)
                        for r in range(3):
                            kb.op(pe, lambda r=r: PE.matmul(pN[:, 0:128], lhsT=onesf[:], rhs=sq[:, r, :], start=(r == 0), stop=(r == 2)), reads=[b_onesf, bsq], writes=bpN)
                        for r in range(2):
                            kb.op(pe, lambda r=r: PE.matmul(pN[:, 128:256], lhsT=onesf[:], rhs=sq[:, 3 + r, :], start=(r == 0), stop=(r == 1)), reads=[b_onesf, bsq], writes=bpN)
                        rsb, brsb = rsbs.next()
                        kb.op(act, lambda: ACT.activation(out=rsb[:, 0, :], in_=pN[:, 0:128], func=AF.Sqrt, scale=1.0 / 384, bias=epsn[:, 0:1]), reads=bpN + [b_epsn], writes=[brsb])
                        kb.op(act, lambda: ACT.activation(out=rsb[:, 1, :], in_=pN[:, 128:256], func=AF.Sqrt, scale=1.0 / 256, bias=epsn[:, 0:1]), reads=bpN + [b_epsn], writes=[brsb])
                        kb.op(dve, lambda: V.reciprocal(out=rsb[:], in_=rsb[:]), reads=[brsb], writes=[brsb])
                        cn, bcn = cns.next()
                        for r in range(5):
                            kb.op(dve, lambda r=r: V.scalar_tensor_tensor(out=cn[:, r, :], in0=pC[:, r * 128:(r + 1) * 128], scalar=nw5[:, r:r + 1],
                                                                          in1=rsb[:, 0 if r < 3 else 1, :], op0=ALU.mult, op1=ALU.mult),
                                  reads=[bpC[r // 4], b_nw5, brsb], writes=[bcn])
                        pQ, bpQ = kb.bank(2)
                        pQr, bpQr = kb.bank(2)
                        for h in range(8):
                            for r in range(3):
                                kb.op(pe, lambda h=h, r=r: PE.matmul(pQ[0:96, h * 128:(h + 1) * 128], lhsT=Wuq[:, r, h * 96:(h + 1) * 96], rhs=cn[:, r, :],
                                                                     start=(r == 0), stop=(r == 2)), reads=[b_Wuq, bcn], writes=[bpQ[h // 4]])
                            for r in range(3):
                                kb.op(pe, lambda h=h, r=r: PE.matmul(pQr[0:96, h * 128:(h + 1) * 128], lhsT=Wuqr[:, r, h * 96:(h + 1) * 96], rhs=cn[:, r, :],
                                                                     start=(r == 0), stop=(r == 2)), reads=[b_Wuqr, bcn], writes=[bpQr[h // 4]])
                        t1, bt1 = t1s.next()
                        t2, bt2 = t2s.next()
                        kb.op(dve, lambda: V.tensor_tensor(out=t1[:], in0=pQ[0:96, :].rearrange("p (h t) -> p h t", h=8),
                                                           in1=cqt[:].unsqueeze(1).to_broadcast([96, 8, 128]), op=ALU.mult), reads=bpQ + [bcqt], writes=[bt1])
                        kb.op(dve, lambda: V.tensor_tensor(out=t2[:], in0=pQr[0:96, :].rearrange("p (h t) -> p h t", h=8),
                                                           in1=sqt[:].unsqueeze(1).to_broadcast([96, 8, 128]), op=ALU.mult), reads=bpQr + [bsqt], writes=[bt2])
                        QTt, bQTt = QTts.next()
                        kb.op(dve, lambda: V.tensor_tensor(out=QTt[:], in0=t1[:], in1=t2[:], op=ALU.add), reads=[bt1, bt2], writes=[bQTt])
                        kb.dma(pool, Z["QT"][:, 0:96, r0:r0 + 128].rearrange("h f t -> f h t"), QTt[:], bQTt, reads=[bQTt], writes=[ZB["QT"]])
                        pKN, bpKN = kb.bank(2)
                        for h in range(8):
                            for c in range(2):
                                kb.op(pe, lambda h=h, c=c: PE.matmul(pKN[0:64, h * 128:(h + 1) * 128], lhsT=Wukv[:, c, 0, h, :], rhs=cn[:, 3 + c, :],
                                                                     start=(c == 0), stop=(c == 1)), reads=[b_Wukv, bcn], writes=[bpKN[h // 4]])
                        KNt, bKNt = KNts.next()
                        kb.op(act, lambda: ACT.copy(out=KNt[:].rearrange("p h t -> p (h t)"), in_=pKN[0:64, :]), reads=bpKN, writes=[bKNt])
                        kb.dma(pool, Z["KT"][:, 0:64, r0:r0 + 128].rearrange("h f t -> f h t"), KNt[:], bKNt, reads=[bKNt], writes=[ZB["KT"]])
                        pKR, bpKR = kb.bank(1)
                        for kc in range(8):
                            kb.op(pe, lambda kc=kc: PE.matmul(pKR[0:32, 0:128], lhsT=Win[:, kc, 2688:2720], rhs=xnT[:, kc, :], start=(kc == 0), stop=(kc == 7)),
                                  reads=[b_Win, bxnT], writes=bpKR)
                        for kc in range(8):
                            kb.op(pe, lambda kc=kc: PE.matmul(pKR[0:32, 128:256], lhsT=Wkrr[:, kc, :], rhs=xnT[:, kc, :], start=(kc == 0), stop=(kc == 7)),
                                  reads=[b_Wkrr, bxnT], writes=bpKR)
                        kr1, bkr1 = kr1s.next()
                        kr2, bkr2 = kr2s.next()
                        kb.op(dve, lambda: V.tensor_tensor(out=kr1[:], in0=pKR[0:32, 0:128], in1=ckt[:], op=ALU.mult), reads=bpKR + [bckt], writes=[bkr1])
                        kb.op(dve, lambda: V.tensor_tensor(out=kr2[:], in0=pKR[0:32, 128:256], in1=skt[:], op=ALU.mult), reads=bpKR + [bskt], writes=[bkr2])
                        KRt, bKRt = KRts.next()
                        kb.op(dve, lambda: V.tensor_tensor(out=KRt[:], in0=kr1[:], in1=kr2[:], op=ALU.add), reads=[bkr1, bkr2], writes=[bKRt])
                        for h in range(8):
                            kb.dma(pool, Z["KT"][h, 64:96, r0:r0 + 128], KRt[:], bKRt, reads=[bKRt], writes=[ZB["KT"]])
                        pV, bpV = kb.bank(1)
                        for c in range(2):
                            kb.op(pe, lambda c=c: PE.matmul(pV, lhsT=cn[:, 3 + c, :], rhs=Wukv[:, c, 1, :, :].rearrange("p h d -> p (h d)"),
                                                            start=(c == 0), stop=(c == 1)), reads=[bcn, b_Wukv], writes=bpV)
                        Vt, bVt = Vts.next()
                        kb.op(act, lambda: ACT.copy(out=Vt[:, :, 0:64], in_=pV.rearrange("p (h d) -> p h d", h=8)), reads=bpV, writes=[bVt])
                        kb.dma(pool, Z["VS"][r0:r0 + 128, :], Vt[:].rearrange("p h d -> p (h d)"), bVt, reads=[bVt], writes=[ZB["VS"]])
                        sq2, bsq2 = sq2s.next()
                        nrow, bnrow = nrows.next()
                        kb.op(act, lambda: ACT.activation(out=sq2[:], in_=QTt[:].rearrange("p h t -> p (h t)"), func=AF.Square), reads=[bQTt], writes=[bsq2])
                        pR, bpR = kb.bank(2)
                        for n in range(2):
                            kb.op(pe, lambda n=n: PE.matmul(pR[0:1, n * 512:(n + 1) * 512], lhsT=onesf[0:96, 0:1], rhs=sq2[:, n * 512:(n + 1) * 512], start=True, stop=True),
                                  reads=[b_onesf, bsq2], writes=[bpR[n]])
                        kb.op(dve, lambda: V.tensor_copy(out=nrow[:, 0, :], in_=pR[0:1, :]), reads=bpR, writes=[bnrow])
                        nbr, bnbr = nbrs.next()
                        kb.op(act, lambda: ACT.activation(out=nrow[:, 0, :], in_=nrow[:, 0, :], func=AF.Sqrt), reads=[bnrow], writes=[bnrow])
                        kb.op(dve, lambda: V.tensor_scalar(out=nbr[:], in0=nrow[:, 0, :], scalar1=-1.0, scalar2=None, op0=ALU.mult), reads=[bnrow], writes=[bnbr])
                        kb.dma(pool, Z["QT"][:, 96:97, r0:r0 + 128].rearrange("h o t -> o h t"), nbr[:].rearrange("o (h t) -> o h t", h=8), bnbr, reads=[bnbr], writes=[ZB["QT"]])
                        kb.dma(pool, Z["KT"][:, 96:97, r0:r0 + 128].rearrange("h o t -> o h t"), one8[:].rearrange("o (h t) -> o h t", h=8), b_one8, reads=[b_one8], writes=[ZB["KT"]])
                        kb.op(act, lambda: ACT.activation(out=sq2[0:64, :], in_=KNt[:].rearrange("p h t -> p (h t)"), func=AF.Square), reads=[bKNt, bsq2], writes=[bsq2])
                        sq3, bsq3 = sq3s.next()
                        kb.op(act, lambda: ACT.activation(out=sq3[:], in_=KRt[:], func=AF.Square), reads=[bKRt], writes=[bsq3])
                        pR2, bpR2 = kb.bank(2)
                        for n in range(2):
                            kb.op(pe, lambda n=n: PE.matmul(pR2[0:1, n * 512:(n + 1) * 512], lhsT=onesf[0:64, 0:1], rhs=sq2[0:64, n * 512:(n + 1) * 512], start=True, stop=True),
                                  reads=[b_onesf, bsq2], writes=[bpR2[n]])
                        pR3, bpR3 = kb.bank(1)
                        kb.op(pe, lambda: PE.matmul(pR3[0:1, 0:128], lhsT=onesf[0:32, 0:1], rhs=sq3[:], start=True, stop=True), reads=[b_onesf, bsq3], writes=bpR3)
                        kr_, bkr_ = krn.next()
                        kb.op(dve, lambda: V.tensor_copy(out=kr_[:], in_=pR3[0:1, 0:128]), reads=bpR3, writes=[bkr_])
                        kb.op(dve, lambda: V.tensor_tensor(out=nrow[:, 1, :].rearrange("o (h t) -> o h t", h=8), in0=pR2[0:1, :].rearrange("o (h t) -> o h t", h=8),
                                                           in1=kr_[:].unsqueeze(1).to_broadcast([1, 8, 128]), op=ALU.add), reads=bpR2 + [bkr_], writes=[bnrow])
                        kb.dma(pool, Z["KN"][:, r0:r0 + 128].rearrange("(o h) t -> o h t", o=1), nrow[:, 1, :].rearrange("o (h t) -> o h t", h=8), bnrow, reads=[bnrow], writes=[ZB["KN"]])

                def exhaust(gen):
                    if gen is not None:
                        for _ in gen:
                            pass
                g_prev = None
                for t in range(nt):
                    g_cur = tile_gen(t)
                    next(g_cur)
                    exhaust(g_prev)
                    g_prev = g_cur
                exhaust(g_prev)
                kb.barrier()

        def phase2(si, S):
            nkt = S // 128
            QBLK = 512 if S >= 512 else S
            nqb = S // QBLK
            with ExitStack() as st:
                KThs = kb.slots([97, S], BF16, 2, "KTh", st)
                QThs = kb.slots([97, S], BF16, 2, "QTh", st)
                Vhs = kb.slots([128, nkt, 65], BF16, 2, "Vh", st)
                kn2, b_kn2 = kb.sbb([128, S // 128], F32, "kn2", st)
                kst, b_kst = kb.sbb([128, 4], F32, "kst", st)
                krow, b_krow = kb.sbb([1, 132], F32, "krow", st)
                kmx, b_kmx = kb.sbb([128, 1], F32, "kmx", st)
                PTs = kb.slots([128, QBLK], BF16, 5, "PT", st)
                osbs = kb.slots([65, QBLK], F32, 2, "osb", st)
                rds = kb.slots([64, QBLK], F32, 2, "rd", st)
                mos = kb.slots([64, QBLK], BF16, 2, "mo", st)
                Esel, b_Esel = kb.sbb([65, 64], F32, "Esel", st)
                kb.op(dve, lambda: V.memset(Esel[:], 0.0), writes=[b_Esel])
                kb.op(dve, lambda: V.memset(Esel[64:65, :], 1.0), writes=[b_Esel])
                kb.set_banks(range(2, 8))
                for h in range(8):
                    KTh, bKTh = KThs.next()
                    QTh, bQTh = QThs.next()
                    Vh, bVh = Vhs.next()
                    kb.dma(sp, KTh[:, :], Z["KT"][h, :, 0:S], bKTh, reads=[ZB["KT"]], writes=[bKTh])
                    kb.dma(sp, QTh[:, :], Z["QT"][h, :, 0:S], bQTh, reads=[ZB["QT"]], writes=[bQTh])
                    with nc.allow_non_contiguous_dma(reason="per-head V rows (130B)"):
                        kb.dma(sp, Vh[:], Z["VS"][0:S, h * 65:(h + 1) * 65].rearrange("(kt p) c -> p kt c", p=128), bVh, reads=[ZB["VS"]], writes=[bVh])
                    kb.dma(sp, kn2[:], Z["KN"][h, 0:S].rearrange("(p f) -> p f", p=128), b_kn2, reads=[ZB["KN"]], writes=[b_kn2])
                    kb.op(dve, lambda: V.tensor_reduce(out=kst[:, 0:1], in_=kn2[:], axis=AX.X, op=ALU.max), reads=[b_kn2], writes=[b_kst])
                    pK1, bpK1 = kb.bank(1)
                    kb.op(pe, lambda pK1=pK1: PE.transpose(pK1[0:1, 0:128], kst[:, 0:1], ident[:]), reads=[b_kst, b_ident], writes=bpK1)
                    kb.op(dve, lambda pK1=pK1: V.tensor_copy(out=krow[:, 0:128], in_=pK1[0:1, 0:128]), reads=bpK1, writes=[b_krow])
                    kb.op(dve, lambda: V.tensor_reduce(out=krow[:, 128:129], in_=krow[:, 0:128], axis=AX.X, op=ALU.max), reads=[b_krow], writes=[b_krow])
                    pK2, bpK2 = kb.bank(1)
                    kb.op(pe, lambda pK2=pK2: PE.matmul(pK2[:, 0:1], lhsT=onesf[0:1, :], rhs=krow[:, 128:129], start=True, stop=True), reads=[b_onesf, b_krow], writes=bpK2)
                    kb.op(act, lambda pK2=pK2: ACT.activation(out=kmx[:], in_=pK2[:, 0:1], func=AF.Sqrt), reads=bpK2, writes=[b_kmx])
                    kb.op(dve, lambda QTh=QTh: V.tensor_scalar(out=QTh[96:97, :], in0=QTh[96:97, :], scalar1=kmx[96:97, 0:1], scalar2=None, op0=ALU.mult),
                          reads=[bQTh, b_kmx], writes=[bQTh])
                    for qb in range(nqb):
                        q0 = qb * QBLK
                        pO, bpO = kb.fixed_bank(qb % 2, 1)
                        LA = 2
                        pend = []

                        def emit_s(kt):
                            pS, bpS = kb.bank(1)
                            kb.op(pe, lambda: PE.matmul(pS[:, 0:QBLK], lhsT=KTh[:, kt * 128:(kt + 1) * 128], rhs=QTh[:, q0:q0 + QBLK], start=True, stop=True),
                                  reads=[bKTh, bQTh], writes=bpS)
                            PT, bPT = PTs.next()
                            kb.op(act, lambda: ACT.activation(out=PT[:], in_=pS[:, 0:QBLK], func=AF.Exp), reads=bpS, writes=[bPT])
                            pend.append((kt, PT, bPT))

                        def emit_pv():
                            pkt, pPT, pbPT = pend.pop(0)
                            kb.op(pe, lambda: PE.matmul(pO[0:65, 0:QBLK], lhsT=Vh[:, pkt, :], rhs=pPT[:], start=(pkt == 0), stop=(pkt == nkt - 1)),
                                  reads=[bVh, pbPT], writes=bpO)
                        for kt in range(nkt):
                            emit_s(kt)
                            if len(pend) > LA:
                                emit_pv()
                        while pend:
                            emit_pv()
                        osb, bosb = osbs.next()
                        kb.op(dve, lambda: V.tensor_copy(out=osb[:], in_=pO[0:65, 0:QBLK]), reads=bpO, writes=[bosb])
                        pD, bpD = kb.bank(1)
                        kb.op(pe, lambda: PE.matmul(pD[0:64, 0:QBLK], lhsT=Esel[:], rhs=osb[:], start=True, stop=True), reads=[b_Esel, bosb], writes=bpD)
                        rd, brd = rds.next()
                        kb.op(dve, lambda: V.reciprocal(out=rd[:], in_=pD[0:64, 0:QBLK]), reads=bpD, writes=[brd])
                        mo, bmo = mos.next()
                        kb.op(dve, lambda: V.tensor_tensor(out=mo[:], in0=osb[0:64, :], in1=rd[:], op=ALU.mult), reads=[bosb, brd], writes=[bmo])
                        kb.dma(pool, Z["MO"][h, :, q0:q0 + QBLK], mo[:], bmo, reads=[bmo], writes=[ZB["MO"]])
                kb.barrier()

        def phase3a(si, S):
            nt = S // 128
            X = A[f"x{si}"]
            MEM = A[f"mem{si}"]
            kb.set_banks(range(8))
            with ExitStack() as st:
                work = dict(junk=kb.slots([128, 1024], BF16, 1, "junk", st), ss=kb.slots([128, 4], F32, 2, "ss", st),
                            xs=kb.slots([128, 1024], F32, 2, "xs", st))
                KmT, b_KmT = kb.sbb([128, 4, 2, 256], BF16, "KmT", st)
                Vm, b_Vm = kb.sbb([128, 2, 1024], BF16, "Vm", st)
                ncw, b_ncw = load_cols("norm_ca_w", 8, st)
                with ExitStack() as st2:
                    Wkv, b_Wkv = kb.sbb([128, 8, 2048], BF16, "Wkv", st2)
                    for kc in range(8):
                        load_cast(Wkv[:, kc, :], b_Wkv, A["ca_wkv"][0, kc * 128:(kc + 1) * 128, :])
                    nmm, b_nmm = load_cols("norm_mem_w", 8, st2)
                    memT, b_memT = kb.sbb([128, 8, 256], BF16, "memT", st2)
                    mts = kb.slots([128, 1024], F32, 2, "memt", st2)
                    for m in range(2):
                        mt, bmt = mts.next()
                        kb.dma(sp, mt[:], MEM[m * 128:(m + 1) * 128, :], bmt, writes=[bmt])
                        norm_transpose(mt[:], bmt, nmm, b_nmm, memT, b_memT, m * 128, work)
                    for j in range(8):
                        pK, bpK = kb.bank(1)
                        for kc in range(8):
                            kb.op(pe, lambda j=j, kc=kc, pK=pK: PE.matmul(pK[:, 0:256], lhsT=Wkv[:, kc, j * 128:(j + 1) * 128], rhs=memT[:, kc, :], start=(kc == 0), stop=(kc == 7)),
                                  reads=[b_Wkv, b_memT], writes=bpK)
                        kb.op(act, lambda j=j, pK=pK: ACT.copy(out=KmT[:, j // 2, j % 2, :], in_=pK[:, 0:256]), reads=bpK, writes=[b_KmT])
                    for mc in range(2):
                        for n in range(2):
                            pVm, bpVm = kb.bank(1)
                            for kc in range(8):
                                kb.op(pe, lambda mc=mc, n=n, kc=kc, pVm=pVm: PE.matmul(pVm, lhsT=memT[:, kc, mc * 128:(mc + 1) * 128], rhs=Wkv[:, kc, 1024 + n * 512:1024 + (n + 1) * 512],
                                                                                  start=(kc == 0), stop=(kc == 7)), reads=[b_memT, b_Wkv], writes=bpVm)
                            kb.op(act, lambda mc=mc, n=n, pVm=pVm: ACT.copy(out=Vm[:, mc, n * 512:(n + 1) * 512], in_=pVm), reads=bpVm, writes=[b_Vm])
                    kb.barrier()
                Wor, b_Wor = kb.sbb([128, 4, 1024], BF16, "Wor", st)
                for r in range(4):
                    load_cast(Wor[:, r, :], b_Wor, A["w_out"][0, r * 128:(r + 1) * 128, :])
                Wom, b_Wom = kb.sbb([64, 8, 1024], BF16, "Wom", st)
                for h in range(8):
                    load_cast(Wom[:, h, :], b_Wom, A["w_out"][0, 512 + h * 64:512 + (h + 1) * 64, :])
                Wq, b_Wq = kb.sbb([128, 8, 1024], BF16, "Wq", st)
                Wo, b_Wo = kb.sbb([128, 8, 1024], BF16, "Wo", st)
                for kc in range(8):
                    load_cast(Wq[:, kc, :], b_Wq, A["ca_wq"][0, kc * 128:(kc + 1) * 128, :])
                    load_cast(Wo[:, kc, :], b_Wo, A["ca_wo"][0, kc * 128:(kc + 1) * 128, :])
                gnw, b_gnw = kb.sbb([128, 512], F32, "gnw", st)
                kb.dma(sp, gnw[:], A["ret_gn_w"][0, :].partition_broadcast(128), b_gnw, writes=[b_gnw])
                dec, b_dec = kb.sbb([128, 8], F32, "dec3", st)
                kb.dma(sp, dec[:], A["ret_decay_bwd"][0, :].partition_broadcast(128), b_dec, writes=[b_dec])
                kb.op(act, lambda: ACT.activation(out=dec[:], in_=dec[:], func=AF.Exp, scale=-1.0), reads=[b_dec], writes=[b_dec])
                kb.op(act, lambda: ACT.activation(out=dec[:], in_=dec[:], func=AF.Ln, bias=1.0), reads=[b_dec], writes=[b_dec])
                kb.op(act, lambda: ACT.activation(out=dec[:], in_=dec[:], func=AF.Exp, scale=-128.0), reads=[b_dec], writes=[b_dec])
                GCb, b_GCb = kb.sbb([128, 4], F32, "GCb", st)
                decv = dec[:].rearrange("p (a two) -> p a two", two=2)
                kb.op(dve, lambda: V.tensor_copy(out=GCb[0:64, :], in_=decv[0:64, :, 0]), reads=[b_dec], writes=[b_GCb])
                kb.op(dve, lambda: V.tensor_copy(out=GCb[64:128, :], in_=decv[64:128, :, 1]), reads=[b_dec], writes=[b_GCb])
                Rb, b_Rb = kb.sbb([128, 4, 64], F32, "Rb", st)
                Rbb, b_Rbb = kb.sbb([128, 4, 128], BF16, "Rbb", st)
                kb.op(dve, lambda: V.memset(Rb[:], 0.0), writes=[b_Rb])
                kb.op(dve, lambda: V.memset(Rbb[:], 0.0), writes=[b_Rbb])
                yps = kb.slots([128, 512], F32, 2, "yp3", st)
                sgs = kb.slots([128, 512], F32, 2, "sg3", st)
                qbTs = kb.slots([128, 4, 128], BF16, 2, "qbT3", st)
                bbs = kb.slots([128, 4, 64], F32, 2, "bb3", st)
                xts = kb.slots([128, 1024], F32, 2, "xt3", st)
                mots = kb.slots([64, 8, 128], BF16, 2, "mot", st)
                ys = kb.slots([128, 8, 64], F32, 2, "y3", st)
                ycs = kb.slots([128, 8, 64], F32, 2, "yc3", st)
                y2s = kb.slots([128, 8, 64], F32, 2, "ysq3", st)
                sts = kb.slots([128, 8, 4], F32, 2, "st3", st)
                ros = kb.slots([128, 512], F32, 2, "ro3", st)
                roTs = kb.slots([128, 4, 128], BF16, 2, "roT", st)
                x1s = kb.slots([128, 1024], F32, 2, "x1", st)
                hnTs = kb.slots([128, 8, 128], BF16, 2, "hnT", st)
                qcTs = kb.slots([128, 8, 128], BF16, 2, "qcT", st)
                mxs = kb.slots([128, 4, 4], F32, 2, "mx3", st)
                Ps = kb.slots([128, 4, 256], F32, 2, "P3", st)
                PnTs = kb.slots([128, 8, 128], BF16, 2, "PnT", st)
                oTs = kb.slots([128, 8, 128], BF16, 2, "oT", st)
                x2s = kb.slots([128, 1024], F32, 2, "x2", st)
                def tile_gen(t):
                        r0 = t * 128
                        yp, byp = yps.next()
                        sg, bsg = sgs.next()
                        qbT, bqbT = qbTs.next()
                        bb, bbb = bbs.next()
                        xt, bx = xts.next()
                        mot, bmot = mots.next()
                        kb.dma(sp, yp[:], Z["YP"][r0:r0 + 128, :], byp, reads=[ZB["YP"]], writes=[byp])
                        kb.dma(sp, sg[:], Z["SG"][r0:r0 + 128, :], bsg, reads=[ZB["SG"]], writes=[bsg])
                        kb.dma(sp, qbT[:].rearrange("p a t -> p (a t)"), Z["QB"][t], bqbT, reads=[ZB["QB"]], writes=[bqbT])
                        kb.dma(sp, bb[:].rearrange("p a v -> p (a v)"), Z["BB"][t], bbb, reads=[ZB["BB"]], writes=[bbb])
                        kb.dma(sp, xt[:], X[r0:r0 + 128, :], bx, writes=[bx])
                        kb.dma(sp, mot[:], Z["MO"][:, :, r0:r0 + 128].rearrange("h v t -> v h t"), bmot, reads=[ZB["MO"]], writes=[bmot])
                        pY, bpY = kb.bank(1)
                        for a in range(4):
                            kb.op(pe, lambda a=a: PE.matmul(pY[:, a * 128:(a + 1) * 128], lhsT=qbT[:, a, :], rhs=Rbb[:, a, :], start=True, stop=True),
                                  reads=[bqbT, b_Rbb], writes=bpY)
                        y, by = ys.next()
                        kb.op(dve, lambda: V.tensor_tensor(out=y[:].rearrange("p h d -> p (h d)"), in0=pY, in1=yp[:], op=ALU.add), reads=bpY + [byp], writes=[by])
                        kb.op(dve, lambda: V.tensor_tensor(out=Rb[:], in0=Rb[:], in1=GCb[:, :].unsqueeze(2).to_broadcast([128, 4, 64]), op=ALU.mult), reads=[b_Rb, b_GCb], writes=[b_Rb])
                        kb.op(dve, lambda: V.tensor_tensor(out=Rb[:], in0=Rb[:], in1=bb[:], op=ALU.add), reads=[b_Rb, bbb], writes=[b_Rb])
                        kb.op(dve, lambda: V.tensor_copy(out=Rbb[0:64, :, 0:64], in_=Rb[0:64]), reads=[b_Rb], writes=[b_Rbb])
                        kb.op(dve, lambda: V.tensor_copy(out=Rbb[64:128, :, 64:128], in_=Rb[64:128]), reads=[b_Rb], writes=[b_Rbb])
                        stt, bst = sts.next()
                        yc, byc = ycs.next()
                        ysq, bysq = y2s.next()
                        kb.op(dve, lambda: V.tensor_reduce(out=stt[:, :, 0], in_=y[:], axis=AX.X, op=ALU.add), reads=[by], writes=[bst])
                        kb.op(dve, lambda: V.tensor_scalar(out=stt[:, :, 1], in0=stt[:, :, 0], scalar1=1.0 / 64, scalar2=None, op0=ALU.mult), reads=[bst], writes=[bst])
                        kb.op(dve, lambda: V.tensor_tensor(out=yc[:], in0=y[:], in1=stt[:, :, 1:2].to_broadcast([128, 8, 64]), op=ALU.subtract), reads=[by, bst], writes=[byc])
                        kb.op(dve, lambda: V.tensor_tensor(out=ysq[:], in0=yc[:], in1=yc[:], op=ALU.mult), reads=[byc], writes=[bysq])
                        kb.op(dve, lambda: V.tensor_reduce(out=stt[:, :, 2], in_=ysq[:], axis=AX.X, op=ALU.add), reads=[bysq], writes=[bst])
                        kb.op(act, lambda: ACT.activation(out=stt[:, :, 3], in_=stt[:, :, 2], func=AF.Sqrt, scale=1.0 / 64, bias=epsg[:, 0:1]), reads=[bst, b_epsg], writes=[bst])
                        kb.op(dve, lambda: V.reciprocal(out=stt[:, :, 3], in_=stt[:, :, 3]), reads=[bst], writes=[bst])
                        kb.op(dve, lambda: V.tensor_tensor(out=yc[:], in0=yc[:], in1=stt[:, :, 3:4].to_broadcast([128, 8, 64]), op=ALU.mult), reads=[byc, bst], writes=[byc])
                        ro, bro = ros.next()
                        kb.op(dve, lambda: V.tensor_tensor(out=ro[:], in0=yc[:].rearrange("p h d -> p (h d)"), in1=gnw[:], op=ALU.mult), reads=[byc, b_gnw], writes=[bro])
                        kb.op(dve, lambda: V.tensor_tensor(out=ro[:], in0=ro[:], in1=sg[:], op=ALU.mult), reads=[bro, bsg], writes=[bro])
                        pT, bpT = kb.bank(1)
                        for r in range(4):
                            kb.op(pe, lambda r=r: PE.transpose(pT[:, r * 128:(r + 1) * 128], ro[:, r * 128:(r + 1) * 128], ident[:]), reads=[bro, b_ident], writes=bpT)
                        roT, broT = roTs.next()
                        kb.op(act, lambda: ACT.copy(out=roT[:].rearrange("p r t -> p (r t)"), in_=pT), reads=bpT, writes=[broT])
                        pM, bpM = kb.bank(2)
                        for n in range(2):
                            for r in range(4):
                                kb.op(pe, lambda n=n, r=r: PE.matmul(pM[:, n * 512:(n + 1) * 512], lhsT=roT[:, r, :], rhs=Wor[:, r, n * 512:(n + 1) * 512], start=(r == 0), stop=False),
                                      reads=[broT, b_Wor], writes=[bpM[n]])
                            for h in range(8):
                                kb.op(pe, lambda n=n, h=h: PE.matmul(pM[:, n * 512:(n + 1) * 512], lhsT=mot[:, h, :], rhs=Wom[:, h, n * 512:(n + 1) * 512], start=False, stop=(h == 7)),
                                      reads=[bmot, b_Wom], writes=[bpM[n]])
                        x1, bx1 = x1s.next()
                        kb.op(dve, lambda: V.tensor_tensor(out=x1[:], in0=pM, in1=xt[:], op=ALU.add), reads=bpM + [bx], writes=[bx1])
                        yield
                        hnT, bhnT = hnTs.next()
                        norm_transpose(x1[:], bx1, ncw, b_ncw, hnT, bhnT, 0, work)
                        pQ, bpQ = kb.bank(2)
                        for j in range(8):
                            for kc in range(8):
                                kb.op(pe, lambda j=j, kc=kc: PE.matmul(pQ[:, j * 128:(j + 1) * 128], lhsT=Wq[:, kc, j * 128:(j + 1) * 128], rhs=hnT[:, kc, :], start=(kc == 0), stop=(kc == 7)),
                                      reads=[b_Wq, bhnT], writes=[bpQ[j // 4]])
                        qcT, bqcT = qcTs.next()
                        kb.op(act, lambda: ACT.activation(out=qcT[:].rearrange("p j t -> p (j t)"), in_=pQ, func=AF.Copy, scale=1.0 / 16), reads=bpQ, writes=[bqcT])
                        pS, bpS = kb.bank(2)
                        for hd in range(4):
                            for dc in range(2):
                                kb.op(pe, lambda hd=hd, dc=dc: PE.matmul(pS[:, hd * 256:(hd + 1) * 256], lhsT=qcT[:, hd * 2 + dc, :], rhs=KmT[:, hd, dc, :], start=(dc == 0), stop=(dc == 1)),
                                      reads=[bqcT, b_KmT], writes=[bpS[hd // 2]])
                        mx, bmx = mxs.next()
                        kb.op(dve, lambda: V.tensor_reduce(out=mx[:, :, 0], in_=pS.rearrange("p (h m) -> p h m", h=4), axis=AX.X, op=ALU.max), reads=bpS, writes=[bmx])
                        kb.op(dve, lambda: V.tensor_scalar(out=mx[:, :, 1], in0=mx[:, :, 0], scalar1=-1.0, scalar2=None, op0=ALU.mult), reads=[bmx], writes=[bmx])
                        P_, bP = Ps.next()
                        for hd in range(4):
                            kb.op(act, lambda hd=hd: ACT.activation(out=P_[:, hd, :], in_=pS[:, hd * 256:(hd + 1) * 256], func=AF.Exp, bias=mx[:, hd, 1:2], accum_out=mx[:, hd, 2:3]),
                                  reads=[bpS[hd // 2], bmx], writes=[bP, bmx])
                        kb.op(dve, lambda: V.reciprocal(out=mx[:, :, 3], in_=mx[:, :, 2]), reads=[bmx], writes=[bmx])
                        kb.op(dve, lambda: V.tensor_tensor(out=P_[:], in0=P_[:], in1=mx[:, :, 3:4].to_broadcast([128, 4, 256]), op=ALU.mult), reads=[bP, bmx], writes=[bP])
                        pPT, bpPT = kb.bank(2)
                        Pf = P_[:].rearrange("p h m -> p (h m)")
                        for j in range(8):
                            kb.op(pe, lambda j=j: PE.transpose(pPT[:, j * 128:(j + 1) * 128], Pf[:, j * 128:(j + 1) * 128], ident[:]), reads=[bP, b_ident], writes=[bpPT[j // 4]])
                        PnT, bPnT = PnTs.next()
                        kb.op(act, lambda: ACT.copy(out=PnT[:].rearrange("p j t -> p (j t)"), in_=pPT), reads=bpPT, writes=[bPnT])
                        pO, bpO = kb.bank(2)
                        for hd in range(4):
                            for dd in range(2):
                                for mc in range(2):
                                    kb.op(pe, lambda hd=hd, dd=dd, mc=mc: PE.matmul(pO[:, (hd * 2 + dd) * 128:(hd * 2 + dd + 1) * 128],
                                                                                    lhsT=Vm[:, mc, hd * 256 + dd * 128:hd * 256 + (dd + 1) * 128], rhs=PnT[:, hd * 2 + mc, :],
                                                                                    start=(mc == 0), stop=(mc == 1)), reads=[b_Vm, bPnT], writes=[bpO[(hd * 2 + dd) // 4]])
                        oT, boT = oTs.next()
                        kb.op(act, lambda: ACT.copy(out=oT[:].rearrange("p j t -> p (j t)"), in_=pO), reads=bpO, writes=[boT])
                        pC, bpC = kb.bank(2)
                        for n in range(2):
                            for j in range(8):
                                kb.op(pe, lambda n=n, j=j: PE.matmul(pC[:, n * 512:(n + 1) * 512], lhsT=oT[:, j, :], rhs=Wo[:, j, n * 512:(n + 1) * 512], start=(j == 0), stop=(j == 7)),
                                      reads=[boT, b_Wo], writes=[bpC[n]])
                        x2, bx2 = x2s.next()
                        kb.op(dve, lambda: V.tensor_tensor(out=x2[:], in0=pC, in1=x1[:], op=ALU.add), reads=bpC + [bx1], writes=[bx2])
                        kb.dma(pool, Z["X2"][r0:r0 + 128, :], x2[:], bx2, reads=[bx2], writes=[ZB["X2"]])

                def exhaust(gen):
                    if gen is not None:
                        for _ in gen:
                            pass
                g_prev = None
                for t in reversed(range(nt)):
                    g_cur = tile_gen(t)
                    next(g_cur)
                    exhaust(g_prev)
                    g_prev = g_cur
                exhaust(g_prev)
                kb.barrier()

        def phase3b(si, S):
            TB = 256
            nb = S // TB
            Y = A[f"y{si}"]
            with ExitStack() as st:
                work = dict(junk=kb.slots([128, 1024], BF16, 1, "junk", st), ss=kb.slots([128, 4], F32, 2, "ss", st),
                            xs=kb.slots([128, 1024], F32, 1, "xs", st))
                kb.set_banks(range(8))
                Wpq, b_Wpq = kb.sbb([128, 8, 2048], BF16, "Wpq", st)
                for kc in range(8):
                    load_cast(Wpq[:, kc, :], b_Wpq, A["peer_wq"][0, kc * 128:(kc + 1) * 128, :])
                nfw, b_nfw = load_cols("norm_ffn_w", 8, st)
                fnw, b_fnw = kb.sbb([128, 1024], F32, "fnw", st)
                kb.dma(sp, fnw[:], A["final_norm_w"].partition_broadcast(128), b_fnw, writes=[b_fnw])
                SKT, b_SKT = kb.sbb([128, 16, 128], BF16, "SKT", st)
                sks = kb.slots([128, 128], F32, 2, "skld", st)
                for g in range(16):
                    h, half = g // 2, g % 2
                    skt, bskt = sks.next()
                    kb.dma(sp, skt[:], A["peer_sub_keys"][0, half, h, :, :], bskt, writes=[bskt])
                    pT, bpT = kb.bank(1)
                    kb.op(pe, lambda skt=skt, pT=pT: PE.transpose(pT[:, 0:128], skt[:], ident[:]), reads=[bskt, b_ident], writes=bpT)
                    kb.op(act, lambda g=g, pT=pT: ACT.copy(out=SKT[:, g, :], in_=pT[:, 0:128]), reads=bpT, writes=[b_SKT])
                iota16, b_iota16 = kb.sbb([128, 16], F32, "iota16", st)
                kb.op(dve, lambda: V.tensor_copy(out=iota16[:], in_=iota[:, 0:16]), reads=[b_iota], writes=[b_iota16])
                x2ts = kb.slots([128, 2, 1024], F32, 2, "x2b", st)
                xn3Ts = kb.slots([128, 8, TB], BF16, 2, "xn3T", st)
                qpTs = kb.slots([128, 16, TB], BF16, 1, "qpT", st)
                ssbs = kb.slots([128, 16, 128], F32, 1, "ssb", st)
                reps = kb.slots([128, 256], F32, 2, "rep", st)
                v16s = kb.slots([128, 16, 16], F32, 1, "v16", st)
                ix16s = kb.slots([128, 16, 16], U32, 1, "ix16", st)
                ixfs = kb.slots([128, 16, 16], F32, 1, "ixf", st)
                cands = kb.slots([128, 8, 256], F32, 1, "cand", st)
                m2s = kb.slots([128, 8, 16], F32, 1, "m2", st)
                p2s = kb.slots([128, 8, 16], U32, 1, "p2", st)
                abs_ = kb.slots([128, 2, 128], U32, 1, "abu", st)
                abfs = kb.slots([128, 2, 128], F32, 1, "abf", st)
                ohs = kb.slots([128, 8, 16, 16], BF16, 1, "oh", st)
                sel3s = kb.slots([128, 3, 128], F32, 1, "sel3", st)
                zs = kb.slots([128, 8, 2], F32, 1, "z", st)
                T3s = kb.slots([128, 3, 128], F32, 2, "T3", st)
                Rs = kb.slots([128, 8, 128], BF16, 2, "Roh", st)
                L0s = kb.slots([128, 8, 128], BF16, 2, "L0oh", st)
                Gm, b_Gm = kb.sbb([128, TB, 128], BF16, "Gm", st)
                UTs = kb.slots([128, 8, 128], BF16, 3, "UTi", st)
                Vis = kb.slots([128, 1024], BF16, 3, "Vi", st)
                gas = kb.slots([128, TB], F32, 3, "ga", st)
                Wds = kb.slots([128, TB], BF16, 4, "Wd", st)
                x3s = kb.slots([128, 1024], F32, 1, "x3", st)

                def mkpool(idx):
                    return {"idx": list(idx), "i": 0}

                def pbank(pool, n=1):
                    L = len(pool["idx"])
                    while True:
                        i = pool["i"] % L
                        sel = pool["idx"][i:i + n]
                        if len(sel) == n and all(sel[k] == sel[0] + k for k in range(n)):
                            break
                        pool["i"] += 1
                    pool["i"] += n
                    b0 = sel[0]
                    return kb.PS[:, b0 * 512:(b0 + n) * 512], [kb.bank_bufs[bb] for bb in sel]
                poolR, poolM, poolG = mkpool([4, 5]), mkpool([6, 7]), mkpool([4, 5, 6, 7])
                bankR = lambda n=1: pbank(poolR, n)
                bankM = lambda n=1: pbank(poolM, n)
                bankG = lambda n=1: pbank(poolG, n)
                routed = {}

                def routing_gen(b):
                    r0 = b * TB
                    x2b, bx2b = x2ts.next()
                    kb.dma(sp, x2b[:], Z["X2"][r0:r0 + TB, :].rearrange("(t p) d -> p t d", p=128), bx2b, reads=[ZB["X2"]], writes=[bx2b])
                    xn3T, bxn3T = xn3Ts.next()
                    for tt in range(2):
                        norm_transpose(x2b[:, tt, :], bx2b, nfw, b_nfw, xn3T, bxn3T, tt * 128, work, bankfn=bankR)
                        yield
                    qpT, bqpT = qpTs.next()
                    for g in range(16):
                        pQ, bpQ = bankR(1)
                        for kc in range(8):
                            kb.op(pe, lambda: PE.matmul(pQ[:, 0:TB], lhsT=Wpq[:, kc, g * 128:(g + 1) * 128], rhs=xn3T[:, kc, :], start=(kc == 0), stop=(kc == 7)),
                                  reads=[b_Wpq, bxn3T], writes=bpQ)
                        kb.op(act, lambda: ACT.copy(out=qpT[:, g, :], in_=pQ[:, 0:TB]), reads=bpQ, writes=[bqpT])
                        yield
                    T3l = []
                    for tt in range(2):
                        ssb, bssb = ssbs.next()
                        for q4 in range(4):
                            pS, bpS = bankR(1)
                            for gg in range(4):
                                g = q4 * 4 + gg
                                kb.op(pe, lambda: PE.matmul(pS[:, gg * 128:(gg + 1) * 128], lhsT=qpT[:, g, tt * 128:(tt + 1) * 128], rhs=SKT[:, g, :], start=True, stop=True),
                                      reads=[bqpT, b_SKT], writes=bpS)
                            kb.op(act, lambda: ACT.copy(out=ssb[:, q4 * 4:(q4 + 1) * 4, :].rearrange("p g k -> p (g k)"), in_=pS), reads=bpS, writes=[bssb])
                            yield
                        v16, bv16 = v16s.next()
                        ix16, bix16 = ix16s.next()
                        for g in range(16):
                            src = ssb[:, g, :]
                            rep, brep = reps.next()
                            kb.op(dve, lambda: V.max(out=v16[:, g, 0:8], in_=src), reads=[bssb], writes=[bv16])
                            kb.op(dve, lambda: V.match_replace(out=rep[:, 0:128], in_to_replace=v16[:, g, 0:8], in_values=src, imm_value=NEG), reads=[bssb, bv16], writes=[brep])
                            kb.op(dve, lambda: V.max(out=v16[:, g, 8:16], in_=rep[:, 0:128]), reads=[brep], writes=[bv16])
                            kb.op(dve, lambda: V.max_index(out=ix16[:, g, 0:8], in_max=v16[:, g, 0:8], in_values=src), reads=[bssb, bv16], writes=[bix16])
                            kb.op(dve, lambda: V.max_index(out=ix16[:, g, 8:16], in_max=v16[:, g, 8:16], in_values=rep[:, 0:128]), reads=[brep, bv16], writes=[bix16])
                            yield
                        ixf, bixf = ixfs.next()
                        kb.op(dve, lambda: V.tensor_copy(out=ixf[:], in_=ix16[:]), reads=[bix16], writes=[bixf])
                        cand, bcand = cands.next()
                        v4 = v16[:].rearrange("p (h two) k -> p h two k", two=2)
                        kb.op(dve, lambda: V.tensor_tensor(out=cand[:].rearrange("p h (a b) -> p h a b", a=16),
                                                           in0=v4[:, :, 0, :].unsqueeze(3).to_broadcast([128, 8, 16, 16]),
                                                           in1=v4[:, :, 1, :].unsqueeze(2).to_broadcast([128, 8, 16, 16]), op=ALU.add), reads=[bv16], writes=[bcand])
                        yield
                        m2, bm2 = m2s.next()
                        p2, bp2 = p2s.next()
                        for h in range(8):
                            rep, brep = reps.next()
                            kb.op(dve, lambda: V.max(out=m2[:, h, 0:8], in_=cand[:, h, :]), reads=[bcand], writes=[bm2])
                            kb.op(dve, lambda: V.match_replace(out=rep[:], in_to_replace=m2[:, h, 0:8], in_values=cand[:, h, :], imm_value=NEG), reads=[bcand, bm2], writes=[brep])
                            kb.op(dve, lambda: V.max(out=m2[:, h, 8:16], in_=rep[:]), reads=[brep], writes=[bm2])
                            kb.op(dve, lambda: V.max_index(out=p2[:, h, 0:8], in_max=m2[:, h, 0:8], in_values=cand[:, h, :]), reads=[bcand, bm2], writes=[bp2])
                            kb.op(dve, lambda: V.max_index(out=p2[:, h, 8:16], in_max=m2[:, h, 8:16], in_values=rep[:]), reads=[brep, bm2], writes=[bp2])
                            yield
                        abu, babu = abs_.next()
                        abf, babf = abfs.next()
                        p2f = p2[:].rearrange("p h k -> p (h k)")
                        kb.op(dve, lambda: V.tensor_single_scalar(out=abu[:, 0, :], in_=p2f, scalar=4, op=ALU.logical_shift_right), reads=[bp2], writes=[babu])
                        kb.op(dve, lambda: V.tensor_single_scalar(out=abu[:, 1, :], in_=p2f, scalar=15, op=ALU.bitwise_and), reads=[bp2], writes=[babu])
                        kb.op(dve, lambda: V.tensor_copy(out=abf[:], in_=abu[:]), reads=[babu], writes=[babf])
                        yield
                        sel3, bsel3 = sel3s.next()
                        ix4 = ixf[:].rearrange("p (h two) k -> p h two k", two=2)
                        for w in range(2):
                            oh, boh = ohs.next()
                            kb.op(dve, lambda: V.tensor_tensor(out=oh[:], in0=abf[:, w, :].rearrange("p (h k) -> p h k", h=8).unsqueeze(3).to_broadcast([128, 8, 16, 16]),
                                                               in1=iota16[:].unsqueeze(1).unsqueeze(1).to_broadcast([128, 8, 16, 16]), op=ALU.is_equal),
                                  reads=[babf, b_iota16], writes=[boh])
                            yield
                            kb.op(dve, lambda: V.tensor_tensor(out=oh[:], in0=oh[:], in1=ix4[:, :, w, :].unsqueeze(2).to_broadcast([128, 8, 16, 16]), op=ALU.mult),
                                  reads=[boh, bixf], writes=[boh])
                            yield
                            kb.op(dve, lambda: V.tensor_reduce(out=sel3[:, w, :].rearrange("p (h k) -> p h k", h=8), in_=oh[:], axis=AX.X, op=ALU.add),
                                  reads=[boh], writes=[bsel3])
                            yield
                        z, bz = zs.next()
                        g3 = sel3[:, 2, :].rearrange("p (h k) -> p h k", h=8)
                        kb.op(dve, lambda: V.tensor_tensor(out=g3, in0=m2[:], in1=m2[:, :, 0:1].to_broadcast([128, 8, 16]), op=ALU.subtract), reads=[bm2], writes=[bsel3])
                        kb.op(act, lambda: ACT.activation(out=sel3[:, 2, :], in_=sel3[:, 2, :], func=AF.Exp), reads=[bsel3], writes=[bsel3])
                        kb.op(dve, lambda: V.tensor_reduce(out=z[:, :, 0], in_=g3, axis=AX.X, op=ALU.add), reads=[bsel3], writes=[bz])
                        kb.op(dve, lambda: V.reciprocal(out=z[:, :, 1], in_=z[:, :, 0]), reads=[bz], writes=[bz])
                        kb.op(dve, lambda: V.tensor_tensor(out=g3, in0=g3, in1=z[:, :, 1:2].to_broadcast([128, 8, 16]), op=ALU.mult), reads=[bsel3, bz], writes=[bsel3])
                        yield
                        pT, bpT = bankR(1)
                        for w in range(3):
                            kb.op(pe, lambda: PE.transpose(pT[:, w * 128:(w + 1) * 128], sel3[:, w, :], ident[:]), reads=[bsel3, b_ident], writes=bpT)
                        T3, bT3 = T3s.next()
                        kb.op(act, lambda: ACT.copy(out=T3[:].rearrange("p w t -> p (w t)"), in_=pT[:, 0:384]), reads=bpT, writes=[bT3])
                        T3l.append((T3, bT3))
                        yield
                    routed[b] = (x2b, bx2b, xn3T, bxn3T, T3l)

                def exhaust(gen):
                    if gen is not None:
                        for _ in gen:
                            pass

                pO0, bpO0 = kb.fixed_bank(0, 2)
                pO1, bpO1 = kb.fixed_bank(2, 2)
                pOs = ((pO0, bpO0), (pO1, bpO1))
                exhaust(routing_gen(0))
                for b in range(nb):
                    r0 = b * TB
                    x2b, bx2b, xn3T, bxn3T, T3l = routed.pop(b)
                    for tt in range(2):
                        T3, bT3 = T3l[tt]
                        for hf in range(16):
                            t0 = hf * 8
                            R_, bR = Rs.next()
                            L_, bL = L0s.next()
                            io_b = iota[:].unsqueeze(1).to_broadcast([128, 8, 128])
                            kb.op(dve, lambda: V.tensor_tensor(out=R_[:], in0=io_b, in1=T3[:, 1, t0:t0 + 8].unsqueeze(2).to_broadcast([128, 8, 128]), op=ALU.is_equal),
                                  reads=[b_iota, bT3], writes=[bR])
                            kb.op(dve, lambda: V.tensor_tensor(out=L_[:], in0=io_b, in1=T3[:, 0, t0:t0 + 8].unsqueeze(2).to_broadcast([128, 8, 128]), op=ALU.is_equal),
                                  reads=[b_iota, bT3], writes=[bL])
                            kb.op(dve, lambda: V.tensor_tensor(out=L_[:], in0=L_[:], in1=T3[:, 2, t0:t0 + 8].unsqueeze(2).to_broadcast([128, 8, 128]), op=ALU.mult),
                                  reads=[bL, bT3], writes=[bL])
                            for q in range(2):
                                pG, bpG = bankG(1)
                                for u in range(4):
                                    tl = q * 4 + u
                                    kb.op(pe, lambda: PE.matmul(pG[:, u * 128:(u + 1) * 128], lhsT=R_[:, tl, :], rhs=L_[:, tl, :], start=True, stop=True),
                                          reads=[bR, bL], writes=bpG)
                                tg = tt * 128 + t0 + q * 4
                                kb.op(act, lambda: ACT.copy(out=Gm[:, tg:tg + 4, :].rearrange("p t i -> p (t i)"), in_=pG), reads=bpG, writes=[b_Gm])
                    gen = routing_gen(b + 1) if b + 1 < nb else None

                    def emit_A(i):
                        UTi, bUTi = UTs.next()
                        Vi, bVi = Vis.next()
                        kb.dma(sp, UTi[:].rearrange("p k j -> p (k j)"), Z["UT"][i], bUTi, reads=[ZB["UT"]], writes=[bUTi])
                        kb.dma(sp, Vi[:], Z["VB"][i], bVi, reads=[ZB["VB"]], writes=[bVi])
                        pA, bpA = bankM(1)
                        for kc in range(8):
                            kb.op(pe, lambda: PE.matmul(pA[:, 0:TB], lhsT=UTi[:, kc, :], rhs=xn3T[:, kc, :], start=(kc == 0), stop=(kc == 7)),
                                  reads=[bUTi, bxn3T], writes=bpA)
                        ga, bga = gas.next()
                        kb.op(act, lambda: ACT.activation(out=ga[:], in_=pA[:, 0:TB], func=AF.Gelu), reads=bpA, writes=[bga])
                        Wd, bWd = Wds.next()
                        kb.op(dve, lambda: V.tensor_tensor(out=Wd[:], in0=ga[:], in1=Gm[:, :, i], op=ALU.mult), reads=[bga, b_Gm], writes=[bWd])
                        return (Wd, bWd, Vi, bVi)

                    def emit_O(i, cur):
                        Wd, bWd, Vi, bVi = cur
                        for tt in range(2):
                            pO, bpO = pOs[tt]
                            for n in range(2):
                                kb.op(pe, lambda: PE.matmul(pO[:, n * 512:(n + 1) * 512], lhsT=Wd[:, tt * 128:(tt + 1) * 128], rhs=Vi[:, n * 512:(n + 1) * 512],
                                                            start=(i == 0), stop=(i == NE - 1)), reads=[bWd, bVi], writes=[bpO[n]])
                    LAP = 2
                    pendA = []
                    for i in range(NE):
                        pendA.append((i, emit_A(i)))
                        if len(pendA) > LAP:
                            j, cur = pendA.pop(0)
                            emit_O(j, cur)
                        if gen is not None:
                            next(gen, None)
                    while pendA:
                        j, cur = pendA.pop(0)
                        emit_O(j, cur)
                    exhaust(gen)
                    for tt in range(2):
                        pO, bpO = pOs[tt]
                        x3, bx3 = x3s.next()
                        kb.op(dve, lambda: V.tensor_tensor(out=x3[:], in0=pO, in1=x2b[:, tt, :], op=ALU.add), reads=bpO + [bx2b], writes=[bx3])
                        junk, b_junk = work["junk"].next()
                        ss, b_ss = work["ss"].next()
                        kb.op(act, lambda: ACT.activation(out=junk[:], in_=x3[:], func=AF.Square, accum_out=ss[:, 0:1]), reads=[bx3], writes=[b_junk, b_ss])
                        kb.op(act, lambda: ACT.activation(out=ss[:, 1:2], in_=ss[:, 0:1], func=AF.Sqrt, scale=1.0 / D, bias=epsn[:, 0:1]), reads=[b_ss, b_epsn], writes=[b_ss])
                        kb.op(dve, lambda: V.reciprocal(out=ss[:, 2:3], in_=ss[:, 1:2]), reads=[b_ss], writes=[b_ss])
                        kb.op(dve, lambda: V.scalar_tensor_tensor(out=x3[:], in0=x3[:], scalar=ss[:, 2:3], in1=fnw[:], op0=ALU.mult, op1=ALU.mult),
                              reads=[bx3, b_ss, b_fnw], writes=[bx3])
                        kb.dma(pool, Y[r0 + tt * 128:r0 + (tt + 1) * 128, :], x3[:], bx3, reads=[bx3])
                kb.barrier()

        if prepass:
            kb.begin_phase()
            prepass_peer()
            kb.end_phase()
        nph = 0
        for si, S in enumerate(S_list):
            for ph in (phase1, phase2, phase3a, phase3b):
                if nph >= upto:
                    break
                nph += 1
                kb.set_banks(range(8))
                kb.begin_phase()
                ph(si, S)
                kb.end_phase()
        kb.barrier()
        kb.stats = {E.name: E.n for E in kb.engs}
        kb.stats["nsem"] = kb.nsem
        kb.stats["nops"] = kb.nops
        build.last_stats = kb.stats
    return nc


_CACHE = {}


def kernel(**inputs):
    ncores = 8
    S0 = inputs["x_prompt"].shape[1]
    S1 = inputs["x_sample"].shape[1]
    key = (S0, S1)
    if key not in _CACHE:
        _CACHE[key] = build([S0, S1])
    nc = _CACHE[key]
    consts = host_consts(max(S0, S1))
    in_maps = []
    for c in range(ncores):
        m = {"x0": np.ascontiguousarray(inputs["x_prompt"][c], dtype=np.float32),
             "x1": np.ascontiguousarray(inputs["x_sample"][c], dtype=np.float32),
             "mem0": np.ascontiguousarray(inputs["mem_prompt"][c], dtype=np.float32),
             "mem1": np.ascontiguousarray(inputs["mem_sample"][c], dtype=np.float32)}
        for n in W_NAMES:
            m[n] = np.ascontiguousarray(inputs[n], dtype=np.float32)
        m.update(consts)
        in_maps.append(m)
    res = run_bass_kernel_spmd(nc, in_maps, core_ids=list(range(ncores)))
    y0 = np.stack([np.asarray(res.results[c]["y0"], dtype=np.float32) for c in range(ncores)], 0)
    y1 = np.stack([np.asarray(res.results[c]["y1"], dtype=np.float32) for c in range(ncores)], 0)
    return (y0, y1)
```
